# Optimizing a Trainium2 kernel written in Bass

```python
import math
import jax, jax.numpy as jnp
from jax import lax
import numpy as np

D_MODEL = 1024
BATCH = 16
SEQ = 2048
DEPTH = 1

N_META = 16
GRID_W = 64
D_FF = 2816
SSM_WIDTH = D_MODEL // 2
SSM_GROUP = 16
SSM_GROUPS = SSM_WIDTH // SSM_GROUP
SSM_STATE = 64
NA_HEAD_DIM = 64
NA_WIDTH = D_MODEL // 2
NA_HEADS = NA_WIDTH // NA_HEAD_DIM
WIN_H = 8
WIN_W = 16
PROJ_SPLITS = (SSM_WIDTH, SSM_WIDTH + NA_WIDTH, SSM_WIDTH + 2 * NA_WIDTH,
               SSM_WIDTH + 3 * NA_WIDTH, SSM_WIDTH + 3 * NA_WIDTH + D_MODEL)
PROJ_WIDTH = SSM_WIDTH + 3 * NA_WIDTH + 2 * D_MODEL
RMS_EPS = 1e-6
DT_MIN = 1e-3
DT_MAX = 1e-1
A_RE_MAX = -1e-4
MASK_VALUE = -1e30

kernel_name = "hybrid_s5_natten_macaron_encoder"


def rms_norm(x, g):
    x32 = x.astype(jnp.float32)
    y = x32 * lax.rsqrt(jnp.mean(x32 * x32, axis=-1, keepdims=True) + RMS_EPS)
    return (y * g.astype(jnp.float32)).astype(x.dtype)


def swiglu_ffn(x, w_in, w_out):
    gate, up = jnp.split(x @ w_in, 2, axis=-1)
    return (jax.nn.silu(gate) * up) @ w_out


def _complex_linear_combine(e1, e2):
    a1r, a1i, b1r, b1i = e1
    a2r, a2i, b2r, b2i = e2
    ar = a2r * a1r - a2i * a1i
    ai = a2r * a1i + a2i * a1r
    br = a2r * b1r - a2i * b1i + b2r
    bi = a2r * b1i + a2i * b1r + b2i
    return (ar, ai, br, bi)


def s5_direction(u_g, a_re, a_im, log_dt, b_re, b_im, reverse):
    dt = jnp.exp(log_dt.astype(jnp.float32))[:, None]
    a_re = jnp.minimum(a_re.astype(jnp.float32), A_RE_MAX)
    a_im = a_im.astype(jnp.float32)
    mag = jnp.exp(a_re * dt)
    lb_re = mag * jnp.cos(a_im * dt)
    lb_im = mag * jnp.sin(a_im * dt)
    den = a_re * a_re + a_im * a_im
    nr = lb_re - 1.0
    ni = lb_im
    f_re = (nr * a_re + ni * a_im) / den
    f_im = (ni * a_re - nr * a_im) / den
    b_re = b_re.astype(jnp.float32)
    b_im = b_im.astype(jnp.float32)
    bb_re = f_re[..., None] * b_re - f_im[..., None] * b_im
    bb_im = f_re[..., None] * b_im + f_im[..., None] * b_re
    bu_re = jnp.einsum('btgp,gnp->btgn', u_g, bb_re)
    bu_im = jnp.einsum('btgp,gnp->btgn', u_g, bb_im)
    seq_len = u_g.shape[1]
    a_r = jnp.broadcast_to(lb_re[None, None], (1, seq_len) + lb_re.shape)
    a_i = jnp.broadcast_to(lb_im[None, None], (1, seq_len) + lb_im.shape)
    _, _, h_re, h_im = lax.associative_scan(
        _complex_linear_combine, (a_r, a_i, bu_re, bu_im), axis=1, reverse=reverse)
    return h_re, h_im


def s5_mixer(u, a_re_f, a_im_f, log_dt_f, a_re_b, a_im_b, log_dt_b,
             b_re, b_im, c_re, c_im, d, w_glu):
    bsz, seq_len, _ = u.shape
    u_g = u.astype(jnp.float32).reshape(bsz, seq_len, SSM_GROUPS, SSM_GROUP)
    hf_re, hf_im = s5_direction(u_g, a_re_f, a_im_f, log_dt_f, b_re, b_im, False)
    hb_re, hb_im = s5_direction(u_g, a_re_b, a_im_b, log_dt_b, b_re, b_im, True)
    h_re = hf_re + hb_re
    h_im = hf_im + hb_im
    y = (jnp.einsum('btgn,gpn->btgp', h_re, c_re.astype(jnp.float32))
         - jnp.einsum('btgn,gpn->btgp', h_im, c_im.astype(jnp.float32))
         + d.astype(jnp.float32) * u_g)
    z = jax.nn.gelu(y.reshape(bsz, seq_len, SSM_WIDTH))
    z = z * jax.nn.sigmoid(z @ w_glu.astype(jnp.float32))
    return z.astype(u.dtype)


def neighbourhood_attention(q, k, v, rpb):
    bsz, seq_len, _ = q.shape
    n_real = seq_len - N_META
    rows = n_real // GRID_W
    kh = min(WIN_H, rows)
    scale = NA_HEAD_DIM ** -0.5
    q = q * scale

    def meta_heads(t):
        return t[:, :N_META].reshape(bsz, N_META, NA_HEADS, NA_HEAD_DIM).transpose(0, 2, 1, 3)

    def grid_heads(t):
        return t[:, N_META:].reshape(bsz, rows, GRID_W, NA_HEADS, NA_HEAD_DIM).transpose(0, 3, 1, 2, 4)

    q_m, k_m, v_m = meta_heads(q), meta_heads(k), meta_heads(v)
    q_g, k_g, v_g = grid_heads(q), grid_heads(k), grid_heads(v)

    cols = jnp.arange(GRID_W)
    col_start = jnp.clip(cols - WIN_W // 2, 0, GRID_W - WIN_W)
    col_valid = (cols[None, :] >= col_start[:, None]) & (cols[None, :] < col_start[:, None] + WIN_W)
    col_idx = jnp.clip(cols[None, :] - cols[:, None] + WIN_W - 1, 0, 2 * WIN_W - 2)
    rpb_c = jnp.take(rpb.astype(jnp.float32), col_idx, axis=2)
    rpb_c = jnp.where(col_valid[None, None], rpb_c, MASK_VALUE)

    def row_block(r):
        row_start = jnp.clip(r - kh // 2, 0, rows - kh)
        k_rows = lax.dynamic_slice_in_dim(k_g, row_start, kh, axis=2)
        v_rows = lax.dynamic_slice_in_dim(v_g, row_start, kh, axis=2)
        q_r = lax.dynamic_index_in_dim(q_g, r, axis=2, keepdims=False)
        row_idx = row_start + jnp.arange(kh) - r + WIN_H - 1
        bias = jnp.take(rpb_c, row_idx, axis=1).transpose(0, 2, 1, 3)
        s_win = jnp.einsum('bhqd,bhrkd->bhqrk', q_r, k_rows).astype(jnp.float32) + bias[None]
        s_meta = jnp.einsum('bhqd,bhmd->bhqm', q_r, k_m).astype(jnp.float32)
        s = jnp.concatenate([s_win.reshape(bsz, NA_HEADS, GRID_W, kh * GRID_W), s_meta], axis=-1)
        p = jax.nn.softmax(s, axis=-1).astype(v.dtype)
        p_win = p[..., :kh * GRID_W].reshape(bsz, NA_HEADS, GRID_W, kh, GRID_W)
        p_meta = p[..., kh * GRID_W:]
        return (jnp.einsum('bhqrk,bhrkd->bhqd', p_win, v_rows)
                + jnp.einsum('bhqm,bhmd->bhqd', p_meta, v_m))

    o_grid = lax.map(row_block, jnp.arange(rows))
    o_real = o_grid.transpose(1, 0, 3, 2, 4).reshape(bsz, n_real, NA_WIDTH)

    s_mm = jnp.einsum('bhmd,bhnd->bhmn', q_m, k_m).astype(jnp.float32)
    p_mm = jax.nn.softmax(s_mm, axis=-1).astype(v.dtype)
    o_meta = jnp.einsum('bhmn,bhnd->bhmd', p_mm, v_m).transpose(0, 2, 1, 3).reshape(bsz, N_META, NA_WIDTH)
    return jnp.concatenate([o_meta, o_real], axis=1)


def hybrid_mixer(hn, w_in, a_re_f, a_im_f, log_dt_f, a_re_b, a_im_b, log_dt_b,
                 b_re, b_im, c_re, c_im, d, w_glu, rpb, w_branch_ssm, w_branch_na, w_out):
    proj = hn @ w_in
    u, q, k, v, g_ssm, g_na = jnp.split(proj, PROJ_SPLITS, axis=-1)
    y_ssm = s5_mixer(u, a_re_f, a_im_f, log_dt_f, a_re_b, a_im_b, log_dt_b,
                     b_re, b_im, c_re, c_im, d, w_glu) @ w_branch_ssm
    y_na = neighbourhood_attention(q, k, v, rpb) @ w_branch_na
    merged = jax.nn.sigmoid(g_ssm) * y_ssm + jax.nn.sigmoid(g_na) * y_na
    return merged @ w_out


def setup_inputs(seed: int = 0) -> dict:
    key = jax.random.key(seed)
    ks = iter(jax.random.split(key, 32))

    def nrm(shape, scale):
        return jax.random.normal(next(ks), shape, jnp.float32) * scale

    def gain(shape):
        return 1.0 + nrm(shape, 0.02)

    L, G, N, P = DEPTH, SSM_GROUPS, SSM_STATE, SSM_GROUP
    x = nrm((BATCH, SEQ, D_MODEL), 1.0)
    meta_tokens = nrm((N_META, D_MODEL), 1.0)
    norm_ffn1 = gain((L, D_MODEL))
    w_ffn1_in = nrm((L, D_MODEL, 2 * D_FF), D_MODEL ** -0.5)
    w_ffn1_out = nrm((L, D_FF, D_MODEL), D_FF ** -0.5)
    norm_mix = gain((L, D_MODEL))
    w_in = nrm((L, D_MODEL, PROJ_WIDTH), D_MODEL ** -0.5)
    n_idx = jnp.arange(N, dtype=jnp.float32)
    ssm_a_re_fwd = -0.5 + nrm((L, G, N), 0.01)
    ssm_a_im_fwd = math.pi * n_idx + nrm((L, G, N), 0.01)
    ssm_log_dt_fwd = jax.random.uniform(next(ks), (L, G), jnp.float32, math.log(DT_MIN), math.log(DT_MAX))
    ssm_a_re_bwd = -0.5 + nrm((L, G, N), 0.01)
    ssm_a_im_bwd = math.pi * n_idx + nrm((L, G, N), 0.01)
    ssm_log_dt_bwd = jax.random.uniform(next(ks), (L, G), jnp.float32, math.log(DT_MIN), math.log(DT_MAX))
    ssm_b_re = nrm((L, G, N, P), (2 * P) ** -0.5)
    ssm_b_im = nrm((L, G, N, P), (2 * P) ** -0.5)
    ssm_c_re = nrm((L, G, P, N), N ** -0.5)
    ssm_c_im = nrm((L, G, P, N), N ** -0.5)
    ssm_d = nrm((L, G, P), 1.0)
    w_glu = nrm((L, SSM_WIDTH, SSM_WIDTH), SSM_WIDTH ** -0.5)
    na_rpb = nrm((L, NA_HEADS, 2 * WIN_H - 1, 2 * WIN_W - 1), 0.02)
    w_branch_ssm = nrm((L, SSM_WIDTH, D_MODEL), SSM_WIDTH ** -0.5)
    w_branch_na = nrm((L, NA_WIDTH, D_MODEL), NA_WIDTH ** -0.5)
    w_out = nrm((L, D_MODEL, D_MODEL), D_MODEL ** -0.5)
    norm_ffn2 = gain((L, D_MODEL))
    w_ffn2_in = nrm((L, D_MODEL, 2 * D_FF), D_MODEL ** -0.5)
    w_ffn2_out = nrm((L, D_FF, D_MODEL), D_FF ** -0.5)
    norm_final = gain((D_MODEL,))
    return {"x": x, "meta_tokens": meta_tokens,
            "norm_ffn1": norm_ffn1, "w_ffn1_in": w_ffn1_in, "w_ffn1_out": w_ffn1_out,
            "norm_mix": norm_mix, "w_in": w_in,
            "ssm_a_re_fwd": ssm_a_re_fwd, "ssm_a_im_fwd": ssm_a_im_fwd, "ssm_log_dt_fwd": ssm_log_dt_fwd,
            "ssm_a_re_bwd": ssm_a_re_bwd, "ssm_a_im_bwd": ssm_a_im_bwd, "ssm_log_dt_bwd": ssm_log_dt_bwd,
            "ssm_b_re": ssm_b_re, "ssm_b_im": ssm_b_im, "ssm_c_re": ssm_c_re, "ssm_c_im": ssm_c_im,
            "ssm_d": ssm_d, "w_glu": w_glu, "na_rpb": na_rpb,
            "w_branch_ssm": w_branch_ssm, "w_branch_na": w_branch_na, "w_out": w_out,
            "norm_ffn2": norm_ffn2, "w_ffn2_in": w_ffn2_in, "w_ffn2_out": w_ffn2_out,
            "norm_final": norm_final}


def reference(x, meta_tokens, norm_ffn1, w_ffn1_in, w_ffn1_out, norm_mix, w_in,
              ssm_a_re_fwd, ssm_a_im_fwd, ssm_log_dt_fwd, ssm_a_re_bwd, ssm_a_im_bwd, ssm_log_dt_bwd,
              ssm_b_re, ssm_b_im, ssm_c_re, ssm_c_im, ssm_d, w_glu, na_rpb,
              w_branch_ssm, w_branch_na, w_out, norm_ffn2, w_ffn2_in, w_ffn2_out, norm_final):
    bsz = x.shape[0]
    meta = jnp.broadcast_to(meta_tokens[None].astype(x.dtype), (bsz, N_META, x.shape[-1]))
    h = jnp.concatenate([meta, x], axis=1)
    for l in range(DEPTH):
        h = h + 0.5 * swiglu_ffn(rms_norm(h, norm_ffn1[l]), w_ffn1_in[l], w_ffn1_out[l])
        h = h + hybrid_mixer(rms_norm(h, norm_mix[l]), w_in[l],
                             ssm_a_re_fwd[l], ssm_a_im_fwd[l], ssm_log_dt_fwd[l],
                             ssm_a_re_bwd[l], ssm_a_im_bwd[l], ssm_log_dt_bwd[l],
                             ssm_b_re[l], ssm_b_im[l], ssm_c_re[l], ssm_c_im[l], ssm_d[l],
                             w_glu[l], na_rpb[l], w_branch_ssm[l], w_branch_na[l], w_out[l])
        h = h + 0.5 * swiglu_ffn(rms_norm(h, norm_ffn2[l]), w_ffn2_in[l], w_ffn2_out[l])
    h = rms_norm(h, norm_final)
    return h[:, N_META:]
```

```python
import numpy as np
from contextlib import ExitStack
import concourse.bass as bass
import concourse.mybir as mybir
from concourse.bass_utils import run_bass_kernel_spmd

F32 = mybir.dt.float32
BF16 = mybir.dt.bfloat16
ALU = mybir.AluOpType
AF = mybir.ActivationFunctionType

D = 1024
T = 2048
NM = 16
DFF = 2816
NFF = 22
NG = 32
NBLK = 258
NB_LOCAL = 2
NCORES = 8
FF_SPLITS = [(0, 8), (8, 15), (15, 22)]
NSLOT = 2
TWO_PI = 6.283185307179586


class Ctr:
    def __init__(self, sem):
        self.sem = sem
        self.cnt = 0


class Buf:
    __slots__ = ("w", "r", "name")

    def __init__(self, name=""):
        self.w = None
        self.r = {}
        self.name = name


class Eng:
    def __init__(self, h, ctr, name):
        self.h = h
        self.ctr = ctr
        self.seen = {}
        self.name = name


class KB:
    def __init__(self, nc, es):
        self.nc = nc
        self.es = es

        def mk(h, n):
            return Eng(h, Ctr(es.enter_context(nc.semaphore(n))), n)

        self.pe = mk(nc.tensor, "s_pe")
        self.act = mk(nc.scalar, "s_act")
        self.dve = mk(nc.vector, "s_dve")
        self.pool = mk(nc.gpsimd, "s_pool")
        self.sp = mk(nc.sync, "s_sp")
        self.engs = [self.pe, self.act, self.dve, self.pool, self.sp]

    def ctr(self, name):
        return Ctr(self.es.enter_context(self.nc.semaphore(name)))

    def _need(self, E, dep, raw):
        ctr, val = dep
        if ctr is E.ctr and not raw and E is self.pe:
            return
        if E.seen.get(id(ctr), 0) >= val:
            return
        E.h.wait_ge(ctr.sem, val)
        E.seen[id(ctr)] = val

    def _deps(self, E, reads, writes):
        for b in reads:
            if b.w is not None:
                self._need(E, b.w, True)
        for b in writes:
            if b.w is not None:
                self._need(E, b.w, False)
            for dep in b.r.values():
                self._need(E, dep, False)

    def op(self, E, fn, reads=(), writes=()):
        self._deps(E, reads, writes)
        ins = fn()
        E.ctr.cnt += 1
        ins.then_inc(E.ctr.sem, 1)
        for b in writes:
            b.w = (E.ctr, E.ctr.cnt)
            b.r = {}
        for b in reads:
            b.r[id(E.ctr)] = (E.ctr, E.ctr.cnt)

    def dma(self, Q, ctr, out, in_, reads=(), writes=()):
        self._deps(Q, reads, writes)
        Q.h.dma_start(out=out, in_=in_).then_inc(ctr.sem, 16)
        ctr.cnt += 16
        for b in writes:
            b.w = (ctr, ctr.cnt)
            b.r = {}
        for b in reads:
            b.r[id(ctr)] = (ctr, ctr.cnt)

    def barrier(self):
        for E in self.engs:
            for O in self.engs:
                if O is not E and O.ctr.cnt > 0:
                    self._need(E, (O.ctr, O.ctr.cnt), True)

    def wait_all(self, E, ctrs):
        for c in ctrs:
            if c.cnt > 0:
                self._need(E, (c, c.cnt), True)


class Stream:
    def __init__(self, kb, nc, es, nslot, cols):
        self.kb = kb
        self.items = []
        self.next_load = 0
        self.next_get = 0
        self.nslot = nslot
        self.slots = [es.enter_context(nc.sbuf_tensor(f"ring{i}", [128, cols], BF16)) for i in range(nslot)]
        self.bufs = [Buf(f"ring{i}") for i in range(nslot)]
        self.ctrs = [kb.ctr(f"s_ring{i}") for i in range(nslot)]

    def extend(self, items):
        self.items += items

    def _load(self, n):
        s = n % self.nslot
        src, ncols = self.items[n]
        self.kb.dma(self.kb.pool, self.ctrs[s], self.slots[s][:, 0:ncols], src, writes=[self.bufs[s]])

    def get(self):
        i = self.next_get
        self.next_get += 1
        while self.next_load < min(len(self.items), i + self.nslot):
            self._load(self.next_load)
            self.next_load += 1
        s = i % self.nslot
        return self.slots[s], self.bufs[s]


def sap(t, part0, nparts, off, dims):
    row = 1
    for d in t.shape[1:]:
        row *= d
    return bass.AP(t, part0 * row + off, [[row, nparts]] + [list(d) for d in dims])


def build_program(nb_local=NB_LOCAL, stage=99, dbg=False):
    nc = bass.Bass("TRN2", target_bir_lowering=False)

    def din(name, shape, dt=F32):
        return nc.dram_tensor(name, list(shape), dt, kind="ExternalInput").ap()

    xT = din("xT", [nb_local, 8, 128, T])
    metaT = din("metaT", [128, 8, NM])
    gains_d = din("gains", [128, 4, 8])
    w1i = din("w1i", [NFF, 128, 8 * 256])
    w1o = din("w1o", [8, 128, NFF * 128])
    w2i = din("w2i", [NFF, 128, 8 * 256])
    w2o = din("w2o", [8, 128, NFF * 128])
    consts_d = din("consts", [128, 3, 128])
    wx = din("wx", [32, 128, 1024])
    wv2 = din("wv2", [2, 128, 2048])
    wglu_d = din("wglu", [4, 128, 512])
    wbs_d = din("wbs", [8, 128, 512])
    wbn_d = din("wbn", [8, 128, 512])
    wo_d = din("wo", [8, 128, 1024])
    rpbpad = din("rpbpad", [8, 15, 128])
    colmask_d = din("colmask", [128, 64])
    s5c_d = din("s5c", [128, 3, 32])
    s5cc_d = din("s5cc", [128, 2, 512])
    s5cb_d = din("s5cb", [128, 2, 512])
    s5ce_d = din("s5ce", [128, 4, 8])
    s5d_d = din("s5d", [128, 32])
    s5m_d = din("s5m", [128, 3, 128])
    u_scr = nc.dram_tensor("u_scr", [512, 8, NBLK], BF16, kind="Internal").ap()
    z_scr = nc.dram_tensor("z_scr", [512, 8, NBLK], BF16, kind="Internal").ap()
    u_scr2 = nc.dram_tensor("u_scr2", [128, NG, NBLK], BF16, kind="Internal").ap()
    z_scr2 = nc.dram_tensor("z_scr2", [128, NG, NBLK], BF16, kind="Internal").ap()
    outT = nc.dram_tensor("outT", [nb_local, 8, 128, T], F32, kind="ExternalOutput").ap()
    dbg_outs = {}

    with ExitStack() as es:
        kb = KB(nc, es)
        pe, act, dve, pool, sp = kb.pe, kb.act, kb.dve, kb.pool, kb.sp

        def sb(name, shape, dt):
            return es.enter_context(nc.sbuf_tensor(name, list(shape), dt))

        uid = [0]

        def uname(n):
            uid[0] += 1
            return f"{n}_{uid[0]}"


        Etab = sb("Etab", [128, 4, 15, 64], BF16)
        Etabb = Buf()
        kmT = sb("kmT", [128, 4, NM], BF16)
        kmTb = Buf()
        vm = sb("vm", [128, 4, 64], BF16)
        vmb = Buf()
        umT = sb("umT", [128, 4, 8, 2], BF16)
        umTb = Buf()
        WB = sb("WB", [128, NG, 2, 128], BF16)
        WBb = Buf()
        WC = sb("WC", [128, NG, 2, 128], BF16)
        WCb = Buf()
        WT = sb("WT", [128, NG, 128], BF16)
        WTb = Buf()
        scA = sb("scA", [128, 2, NG], F32)
        scB = sb("scB", [128, 2, NG], F32)
        scb = Buf()
        u_scr_b = Buf()
        z_scr_b = Buf()
        u2b = [Buf() for j in range(8)]
        zsb = [Buf() for j in range(8)]
        c_scr = kb.ctr("s_scr")

        hb = [[Buf(f"h{k}_{c}") for c in range(4)] for k in range(8)]
        hmb = [Buf(f"hm{k}") for k in range(8)]
        gains = sb("gains_sb", [128, 4, 8], F32)
        gainsb = Buf("gains")
        cst = sb("cst", [128, 3, 128], BF16)
        cstb = Buf("cst")
        epsb_t = sb("eps", [128, 1], F32)
        epsb = Buf("eps")
        pb = [es.enter_context(nc.psum_tensor(f"pb{i}", [128, 512], F32)) for i in range(8)]
        pbb = [Buf(f"pb{i}") for i in range(8)]
        ident = cst[:, 0, :]
        ones_bf = cst[:, 1, :]
        blockones = cst[:, 2, :]

        c_misc = kb.ctr("s_misc")
        c_x = kb.ctr("s_x")
        c_out = kb.ctr("s_out")

        kb.dma(sp, c_misc, gains[:, :, :], gains_d[:, :, :], writes=[gainsb])
        c_cst = kb.ctr("s_cst")
        kb.dma(pool, c_cst, cst[:, :, :], consts_d[:, :, :], writes=[cstb])
        kb.op(dve, lambda: nc.vector.memset(epsb_t[:, :], 1e-6), writes=[epsb])

        def load_x(bl):
            for k in range(8):
                kb.dma(sp, c_x, hT[:, k, :], xT[bl, k, :, :], writes=hb[k])
            for k in range(8):
                for c in range(4):
                    hb[k][c].w = (c_x, c_x.cnt)

        def main_chunks(xn, xnb, at, atb):
            chs = []
            for c in range(4):
                chs.append(dict(
                    N=512,
                    h=(lambda k, c=c: hT[:, k, c * 512:(c + 1) * 512]),
                    hb=(lambda k, c=c: hb[k][c]),
                    xn=(lambda k, c=c: xn[:, k, c * 512:(c + 1) * 512]),
                    xnb=(lambda k, c=c: xnb[k][c]),
                    at=(lambda kk, c=c: at[:, kk, c * 512:(c + 1) * 512]),
                    atb=(lambda kk, c=c: atb[kk][c]),
                ))
            return chs

        def norm_chunk(ch, gi, scr, out_fn=None, outb_fn=None):
            N = ch["N"]
            sq, sqb, sd, sdb, rstd, rstdb = scr
            pR, pRb = pb[7], pbb[7]
            for k in range(8):
                kb.op(act, lambda: nc.scalar.activation(out=sq[k % 2][:, :N], in_=ch["h"](k), func=AF.Square),
                      reads=[ch["hb"](k)], writes=[sqb[k % 2]])
                kb.op(pe, lambda: nc.tensor.matmul(pR[:, :N], lhsT=ones_bf, rhs=sq[k % 2][:, :N],
                                                   start=(k == 0), stop=(k == 7)),
                      reads=[sqb[k % 2], cstb], writes=[pRb])
            kb.op(act, lambda: nc.scalar.activation(out=sd[:, :N], in_=pR[:, :N], func=AF.Sqrt,
                                                    scale=1.0 / D, bias=epsb_t[:, 0:1]),
                  reads=[pRb, epsb], writes=[sdb])
            kb.op(dve, lambda: nc.vector.reciprocal(out=rstd[:, :N], in_=sd[:, :N]), reads=[sdb], writes=[rstdb])
            for k in range(8):
                o = ch["xn"](k) if out_fn is None else out_fn(k)
                ob = ch["xnb"](k) if outb_fn is None else outb_fn(k)
                kb.op(dve, lambda: nc.vector.scalar_tensor_tensor(
                    out=o, in0=ch["h"](k), scalar=gains[:, gi, k:k + 1], in1=rstd[:, :N],
                    op0=ALU.mult, op1=ALU.mult),
                    reads=[ch["hb"](k), rstdb, gainsb], writes=[ob])

        def ffn_items(wi, wo):
            items = []
            for (a, b) in FF_SPLITS:
                for m in range(a, b):
                    items.append((wi[m, :, :], 2048))
                for mo in range(8):
                    items.append((wo[mo, :, a * 128:b * 128], (b - a) * 128))
            return items

        def ffn(chunks, gi, scr, sg, sgb):
            for ch in chunks:
                norm_chunk(ch, gi, scr)
            gu = 0
            oi = 0
            for (a, b) in FF_SPLITS:
                nk = b - a
                for m in range(a, b):
                    ws, wb = stream.get()
                    for ch in chunks:
                        N = ch["N"]
                        pG, pGb = pb[gu % 2], pbb[gu % 2]
                        pU, pUb = pb[2 + gu % 2], pbb[2 + gu % 2]
                        for k in range(8):
                            kb.op(pe, lambda: nc.tensor.matmul(pG[:, :N], lhsT=ws[:, k * 256:k * 256 + 128],
                                                               rhs=ch["xn"](k), start=(k == 0), stop=(k == 7)),
                                  reads=[wb, ch["xnb"](k)], writes=[pGb])
                        for k in range(8):
                            kb.op(pe, lambda: nc.tensor.matmul(pU[:, :N], lhsT=ws[:, k * 256 + 128:k * 256 + 256],
                                                               rhs=ch["xn"](k), start=(k == 0), stop=(k == 7)),
                                  reads=[wb, ch["xnb"](k)], writes=[pUb])
                        s_ = sg[gu % 2]
                        kb.op(act, lambda: nc.scalar.activation(out=s_[:, :N], in_=pG[:, :N], func=AF.Silu),
                              reads=[pGb], writes=[sgb[gu % 2]])
                        kb.op(dve, lambda: nc.vector.tensor_tensor(out=ch["at"](m - a), in0=s_[:, :N], in1=pU[:, :N],
                                                                   op=ALU.mult),
                              reads=[sgb[gu % 2], pUb], writes=[ch["atb"](m - a)])
                        gu += 1
                for mo in range(8):
                    ws, wb = stream.get()
                    for ch in chunks:
                        N = ch["N"]
                        pO, pOb = pb[4 + oi % 2], pbb[4 + oi % 2]
                        for kk in range(nk):
                            kb.op(pe, lambda: nc.tensor.matmul(pO[:, :N], lhsT=ws[:, kk * 128:(kk + 1) * 128],
                                                               rhs=ch["at"](kk), start=(kk == 0), stop=(kk == nk - 1)),
                                  reads=[wb, ch["atb"](kk)], writes=[pOb])
                        hk = ch["h"](mo)
                        kb.op(dve, lambda: nc.vector.scalar_tensor_tensor(out=hk, in0=pO[:, :N], scalar=0.5, in1=hk,
                                                                          op0=ALU.mult, op1=ALU.add),
                              reads=[pOb, ch["hb"](mo)], writes=[ch["hb"](mo)])
                        oi += 1

        def ffn_phase(bl, gi, with_meta):
            with ExitStack() as ph:
                def psb(name, shape, dt):
                    return ph.enter_context(nc.sbuf_tensor(uname(name), list(shape), dt))
                xn = psb("xn", [128, 8, T], BF16)
                xnb = [[Buf() for c in range(4)] for k in range(8)]
                at = psb("at", [128, 8, T], BF16)
                atb = [[Buf() for c in range(4)] for k in range(8)]
                sq = [psb(f"sq{i}", [128, 512], BF16) for i in range(2)]
                sqb = [Buf(), Buf()]
                sd = psb("sd", [128, 512], F32)
                rstd = psb("rstd", [128, 512], F32)
                sg = [psb(f"sg{i}", [128, 512], BF16) for i in range(2)]
                sgb = [Buf(), Buf()]
                scr = (sq, sqb, sd, Buf(), rstd, Buf())
                chunks = main_chunks(xn, xnb, at, atb)
                if with_meta:
                    xnm = psb("xnm", [128, 8, NM], BF16)
                    xnmb = [Buf() for k in range(8)]
                    atm = psb("atm", [128, 8, NM], BF16)
                    atmb = [Buf() for k in range(8)]
                    chunks.append(dict(N=NM, h=lambda k: hm[:, k, :], hb=lambda k: hmb[k],
                                       xn=lambda k: xnm[:, k, :], xnb=lambda k: xnmb[k],
                                       at=lambda kk: atm[:, kk, :], atb=lambda kk: atmb[kk]))
                ffn(chunks, gi, scr, sg, sgb)
                kb.barrier()

        def final_phase(bl):
            with ExitStack() as ph:
                def psb(name, shape, dt):
                    return ph.enter_context(nc.sbuf_tensor(uname(name), list(shape), dt))
                sq = [psb(f"fsq{i}", [128, 512], BF16) for i in range(2)]
                sd = psb("fsd", [128, 512], F32)
                rstd = psb("frstd", [128, 512], F32)
                scr = (sq, [Buf(), Buf()], sd, Buf(), rstd, Buf())
                chunks = main_chunks(None, None, None, None)
                for c, ch in enumerate(chunks):
                    norm_chunk(ch, 3, scr, out_fn=ch["h"], outb_fn=ch["hb"])
                for k in range(8):
                    kb.dma(sp, c_out, outT[bl, k, :, :], hT[:, k, :], reads=hb[k])
                kb.barrier()

        def dump(name, ap, shape, reads):
            t = nc.dram_tensor("dbg_" + name, list(shape), ap.dtype, kind="ExternalOutput").ap()
            dbg_outs[name] = t
            kb.dma(sp, c_out, t, ap, reads=reads)

        def psbf(ph):
            def f(name, shape, dt):
                return ph.enter_context(nc.sbuf_tensor(uname(name), list(shape), dt))
            return f

        def V(fn, reads, writes):
            kb.op(dve, fn, reads=reads, writes=writes)

        def A(fn, reads, writes):
            kb.op(act, fn, reads=reads, writes=writes)

        def setup_na():
            with ExitStack() as ph:
                psb = psbf(ph)
                c_s = kb.ctr("s_setna")
                raw = psb("eraw", [128, 4, 15, 64], F32)
                rawb = Buf()
                exn = psb("eexp", [128, 4, 15, 64], F32)
                exb = Buf()
                cm = psb("cmask", [128, 64], F32)
                cmb = Buf()
                kb.dma(sp, c_s, cm[:, :], colmask_d[:, :], writes=[cmb])
                for hp in range(4):
                    for h2 in range(2):
                        h = 2 * hp + h2
                        src = bass.AP(rpbpad.tensor, h * 15 * 128, [[1, 64], [128, 15], [1, 64]])
                        kb.dma(sp, c_s, raw[h2 * 64:(h2 + 1) * 64, hp, :, :], src, writes=[rawb])
                rawb.w = (c_s, c_s.cnt)
                cmb.w = (c_s, c_s.cnt)
                A(lambda: nc.scalar.activation(out=sap(exn, 0, 128, 0, [[1, 3840]]), in_=sap(raw, 0, 128, 0, [[1, 3840]]),
                                               func=AF.Exp), [rawb], [exb])
                V(lambda: nc.vector.tensor_tensor(out=sap(Etab, 0, 128, 0, [[64, 60], [1, 64]]),
                                                  in0=sap(exn, 0, 128, 63, [[64, 60], [-1, 64]]),
                                                  in1=sap(cm, 0, 128, 0, [[0, 60], [1, 64]]), op=ALU.mult),
                  [exb, cmb], [Etabb])
                kb.barrier()

        MAGIC = 12582912.0

        def cexp(xr, xi, ore, oim, t, F_, b):
            R = [b]
            A(lambda: nc.scalar.activation(out=xr, in_=xr, func=AF.Exp), R, R)
            V(lambda: nc.vector.tensor_scalar(out=xi, in0=xi, scalar1=1.0 / TWO_PI, scalar2=None, op0=ALU.mult), R, R)
            for which, o in ((0, oim), (1, ore)):
                if which == 1:
                    V(lambda: nc.vector.tensor_scalar(out=xi, in0=xi, scalar1=0.25, scalar2=None, op0=ALU.add), R, R)
                V(lambda: nc.vector.tensor_scalar(out=t, in0=xi, scalar1=MAGIC, scalar2=None, op0=ALU.add), R, R)
                V(lambda: nc.vector.tensor_scalar(out=t, in0=t, scalar1=MAGIC, scalar2=None, op0=ALU.subtract), R, R)
                V(lambda: nc.vector.tensor_tensor(out=t, in0=xi, in1=t, op=ALU.subtract), R, R)
                A(lambda: nc.scalar.activation(out=t, in_=t, func=AF.Sin, scale=TWO_PI), R, R)
                V(lambda: nc.vector.tensor_tensor(out=o, in0=xr, in1=t, op=ALU.mult), R, R)

        def cmul(ore, oim, ar_, ai_, br_, bi_, t1, t2, R, W, neg_im=False):
            V(lambda: nc.vector.tensor_tensor(out=t1, in0=ar_, in1=br_, op=ALU.mult), R, R)
            V(lambda: nc.vector.tensor_tensor(out=t2, in0=ai_, in1=bi_, op=ALU.mult), R, R)
            V(lambda: nc.vector.tensor_tensor(out=ore, in0=t1, in1=t2, op=ALU.subtract), R, W)
            V(lambda: nc.vector.tensor_tensor(out=t1, in0=ar_, in1=bi_, op=ALU.mult), R, R)
            V(lambda: nc.vector.tensor_tensor(out=t2, in0=ai_, in1=br_, op=ALU.mult), R, R)
            if neg_im:
                V(lambda: nc.vector.scalar_tensor_tensor(out=oim, in0=t1, scalar=-1.0, in1=t2, op0=ALU.mult,
                                                         op1=ALU.subtract), R, W)
            else:
                V(lambda: nc.vector.tensor_tensor(out=oim, in0=t1, in1=t2, op=ALU.add), R, W)

        def lam_and_f(are, aim, ldt, ardt, aidt, lre, lim, fre, fim, t1, t2, t3, F_, b):
            R = [b]
            A(lambda: nc.scalar.activation(out=ldt, in_=ldt, func=AF.Exp), R, R)
            V(lambda: nc.vector.tensor_scalar(out=are, in0=are, scalar1=-1e-4, scalar2=None, op0=ALU.min), R, R)
            V(lambda: nc.vector.tensor_tensor(out=ardt, in0=are, in1=ldt, op=ALU.mult), R, R)
            V(lambda: nc.vector.tensor_tensor(out=aidt, in0=aim, in1=ldt, op=ALU.mult), R, R)
            V(lambda: nc.vector.tensor_copy(out=t1, in_=ardt), R, R)
            V(lambda: nc.vector.tensor_copy(out=t2, in_=aidt), R, R)
            cexp(t1, t2, lre, lim, t3, F_, b)
            V(lambda: nc.vector.tensor_tensor(out=t1, in0=are, in1=are, op=ALU.mult), R, R)
            V(lambda: nc.vector.tensor_tensor(out=t2, in0=aim, in1=aim, op=ALU.mult), R, R)
            V(lambda: nc.vector.tensor_tensor(out=t3, in0=t1, in1=t2, op=ALU.add), R, R)
            V(lambda: nc.vector.reciprocal(out=t3, in_=t3), R, R)
            V(lambda: nc.vector.tensor_scalar(out=t1, in0=lre, scalar1=-1.0, scalar2=None, op0=ALU.add), R, R)
            V(lambda: nc.vector.tensor_tensor(out=fre, in0=t1, in1=are, op=ALU.mult), R, R)
            V(lambda: nc.vector.tensor_tensor(out=t2, in0=lim, in1=aim, op=ALU.mult), R, R)
            V(lambda: nc.vector.tensor_tensor(out=fre, in0=fre, in1=t2, op=ALU.add), R, R)
            V(lambda: nc.vector.tensor_tensor(out=fre, in0=fre, in1=t3, op=ALU.mult), R, R)
            V(lambda: nc.vector.tensor_tensor(out=fim, in0=lim, in1=are, op=ALU.mult), R, R)
            V(lambda: nc.vector.tensor_tensor(out=t2, in0=t1, in1=aim, op=ALU.mult), R, R)
            V(lambda: nc.vector.tensor_tensor(out=fim, in0=fim, in1=t2, op=ALU.subtract), R, R)
            V(lambda: nc.vector.tensor_tensor(out=fim, in0=fim, in1=t3, op=ALU.mult), R, R)

        def setup_s5():
            c_s = kb.ctr("s_sets5")
            with ExitStack() as ph:
                psb = psbf(ph)
                b = Buf()
                F_ = NG
                names = ["are", "aim", "ldt", "ardt", "aidt", "lre", "lim", "fre", "fim", "t1", "t2", "t3"]
                tl = {n: psb("C" + n, [128, F_], F32) for n in names}
                fl = {n: tl[n][:, :] for n in names}
                Cc = psb("CC", [128, 2, 512], F32)
                Bc = psb("CB", [128, 2, 512], F32)
                eps = psb("Ceps", [128, 4, 8], F32)
                Dt = psb("CD", [128, NG], F32)
                msk = psb("Cmsk", [128, 3, 128], F32)
                for wi_, n in enumerate(["are", "aim", "ldt"]):
                    kb.dma(sp, c_s, tl[n][:, :], s5c_d[:, wi_, :], writes=[b])
                kb.dma(sp, c_s, Cc[:, :, :], s5cc_d[:, :, :], writes=[b])
                kb.dma(sp, c_s, Bc[:, :, :], s5cb_d[:, :, :], writes=[b])
                kb.dma(sp, c_s, eps[:, :, :], s5ce_d[:, :, :], writes=[b])
                kb.dma(sp, c_s, Dt[:, :], s5d_d[:, :], writes=[b])
                kb.dma(sp, c_s, msk[:, :, :], s5m_d[:, :, :], writes=[b])
                b.w = (c_s, c_s.cnt)
                R = [b]
                lam_and_f(fl["are"], fl["aim"], fl["ldt"], fl["ardt"], fl["aidt"], fl["lre"], fl["lim"],
                          fl["fre"], fl["fim"], fl["t1"], fl["t2"], fl["t3"], F_, b)
                V(lambda: nc.vector.tensor_scalar(out=fl["t1"], in0=fl["ardt"], scalar1=8.0, scalar2=None, op0=ALU.mult), R, R)
                V(lambda: nc.vector.tensor_scalar(out=fl["t2"], in0=fl["aidt"], scalar1=8.0, scalar2=None, op0=ALU.mult), R, R)
                cexp(fl["t1"], fl["t2"], fl["lre"], fl["lim"], fl["t3"], F_, b)
                V(lambda: nc.vector.tensor_copy(out=scA[:, 0, :], in_=fl["lre"]), R, [scb])
                V(lambda: nc.vector.tensor_copy(out=scA[:, 1, :], in_=fl["lre"]), R, [scb])
                V(lambda: nc.vector.tensor_scalar(out=scB[:, 0, :], in0=fl["lim"], scalar1=-1.0, scalar2=None, op0=ALU.mult),
                  R, [scb])
                V(lambda: nc.vector.tensor_copy(out=scB[:, 1, :], in_=fl["lim"]), R, [scb])
                pw = {}
                G8 = NG * 8
                xr = psb("Cxr", [128, G8], F32)
                xi = psb("Cxi", [128, G8], F32)
                tt = psb("Ctt", [128, G8], F32)
                for ei, nm in enumerate(["C", "G", "A", "B"]):
                    pr = psb("Cp%sr" % nm, [128, G8], F32)
                    pi_ = psb("Cp%si" % nm, [128, G8], F32)
                    e_b = sap(eps, 0, 128, ei * 8, [[0, NG], [1, 8]])
                    V(lambda: nc.vector.tensor_tensor(out=sap(xr, 0, 128, 0, [[8, NG], [1, 8]]),
                                                      in0=sap(tl["ardt"], 0, 128, 0, [[1, NG], [0, 8]]), in1=e_b,
                                                      op=ALU.mult), R, R)
                    V(lambda: nc.vector.tensor_tensor(out=sap(xi, 0, 128, 0, [[8, NG], [1, 8]]),
                                                      in0=sap(tl["aidt"], 0, 128, 0, [[1, NG], [0, 8]]), in1=e_b,
                                                      op=ALU.mult), R, R)
                    cexp(xr[:, :], xi[:, :], pr[:, :], pi_[:, :], tt[:, :], G8, b)
                    pw[nm] = (pr, pi_)
                fr_b = sap(tl["fre"], 0, 128, 0, [[1, NG], [0, 8]])
                fi_b = sap(tl["fim"], 0, 128, 0, [[1, NG], [0, 8]])

                def g8(t_):
                    return sap(t_, 0, 128, 0, [[8, NG], [1, 8]])
                far = psb("Cfar", [128, G8], F32)
                fai = psb("Cfai", [128, G8], F32)
                cmul(g8(far), g8(fai), fr_b, fi_b, g8(pw["A"][0]), g8(pw["A"][1]), g8(xr), g8(xi), R, R)
                BIG = NG * 8 * 16
                t1b = psb("Ct1b", [128, BIG], F32)
                t2b = psb("Ct2b", [128, BIG], F32)
                Are = psb("CAre", [128, NG, 128], BF16)
                Aim = psb("CAim", [128, NG, 128], BF16)
                Gre = psb("CGre", [128, NG, 128], BF16)
                Gim = psb("CGim", [128, NG, 128], BF16)

                def bigv(t_):
                    return sap(t_, 0, 128, 0, [[128, NG], [16, 8], [1, 16]])

                def pwv(t_):
                    return sap(t_, 0, 128, 0, [[8, NG], [1, 8], [0, 16]])

                def cpv(t_, c):
                    return sap(t_, 0, 128, c * 512, [[16, NG], [0, 8], [1, 16]])
                cmul(sap(WC, 0, 128, 0, [[256, NG], [16, 8], [1, 16]]), sap(WC, 0, 128, 128, [[256, NG], [16, 8], [1, 16]]),
                     cpv(Cc, 0), cpv(Cc, 1), pwv(pw["C"][0]), pwv(pw["C"][1]), bigv(t1b), bigv(t2b), R, [b, WCb], neg_im=True)
                cmul(bigv(Gre), bigv(Gim), cpv(Cc, 0), cpv(Cc, 1), pwv(pw["G"][0]), pwv(pw["G"][1]), bigv(t1b), bigv(t2b),
                     R, R, neg_im=True)
                cmul(bigv(Are), bigv(Aim), pwv(far), pwv(fai), cpv(Bc, 0), cpv(Bc, 1), bigv(t1b), bigv(t2b), R, R)
                fbr = psb("Cfbr", [128, G8], F32)
                fbi = psb("Cfbi", [128, G8], F32)
                cmul(g8(fbr), g8(fbi), fr_b, fi_b, g8(pw["B"][0]), g8(pw["B"][1]), g8(xr), g8(xi), R, R)
                PBc = [psb("CPBre", [128, NG, 128], BF16), psb("CPBim", [128, NG, 128], BF16)]
                cmul(bigv(PBc[0]), bigv(PBc[1]), pwv(fbr), pwv(fbi), cpv(Bc, 0), cpv(Bc, 1), bigv(t1b), bigv(t2b), R, R)
                for g2 in range(NG // 2):
                    pw_, pwb_ = pb[2 + g2 % 2], pbb[2 + g2 % 2]
                    for gg in range(2):
                        for c in range(2):
                            blk = gg * 2 + c
                            kb.op(pe, lambda: nc.tensor.matmul(pw_[:, blk * 128:(blk + 1) * 128], lhsT=PBc[c][:, 2 * g2 + gg, :],
                                                               rhs=ident, start=True, stop=True),
                                  reads=[b, cstb], writes=[pwb_])
                    V(lambda: nc.vector.tensor_copy(out=sap(WB, 0, 128, 2 * g2 * 256, [[1, 512]]), in_=pw_[:, :]),
                      [pwb_], [WBb])
                tmpf = psb("Ctmpf", [128, 512], F32)
                for g4 in range(NG // 4):
                    pf, pfb = pb[0], pbb[0]
                    pq, pqb = pb[1], pbb[1]
                    for gg in range(4):
                        g = g4 * 4 + gg
                        for (P_, Pb_, lo) in ((pf, pfb, 0), (pq, pqb, 64)):
                            kb.op(pe, lambda: nc.tensor.matmul(P_[:, gg * 128:(gg + 1) * 128], lhsT=Are[lo:lo + 64, g, :],
                                                               rhs=Gre[lo:lo + 64, g, :], start=True, stop=False),
                                  reads=[b], writes=[Pb_])
                            kb.op(pe, lambda: nc.tensor.matmul(P_[:, gg * 128:(gg + 1) * 128], lhsT=Aim[lo:lo + 64, g, :],
                                                               rhs=Gim[lo:lo + 64, g, :], start=False, stop=True),
                                  reads=[b], writes=[Pb_])
                    mL = sap(msk, 0, 128, 0, [[0, 4], [1, 128]])
                    mU = sap(msk, 0, 128, 128, [[0, 4], [1, 128]])
                    t4 = sap(tmpf, 0, 128, 0, [[128, 4], [1, 128]])
                    V(lambda: nc.vector.tensor_tensor(out=t4, in0=sap(pf, 0, 128, 0, [[128, 4], [1, 128]]), in1=mL,
                                                      op=ALU.mult), [pfb, b], [b])
                    V(lambda: nc.vector.tensor_tensor(out=sap(t1b, 0, 128, 0, [[128, 4], [1, 128]]),
                                                      in0=sap(pq, 0, 128, 0, [[128, 4], [1, 128]]), in1=mU, op=ALU.mult),
                      [pqb, b], [b])
                    V(lambda: nc.vector.tensor_tensor(out=t4, in0=t4, in1=sap(t1b, 0, 128, 0, [[128, 4], [1, 128]]),
                                                      op=ALU.add), R, R)
                    for gg in range(4):
                        g = g4 * 4 + gg
                        V(lambda: nc.vector.scalar_tensor_tensor(out=WT[:, g, :], in0=msk[:, 2, :], scalar=Dt[:, g:g + 1],
                                                                 in1=tmpf[:, gg * 128:(gg + 1) * 128], op0=ALU.mult,
                                                                 op1=ALU.add), R, [b, WTb])
                kb.barrier()
        def mixer_items():
            items = []
            for mt in range(12):
                items.append((wx[mt, :, :], 1024))
            for half in range(2):
                items.append((wv2[half, :, :], 2048))
            return items

        def krs_for(qg):
            out = []
            for kr in range(32):
                rows = [r for r in range(8 * qg, 8 * qg + 8)
                        if min(max(r - 4, 0), 24) <= kr <= min(max(r - 4, 0), 24) + 7]
                if rows:
                    out.append((kr, rows[0], rows[-1]))
            return out

        def inproj_phase(bl, qT, qTb, kT, kTb, vd, vdb):
            with ExitStack() as ph:
                psb = psbf(ph)
                xn2 = psb("xn2", [128, 8, T], BF16)
                xn2b = [[Buf() for c in range(4)] for k in range(8)]
                sq = [psb(f"sq{i}", [128, 512], BF16) for i in range(2)]
                scr = (sq, [Buf(), Buf()], pb[6], pbb[6], pb[6], pbb[6])
                chunks = main_chunks(xn2, xn2b, None, None)
                with_meta = (bl == 0)
                if with_meta:
                    xnm = psb("xn2m", [128, 8, NM], BF16)
                    xnmb = [Buf() for k in range(8)]
                    chunks.append(dict(N=NM, h=lambda k: hm[:, k, :], hb=lambda k: hmb[k],
                                       xn=lambda k: xnm[:, k, :], xnb=lambda k: xnmb[k]))
                for ch in chunks:
                    norm_chunk(ch, 1, scr)
                ust = [psb(f"ust{i}", [128, 8, 64], BF16) for i in range(2)]
                ustb = [Buf(), Buf()]
                ui = 0
                it = 0
                for mt in range(12):
                    ws, wb = stream.get()
                    for ci, ch in enumerate(chunks):
                        N = ch["N"]
                        is_meta = (ci == 4)
                        if is_meta and 4 <= mt < 8:
                            continue
                        pS, pSb = pb[it % 4], pbb[it % 4]
                        it += 1
                        for k in range(8):
                            kb.op(pe, lambda: nc.tensor.matmul(pS[:, :N], lhsT=ws[:, k * 128:(k + 1) * 128], rhs=ch["xn"](k),
                                                               start=(k == 0), stop=(k == 7)),
                                  reads=[wb, ch["xnb"](k)], writes=[pSb])
                        if mt < 4:
                            if is_meta:
                                o = sap(umT, 0, 128, mt * 16, [[2, 8], [1, 2]])
                                i_ = sap(pS, 0, 128, 0, [[1, 8], [8, 2]])
                                V(lambda: nc.vector.tensor_copy(out=o, in_=i_), [pSb], [umTb])
                            else:
                                us_, usb_ = ust[ui % 2], ustb[ui % 2]
                                ui += 1
                                i_ = sap(pS, 0, 128, 0, [[1, 8], [8, 64]])
                                V(lambda: nc.vector.tensor_copy(out=us_[:, :, :], in_=i_), [pSb], [usb_])
                                kb.dma(sp, c_scr, u_scr[mt * 128:(mt + 1) * 128, :, 2 + 64 * ci:2 + 64 * ci + 64], us_[:, :, :],
                                       reads=[usb_], writes=[u_scr_b])
                        elif mt < 8:
                            A(lambda: nc.scalar.copy(out=qT[:, mt - 4, ci * 512:(ci + 1) * 512], in_=pS[:, :N]),
                              [pSb], [qTb[mt - 4][ci]])
                        else:
                            if is_meta:
                                A(lambda: nc.scalar.copy(out=kmT[:, mt - 8, :], in_=pS[:, :N]), [pSb], [kmTb])
                            else:
                                A(lambda: nc.scalar.copy(out=kT[:, mt - 8, ci * 512:(ci + 1) * 512], in_=pS[:, :N]),
                                  [pSb], [kTb[mt - 8]])
                    if mt < 4:
                        kb.dma(sp, c_scr, u_scr[mt * 128:(mt + 1) * 128, :, 0:2], umT[:, mt, :, :], reads=[umTb],
                               writes=[u_scr_b])
                for j in range(8):
                    srcA = bass.AP(u_scr.tensor, j * NBLK, [[8 * NBLK, 16], [16 * 8 * NBLK, NG], [1, NBLK]])
                    dstA = bass.AP(u_scr2.tensor, j * 16 * NG * NBLK, [[NG * NBLK, 16], [NBLK, NG], [1, NBLK]])
                    kb.dma(sp, c_scr, dstA, srcA, reads=[u_scr_b], writes=[u2b[j]])
                for j in range(8):
                    u2b[j].w = (c_scr, c_scr.cnt)
                for half in range(2):
                    ws, wb = stream.get()
                    for w in range(-1, 32):
                        pS, pSb = pb[it % 4], pbb[it % 4]
                        it += 1
                        if w == -1:
                            t0, M, p0 = 0, 64, 64
                        elif w == 31:
                            t0, M, p0 = 64 * 31, 64, 0
                        else:
                            t0, M, p0 = 64 * w, 128, 0
                        rb_ = [xn2b[0][t0 // 512], xn2b[0][(t0 + M - 1) // 512]]
                        for k in range(8):
                            kb.op(pe, lambda: nc.tensor.matmul(pS[p0:p0 + M, 0:256], lhsT=xn2[:, k, t0:t0 + M],
                                                               rhs=ws[:, k * 256:(k + 1) * 256], start=(k == 0), stop=(k == 7)),
                                  reads=[wb, xn2b[k][t0 // 512], xn2b[k][(t0 + M - 1) // 512]], writes=[pSb])
                        if w >= 0:
                            V(lambda: nc.vector.tensor_copy(out=vd[0:64, w, 2 * half:2 * half + 2, :],
                                                            in_=sap(pS, 0, 64, 0, [[128, 2], [1, 64]])), [pSb], [vdb])
                        if w < 31:
                            A(lambda: nc.scalar.copy(out=vd[64:128, w + 1, 2 * half:2 * half + 2, :],
                                                     in_=sap(pS, 64, 64, 64, [[128, 2], [1, 64]])), [pSb], [vdb])
                    if with_meta:
                        pS, pSb = pb[it % 4], pbb[it % 4]
                        it += 1
                        for p0 in (0, 64):
                            for k in range(8):
                                kb.op(pe, lambda: nc.tensor.matmul(pS[p0:p0 + NM, 0:256], lhsT=xnm[:, k, :],
                                                                   rhs=ws[:, k * 256:(k + 1) * 256], start=(k == 0),
                                                                   stop=(k == 7)),
                                      reads=[wb, xnmb[k]], writes=[pSb])
                        V(lambda: nc.vector.tensor_copy(out=vm[0:NM, 2 * half:2 * half + 2, :],
                                                        in_=sap(pS, 0, NM, 0, [[128, 2], [1, 64]])), [pSb], [vmb])
                        V(lambda: nc.vector.tensor_copy(out=vm[64:64 + NM, 2 * half:2 * half + 2, :],
                                                        in_=sap(pS, 64, NM, 64, [[128, 2], [1, 64]])), [pSb], [vmb])
                kb.barrier()

        def na_phase(bl, qT, qTb, kT, kTb, vd, vdb, oT, oTb):
            LA = 3
            NSB = 4
            with ExitStack() as ph:
                psb = psbf(ph)
                exs = [psb(f"nex{i}", [128, 512], BF16) for i in range(NSB)]
                exb = [Buf() for i in range(NSB)]
                Ps = [psb(f"nP{i}", [128, 512], BF16) for i in range(NSB)]
                Pb = [Buf() for i in range(NSB)]
                pms = [psb(f"npm{i}", [128, 512], BF16) for i in range(2)]
                pmb = [Buf(), Buf()]
                rec = psb("nrec", [128, 512], F32)
                recb = Buf()
                for i in range(2):
                    V(lambda: nc.vector.memset(pms[i][:, :], 0.0), [], [pmb[i]])
                tiles_ = []
                gi_ = 0
                for hp in range(4):
                    for qg in range(4):
                        lst = krs_for(qg)
                        tiles_.append(dict(kind="meta", hp=hp, qg=qg, gi=gi_, first=True, last=False))
                        for idx, (kr, ra, rb2) in enumerate(lst):
                            tiles_.append(dict(kind="kr", hp=hp, qg=qg, gi=gi_, kr=kr, ra=ra, rb=rb2, first=False,
                                               last=(idx == len(lst) - 1)))
                        gi_ += 1

                def emit_S(t, tl_):
                    hp, qg = tl_["hp"], tl_["qg"]
                    pS, pSb = pb[t % NSB], pbb[t % NSB]
                    if tl_["kind"] == "meta":
                        tok0 = 512 * qg
                        for lo in (0, 64):
                            kb.op(pe, lambda: nc.tensor.matmul(pS[lo:lo + NM, 0:512], lhsT=kmT[lo:lo + 64, hp, :],
                                                               rhs=qT[lo:lo + 64, hp, tok0:tok0 + 512], start=True, stop=True),
                                  reads=[kmTb, qTb[hp][qg]], writes=[pSb])
                    else:
                        kr, ra, rb2 = tl_["kr"], tl_["ra"], tl_["rb"]
                        N = 64 * (rb2 - ra + 1)
                        for lo in (0, 64):
                            kb.op(pe, lambda: nc.tensor.matmul(pS[lo:lo + 64, 0:N], lhsT=kT[lo:lo + 64, hp, 64 * kr:64 * kr + 64],
                                                               rhs=qT[lo:lo + 64, hp, 64 * ra:64 * (rb2 + 1)],
                                                               start=True, stop=True),
                                  reads=[kTb[hp], qTb[hp][qg]], writes=[pSb])

                def emit_E(t, tl_):
                    hp, qg = tl_["hp"], tl_["qg"]
                    pS, pSb = pb[t % NSB], pbb[t % NSB]
                    if tl_["kind"] == "meta":
                        pm, pmb_ = pms[tl_["gi"] % 2], pmb[tl_["gi"] % 2]
                        for lo in (0, 64):
                            A(lambda: nc.scalar.activation(out=pm[lo:lo + NM, :], in_=pS[lo:lo + NM, 0:512], func=AF.Exp,
                                                           scale=0.125), [pSb], [pmb_])
                    else:
                        kr, ra, rb2 = tl_["kr"], tl_["ra"], tl_["rb"]
                        N = 64 * (rb2 - ra + 1)
                        ex_, exb_ = exs[t % NSB], exb[t % NSB]
                        P_, Pb_ = Ps[t % NSB], Pb[t % NSB]
                        A(lambda: nc.scalar.activation(out=ex_[:, 0:N], in_=pS[:, 0:N], func=AF.Exp, scale=0.125),
                          [pSb], [exb_])
                        rho = ra - kr + 7
                        V(lambda: nc.vector.tensor_tensor(out=P_[:, 0:N], in0=ex_[:, 0:N],
                                                          in1=sap(Etab, 0, 128, (hp * 15 + rho) * 64, [[1, N]]),
                                                          op=ALU.mult), [exb_, Etabb], [Pb_])

                def emit_PV(t, tl_):
                    hp, qg, g_ = tl_["hp"], tl_["qg"], tl_["gi"]
                    pO, pOb = pb[4 + g_ % 2], pbb[4 + g_ % 2]
                    pD, pDb = pb[6 + g_ % 2], pbb[6 + g_ % 2]
                    if tl_["kind"] == "meta":
                        pm, pmb_ = pms[g_ % 2], pmb[g_ % 2]
                        for lo in (0, 64):
                            kb.op(pe, lambda: nc.tensor.matmul(pO[lo:lo + 64, :], lhsT=vm[lo:lo + NM, hp, :],
                                                               rhs=pm[lo:lo + NM, :], start=True, stop=False),
                                  reads=[vmb, pmb_], writes=[pOb])
                        kb.op(pe, lambda: nc.tensor.matmul(pD[:, :], lhsT=blockones, rhs=pm[:, :], start=True, stop=False),
                              reads=[cstb, pmb_], writes=[pDb])
                    else:
                        kr, ra, rb2 = tl_["kr"], tl_["ra"], tl_["rb"]
                        N = 64 * (rb2 - ra + 1)
                        c0 = 64 * (ra - 8 * qg)
                        last = tl_["last"]
                        P_, Pb_ = Ps[t % NSB], Pb[t % NSB]
                        for lo in (0, 64):
                            kb.op(pe, lambda: nc.tensor.matmul(pO[lo:lo + 64, c0:c0 + N], lhsT=vd[lo:lo + 64, kr, hp, :],
                                                               rhs=P_[lo:lo + 64, 0:N], start=False, stop=last),
                                  reads=[vdb, Pb_], writes=[pOb])
                        kb.op(pe, lambda: nc.tensor.matmul(pD[:, c0:c0 + N], lhsT=blockones, rhs=P_[:, 0:N],
                                                           start=False, stop=last),
                              reads=[cstb, Pb_], writes=[pDb])
                        if last:
                            tok0 = 512 * qg
                            V(lambda: nc.vector.reciprocal(out=rec[:, :], in_=pD[:, :]), [pDb], [recb])
                            V(lambda: nc.vector.tensor_tensor(out=oT[:, hp, tok0:tok0 + 512], in0=pO[:, :], in1=rec[:, :],
                                                              op=ALU.mult), [pOb, recb], [oTb[hp][qg]])

                nt = len(tiles_)
                for t in range(min(LA, nt)):
                    emit_S(t, tiles_[t])
                for t in range(nt):
                    emit_E(t, tiles_[t])
                    if t + LA < nt:
                        emit_S(t + LA, tiles_[t + LA])
                    emit_PV(t, tiles_[t])
                kb.barrier()

        def s5_phase(bl, zT, zTb):
            with ExitStack() as ph:
                psb = psbf(ph)
                def ubg(g):
                    return sap(zT, 0, 128, g * NBLK, [[1, NBLK]])

                def ubrows(j):
                    return sap(zT, 16 * j, 16, 0, [[NBLK, NG], [1, NBLK]])
                ubb = [Buf() for g in range(NG)]
                Vs = psb("Vs", [128, 2, NG, NBLK], BF16)
                Vsb = Buf()
                Ssb = Buf()
                X = [psb(f"X{i}", [128, 2, NG], F32) for i in range(2)]
                Xb = [Buf(), Buf()]
                t1 = psb("st1", [128, 2, NG], F32)
                t2 = psb("st2", [128, 2, NG], F32)
                tb = Buf()
                g1s = [psb(f"sg1{i}", [128, NBLK], F32) for i in range(2)]
                g2s = [psb(f"sg2{i}", [128, NBLK], F32) for i in range(2)]
                gbs = [Buf(), Buf()]
                kb.dma(sp, c_scr, sap(zT, 0, 128, 0, [[NBLK, NG], [1, NBLK]]), u_scr2[:, :, :], reads=u2b, writes=ubb)
                for g in range(NG):
                    ubb[g].w = (c_scr, c_scr.cnt)
                for g in range(NG):
                    pR_, pRb_ = pb[(2 * g) % 4], pbb[(2 * g) % 4]
                    pI_, pIb_ = pb[(2 * g + 1) % 4], pbb[(2 * g + 1) % 4]
                    kb.op(pe, lambda: nc.tensor.matmul(pR_[:, 0:NBLK], lhsT=WB[:, g, 0, :], rhs=ubg(g), start=True, stop=True),
                          reads=[WBb, ubb[g]], writes=[pRb_])
                    kb.op(pe, lambda: nc.tensor.matmul(pI_[:, 0:NBLK], lhsT=WB[:, g, 1, :], rhs=ubg(g), start=True, stop=True),
                          reads=[WBb, ubb[g]], writes=[pIb_])
                    for c, (P_, Pb_) in enumerate(((pR_, pRb_), (pI_, pIb_))):
                        V(lambda: nc.vector.tensor_copy(out=Vs[0:64, c, g, :], in_=P_[0:64, 0:NBLK]), [Pb_], [Vsb])
                        V(lambda: nc.vector.tensor_copy(out=Vs[64:128, c, g, :], in_=sap(P_, 64, 64, NBLK - 1, [[-1, NBLK]])),
                          [Pb_], [Vsb])
                V(lambda: nc.vector.memset(X[1][:, :, :], 0.0), [], [Xb[1]])
                for i in range(NBLK):
                    Xp, Xpb = X[(i + 1) % 2], Xb[(i + 1) % 2]
                    Xn, Xnb = X[i % 2], Xb[i % 2]
                    sw = sap(Xp, 0, 128, NG, [[-NG, 2], [1, NG]])
                    V(lambda: nc.vector.tensor_tensor(out=t1[:, :, :], in0=scA[:, :, :], in1=Xp[:, :, :], op=ALU.mult),
                      [scb, Xpb], [tb])
                    V(lambda: nc.vector.tensor_tensor(out=t2[:, :, :], in0=scB[:, :, :], in1=sw, op=ALU.mult), [scb, Xpb], [tb])
                    V(lambda: nc.vector.tensor_tensor(out=t1[:, :, :], in0=t1[:, :, :], in1=t2[:, :, :], op=ALU.add), [tb], [tb])
                    vi = sap(Vs, 0, 128, i, [[NG * NBLK, 2], [NBLK, NG]])
                    V(lambda: nc.vector.tensor_tensor(out=Xn[:, :, :], in0=t1[:, :, :], in1=vi, op=ALU.add), [tb, Vsb], [Xnb])
                    A(lambda: nc.scalar.copy(out=vi, in_=Xn[:, :, :]), [Xnb], [Ssb])
                al = [psb(f"al{i}", [128, 2, NBLK], BF16) for i in range(2)]
                alb = [Buf(), Buf()]
                for i in range(2):
                    V(lambda: nc.vector.memset(al[i][:, :, :], 0.0), [], [alb[i]])
                for g in range(NG):
                    pY, pYb = pb[4 + g % 2], pbb[4 + g % 2]
                    a_, ab_ = al[g % 2], alb[g % 2]
                    V(lambda: nc.vector.tensor_copy(out=a_[0:64, :, 1:NBLK],
                                                    in_=sap(Vs, 0, 64, g * NBLK, [[NG * NBLK, 2], [1, NBLK - 1]])),
                      [Ssb], [ab_])
                    V(lambda: nc.vector.tensor_copy(out=a_[64:128, :, 0:NBLK - 1],
                                                    in_=sap(Vs, 64, 64, g * NBLK + NBLK - 2, [[NG * NBLK, 2], [-1, NBLK - 1]])),
                      [Ssb], [ab_])
                    kb.op(pe, lambda: nc.tensor.matmul(pY[:, 0:NBLK], lhsT=WT[:, g, :], rhs=ubg(g), start=True, stop=False),
                          reads=[WTb, ubb[g]], writes=[pYb])
                    for c in range(2):
                        kb.op(pe, lambda: nc.tensor.matmul(pY[:, 0:NBLK], lhsT=WC[:, g, c, :], rhs=a_[:, c, :],
                                                           start=False, stop=(c == 1)),
                              reads=[WCb, ab_], writes=[pYb])
                    g1, g2, gb = g1s[g % 2], g2s[g % 2], gbs[g % 2]
                    A(lambda: nc.scalar.activation(out=g1[:, :], in_=pY[:, 0:NBLK], func=AF.Square), [pYb], [gb])
                    V(lambda: nc.vector.tensor_scalar(out=g1[:, :], in0=g1[:, :], scalar1=0.07135481627, scalar2=1.5957691216,
                                                      op0=ALU.mult, op1=ALU.add), [gb], [gb])
                    V(lambda: nc.vector.tensor_tensor(out=g2[:, :], in0=g1[:, :], in1=pY[:, 0:NBLK], op=ALU.mult), [gb, pYb], [gb])
                    A(lambda: nc.scalar.activation(out=g2[:, :], in_=g2[:, :], func=AF.Sigmoid), [gb], [gb])
                    V(lambda: nc.vector.tensor_tensor(out=ubg(g), in0=g2[:, :], in1=pY[:, 0:NBLK], op=ALU.mult),
                      [gb, pYb], [ubb[g]])
                z2b = Buf()
                kb.dma(sp, c_scr, z_scr2[:, :, :], sap(zT, 0, 128, 0, [[NBLK, NG], [1, NBLK]]), reads=ubb + zsb, writes=[z2b])
                for j in range(8):
                    srcZ = bass.AP(z_scr2.tensor, j * 16 * NG * NBLK, [[NG * NBLK, 16], [NBLK, NG], [1, NBLK]])
                    dstZ = bass.AP(z_scr.tensor, j * NBLK, [[8 * NBLK, 16], [16 * 8 * NBLK, NG], [1, NBLK]])
                    kb.dma(sp, c_scr, dstZ, srcZ, reads=[z2b], writes=[zsb[j]])
                for j in range(8):
                    zsb[j].w = (c_scr, c_scr.cnt)
                for mt in range(4):
                    kb.dma(sp, c_scr, zT[:, mt, :, :], z_scr[mt * 128:(mt + 1) * 128, :, :], reads=zsb, writes=zTb + ubb)
                for b_ in zTb:
                    b_.w = (c_scr, c_scr.cnt)
                kb.barrier()

        def merge_items():
            items = []
            for half in range(2):
                for mo in range(8):
                    items.append((wx[16 + mo, :, :], 1024))
                    items.append((wbs_d[mo, :, :], 512))
                    items.append((wx[24 + mo, :, :], 1024))
                    items.append((wbn_d[mo, :, :], 512))
                for mo in range(8):
                    items.append((wo_d[mo, :, :], 1024))
            return items

        c_wgl = kb.ctr("s_wgl")

        def merge_phase(bl, zT, zTb, oT, oTb):
            with ExitStack() as ph:
                psb = psbf(ph)
                wgl = psb("wgl", [128, 4, 512], BF16)
                wglb = Buf()
                for mt in range(4):
                    kb.dma(pool, c_wgl, wgl[:, mt, :], wglu_d[mt, :, :], writes=[wglb])
                wglb.w = (c_wgl, c_wgl.cnt)
                xn2 = psb("mxn2", [128, 8, 1024], BF16)
                mg = psb("mg", [128, 8, 1024], BF16)
                sq = [psb(f"msq{i}", [128, 512], BF16) for i in range(2)]
                sd = psb("msd", [128, 512], F32)
                rstd = psb("mrstd", [128, 512], F32)
                scr = (sq, [Buf(), Buf()], sd, Buf(), rstd, Buf())
                s1 = [psb(f"ms1{i}", [128, 512], BF16) for i in range(2)]
                s2 = [psb(f"ms2{i}", [128, 512], BF16) for i in range(2)]
                m1 = [psb(f"mm1{i}", [128, 512], BF16) for i in range(2)]
                sb1, sb2, mb1 = [Buf(), Buf()], [Buf(), Buf()], [Buf(), Buf()]

                def zperm(t_, mt, c):
                    return sap(t_, 0, 128, mt * 8 * NBLK + 2 + 64 * c, [[NBLK, 8], [1, 64]])
                for c in range(4):
                    for mt in range(4):
                        pS, pSb = pb[(c % 2) * 4 + mt], pbb[(c % 2) * 4 + mt]
                        for k in range(4):
                            kb.op(pe, lambda: nc.tensor.matmul(pS[:, :], lhsT=wgl[:, mt, k * 128:(k + 1) * 128], rhs=zperm(zT, k, c),
                                                               start=(k == 0), stop=(k == 3)),
                                  reads=[wglb, zTb[c]], writes=[pSb])
                    for mt in range(4):
                        pS, pSb = pb[(c % 2) * 4 + mt], pbb[(c % 2) * 4 + mt]
                        A(lambda: nc.scalar.activation(out=s1[mt % 2][:, :], in_=pS[:, :], func=AF.Sigmoid), [pSb], [sb1[mt % 2]])
                        V(lambda: nc.vector.tensor_tensor(out=zperm(zT, mt, c), in0=zperm(zT, mt, c),
                                                          in1=sap(s1[mt % 2], 0, 128, 0, [[64, 8], [1, 64]]), op=ALU.mult),
                          [sb1[mt % 2], zTb[c]], [zTb[c]])
                for half in range(2):
                    xnb = [[Buf() for c in range(2)] for k in range(8)]
                    mgb = [[Buf() for c in range(2)] for k in range(8)]
                    chs = []
                    for c2 in range(2):
                        c = 2 * half + c2
                        chs.append(dict(N=512, h=(lambda k, c=c: hT[:, k, c * 512:(c + 1) * 512]), hb=(lambda k, c=c: hb[k][c]),
                                        xn=(lambda k, c2=c2: xn2[:, k, c2 * 512:(c2 + 1) * 512]),
                                        xnb=(lambda k, c2=c2: xnb[k][c2])))
                    for ch in chs:
                        norm_chunk(ch, 1, scr)
                    for mo in range(8):
                        wgs, wgsb = stream.get()
                        for c2, ch in enumerate(chs):
                            p1, p1b = pb[c2], pbb[c2]
                            for k in range(8):
                                kb.op(pe, lambda: nc.tensor.matmul(p1[:, :], lhsT=wgs[:, k * 128:(k + 1) * 128], rhs=ch["xn"](k),
                                                                   start=(k == 0), stop=(k == 7)),
                                      reads=[wgsb, ch["xnb"](k)], writes=[p1b])
                            A(lambda: nc.scalar.activation(out=s1[c2][:, :], in_=p1[:, :], func=AF.Sigmoid), [p1b], [sb1[c2]])
                        wbs_, wbsb = stream.get()
                        for c2, ch in enumerate(chs):
                            c = 2 * half + c2
                            p3, p3b = pb[2 + c2], pbb[2 + c2]
                            for k in range(4):
                                kb.op(pe, lambda: nc.tensor.matmul(p3[:, :], lhsT=wbs_[:, k * 128:(k + 1) * 128], rhs=zperm(zT, k, c),
                                                                   start=(k == 0), stop=(k == 3)),
                                      reads=[wbsb, zTb[c]], writes=[p3b])
                            V(lambda: nc.vector.tensor_tensor(out=sap(m1[c2], 0, 128, 0, [[8, 64], [1, 8]]),
                                                              in0=sap(s1[c2], 0, 128, 0, [[8, 64], [1, 8]]),
                                                              in1=sap(p3, 0, 128, 0, [[1, 64], [64, 8]]), op=ALU.mult),
                              [sb1[c2], p3b], [mb1[c2]])
                        wgn, wgnb = stream.get()
                        for c2, ch in enumerate(chs):
                            p2, p2b = pb[4 + c2], pbb[4 + c2]
                            for k in range(8):
                                kb.op(pe, lambda: nc.tensor.matmul(p2[:, :], lhsT=wgn[:, k * 128:(k + 1) * 128], rhs=ch["xn"](k),
                                                                   start=(k == 0), stop=(k == 7)),
                                      reads=[wgnb, ch["xnb"](k)], writes=[p2b])
                            A(lambda: nc.scalar.activation(out=s2[c2][:, :], in_=p2[:, :], func=AF.Sigmoid), [p2b], [sb2[c2]])
                        wbn_, wbnb = stream.get()
                        for c2, ch in enumerate(chs):
                            c = 2 * half + c2
                            p4, p4b = pb[6 + c2], pbb[6 + c2]
                            for k in range(4):
                                kb.op(pe, lambda: nc.tensor.matmul(p4[:, :], lhsT=wbn_[:, k * 128:(k + 1) * 128],
                                                                   rhs=oT[:, k, c * 512:(c + 1) * 512], start=(k == 0), stop=(k == 3)),
                                      reads=[wbnb, oTb[k][c]], writes=[p4b])
                            V(lambda: nc.vector.tensor_tensor(out=s2[c2][:, :], in0=s2[c2][:, :], in1=p4[:, :], op=ALU.mult),
                              [sb2[c2], p4b], [sb2[c2]])
                            V(lambda: nc.vector.tensor_tensor(out=mg[:, mo, c2 * 512:(c2 + 1) * 512], in0=m1[c2][:, :],
                                                              in1=s2[c2][:, :], op=ALU.add), [mb1[c2], sb2[c2]], [mgb[mo][c2]])
                    for mo in range(8):
                        ws, wb = stream.get()
                        for c2 in range(2):
                            c = 2 * half + c2
                            pO, pOb = pb[(2 * mo + c2) % 8], pbb[(2 * mo + c2) % 8]
                            for k in range(8):
                                kb.op(pe, lambda: nc.tensor.matmul(pO[:, :], lhsT=ws[:, k * 128:(k + 1) * 128],
                                                                   rhs=mg[:, k, c2 * 512:(c2 + 1) * 512], start=(k == 0), stop=(k == 7)),
                                      reads=[wb, mgb[k][c2]], writes=[pOb])
                            hk = hT[:, mo, c * 512:(c + 1) * 512]
                            V(lambda: nc.vector.tensor_tensor(out=hk, in0=pO[:, :], in1=hk, op=ALU.add),
                              [pOb, hb[mo][c]], [hb[mo][c]])
                kb.barrier()

        def mixer(bl):
            with ExitStack() as mx:
                psb = psbf(mx)
                qT = psb("qoT", [128, 4, T], BF16)
                qTb = [[Buf() for c in range(4)] for i in range(4)]
                qTall = [b_ for r_ in qTb for b_ in r_]
                with ExitStack() as m2:
                    psb2 = psbf(m2)
                    kT = psb2("kT", [128, 4, T], BF16)
                    vd = psb2("vd", [128, 32, 4, 64], BF16)
                    kTb = [Buf() for i in range(4)]
                    vdb = Buf()
                    inproj_phase(bl, qT, qTb, kT, kTb, vd, vdb)
                    if dbg and bl == 0:
                        dump("qT", qT[:, :, :], [128, 4, T], qTall)
                        dump("kT", kT[:, :, :], [128, 4, T], kTb)
                        dump("vd", vd[:, :, :, :], [128, 32, 4, 64], [vdb])
                        kb.barrier()
                        kb.wait_all(sp, [c_out])
                        kb.wait_all(pe, [c_out])
                    if stage >= 7:
                        stream.extend(merge_items())
                    if stage >= 5:
                        na_phase(bl, qT, qTb, kT, kTb, vd, vdb, qT, qTb)
                        if dbg and bl == 0:
                            dump("oT", qT[:, :, :], [128, 4, T], qTall)
                            for E_ in (pe, act, dve, pool):
                                kb.wait_all(E_, [c_out])
                zT = psb("zT", [128, 4, 8, NBLK], BF16)
                zTb = [Buf() for c in range(4)]
                if stage >= 6:
                    s5_phase(bl, zT, zTb)
                    if dbg and bl == 0:
                        dump("zT", zT[:, :, :, :], [128, 4, 8, NBLK], zTb)
                        for E_ in (pe, act, dve, pool):
                            kb.wait_all(E_, [c_out])
                if stage >= 7:
                    merge_phase(bl, zT, zTb, qT, qTb)

        allhb = [hb[k][c] for k in range(8) for c in range(4)]
        if stage >= 2:
            setup_na()
        if stage >= 3:
            setup_s5()
        hT = sb("hT", [128, 8, T], F32)
        hm = sb("hm", [128, 8, NM], F32)
        stream = Stream(kb, nc, es, NSLOT, 2048)
        c_hm = kb.ctr("s_hm")
        kb.dma(sp, c_hm, hm[:, :, :], metaT[:, :, :], writes=hmb)
        for bl in range(nb_local):
            load_x(bl)
            stream.extend(ffn_items(w1i, w1o))
            if stage >= 4:
                stream.extend(mixer_items())
            ffn_phase(bl, 0, with_meta=(bl == 0))
            if dbg and bl == 0:
                dump("h1", hT[:, :, :], [128, 8, T], allhb)
                dump("hm1", hm[:, :, :], [128, 8, NM], hmb)
            if stage >= 4:
                mixer(bl)
                if dbg and bl == 0:
                    dump("h2", hT[:, :, :], [128, 8, T], allhb)
            stream.extend(ffn_items(w2i, w2o))
            ffn_phase(bl, 2, with_meta=False)
            final_phase(bl)
        kb.wait_all(sp, [c_out])
    return nc, dbg_outs


def host_inputs(inputs, core, nb_local=NB_LOCAL):
    f = np.float32
    x = inputs["x"]
    b0 = core * nb_local
    xT = np.ascontiguousarray(
        x[b0:b0 + nb_local].transpose(0, 2, 1).reshape(nb_local, 8, 128, T)).astype(f)
    metaT = np.ascontiguousarray(inputs["meta_tokens"].T.reshape(8, 128, NM).transpose(1, 0, 2)).astype(f)
    g = np.stack([inputs["norm_ffn1"][0], inputs["norm_mix"][0], inputs["norm_ffn2"][0], inputs["norm_final"]])
    gains = np.ascontiguousarray(g.reshape(4, 8, 128).transpose(2, 0, 1)).astype(f)

    def ffn_in(w):
        w = w.reshape(8, 128, 2, NFF, 128)
        return np.ascontiguousarray(w.transpose(3, 1, 0, 2, 4).reshape(NFF, 128, 8 * 256)).astype(f)

    def ffn_out(w):
        w = w.reshape(NFF, 128, 8, 128)
        return np.ascontiguousarray(w.transpose(2, 1, 0, 3).reshape(8, 128, NFF * 128)).astype(f)

    consts = np.zeros((128, 3, 128), f)
    consts[:, 0, :] = np.eye(128)
    consts[:, 1, :] = 1.0
    consts[:64, 2, :64] = 1.0
    consts[64:, 2, 64:] = 1.0
    w_in = inputs["w_in"][0]

    def tiles(w, nk, nm):
        return np.ascontiguousarray(w.reshape(nk, 128, nm, 128).transpose(2, 1, 0, 3).reshape(nm, 128, nk * 128)).astype(f)
    wx_ = tiles(w_in, 8, 32)
    wv = w_in[:, 1536:2048].reshape(8, 128, 2, 256)
    wv2 = np.ascontiguousarray(wv.transpose(2, 1, 0, 3).reshape(2, 128, 2048)).astype(f)
    rpb = inputs["na_rpb"][0]
    rpbpad = np.zeros((8, 15, 128), f)
    rpbpad[:, :, 48:48 + 31] = rpb[:, ::-1, :]
    qc = np.arange(64)
    cs = np.clip(qc - 8, 0, 48)
    kc = np.arange(64)
    cm = ((kc[:, None] >= cs[None, :]) & (kc[:, None] < cs[None, :] + 16)).astype(f)
    colmask = np.concatenate([cm, cm], axis=0)
    are = np.stack([inputs["ssm_a_re_fwd"][0], inputs["ssm_a_re_bwd"][0]])
    aim = np.stack([inputs["ssm_a_im_fwd"][0], inputs["ssm_a_im_bwd"][0]])
    ldt = np.stack([inputs["ssm_log_dt_fwd"][0], inputs["ssm_log_dt_bwd"][0]])
    ldt_gn = np.broadcast_to(ldt[:, :, None], (2, 32, 64))
    Bre, Bim = inputs["ssm_b_re"][0], inputs["ssm_b_im"][0]
    Cre, Cim = inputs["ssm_c_re"][0], inputs["ssm_c_im"][0]

    def layC2(a):
        return a.transpose(0, 2, 1).reshape(128, 32)
    s5c = np.ascontiguousarray(np.stack([layC2(are), layC2(aim), layC2(ldt_gn)], axis=1)).astype(f)

    def layCC(Cm):
        t = Cm.transpose(2, 0, 1).reshape(64, 512)
        return np.concatenate([t, t], axis=0)

    def layCB(Bm):
        t = Bm.transpose(1, 0, 2).reshape(64, 512)
        return np.concatenate([t, t], axis=0)
    s5cc = np.ascontiguousarray(np.stack([layCC(Cre), layCC(Cim)], axis=1)).astype(f)
    s5cb = np.ascontiguousarray(np.stack([layCB(Bre), layCB(Bim)], axis=1)).astype(f)
    j8 = np.arange(8)
    s5ce = np.zeros((128, 4, 8), f)
    s5ce[:64, 3] = 7 - j8
    s5ce[64:, 3] = j8
    s5ce[:64, 0] = j8 + 1
    s5ce[:64, 1] = j8
    s5ce[:64, 2] = -j8
    s5ce[64:, 0] = 8 - j8
    s5ce[64:, 1] = -j8
    s5ce[64:, 2] = j8
    Dm = inputs["ssm_d"][0]
    s5d = np.ascontiguousarray(np.broadcast_to(Dm.T[None], (8, 16, 32)).reshape(128, 32)).astype(f)
    ii = np.repeat(np.arange(8), 16)
    s5m = np.stack([(ii[:, None] <= ii[None, :]), (ii[:, None] >= ii[None, :]), np.eye(128, dtype=bool)], axis=1).astype(f)
    extra = {
        "wx": wx_, "wv2": wv2, "wglu": tiles(inputs["w_glu"][0], 4, 4), "wbs": tiles(inputs["w_branch_ssm"][0], 4, 8),
        "wbn": tiles(inputs["w_branch_na"][0], 4, 8), "wo": tiles(inputs["w_out"][0], 8, 8),
        "rpbpad": rpbpad, "colmask": colmask, "s5c": s5c, "s5cc": s5cc,
        "s5cb": s5cb, "s5ce": s5ce, "s5d": s5d, "s5m": np.ascontiguousarray(s5m),
    }
    return {
        **extra,
        "xT": xT, "metaT": metaT, "gains": gains,
        "w1i": ffn_in(inputs["w_ffn1_in"][0]), "w1o": ffn_out(inputs["w_ffn1_out"][0]),
        "w2i": ffn_in(inputs["w_ffn2_in"][0]), "w2o": ffn_out(inputs["w_ffn2_out"][0]),
        "consts": consts,
    }


def kernel(**inputs):
    inputs = {k: np.asarray(v) for k, v in inputs.items()}
    nc, _ = build_program()
    in_maps = [host_inputs(inputs, c) for c in range(NCORES)]
    res = run_bass_kernel_spmd(nc, in_maps, core_ids=list(range(NCORES)))
    out = np.empty((16, T, D), np.float32)
    for c in range(NCORES):
        o = np.asarray(res.results[c]["outT"])
        out[c * NB_LOCAL:(c + 1) * NB_LOCAL] = o.reshape(NB_LOCAL, D, T).transpose(0, 2, 1)
    return out
```

```python
import numpy as np
from contextlib import ExitStack
import concourse.bass as bass
import concourse.mybir as mybir
from concourse.bass_utils import run_bass_kernel_spmd

F32 = mybir.dt.float32
BF16 = mybir.dt.bfloat16
ALU = mybir.AluOpType
AF = mybir.ActivationFunctionType

D = 1024
T = 2048
NM = 16
DFF = 2816
NFF = 22
NG = 32
NBLK = 258
NB_LOCAL = 2
NCORES = 8
FF_SPLITS = [(0, 8), (8, 15), (15, 22)]
NSLOT = 2
TWO_PI = 6.283185307179586


class Ctr:
    def __init__(self, sem):
        self.sem = sem
        self.cnt = 0


class Buf:
    __slots__ = ("w", "r", "name")

    def __init__(self, name=""):
        self.w = None
        self.r = {}
        self.name = name


class Eng:
    def __init__(self, h, ctr, name):
        self.h = h
        self.ctr = ctr
        self.seen = {}
        self.name = name


class KB:
    def __init__(self, nc, es):
        self.nc = nc
        self.es = es

        def mk(h, n):
            return Eng(h, Ctr(es.enter_context(nc.semaphore(n))), n)

        self.pe = mk(nc.tensor, "s_pe")
        self.act = mk(nc.scalar, "s_act")
        self.dve = mk(nc.vector, "s_dve")
        self.pool = mk(nc.gpsimd, "s_pool")
        self.sp = mk(nc.sync, "s_sp")
        self.engs = [self.pe, self.act, self.dve, self.pool, self.sp]

    def ctr(self, name):
        return Ctr(self.es.enter_context(self.nc.semaphore(name)))

    def _need(self, E, dep, raw):
        ctr, val = dep
        if ctr is E.ctr and not raw:
            return
        if E.seen.get(id(ctr), 0) >= val:
            return
        E.h.wait_ge(ctr.sem, val)
        E.seen[id(ctr)] = val

    def _deps(self, E, reads, writes):
        for b in reads:
            if b.w is not None:
                self._need(E, b.w, True)
        for b in writes:
            if b.w is not None:
                self._need(E, b.w, False)
            for dep in b.r.values():
                self._need(E, dep, False)

    def op(self, E, fn, reads=(), writes=()):
        self._deps(E, reads, writes)
        ins = fn()
        E.ctr.cnt += 1
        ins.then_inc(E.ctr.sem, 1)
        for b in writes:
            b.w = (E.ctr, E.ctr.cnt)
            b.r = {}
        for b in reads:
            b.r[id(E.ctr)] = (E.ctr, E.ctr.cnt)

    def dma(self, Q, ctr, out, in_, reads=(), writes=()):
        self._deps(Q, reads, writes)
        Q.h.dma_start(out=out, in_=in_).then_inc(ctr.sem, 16)
        ctr.cnt += 16
        for b in writes:
            b.w = (ctr, ctr.cnt)
            b.r = {}
        for b in reads:
            b.r[id(ctr)] = (ctr, ctr.cnt)

    def barrier(self):
        for E in self.engs:
            for O in self.engs:
                if O is not E and O.ctr.cnt > 0:
                    self._need(E, (O.ctr, O.ctr.cnt), True)

    def wait_all(self, E, ctrs):
        for c in ctrs:
            if c.cnt > 0:
                self._need(E, (c, c.cnt), True)


class Stream:
    def __init__(self, kb, nc, es, nslot, cols):
        self.kb = kb
        self.items = []
        self.next_load = 0
        self.next_get = 0
        self.nslot = nslot
        self.slots = [es.enter_context(nc.sbuf_tensor(f"ring{i}", [128, cols], BF16)) for i in range(nslot)]
        self.bufs = [Buf(f"ring{i}") for i in range(nslot)]
        self.ctrs = [kb.ctr(f"s_ring{i}") for i in range(nslot)]

    def extend(self, items):
        self.items += items

    def _load(self, n):
        s = n % self.nslot
        src, ncols = self.items[n]
        self.kb.dma(self.kb.pool, self.ctrs[s], self.slots[s][:, 0:ncols], src, writes=[self.bufs[s]])

    def get(self):
        i = self.next_get
        self.next_get += 1
        while self.next_load < min(len(self.items), i + self.nslot):
            self._load(self.next_load)
            self.next_load += 1
        s = i % self.nslot
        return self.slots[s], self.bufs[s]


def sap(t, part0, nparts, off, dims):
    row = 1
    for d in t.shape[1:]:
        row *= d
    return bass.AP(t, part0 * row + off, [[row, nparts]] + [list(d) for d in dims])


def build_program(nb_local=NB_LOCAL, stage=99, dbg=False):
    nc = bass.Bass("TRN2", target_bir_lowering=False)

    def din(name, shape, dt=F32):
        return nc.dram_tensor(name, list(shape), dt, kind="ExternalInput").ap()

    xT = din("xT", [nb_local, 8, 128, T])
    metaT = din("metaT", [128, 8, NM])
    gains_d = din("gains", [128, 4, 8])
    w1i = din("w1i", [NFF, 128, 8 * 256])
    w1o = din("w1o", [8, 128, NFF * 128])
    w2i = din("w2i", [NFF, 128, 8 * 256])
    w2o = din("w2o", [8, 128, NFF * 128])
    consts_d = din("consts", [128, 3, 128])
    wx = din("wx", [32, 128, 1024])
    wv2 = din("wv2", [2, 128, 2048])
    wglu_d = din("wglu", [4, 128, 512])
    wbs_d = din("wbs", [8, 128, 512])
    wbn_d = din("wbn", [8, 128, 512])
    wo_d = din("wo", [8, 128, 1024])
    rpbpad = din("rpbpad", [8, 15, 128])
    colmask_d = din("colmask", [128, 64])
    s5c_d = din("s5c", [128, 3, 32])
    s5cc_d = din("s5cc", [128, 2, 512])
    s5cb_d = din("s5cb", [128, 2, 512])
    s5ce_d = din("s5ce", [128, 4, 8])
    s5d_d = din("s5d", [128, 32])
    s5m_d = din("s5m", [128, 3, 128])
    u_scr = nc.dram_tensor("u_scr", [512, 8, NBLK], BF16, kind="Internal").ap()
    z_scr = nc.dram_tensor("z_scr", [512, 8, NBLK], BF16, kind="Internal").ap()
    u_scr2 = nc.dram_tensor("u_scr2", [128, NG, NBLK], BF16, kind="Internal").ap()
    z_scr2 = nc.dram_tensor("z_scr2", [128, NG, NBLK], BF16, kind="Internal").ap()
    outT = nc.dram_tensor("outT", [nb_local, 8, 128, T], F32, kind="ExternalOutput").ap()
    dbg_outs = {}

    with ExitStack() as es:
        kb = KB(nc, es)
        pe, act, dve, pool, sp = kb.pe, kb.act, kb.dve, kb.pool, kb.sp

        def sb(name, shape, dt):
            return es.enter_context(nc.sbuf_tensor(name, list(shape), dt))

        uid = [0]

        def uname(n):
            uid[0] += 1
            return f"{n}_{uid[0]}"


        Etab = sb("Etab", [128, 4, 15, 64], BF16)
        Etabb = Buf()
        kmT = sb("kmT", [128, 4, NM], BF16)
        kmTb = Buf()
        vm = sb("vm", [128, 4, 64], BF16)
        vmb = Buf()
        umT = sb("umT", [128, 4, 8, 2], BF16)
        umTb = Buf()
        WB = sb("WB", [128, NG, 2, 128], BF16)
        WBb = Buf()
        WC = sb("WC", [128, NG, 2, 128], BF16)
        WCb = Buf()
        WT = sb("WT", [128, NG, 128], BF16)
        WTb = Buf()
        scA = sb("scA", [128, 2, NG], F32)
        scB = sb("scB", [128, 2, NG], F32)
        scb = Buf()
        u_scr_b = Buf()
        z_scr_b = Buf()
        u2b = [Buf() for j in range(8)]
        zsb = [Buf() for j in range(8)]
        c_scr = kb.ctr("s_scr")

        hb = [[Buf(f"h{k}_{c}") for c in range(4)] for k in range(8)]
        hmb = [Buf(f"hm{k}") for k in range(8)]
        gains = sb("gains_sb", [128, 4, 8], F32)
        gainsb = Buf("gains")
        cst = sb("cst", [128, 3, 128], BF16)
        cstb = Buf("cst")
        epsb_t = sb("eps", [128, 1], F32)
        epsb = Buf("eps")
        pb = [es.enter_context(nc.psum_tensor(f"pb{i}", [128, 512], F32)) for i in range(8)]
        pbb = [Buf(f"pb{i}") for i in range(8)]
        ident = cst[:, 0, :]
        ones_bf = cst[:, 1, :]
        blockones = cst[:, 2, :]

        c_misc = kb.ctr("s_misc")
        c_x = kb.ctr("s_x")
        c_out = kb.ctr("s_out")

        kb.dma(sp, c_misc, gains[:, :, :], gains_d[:, :, :], writes=[gainsb])
        c_cst = kb.ctr("s_cst")
        kb.dma(pool, c_cst, cst[:, :, :], consts_d[:, :, :], writes=[cstb])
        kb.op(dve, lambda: nc.vector.memset(epsb_t[:, :], 1e-6), writes=[epsb])

        def load_x(bl):
            for k in range(8):
                kb.dma(sp, c_x, hT[:, k, :], xT[bl, k, :, :], writes=hb[k])
            for k in range(8):
                for c in range(4):
                    hb[k][c].w = (c_x, c_x.cnt)

        def main_chunks(xn, xnb, at, atb):
            chs = []
            for c in range(4):
                chs.append(dict(
                    N=512,
                    h=(lambda k, c=c: hT[:, k, c * 512:(c + 1) * 512]),
                    hb=(lambda k, c=c: hb[k][c]),
                    xn=(lambda k, c=c: xn[:, k, c * 512:(c + 1) * 512]),
                    xnb=(lambda k, c=c: xnb[k][c]),
                    at=(lambda kk, c=c: at[:, kk, c * 512:(c + 1) * 512]),
                    atb=(lambda kk, c=c: atb[kk][c]),
                ))
            return chs

        def norm_chunk(ch, gi, scr, out_fn=None, outb_fn=None):
            N = ch["N"]
            sq, sqb, sd, sdb, rstd, rstdb = scr
            pR, pRb = pb[7], pbb[7]
            for k in range(8):
                kb.op(act, lambda: nc.scalar.activation(out=sq[k % 2][:, :N], in_=ch["h"](k), func=AF.Square),
                      reads=[ch["hb"](k)], writes=[sqb[k % 2]])
                kb.op(pe, lambda: nc.tensor.matmul(pR[:, :N], lhsT=ones_bf, rhs=sq[k % 2][:, :N],
                                                   start=(k == 0), stop=(k == 7)),
                      reads=[sqb[k % 2], cstb], writes=[pRb])
            kb.op(act, lambda: nc.scalar.activation(out=sd[:, :N], in_=pR[:, :N], func=AF.Sqrt,
                                                    scale=1.0 / D, bias=epsb_t[:, 0:1]),
                  reads=[pRb, epsb], writes=[sdb])
            kb.op(dve, lambda: nc.vector.reciprocal(out=rstd[:, :N], in_=sd[:, :N]), reads=[sdb], writes=[rstdb])
            for k in range(8):
                o = ch["xn"](k) if out_fn is None else out_fn(k)
                ob = ch["xnb"](k) if outb_fn is None else outb_fn(k)
                kb.op(dve, lambda: nc.vector.scalar_tensor_tensor(
                    out=o, in0=ch["h"](k), scalar=gains[:, gi, k:k + 1], in1=rstd[:, :N],
                    op0=ALU.mult, op1=ALU.mult),
                    reads=[ch["hb"](k), rstdb, gainsb], writes=[ob])

        def ffn_items(wi, wo):
            items = []
            for (a, b) in FF_SPLITS:
                for m in range(a, b):
                    items.append((wi[m, :, :], 2048))
                for mo in range(8):
                    items.append((wo[mo, :, a * 128:b * 128], (b - a) * 128))
            return items

        def ffn(chunks, gi, scr, sg, sgb):
            for ch in chunks:
                norm_chunk(ch, gi, scr)
            gu = 0
            oi = 0
            for (a, b) in FF_SPLITS:
                nk = b - a
                for m in range(a, b):
                    ws, wb = stream.get()
                    for ch in chunks:
                        N = ch["N"]
                        pG, pGb = pb[gu % 2], pbb[gu % 2]
                        pU, pUb = pb[2 + gu % 2], pbb[2 + gu % 2]
                        for k in range(8):
                            kb.op(pe, lambda: nc.tensor.matmul(pG[:, :N], lhsT=ws[:, k * 256:k * 256 + 128],
                                                               rhs=ch["xn"](k), start=(k == 0), stop=(k == 7)),
                                  reads=[wb, ch["xnb"](k)], writes=[pGb])
                        for k in range(8):
                            kb.op(pe, lambda: nc.tensor.matmul(pU[:, :N], lhsT=ws[:, k * 256 + 128:k * 256 + 256],
                                                               rhs=ch["xn"](k), start=(k == 0), stop=(k == 7)),
                                  reads=[wb, ch["xnb"](k)], writes=[pUb])
                        s_ = sg[gu % 2]
                        kb.op(act, lambda: nc.scalar.activation(out=s_[:, :N], in_=pG[:, :N], func=AF.Silu),
                              reads=[pGb], writes=[sgb[gu % 2]])
                        kb.op(dve, lambda: nc.vector.tensor_tensor(out=ch["at"](m - a), in0=s_[:, :N], in1=pU[:, :N],
                                                                   op=ALU.mult),
                              reads=[sgb[gu % 2], pUb], writes=[ch["atb"](m - a)])
                        gu += 1
                for mo in range(8):
                    ws, wb = stream.get()
                    for ch in chunks:
                        N = ch["N"]
                        pO, pOb = pb[4 + oi % 2], pbb[4 + oi % 2]
                        for kk in range(nk):
                            kb.op(pe, lambda: nc.tensor.matmul(pO[:, :N], lhsT=ws[:, kk * 128:(kk + 1) * 128],
                                                               rhs=ch["at"](kk), start=(kk == 0), stop=(kk == nk - 1)),
                                  reads=[wb, ch["atb"](kk)], writes=[pOb])
                        hk = ch["h"](mo)
                        kb.op(dve, lambda: nc.vector.scalar_tensor_tensor(out=hk, in0=pO[:, :N], scalar=0.5, in1=hk,
                                                                          op0=ALU.mult, op1=ALU.add),
                              reads=[pOb, ch["hb"](mo)], writes=[ch["hb"](mo)])
                        oi += 1

        def ffn_phase(bl, gi, with_meta):
            with ExitStack() as ph:
                def psb(name, shape, dt):
                    return ph.enter_context(nc.sbuf_tensor(uname(name), list(shape), dt))
                xn = psb("xn", [128, 8, T], BF16)
                xnb = [[Buf() for c in range(4)] for k in range(8)]
                at = psb("at", [128, 8, T], BF16)
                atb = [[Buf() for c in range(4)] for k in range(8)]
                sq = [psb(f"sq{i}", [128, 512], BF16) for i in range(2)]
                sqb = [Buf(), Buf()]
                sd = psb("sd", [128, 512], F32)
                rstd = psb("rstd", [128, 512], F32)
                sg = [psb(f"sg{i}", [128, 512], BF16) for i in range(2)]
                sgb = [Buf(), Buf()]
                scr = (sq, sqb, sd, Buf(), rstd, Buf())
                chunks = main_chunks(xn, xnb, at, atb)
                if with_meta:
                    xnm = psb("xnm", [128, 8, NM], BF16)
                    xnmb = [Buf() for k in range(8)]
                    atm = psb("atm", [128, 8, NM], BF16)
                    atmb = [Buf() for k in range(8)]
                    chunks.append(dict(N=NM, h=lambda k: hm[:, k, :], hb=lambda k: hmb[k],
                                       xn=lambda k: xnm[:, k, :], xnb=lambda k: xnmb[k],
                                       at=lambda kk: atm[:, kk, :], atb=lambda kk: atmb[kk]))
                ffn(chunks, gi, scr, sg, sgb)
                kb.barrier()

        def final_phase(bl):
            with ExitStack() as ph:
                def psb(name, shape, dt):
                    return ph.enter_context(nc.sbuf_tensor(uname(name), list(shape), dt))
                sq = [psb(f"fsq{i}", [128, 512], BF16) for i in range(2)]
                sd = psb("fsd", [128, 512], F32)
                rstd = psb("frstd", [128, 512], F32)
                scr = (sq, [Buf(), Buf()], sd, Buf(), rstd, Buf())
                chunks = main_chunks(None, None, None, None)
                for c, ch in enumerate(chunks):
                    norm_chunk(ch, 3, scr, out_fn=ch["h"], outb_fn=ch["hb"])
                for k in range(8):
                    kb.dma(sp, c_out, outT[bl, k, :, :], hT[:, k, :], reads=hb[k])
                kb.barrier()

        def dump(name, ap, shape, reads):
            t = nc.dram_tensor("dbg_" + name, list(shape), ap.dtype, kind="ExternalOutput").ap()
            dbg_outs[name] = t
            kb.dma(sp, c_out, t, ap, reads=reads)

        def psbf(ph):
            def f(name, shape, dt):
                return ph.enter_context(nc.sbuf_tensor(uname(name), list(shape), dt))
            return f

        def V(fn, reads, writes):
            kb.op(dve, fn, reads=reads, writes=writes)

        def A(fn, reads, writes):
            kb.op(act, fn, reads=reads, writes=writes)

        def setup_na():
            with ExitStack() as ph:
                psb = psbf(ph)
                c_s = kb.ctr("s_setna")
                raw = psb("eraw", [128, 4, 15, 64], F32)
                rawb = Buf()
                exn = psb("eexp", [128, 4, 15, 64], F32)
                exb = Buf()
                cm = psb("cmask", [128, 64], F32)
                cmb = Buf()
                kb.dma(sp, c_s, cm[:, :], colmask_d[:, :], writes=[cmb])
                for hp in range(4):
                    for h2 in range(2):
                        h = 2 * hp + h2
                        src = bass.AP(rpbpad.tensor, h * 15 * 128, [[1, 64], [128, 15], [1, 64]])
                        kb.dma(sp, c_s, raw[h2 * 64:(h2 + 1) * 64, hp, :, :], src, writes=[rawb])
                rawb.w = (c_s, c_s.cnt)
                cmb.w = (c_s, c_s.cnt)
                A(lambda: nc.scalar.activation(out=sap(exn, 0, 128, 0, [[1, 3840]]), in_=sap(raw, 0, 128, 0, [[1, 3840]]),
                                               func=AF.Exp), [rawb], [exb])
                V(lambda: nc.vector.tensor_tensor(out=sap(Etab, 0, 128, 0, [[64, 60], [1, 64]]),
                                                  in0=sap(exn, 0, 128, 63, [[64, 60], [-1, 64]]),
                                                  in1=sap(cm, 0, 128, 0, [[0, 60], [1, 64]]), op=ALU.mult),
                  [exb, cmb], [Etabb])
                kb.barrier()

        MAGIC = 12582912.0

        def cexp(xr, xi, ore, oim, t, F_, b):
            R = [b]
            A(lambda: nc.scalar.activation(out=xr, in_=xr, func=AF.Exp), R, R)
            V(lambda: nc.vector.tensor_scalar(out=xi, in0=xi, scalar1=1.0 / TWO_PI, scalar2=None, op0=ALU.mult), R, R)
            for which, o in ((0, oim), (1, ore)):
                if which == 1:
                    V(lambda: nc.vector.tensor_scalar(out=xi, in0=xi, scalar1=0.25, scalar2=None, op0=ALU.add), R, R)
                V(lambda: nc.vector.tensor_scalar(out=t, in0=xi, scalar1=MAGIC, scalar2=None, op0=ALU.add), R, R)
                V(lambda: nc.vector.tensor_scalar(out=t, in0=t, scalar1=MAGIC, scalar2=None, op0=ALU.subtract), R, R)
                V(lambda: nc.vector.tensor_tensor(out=t, in0=xi, in1=t, op=ALU.subtract), R, R)
                A(lambda: nc.scalar.activation(out=t, in_=t, func=AF.Sin, scale=TWO_PI), R, R)
                V(lambda: nc.vector.tensor_tensor(out=o, in0=xr, in1=t, op=ALU.mult), R, R)

        def cmul(ore, oim, ar_, ai_, br_, bi_, t1, t2, R, W, neg_im=False):
            V(lambda: nc.vector.tensor_tensor(out=t1, in0=ar_, in1=br_, op=ALU.mult), R, R)
            V(lambda: nc.vector.tensor_tensor(out=t2, in0=ai_, in1=bi_, op=ALU.mult), R, R)
            V(lambda: nc.vector.tensor_tensor(out=ore, in0=t1, in1=t2, op=ALU.subtract), R, W)
            V(lambda: nc.vector.tensor_tensor(out=t1, in0=ar_, in1=bi_, op=ALU.mult), R, R)
            V(lambda: nc.vector.tensor_tensor(out=t2, in0=ai_, in1=br_, op=ALU.mult), R, R)
            if neg_im:
                V(lambda: nc.vector.scalar_tensor_tensor(out=oim, in0=t1, scalar=-1.0, in1=t2, op0=ALU.mult,
                                                         op1=ALU.subtract), R, W)
            else:
                V(lambda: nc.vector.tensor_tensor(out=oim, in0=t1, in1=t2, op=ALU.add), R, W)

        def lam_and_f(are, aim, ldt, ardt, aidt, lre, lim, fre, fim, t1, t2, t3, F_, b):
            R = [b]
            A(lambda: nc.scalar.activation(out=ldt, in_=ldt, func=AF.Exp), R, R)
            V(lambda: nc.vector.tensor_scalar(out=are, in0=are, scalar1=-1e-4, scalar2=None, op0=ALU.min), R, R)
            V(lambda: nc.vector.tensor_tensor(out=ardt, in0=are, in1=ldt, op=ALU.mult), R, R)
            V(lambda: nc.vector.tensor_tensor(out=aidt, in0=aim, in1=ldt, op=ALU.mult), R, R)
            V(lambda: nc.vector.tensor_copy(out=t1, in_=ardt), R, R)
            V(lambda: nc.vector.tensor_copy(out=t2, in_=aidt), R, R)
            cexp(t1, t2, lre, lim, t3, F_, b)
            V(lambda: nc.vector.tensor_tensor(out=t1, in0=are, in1=are, op=ALU.mult), R, R)
            V(lambda: nc.vector.tensor_tensor(out=t2, in0=aim, in1=aim, op=ALU.mult), R, R)
            V(lambda: nc.vector.tensor_tensor(out=t3, in0=t1, in1=t2, op=ALU.add), R, R)
            V(lambda: nc.vector.reciprocal(out=t3, in_=t3), R, R)
            V(lambda: nc.vector.tensor_scalar(out=t1, in0=lre, scalar1=-1.0, scalar2=None, op0=ALU.add), R, R)
            V(lambda: nc.vector.tensor_tensor(out=fre, in0=t1, in1=are, op=ALU.mult), R, R)
            V(lambda: nc.vector.tensor_tensor(out=t2, in0=lim, in1=aim, op=ALU.mult), R, R)
            V(lambda: nc.vector.tensor_tensor(out=fre, in0=fre, in1=t2, op=ALU.add), R, R)
            V(lambda: nc.vector.tensor_tensor(out=fre, in0=fre, in1=t3, op=ALU.mult), R, R)
            V(lambda: nc.vector.tensor_tensor(out=fim, in0=lim, in1=are, op=ALU.mult), R, R)
            V(lambda: nc.vector.tensor_tensor(out=t2, in0=t1, in1=aim, op=ALU.mult), R, R)
            V(lambda: nc.vector.tensor_tensor(out=fim, in0=fim, in1=t2, op=ALU.subtract), R, R)
            V(lambda: nc.vector.tensor_tensor(out=fim, in0=fim, in1=t3, op=ALU.mult), R, R)

        def setup_s5():
            c_s = kb.ctr("s_sets5")
            with ExitStack() as ph:
                psb = psbf(ph)
                b = Buf()
                F_ = NG
                names = ["are", "aim", "ldt", "ardt", "aidt", "lre", "lim", "fre", "fim", "t1", "t2", "t3"]
                tl = {n: psb("C" + n, [128, F_], F32) for n in names}
                fl = {n: tl[n][:, :] for n in names}
                Cc = psb("CC", [128, 2, 512], F32)
                Bc = psb("CB", [128, 2, 512], F32)
                eps = psb("Ceps", [128, 4, 8], F32)
                Dt = psb("CD", [128, NG], F32)
                msk = psb("Cmsk", [128, 3, 128], F32)
                for wi_, n in enumerate(["are", "aim", "ldt"]):
                    kb.dma(sp, c_s, tl[n][:, :], s5c_d[:, wi_, :], writes=[b])
                kb.dma(sp, c_s, Cc[:, :, :], s5cc_d[:, :, :], writes=[b])
                kb.dma(sp, c_s, Bc[:, :, :], s5cb_d[:, :, :], writes=[b])
                kb.dma(sp, c_s, eps[:, :, :], s5ce_d[:, :, :], writes=[b])
                kb.dma(sp, c_s, Dt[:, :], s5d_d[:, :], writes=[b])
                kb.dma(sp, c_s, msk[:, :, :], s5m_d[:, :, :], writes=[b])
                b.w = (c_s, c_s.cnt)
                R = [b]
                lam_and_f(fl["are"], fl["aim"], fl["ldt"], fl["ardt"], fl["aidt"], fl["lre"], fl["lim"],
                          fl["fre"], fl["fim"], fl["t1"], fl["t2"], fl["t3"], F_, b)
                V(lambda: nc.vector.tensor_scalar(out=fl["t1"], in0=fl["ardt"], scalar1=8.0, scalar2=None, op0=ALU.mult), R, R)
                V(lambda: nc.vector.tensor_scalar(out=fl["t2"], in0=fl["aidt"], scalar1=8.0, scalar2=None, op0=ALU.mult), R, R)
                cexp(fl["t1"], fl["t2"], fl["lre"], fl["lim"], fl["t3"], F_, b)
                V(lambda: nc.vector.tensor_copy(out=scA[:, 0, :], in_=fl["lre"]), R, [scb])
                V(lambda: nc.vector.tensor_copy(out=scA[:, 1, :], in_=fl["lre"]), R, [scb])
                V(lambda: nc.vector.tensor_scalar(out=scB[:, 0, :], in0=fl["lim"], scalar1=-1.0, scalar2=None, op0=ALU.mult),
                  R, [scb])
                V(lambda: nc.vector.tensor_copy(out=scB[:, 1, :], in_=fl["lim"]), R, [scb])
                pw = {}
                G8 = NG * 8
                xr = psb("Cxr", [128, G8], F32)
                xi = psb("Cxi", [128, G8], F32)
                tt = psb("Ctt", [128, G8], F32)
                for ei, nm in enumerate(["C", "G", "A", "B"]):
                    pr = psb("Cp%sr" % nm, [128, G8], F32)
                    pi_ = psb("Cp%si" % nm, [128, G8], F32)
                    e_b = sap(eps, 0, 128, ei * 8, [[0, NG], [1, 8]])
                    V(lambda: nc.vector.tensor_tensor(out=sap(xr, 0, 128, 0, [[8, NG], [1, 8]]),
                                                      in0=sap(tl["ardt"], 0, 128, 0, [[1, NG], [0, 8]]), in1=e_b,
                                                      op=ALU.mult), R, R)
                    V(lambda: nc.vector.tensor_tensor(out=sap(xi, 0, 128, 0, [[8, NG], [1, 8]]),
                                                      in0=sap(tl["aidt"], 0, 128, 0, [[1, NG], [0, 8]]), in1=e_b,
                                                      op=ALU.mult), R, R)
                    cexp(xr[:, :], xi[:, :], pr[:, :], pi_[:, :], tt[:, :], G8, b)
                    pw[nm] = (pr, pi_)
                fr_b = sap(tl["fre"], 0, 128, 0, [[1, NG], [0, 8]])
                fi_b = sap(tl["fim"], 0, 128, 0, [[1, NG], [0, 8]])

                def g8(t_):
                    return sap(t_, 0, 128, 0, [[8, NG], [1, 8]])
                far = psb("Cfar", [128, G8], F32)
                fai = psb("Cfai", [128, G8], F32)
                cmul(g8(far), g8(fai), fr_b, fi_b, g8(pw["A"][0]), g8(pw["A"][1]), g8(xr), g8(xi), R, R)
                BIG = NG * 8 * 16
                t1b = psb("Ct1b", [128, BIG], F32)
                t2b = psb("Ct2b", [128, BIG], F32)
                Are = psb("CAre", [128, NG, 128], BF16)
                Aim = psb("CAim", [128, NG, 128], BF16)
                Gre = psb("CGre", [128, NG, 128], BF16)
                Gim = psb("CGim", [128, NG, 128], BF16)

                def bigv(t_):
                    return sap(t_, 0, 128, 0, [[128, NG], [16, 8], [1, 16]])

                def pwv(t_):
                    return sap(t_, 0, 128, 0, [[8, NG], [1, 8], [0, 16]])

                def cpv(t_, c):
                    return sap(t_, 0, 128, c * 512, [[16, NG], [0, 8], [1, 16]])
                cmul(sap(WC, 0, 128, 0, [[256, NG], [16, 8], [1, 16]]), sap(WC, 0, 128, 128, [[256, NG], [16, 8], [1, 16]]),
                     cpv(Cc, 0), cpv(Cc, 1), pwv(pw["C"][0]), pwv(pw["C"][1]), bigv(t1b), bigv(t2b), R, [b, WCb], neg_im=True)
                cmul(bigv(Gre), bigv(Gim), cpv(Cc, 0), cpv(Cc, 1), pwv(pw["G"][0]), pwv(pw["G"][1]), bigv(t1b), bigv(t2b),
                     R, R, neg_im=True)
                cmul(bigv(Are), bigv(Aim), pwv(far), pwv(fai), cpv(Bc, 0), cpv(Bc, 1), bigv(t1b), bigv(t2b), R, R)
                fbr = psb("Cfbr", [128, G8], F32)
                fbi = psb("Cfbi", [128, G8], F32)
                cmul(g8(fbr), g8(fbi), fr_b, fi_b, g8(pw["B"][0]), g8(pw["B"][1]), g8(xr), g8(xi), R, R)
                PBc = [psb("CPBre", [128, NG, 128], BF16), psb("CPBim", [128, NG, 128], BF16)]
                cmul(bigv(PBc[0]), bigv(PBc[1]), pwv(fbr), pwv(fbi), cpv(Bc, 0), cpv(Bc, 1), bigv(t1b), bigv(t2b), R, R)
                for g2 in range(NG // 2):
                    pw_, pwb_ = pb[2 + g2 % 2], pbb[2 + g2 % 2]
                    for gg in range(2):
                        for c in range(2):
                            blk = gg * 2 + c
                            kb.op(pe, lambda: nc.tensor.matmul(pw_[:, blk * 128:(blk + 1) * 128], lhsT=PBc[c][:, 2 * g2 + gg, :],
                                                               rhs=ident, start=True, stop=True),
                                  reads=[b, cstb], writes=[pwb_])
                    V(lambda: nc.vector.tensor_copy(out=sap(WB, 0, 128, 2 * g2 * 256, [[1, 512]]), in_=pw_[:, :]),
                      [pwb_], [WBb])
                tmpf = psb("Ctmpf", [128, 512], F32)
                for g4 in range(NG // 4):
                    pf, pfb = pb[0], pbb[0]
                    pq, pqb = pb[1], pbb[1]
                    for gg in range(4):
                        g = g4 * 4 + gg
                        for (P_, Pb_, lo) in ((pf, pfb, 0), (pq, pqb, 64)):
                            kb.op(pe, lambda: nc.tensor.matmul(P_[:, gg * 128:(gg + 1) * 128], lhsT=Are[lo:lo + 64, g, :],
                                                               rhs=Gre[lo:lo + 64, g, :], start=True, stop=False),
                                  reads=[b], writes=[Pb_])
                            kb.op(pe, lambda: nc.tensor.matmul(P_[:, gg * 128:(gg + 1) * 128], lhsT=Aim[lo:lo + 64, g, :],
                                                               rhs=Gim[lo:lo + 64, g, :], start=False, stop=True),
                                  reads=[b], writes=[Pb_])
                    mL = sap(msk, 0, 128, 0, [[0, 4], [1, 128]])
                    mU = sap(msk, 0, 128, 128, [[0, 4], [1, 128]])
                    t4 = sap(tmpf, 0, 128, 0, [[128, 4], [1, 128]])
                    V(lambda: nc.vector.tensor_tensor(out=t4, in0=sap(pf, 0, 128, 0, [[128, 4], [1, 128]]), in1=mL,
                                                      op=ALU.mult), [pfb, b], [b])
                    V(lambda: nc.vector.tensor_tensor(out=sap(t1b, 0, 128, 0, [[128, 4], [1, 128]]),
                                                      in0=sap(pq, 0, 128, 0, [[128, 4], [1, 128]]), in1=mU, op=ALU.mult),
                      [pqb, b], [b])
                    V(lambda: nc.vector.tensor_tensor(out=t4, in0=t4, in1=sap(t1b, 0, 128, 0, [[128, 4], [1, 128]]),
                                                      op=ALU.add), R, R)
                    for gg in range(4):
                        g = g4 * 4 + gg
                        V(lambda: nc.vector.scalar_tensor_tensor(out=WT[:, g, :], in0=msk[:, 2, :], scalar=Dt[:, g:g + 1],
                                                                 in1=tmpf[:, gg * 128:(gg + 1) * 128], op0=ALU.mult,
                                                                 op1=ALU.add), R, [b, WTb])
                kb.barrier()
        def mixer_items():
            items = []
            for mt in range(12):
                items.append((wx[mt, :, :], 1024))
            for half in range(2):
                items.append((wv2[half, :, :], 2048))
            return items

        def krs_for(qg):
            out = []
            for kr in range(32):
                rows = [r for r in range(8 * qg, 8 * qg + 8)
                        if min(max(r - 4, 0), 24) <= kr <= min(max(r - 4, 0), 24) + 7]
                if rows:
                    out.append((kr, rows[0], rows[-1]))
            return out

        def inproj_phase(bl, qT, qTb, kT, kTb, vd, vdb):
            with ExitStack() as ph:
                psb = psbf(ph)
                xn2 = psb("xn2", [128, 8, T], BF16)
                xn2b = [[Buf() for c in range(4)] for k in range(8)]
                sq = [psb(f"sq{i}", [128, 512], BF16) for i in range(2)]
                scr = (sq, [Buf(), Buf()], pb[6], pbb[6], pb[6], pbb[6])
                chunks = main_chunks(xn2, xn2b, None, None)
                with_meta = (bl == 0)
                if with_meta:
                    xnm = psb("xn2m", [128, 8, NM], BF16)
                    xnmb = [Buf() for k in range(8)]
                    chunks.append(dict(N=NM, h=lambda k: hm[:, k, :], hb=lambda k: hmb[k],
                                       xn=lambda k: xnm[:, k, :], xnb=lambda k: xnmb[k]))
                for ch in chunks:
                    norm_chunk(ch, 1, scr)
                ust = [psb(f"ust{i}", [128, 8, 64], BF16) for i in range(2)]
                ustb = [Buf(), Buf()]
                ui = 0
                it = 0
                for mt in range(12):
                    ws, wb = stream.get()
                    for ci, ch in enumerate(chunks):
                        N = ch["N"]
                        is_meta = (ci == 4)
                        if is_meta and 4 <= mt < 8:
                            continue
                        pS, pSb = pb[it % 4], pbb[it % 4]
                        it += 1
                        for k in range(8):
                            kb.op(pe, lambda: nc.tensor.matmul(pS[:, :N], lhsT=ws[:, k * 128:(k + 1) * 128], rhs=ch["xn"](k),
                                                               start=(k == 0), stop=(k == 7)),
                                  reads=[wb, ch["xnb"](k)], writes=[pSb])
                        if mt < 4:
                            if is_meta:
                                o = sap(umT, 0, 128, mt * 16, [[2, 8], [1, 2]])
                                i_ = sap(pS, 0, 128, 0, [[1, 8], [8, 2]])
                                V(lambda: nc.vector.tensor_copy(out=o, in_=i_), [pSb], [umTb])
                            else:
                                us_, usb_ = ust[ui % 2], ustb[ui % 2]
                                ui += 1
                                i_ = sap(pS, 0, 128, 0, [[1, 8], [8, 64]])
                                V(lambda: nc.vector.tensor_copy(out=us_[:, :, :], in_=i_), [pSb], [usb_])
                                kb.dma(sp, c_scr, u_scr[mt * 128:(mt + 1) * 128, :, 2 + 64 * ci:2 + 64 * ci + 64], us_[:, :, :],
                                       reads=[usb_], writes=[u_scr_b])
                        elif mt < 8:
                            A(lambda: nc.scalar.copy(out=qT[:, mt - 4, ci * 512:(ci + 1) * 512], in_=pS[:, :N]),
                              [pSb], [qTb[mt - 4][ci]])
                        else:
                            if is_meta:
                                A(lambda: nc.scalar.copy(out=kmT[:, mt - 8, :], in_=pS[:, :N]), [pSb], [kmTb])
                            else:
                                A(lambda: nc.scalar.copy(out=kT[:, mt - 8, ci * 512:(ci + 1) * 512], in_=pS[:, :N]),
                                  [pSb], [kTb[mt - 8]])
                    if mt < 4:
                        kb.dma(sp, c_scr, u_scr[mt * 128:(mt + 1) * 128, :, 0:2], umT[:, mt, :, :], reads=[umTb],
                               writes=[u_scr_b])
                for j in range(8):
                    srcA = bass.AP(u_scr.tensor, j * NBLK, [[8 * NBLK, 16], [16 * 8 * NBLK, NG], [1, NBLK]])
                    dstA = bass.AP(u_scr2.tensor, j * 16 * NG * NBLK, [[NG * NBLK, 16], [NBLK, NG], [1, NBLK]])
                    kb.dma(sp, c_scr, dstA, srcA, reads=[u_scr_b], writes=[u2b[j]])
                for j in range(8):
                    u2b[j].w = (c_scr, c_scr.cnt)
                for half in range(2):
                    ws, wb = stream.get()
                    for w in range(-1, 32):
                        pS, pSb = pb[it % 4], pbb[it % 4]
                        it += 1
                        if w == -1:
                            t0, M, p0 = 0, 64, 64
                        elif w == 31:
                            t0, M, p0 = 64 * 31, 64, 0
                        else:
                            t0, M, p0 = 64 * w, 128, 0
                        rb_ = [xn2b[0][t0 // 512], xn2b[0][(t0 + M - 1) // 512]]
                        for k in range(8):
                            kb.op(pe, lambda: nc.tensor.matmul(pS[p0:p0 + M, 0:256], lhsT=xn2[:, k, t0:t0 + M],
                                                               rhs=ws[:, k * 256:(k + 1) * 256], start=(k == 0), stop=(k == 7)),
                                  reads=[wb, xn2b[k][t0 // 512], xn2b[k][(t0 + M - 1) // 512]], writes=[pSb])
                        if w >= 0:
                            V(lambda: nc.vector.tensor_copy(out=vd[0:64, w, 2 * half:2 * half + 2, :],
                                                            in_=sap(pS, 0, 64, 0, [[128, 2], [1, 64]])), [pSb], [vdb])
                        if w < 31:
                            A(lambda: nc.scalar.copy(out=vd[64:128, w + 1, 2 * half:2 * half + 2, :],
                                                     in_=sap(pS, 64, 64, 64, [[128, 2], [1, 64]])), [pSb], [vdb])
                    if with_meta:
                        pS, pSb = pb[it % 4], pbb[it % 4]
                        it += 1
                        for p0 in (0, 64):
                            for k in range(8):
                                kb.op(pe, lambda: nc.tensor.matmul(pS[p0:p0 + NM, 0:256], lhsT=xnm[:, k, :],
                                                                   rhs=ws[:, k * 256:(k + 1) * 256], start=(k == 0),
                                                                   stop=(k == 7)),
                                      reads=[wb, xnmb[k]], writes=[pSb])
                        V(lambda: nc.vector.tensor_copy(out=vm[0:NM, 2 * half:2 * half + 2, :],
                                                        in_=sap(pS, 0, NM, 0, [[128, 2], [1, 64]])), [pSb], [vmb])
                        V(lambda: nc.vector.tensor_copy(out=vm[64:64 + NM, 2 * half:2 * half + 2, :],
                                                        in_=sap(pS, 64, NM, 64, [[128, 2], [1, 64]])), [pSb], [vmb])
                kb.barrier()

        def na_phase(bl, qT, qTb, kT, kTb, vd, vdb, oT, oTb):
            LA = 3
            NSB = 4
            with ExitStack() as ph:
                psb = psbf(ph)
                exs = [psb(f"nex{i}", [128, 512], BF16) for i in range(NSB)]
                exb = [Buf() for i in range(NSB)]
                Ps = [psb(f"nP{i}", [128, 512], BF16) for i in range(NSB)]
                Pb = [Buf() for i in range(NSB)]
                pms = [psb(f"npm{i}", [128, 512], BF16) for i in range(2)]
                pmb = [Buf(), Buf()]
                rec = psb("nrec", [128, 512], F32)
                recb = Buf()
                for i in range(2):
                    V(lambda: nc.vector.memset(pms[i][:, :], 0.0), [], [pmb[i]])
                tiles_ = []
                gi_ = 0
                for hp in range(4):
                    for qg in range(4):
                        lst = krs_for(qg)
                        tiles_.append(dict(kind="meta", hp=hp, qg=qg, gi=gi_, first=True, last=False))
                        for idx, (kr, ra, rb2) in enumerate(lst):
                            tiles_.append(dict(kind="kr", hp=hp, qg=qg, gi=gi_, kr=kr, ra=ra, rb=rb2, first=False,
                                               last=(idx == len(lst) - 1)))
                        gi_ += 1

                def emit_S(t, tl_):
                    hp, qg = tl_["hp"], tl_["qg"]
                    pS, pSb = pb[t % NSB], pbb[t % NSB]
                    if tl_["kind"] == "meta":
                        tok0 = 512 * qg
                        for lo in (0, 64):
                            kb.op(pe, lambda: nc.tensor.matmul(pS[lo:lo + NM, 0:512], lhsT=kmT[lo:lo + 64, hp, :],
                                                               rhs=qT[lo:lo + 64, hp, tok0:tok0 + 512], start=True, stop=True),
                                  reads=[kmTb, qTb[hp][qg]], writes=[pSb])
                    else:
                        kr, ra, rb2 = tl_["kr"], tl_["ra"], tl_["rb"]
                        N = 64 * (rb2 - ra + 1)
                        for lo in (0, 64):
                            kb.op(pe, lambda: nc.tensor.matmul(pS[lo:lo + 64, 0:N], lhsT=kT[lo:lo + 64, hp, 64 * kr:64 * kr + 64],
                                                               rhs=qT[lo:lo + 64, hp, 64 * ra:64 * (rb2 + 1)],
                                                               start=True, stop=True),
                                  reads=[kTb[hp], qTb[hp][qg]], writes=[pSb])

                def emit_E(t, tl_):
                    hp, qg = tl_["hp"], tl_["qg"]
                    pS, pSb = pb[t % NSB], pbb[t % NSB]
                    if tl_["kind"] == "meta":
                        pm, pmb_ = pms[tl_["gi"] % 2], pmb[tl_["gi"] % 2]
                        for lo in (0, 64):
                            A(lambda: nc.scalar.activation(out=pm[lo:lo + NM, :], in_=pS[lo:lo + NM, 0:512], func=AF.Exp,
                                                           scale=0.125), [pSb], [pmb_])
                    else:
                        kr, ra, rb2 = tl_["kr"], tl_["ra"], tl_["rb"]
                        N = 64 * (rb2 - ra + 1)
                        ex_, exb_ = exs[t % NSB], exb[t % NSB]
                        P_, Pb_ = Ps[t % NSB], Pb[t % NSB]
                        A(lambda: nc.scalar.activation(out=ex_[:, 0:N], in_=pS[:, 0:N], func=AF.Exp, scale=0.125),
                          [pSb], [exb_])
                        rho = ra - kr + 7
                        V(lambda: nc.vector.tensor_tensor(out=P_[:, 0:N], in0=ex_[:, 0:N],
                                                          in1=sap(Etab, 0, 128, (hp * 15 + rho) * 64, [[1, N]]),
                                                          op=ALU.mult), [exb_, Etabb], [Pb_])

                def emit_PV(t, tl_):
                    hp, qg, g_ = tl_["hp"], tl_["qg"], tl_["gi"]
                    pO, pOb = pb[4 + g_ % 2], pbb[4 + g_ % 2]
                    pD, pDb = pb[6 + g_ % 2], pbb[6 + g_ % 2]
                    if tl_["kind"] == "meta":
                        pm, pmb_ = pms[g_ % 2], pmb[g_ % 2]
                        for lo in (0, 64):
                            kb.op(pe, lambda: nc.tensor.matmul(pO[lo:lo + 64, :], lhsT=vm[lo:lo + NM, hp, :],
                                                               rhs=pm[lo:lo + NM, :], start=True, stop=False),
                                  reads=[vmb, pmb_], writes=[pOb])
                        kb.op(pe, lambda: nc.tensor.matmul(pD[:, :], lhsT=blockones, rhs=pm[:, :], start=True, stop=False),
                              reads=[cstb, pmb_], writes=[pDb])
                    else:
                        kr, ra, rb2 = tl_["kr"], tl_["ra"], tl_["rb"]
                        N = 64 * (rb2 - ra + 1)
                        c0 = 64 * (ra - 8 * qg)
                        last = tl_["last"]
                        P_, Pb_ = Ps[t % NSB], Pb[t % NSB]
                        for lo in (0, 64):
                            kb.op(pe, lambda: nc.tensor.matmul(pO[lo:lo + 64, c0:c0 + N], lhsT=vd[lo:lo + 64, kr, hp, :],
                                                               rhs=P_[lo:lo + 64, 0:N], start=False, stop=last),
                                  reads=[vdb, Pb_], writes=[pOb])
                        kb.op(pe, lambda: nc.tensor.matmul(pD[:, c0:c0 + N], lhsT=blockones, rhs=P_[:, 0:N],
                                                           start=False, stop=last),
                              reads=[cstb, Pb_], writes=[pDb])
                        if last:
                            tok0 = 512 * qg
                            V(lambda: nc.vector.reciprocal(out=rec[:, :], in_=pD[:, :]), [pDb], [recb])
                            V(lambda: nc.vector.tensor_tensor(out=oT[:, hp, tok0:tok0 + 512], in0=pO[:, :], in1=rec[:, :],
                                                              op=ALU.mult), [pOb, recb], [oTb[hp][qg]])

                nt = len(tiles_)
                for t in range(min(LA, nt)):
                    emit_S(t, tiles_[t])
                for t in range(nt):
                    emit_E(t, tiles_[t])
                    if t + LA < nt:
                        emit_S(t + LA, tiles_[t + LA])
                    emit_PV(t, tiles_[t])
                kb.barrier()

        def s5_phase(bl, zT, zTb):
            with ExitStack() as ph:
                psb = psbf(ph)
                def ubg(g):
                    return sap(zT, 0, 128, g * NBLK, [[1, NBLK]])

                def ubrows(j):
                    return sap(zT, 16 * j, 16, 0, [[NBLK, NG], [1, NBLK]])
                ubb = [Buf() for g in range(NG)]
                Vs = psb("Vs", [128, 2, NG, NBLK], BF16)
                Vsb = Buf()
                Ssb = Buf()
                X = [psb(f"X{i}", [128, 2, NG], F32) for i in range(2)]
                Xb = [Buf(), Buf()]
                t1 = psb("st1", [128, 2, NG], F32)
                t2 = psb("st2", [128, 2, NG], F32)
                tb = Buf()
                g1s = [psb(f"sg1{i}", [128, NBLK], F32) for i in range(2)]
                g2s = [psb(f"sg2{i}", [128, NBLK], F32) for i in range(2)]
                gbs = [Buf(), Buf()]
                kb.dma(sp, c_scr, sap(zT, 0, 128, 0, [[NBLK, NG], [1, NBLK]]), u_scr2[:, :, :], reads=u2b, writes=ubb)
                for g in range(NG):
                    ubb[g].w = (c_scr, c_scr.cnt)
                for g in range(NG):
                    pR_, pRb_ = pb[(2 * g) % 4], pbb[(2 * g) % 4]
                    pI_, pIb_ = pb[(2 * g + 1) % 4], pbb[(2 * g + 1) % 4]
                    kb.op(pe, lambda: nc.tensor.matmul(pR_[:, 0:NBLK], lhsT=WB[:, g, 0, :], rhs=ubg(g), start=True, stop=True),
                          reads=[WBb, ubb[g]], writes=[pRb_])
                    kb.op(pe, lambda: nc.tensor.matmul(pI_[:, 0:NBLK], lhsT=WB[:, g, 1, :], rhs=ubg(g), start=True, stop=True),
                          reads=[WBb, ubb[g]], writes=[pIb_])
                    for c, (P_, Pb_) in enumerate(((pR_, pRb_), (pI_, pIb_))):
                        V(lambda: nc.vector.tensor_copy(out=Vs[0:64, c, g, :], in_=P_[0:64, 0:NBLK]), [Pb_], [Vsb])
                        V(lambda: nc.vector.tensor_copy(out=Vs[64:128, c, g, :], in_=sap(P_, 64, 64, NBLK - 1, [[-1, NBLK]])),
                          [Pb_], [Vsb])
                V(lambda: nc.vector.memset(X[1][:, :, :], 0.0), [], [Xb[1]])
                for i in range(NBLK):
                    Xp, Xpb = X[(i + 1) % 2], Xb[(i + 1) % 2]
                    Xn, Xnb = X[i % 2], Xb[i % 2]
                    sw = sap(Xp, 0, 128, NG, [[-NG, 2], [1, NG]])
                    V(lambda: nc.vector.tensor_tensor(out=t1[:, :, :], in0=scA[:, :, :], in1=Xp[:, :, :], op=ALU.mult),
                      [scb, Xpb], [tb])
                    V(lambda: nc.vector.tensor_tensor(out=t2[:, :, :], in0=scB[:, :, :], in1=sw, op=ALU.mult), [scb, Xpb], [tb])
                    V(lambda: nc.vector.tensor_tensor(out=t1[:, :, :], in0=t1[:, :, :], in1=t2[:, :, :], op=ALU.add), [tb], [tb])
                    vi = sap(Vs, 0, 128, i, [[NG * NBLK, 2], [NBLK, NG]])
                    V(lambda: nc.vector.tensor_tensor(out=Xn[:, :, :], in0=t1[:, :, :], in1=vi, op=ALU.add), [tb, Vsb], [Xnb])
                    A(lambda: nc.scalar.copy(out=vi, in_=Xn[:, :, :]), [Xnb], [Ssb])
                al = [psb(f"al{i}", [128, 2, NBLK], BF16) for i in range(2)]
                alb = [Buf(), Buf()]
                for i in range(2):
                    V(lambda: nc.vector.memset(al[i][:, :, :], 0.0), [], [alb[i]])
                for g in range(NG):
                    pY, pYb = pb[4 + g % 2], pbb[4 + g % 2]
                    a_, ab_ = al[g % 2], alb[g % 2]
                    V(lambda: nc.vector.tensor_copy(out=a_[0:64, :, 1:NBLK],
                                                    in_=sap(Vs, 0, 64, g * NBLK, [[NG * NBLK, 2], [1, NBLK - 1]])),
                      [Ssb], [ab_])
                    V(lambda: nc.vector.tensor_copy(out=a_[64:128, :, 0:NBLK - 1],
                                                    in_=sap(Vs, 64, 64, g * NBLK + NBLK - 2, [[NG * NBLK, 2], [-1, NBLK - 1]])),
                      [Ssb], [ab_])
                    kb.op(pe, lambda: nc.tensor.matmul(pY[:, 0:NBLK], lhsT=WT[:, g, :], rhs=ubg(g), start=True, stop=False),
                          reads=[WTb, ubb[g]], writes=[pYb])
                    for c in range(2):
                        kb.op(pe, lambda: nc.tensor.matmul(pY[:, 0:NBLK], lhsT=WC[:, g, c, :], rhs=a_[:, c, :],
                                                           start=False, stop=(c == 1)),
                              reads=[WCb, ab_], writes=[pYb])
                    g1, g2, gb = g1s[g % 2], g2s[g % 2], gbs[g % 2]
                    A(lambda: nc.scalar.activation(out=g1[:, :], in_=pY[:, 0:NBLK], func=AF.Square), [pYb], [gb])
                    V(lambda: nc.vector.tensor_scalar(out=g1[:, :], in0=g1[:, :], scalar1=0.07135481627, scalar2=1.5957691216,
                                                      op0=ALU.mult, op1=ALU.add), [gb], [gb])
                    V(lambda: nc.vector.tensor_tensor(out=g2[:, :], in0=g1[:, :], in1=pY[:, 0:NBLK], op=ALU.mult), [gb, pYb], [gb])
                    A(lambda: nc.scalar.activation(out=g2[:, :], in_=g2[:, :], func=AF.Sigmoid), [gb], [gb])
                    V(lambda: nc.vector.tensor_tensor(out=ubg(g), in0=g2[:, :], in1=pY[:, 0:NBLK], op=ALU.mult),
                      [gb, pYb], [ubb[g]])
                z2b = Buf()
                kb.dma(sp, c_scr, z_scr2[:, :, :], sap(zT, 0, 128, 0, [[NBLK, NG], [1, NBLK]]), reads=ubb + zsb, writes=[z2b])
                for j in range(8):
                    srcZ = bass.AP(z_scr2.tensor, j * 16 * NG * NBLK, [[NG * NBLK, 16], [NBLK, NG], [1, NBLK]])
                    dstZ = bass.AP(z_scr.tensor, j * NBLK, [[8 * NBLK, 16], [16 * 8 * NBLK, NG], [1, NBLK]])
                    kb.dma(sp, c_scr, dstZ, srcZ, reads=[z2b], writes=[zsb[j]])
                for j in range(8):
                    zsb[j].w = (c_scr, c_scr.cnt)
                for mt in range(4):
                    kb.dma(sp, c_scr, zT[:, mt, :, :], z_scr[mt * 128:(mt + 1) * 128, :, :], reads=zsb, writes=zTb + ubb)
                for b_ in zTb:
                    b_.w = (c_scr, c_scr.cnt)
                kb.barrier()

        def merge_items():
            items = []
            for half in range(2):
                for mo in range(8):
                    items.append((wx[16 + mo, :, :], 1024))
                    items.append((wbs_d[mo, :, :], 512))
                    items.append((wx[24 + mo, :, :], 1024))
                    items.append((wbn_d[mo, :, :], 512))
                for mo in range(8):
                    items.append((wo_d[mo, :, :], 1024))
            return items

        c_wgl = kb.ctr("s_wgl")

        def merge_phase(bl, zT, zTb, oT, oTb):
            with ExitStack() as ph:
                psb = psbf(ph)
                wgl = psb("wgl", [128, 4, 512], BF16)
                wglb = Buf()
                for mt in range(4):
                    kb.dma(pool, c_wgl, wgl[:, mt, :], wglu_d[mt, :, :], writes=[wglb])
                wglb.w = (c_wgl, c_wgl.cnt)
                xn2 = psb("mxn2", [128, 8, 1024], BF16)
                mg = psb("mg", [128, 8, 1024], BF16)
                sq = [psb(f"msq{i}", [128, 512], BF16) for i in range(2)]
                sd = psb("msd", [128, 512], F32)
                rstd = psb("mrstd", [128, 512], F32)
                scr = (sq, [Buf(), Buf()], sd, Buf(), rstd, Buf())
                s1 = [psb(f"ms1{i}", [128, 512], BF16) for i in range(2)]
                s2 = [psb(f"ms2{i}", [128, 512], BF16) for i in range(2)]
                m1 = [psb(f"mm1{i}", [128, 512], BF16) for i in range(2)]
                sb1, sb2, mb1 = [Buf(), Buf()], [Buf(), Buf()], [Buf(), Buf()]

                def zperm(t_, mt, c):
                    return sap(t_, 0, 128, mt * 8 * NBLK + 2 + 64 * c, [[NBLK, 8], [1, 64]])
                for c in range(4):
                    for mt in range(4):
                        pS, pSb = pb[(c % 2) * 4 + mt], pbb[(c % 2) * 4 + mt]
                        for k in range(4):
                            kb.op(pe, lambda: nc.tensor.matmul(pS[:, :], lhsT=wgl[:, mt, k * 128:(k + 1) * 128], rhs=zperm(zT, k, c),
                                                               start=(k == 0), stop=(k == 3)),
                                  reads=[wglb, zTb[c]], writes=[pSb])
                    for mt in range(4):
                        pS, pSb = pb[(c % 2) * 4 + mt], pbb[(c % 2) * 4 + mt]
                        A(lambda: nc.scalar.activation(out=s1[mt % 2][:, :], in_=pS[:, :], func=AF.Sigmoid), [pSb], [sb1[mt % 2]])
                        V(lambda: nc.vector.tensor_tensor(out=zperm(zT, mt, c), in0=zperm(zT, mt, c),
                                                          in1=sap(s1[mt % 2], 0, 128, 0, [[64, 8], [1, 64]]), op=ALU.mult),
                          [sb1[mt % 2], zTb[c]], [zTb[c]])
                for half in range(2):
                    xnb = [[Buf() for c in range(2)] for k in range(8)]
                    mgb = [[Buf() for c in range(2)] for k in range(8)]
                    chs = []
                    for c2 in range(2):
                        c = 2 * half + c2
                        chs.append(dict(N=512, h=(lambda k, c=c: hT[:, k, c * 512:(c + 1) * 512]), hb=(lambda k, c=c: hb[k][c]),
                                        xn=(lambda k, c2=c2: xn2[:, k, c2 * 512:(c2 + 1) * 512]),
                                        xnb=(lambda k, c2=c2: xnb[k][c2])))
                    for ch in chs:
                        norm_chunk(ch, 1, scr)
                    for mo in range(8):
                        wgs, wgsb = stream.get()
                        for c2, ch in enumerate(chs):
                            p1, p1b = pb[c2], pbb[c2]
                            for k in range(8):
                                kb.op(pe, lambda: nc.tensor.matmul(p1[:, :], lhsT=wgs[:, k * 128:(k + 1) * 128], rhs=ch["xn"](k),
                                                                   start=(k == 0), stop=(k == 7)),
                                      reads=[wgsb, ch["xnb"](k)], writes=[p1b])
                            A(lambda: nc.scalar.activation(out=s1[c2][:, :], in_=p1[:, :], func=AF.Sigmoid), [p1b], [sb1[c2]])
                        wbs_, wbsb = stream.get()
                        for c2, ch in enumerate(chs):
                            c = 2 * half + c2
                            p3, p3b = pb[2 + c2], pbb[2 + c2]
                            for k in range(4):
                                kb.op(pe, lambda: nc.tensor.matmul(p3[:, :], lhsT=wbs_[:, k * 128:(k + 1) * 128], rhs=zperm(zT, k, c),
                                                                   start=(k == 0), stop=(k == 3)),
                                      reads=[wbsb, zTb[c]], writes=[p3b])
                            V(lambda: nc.vector.tensor_tensor(out=sap(m1[c2], 0, 128, 0, [[8, 64], [1, 8]]),
                                                              in0=sap(s1[c2], 0, 128, 0, [[8, 64], [1, 8]]),
                                                              in1=sap(p3, 0, 128, 0, [[1, 64], [64, 8]]), op=ALU.mult),
                              [sb1[c2], p3b], [mb1[c2]])
                        wgn, wgnb = stream.get()
                        for c2, ch in enumerate(chs):
                            p2, p2b = pb[4 + c2], pbb[4 + c2]
                            for k in range(8):
                                kb.op(pe, lambda: nc.tensor.matmul(p2[:, :], lhsT=wgn[:, k * 128:(k + 1) * 128], rhs=ch["xn"](k),
                                                                   start=(k == 0), stop=(k == 7)),
                                      reads=[wgnb, ch["xnb"](k)], writes=[p2b])
                            A(lambda: nc.scalar.activation(out=s2[c2][:, :], in_=p2[:, :], func=AF.Sigmoid), [p2b], [sb2[c2]])
                        wbn_, wbnb = stream.get()
                        for c2, ch in enumerate(chs):
                            c = 2 * half + c2
                            p4, p4b = pb[6 + c2], pbb[6 + c2]
                            for k in range(4):
                                kb.op(pe, lambda: nc.tensor.matmul(p4[:, :], lhsT=wbn_[:, k * 128:(k + 1) * 128],
                                                                   rhs=oT[:, k, c * 512:(c + 1) * 512], start=(k == 0), stop=(k == 3)),
                                      reads=[wbnb, oTb[k][c]], writes=[p4b])
                            V(lambda: nc.vector.tensor_tensor(out=s2[c2][:, :], in0=s2[c2][:, :], in1=p4[:, :], op=ALU.mult),
                              [sb2[c2], p4b], [sb2[c2]])
                            V(lambda: nc.vector.tensor_tensor(out=mg[:, mo, c2 * 512:(c2 + 1) * 512], in0=m1[c2][:, :],
                                                              in1=s2[c2][:, :], op=ALU.add), [mb1[c2], sb2[c2]], [mgb[mo][c2]])
                    for mo in range(8):
                        ws, wb = stream.get()
                        for c2 in range(2):
                            c = 2 * half + c2
                            pO, pOb = pb[(2 * mo + c2) % 8], pbb[(2 * mo + c2) % 8]
                            for k in range(8):
                                kb.op(pe, lambda: nc.tensor.matmul(pO[:, :], lhsT=ws[:, k * 128:(k + 1) * 128],
                                                                   rhs=mg[:, k, c2 * 512:(c2 + 1) * 512], start=(k == 0), stop=(k == 7)),
                                      reads=[wb, mgb[k][c2]], writes=[pOb])
                            hk = hT[:, mo, c * 512:(c + 1) * 512]
                            V(lambda: nc.vector.tensor_tensor(out=hk, in0=pO[:, :], in1=hk, op=ALU.add),
                              [pOb, hb[mo][c]], [hb[mo][c]])
                kb.barrier()

        def mixer(bl):
            with ExitStack() as mx:
                psb = psbf(mx)
                qT = psb("qoT", [128, 4, T], BF16)
                qTb = [[Buf() for c in range(4)] for i in range(4)]
                qTall = [b_ for r_ in qTb for b_ in r_]
                with ExitStack() as m2:
                    psb2 = psbf(m2)
                    kT = psb2("kT", [128, 4, T], BF16)
                    vd = psb2("vd", [128, 32, 4, 64], BF16)
                    kTb = [Buf() for i in range(4)]
                    vdb = Buf()
                    inproj_phase(bl, qT, qTb, kT, kTb, vd, vdb)
                    if dbg and bl == 0:
                        dump("qT", qT[:, :, :], [128, 4, T], qTall)
                        dump("kT", kT[:, :, :], [128, 4, T], kTb)
                        dump("vd", vd[:, :, :, :], [128, 32, 4, 64], [vdb])
                        kb.barrier()
                        kb.wait_all(sp, [c_out])
                        kb.wait_all(pe, [c_out])
                    if stage >= 7:
                        stream.extend(merge_items())
                    if stage >= 5:
                        na_phase(bl, qT, qTb, kT, kTb, vd, vdb, qT, qTb)
                        if dbg and bl == 0:
                            dump("oT", qT[:, :, :], [128, 4, T], qTall)
                            for E_ in (pe, act, dve, pool):
                                kb.wait_all(E_, [c_out])
                zT = psb("zT", [128, 4, 8, NBLK], BF16)
                zTb = [Buf() for c in range(4)]
                if stage >= 6:
                    s5_phase(bl, zT, zTb)
                    if dbg and bl == 0:
                        dump("zT", zT[:, :, :, :], [128, 4, 8, NBLK], zTb)
                        for E_ in (pe, act, dve, pool):
                            kb.wait_all(E_, [c_out])
                if stage >= 7:
                    merge_phase(bl, zT, zTb, qT, qTb)

        allhb = [hb[k][c] for k in range(8) for c in range(4)]
        if stage >= 2:
            setup_na()
        if stage >= 3:
            setup_s5()
        hT = sb("hT", [128, 8, T], F32)
        hm = sb("hm", [128, 8, NM], F32)
        stream = Stream(kb, nc, es, NSLOT, 2048)
        c_hm = kb.ctr("s_hm")
        kb.dma(sp, c_hm, hm[:, :, :], metaT[:, :, :], writes=hmb)
        for bl in range(nb_local):
            load_x(bl)
            stream.extend(ffn_items(w1i, w1o))
            if stage >= 4:
                stream.extend(mixer_items())
            ffn_phase(bl, 0, with_meta=(bl == 0))
            if dbg and bl == 0:
                dump("h1", hT[:, :, :], [128, 8, T], allhb)
                dump("hm1", hm[:, :, :], [128, 8, NM], hmb)
            if stage >= 4:
                mixer(bl)
                if dbg and bl == 0:
                    dump("h2", hT[:, :, :], [128, 8, T], allhb)
            stream.extend(ffn_items(w2i, w2o))
            ffn_phase(bl, 2, with_meta=False)
            final_phase(bl)
        kb.wait_all(sp, [c_out])
    return nc, dbg_outs


def host_inputs(inputs, core, nb_local=NB_LOCAL):
    f = np.float32
    x = inputs["x"]
    b0 = core * nb_local
    xT = np.ascontiguousarray(
        x[b0:b0 + nb_local].transpose(0, 2, 1).reshape(nb_local, 8, 128, T)).astype(f)
    metaT = np.ascontiguousarray(inputs["meta_tokens"].T.reshape(8, 128, NM).transpose(1, 0, 2)).astype(f)
    g = np.stack([inputs["norm_ffn1"][0], inputs["norm_mix"][0], inputs["norm_ffn2"][0], inputs["norm_final"]])
    gains = np.ascontiguousarray(g.reshape(4, 8, 128).transpose(2, 0, 1)).astype(f)

    def ffn_in(w):
        w = w.reshape(8, 128, 2, NFF, 128)
        return np.ascontiguousarray(w.transpose(3, 1, 0, 2, 4).reshape(NFF, 128, 8 * 256)).astype(f)

    def ffn_out(w):
        w = w.reshape(NFF, 128, 8, 128)
        return np.ascontiguousarray(w.transpose(2, 1, 0, 3).reshape(8, 128, NFF * 128)).astype(f)

    consts = np.zeros((128, 3, 128), f)
    consts[:, 0, :] = np.eye(128)
    consts[:, 1, :] = 1.0
    consts[:64, 2, :64] = 1.0
    consts[64:, 2, 64:] = 1.0
    w_in = inputs["w_in"][0]

    def tiles(w, nk, nm):
        return np.ascontiguousarray(w.reshape(nk, 128, nm, 128).transpose(2, 1, 0, 3).reshape(nm, 128, nk * 128)).astype(f)
    wx_ = tiles(w_in, 8, 32)
    wv = w_in[:, 1536:2048].reshape(8, 128, 2, 256)
    wv2 = np.ascontiguousarray(wv.transpose(2, 1, 0, 3).reshape(2, 128, 2048)).astype(f)
    rpb = inputs["na_rpb"][0]
    rpbpad = np.zeros((8, 15, 128), f)
    rpbpad[:, :, 48:48 + 31] = rpb[:, ::-1, :]
    qc = np.arange(64)
    cs = np.clip(qc - 8, 0, 48)
    kc = np.arange(64)
    cm = ((kc[:, None] >= cs[None, :]) & (kc[:, None] < cs[None, :] + 16)).astype(f)
    colmask = np.concatenate([cm, cm], axis=0)
    are = np.stack([inputs["ssm_a_re_fwd"][0], inputs["ssm_a_re_bwd"][0]])
    aim = np.stack([inputs["ssm_a_im_fwd"][0], inputs["ssm_a_im_bwd"][0]])
    ldt = np.stack([inputs["ssm_log_dt_fwd"][0], inputs["ssm_log_dt_bwd"][0]])
    ldt_gn = np.broadcast_to(ldt[:, :, None], (2, 32, 64))
    Bre, Bim = inputs["ssm_b_re"][0], inputs["ssm_b_im"][0]
    Cre, Cim = inputs["ssm_c_re"][0], inputs["ssm_c_im"][0]

    def layC2(a):
        return a.transpose(0, 2, 1).reshape(128, 32)
    s5c = np.ascontiguousarray(np.stack([layC2(are), layC2(aim), layC2(ldt_gn)], axis=1)).astype(f)

    def layCC(Cm):
        t = Cm.transpose(2, 0, 1).reshape(64, 512)
        return np.concatenate([t, t], axis=0)

    def layCB(Bm):
        t = Bm.transpose(1, 0, 2).reshape(64, 512)
        return np.concatenate([t, t], axis=0)
    s5cc = np.ascontiguousarray(np.stack([layCC(Cre), layCC(Cim)], axis=1)).astype(f)
    s5cb = np.ascontiguousarray(np.stack([layCB(Bre), layCB(Bim)], axis=1)).astype(f)
    j8 = np.arange(8)
    s5ce = np.zeros((128, 4, 8), f)
    s5ce[:64, 3] = 7 - j8
    s5ce[64:, 3] = j8
    s5ce[:64, 0] = j8 + 1
    s5ce[:64, 1] = j8
    s5ce[:64, 2] = -j8
    s5ce[64:, 0] = 8 - j8
    s5ce[64:, 1] = -j8
    s5ce[64:, 2] = j8
    Dm = inputs["ssm_d"][0]
    s5d = np.ascontiguousarray(np.broadcast_to(Dm.T[None], (8, 16, 32)).reshape(128, 32)).astype(f)
    ii = np.repeat(np.arange(8), 16)
    s5m = np.stack([(ii[:, None] <= ii[None, :]), (ii[:, None] >= ii[None, :]), np.eye(128, dtype=bool)], axis=1).astype(f)
    extra = {
        "wx": wx_, "wv2": wv2, "wglu": tiles(inputs["w_glu"][0], 4, 4), "wbs": tiles(inputs["w_branch_ssm"][0], 4, 8),
        "wbn": tiles(inputs["w_branch_na"][0], 4, 8), "wo": tiles(inputs["w_out"][0], 8, 8),
        "rpbpad": rpbpad, "colmask": colmask, "s5c": s5c, "s5cc": s5cc,
        "s5cb": s5cb, "s5ce": s5ce, "s5d": s5d, "s5m": np.ascontiguousarray(s5m),
    }
    return {
        **extra,
        "xT": xT, "metaT": metaT, "gains": gains,
        "w1i": ffn_in(inputs["w_ffn1_in"][0]), "w1o": ffn_out(inputs["w_ffn1_out"][0]),
        "w2i": ffn_in(inputs["w_ffn2_in"][0]), "w2o": ffn_out(inputs["w_ffn2_out"][0]),
        "consts": consts,
    }


def kernel(**inputs):
    inputs = {k: np.asarray(v) for k, v in inputs.items()}
    nc, _ = build_program()
    in_maps = [host_inputs(inputs, c) for c in range(NCORES)]
    res = run_bass_kernel_spmd(nc, in_maps, core_ids=list(range(NCORES)))
    out = np.empty((16, T, D), np.float32)
    for c in range(NCORES):
        o = np.asarray(res.results[c]["outT"])
        out[c * NB_LOCAL:(c + 1) * NB_LOCAL] = o.reshape(NB_LOCAL, D, T).transpose(0, 2, 1)
    return out
```

```python
import numpy as np
from contextlib import ExitStack
import concourse.bass as bass
import concourse.mybir as mybir
from concourse.bass_utils import run_bass_kernel_spmd

F32 = mybir.dt.float32
BF16 = mybir.dt.bfloat16
ALU = mybir.AluOpType
AF = mybir.ActivationFunctionType

D = 1024
T = 2048
NM = 16
DFF = 2816
NFF = 22
NG = 32
NBLK = 258
NB_LOCAL = 2
NCORES = 8
FF_SPLITS = [(0, 8), (8, 15), (15, 22)]
NSLOT = 2
TWO_PI = 6.283185307179586


class Ctr:
    def __init__(self, sem):
        self.sem = sem
        self.cnt = 0


class Buf:
    __slots__ = ("w", "r", "name")

    def __init__(self, name=""):
        self.w = None
        self.r = {}
        self.name = name


class Eng:
    def __init__(self, h, ctr, name):
        self.h = h
        self.ctr = ctr
        self.seen = {}
        self.name = name


class KB:
    def __init__(self, nc, es):
        self.nc = nc
        self.es = es

        def mk(h, n):
            return Eng(h, Ctr(es.enter_context(nc.semaphore(n))), n)

        self.pe = mk(nc.tensor, "s_pe")
        self.act = mk(nc.scalar, "s_act")
        self.dve = mk(nc.vector, "s_dve")
        self.pool = mk(nc.gpsimd, "s_pool")
        self.sp = mk(nc.sync, "s_sp")
        self.engs = [self.pe, self.act, self.dve, self.pool, self.sp]

    def ctr(self, name):
        return Ctr(self.es.enter_context(self.nc.semaphore(name)))

    def _need(self, E, dep, raw):
        ctr, val = dep
        if ctr is E.ctr and not raw:
            return
        if E.seen.get(id(ctr), 0) >= val:
            return
        E.h.wait_ge(ctr.sem, val)
        E.seen[id(ctr)] = val

    def _deps(self, E, reads, writes):
        for b in reads:
            if b.w is not None:
                self._need(E, b.w, True)
        for b in writes:
            if b.w is not None:
                self._need(E, b.w, False)
            for dep in b.r.values():
                self._need(E, dep, False)

    def op(self, E, fn, reads=(), writes=()):
        self._deps(E, reads, writes)
        ins = fn()
        E.ctr.cnt += 1
        ins.then_inc(E.ctr.sem, 1)
        for b in writes:
            b.w = (E.ctr, E.ctr.cnt)
            b.r = {}
        for b in reads:
            b.r[id(E.ctr)] = (E.ctr, E.ctr.cnt)

    def dma(self, Q, ctr, out, in_, reads=(), writes=()):
        self._deps(Q, reads, writes)
        Q.h.dma_start(out=out, in_=in_).then_inc(ctr.sem, 16)
        ctr.cnt += 16
        for b in writes:
            b.w = (ctr, ctr.cnt)
            b.r = {}
        for b in reads:
            b.r[id(ctr)] = (ctr, ctr.cnt)

    def barrier(self):
        for E in self.engs:
            for O in self.engs:
                if O is not E and O.ctr.cnt > 0:
                    self._need(E, (O.ctr, O.ctr.cnt), True)

    def wait_all(self, E, ctrs):
        for c in ctrs:
            if c.cnt > 0:
                self._need(E, (c, c.cnt), True)


class Stream:
    def __init__(self, kb, nc, es, nslot, cols):
        self.kb = kb
        self.items = []
        self.next_load = 0
        self.next_get = 0
        self.nslot = nslot
        self.slots = [es.enter_context(nc.sbuf_tensor(f"ring{i}", [128, cols], BF16)) for i in range(nslot)]
        self.bufs = [Buf(f"ring{i}") for i in range(nslot)]
        self.ctrs = [kb.ctr(f"s_ring{i}") for i in range(nslot)]

    def extend(self, items):
        self.items += items

    def _load(self, n):
        s = n % self.nslot
        src, ncols = self.items[n]
        self.kb.dma(self.kb.pool, self.ctrs[s], self.slots[s][:, 0:ncols], src, writes=[self.bufs[s]])

    def get(self):
        i = self.next_get
        self.next_get += 1
        while self.next_load < min(len(self.items), i + self.nslot):
            self._load(self.next_load)
            self.next_load += 1
        s = i % self.nslot
        return self.slots[s], self.bufs[s]


def sap(t, part0, nparts, off, dims):
    row = 1
    for d in t.shape[1:]:
        row *= d
    return bass.AP(t, part0 * row + off, [[row, nparts]] + [list(d) for d in dims])


def build_program(nb_local=NB_LOCAL, stage=99, dbg=False):
    nc = bass.Bass("TRN2", target_bir_lowering=False)

    def din(name, shape, dt=F32):
        return nc.dram_tensor(name, list(shape), dt, kind="ExternalInput").ap()

    xT = din("xT", [nb_local, 8, 128, T])
    metaT = din("metaT", [128, 8, NM])
    gains_d = din("gains", [128, 4, 8])
    w1i = din("w1i", [NFF, 128, 8 * 256])
    w1o = din("w1o", [8, 128, NFF * 128])
    w2i = din("w2i", [NFF, 128, 8 * 256])
    w2o = din("w2o", [8, 128, NFF * 128])
    consts_d = din("consts", [128, 3, 128])
    wx = din("wx", [32, 128, 1024])
    wv2 = din("wv2", [2, 128, 2048])
    wglu_d = din("wglu", [4, 128, 512])
    wbs_d = din("wbs", [8, 128, 512])
    wbn_d = din("wbn", [8, 128, 512])
    wo_d = din("wo", [8, 128, 1024])
    rpbpad = din("rpbpad", [8, 15, 128])
    colmask_d = din("colmask", [128, 64])
    s5c_d = din("s5c", [128, 3, 32])
    s5cc_d = din("s5cc", [128, 2, 512])
    s5cb_d = din("s5cb", [128, 2, 512])
    s5ce_d = din("s5ce", [128, 5, 8])
    s5d_d = din("s5d", [128, 32])
    s5m_d = din("s5m", [128, 3, 128])
    u_scr = nc.dram_tensor("u_scr", [512, 8, NBLK], BF16, kind="Internal").ap()
    z_scr = nc.dram_tensor("z_scr", [512, 8, NBLK], BF16, kind="Internal").ap()
    wb2_scr = nc.dram_tensor("wb2_scr", [128, NG * 512], BF16, kind="Internal").ap()
    u_scr2 = nc.dram_tensor("u_scr2", [128, NG, NBLK], BF16, kind="Internal").ap()
    z_scr2 = nc.dram_tensor("z_scr2", [128, NG, NBLK], BF16, kind="Internal").ap()
    outT = nc.dram_tensor("outT", [nb_local, 8, 128, T], F32, kind="ExternalOutput").ap()
    dbg_outs = {}

    with ExitStack() as es:
        kb = KB(nc, es)
        pe, act, dve, pool, sp = kb.pe, kb.act, kb.dve, kb.pool, kb.sp

        def sb(name, shape, dt):
            return es.enter_context(nc.sbuf_tensor(name, list(shape), dt))

        uid = [0]

        def uname(n):
            uid[0] += 1
            return f"{n}_{uid[0]}"


        Etab = sb("Etab", [128, 4, 15, 64], BF16)
        Etabb = Buf()
        kmT = sb("kmT", [128, 4, NM], BF16)
        kmTb = Buf()
        vm = sb("vm", [128, 4, 64], BF16)
        vmb = Buf()
        umT = sb("umT", [128, 4, 8, 2], BF16)
        umTb = Buf()
        WB = sb("WB", [128, NG, 2, 128], BF16)
        WBb = Buf()
        WC = sb("WC", [128, NG, 2, 128], BF16)
        WCb = Buf()
        WT = sb("WT", [128, NG, 128], BF16)
        WTb = Buf()
        scA = sb("scA", [128, 4, NG], F32)
        scB = sb("scB", [128, 4, NG], F32)
        wb2_b = Buf()
        scb = Buf()
        u_scr_b = Buf()
        z_scr_b = Buf()
        u2b = [Buf() for j in range(8)]
        zsb = [Buf() for j in range(8)]
        c_scr = kb.ctr("s_scr")

        hb = [[Buf(f"h{k}_{c}") for c in range(4)] for k in range(8)]
        hmb = [Buf(f"hm{k}") for k in range(8)]
        gains = sb("gains_sb", [128, 4, 8], F32)
        gainsb = Buf("gains")
        cst = sb("cst", [128, 3, 128], BF16)
        cstb = Buf("cst")
        epsb_t = sb("eps", [128, 1], F32)
        epsb = Buf("eps")
        pb = [es.enter_context(nc.psum_tensor(f"pb{i}", [128, 512], F32)) for i in range(8)]
        pbb = [Buf(f"pb{i}") for i in range(8)]
        ident = cst[:, 0, :]
        ones_bf = cst[:, 1, :]
        blockones = cst[:, 2, :]

        c_misc = kb.ctr("s_misc")
        c_x = kb.ctr("s_x")
        c_out = kb.ctr("s_out")

        kb.dma(sp, c_misc, gains[:, :, :], gains_d[:, :, :], writes=[gainsb])
        c_cst = kb.ctr("s_cst")
        kb.dma(pool, c_cst, cst[:, :, :], consts_d[:, :, :], writes=[cstb])
        kb.op(dve, lambda: nc.vector.memset(epsb_t[:, :], 1e-6), writes=[epsb])

        def load_x(bl):
            for k in range(8):
                kb.dma(sp, c_x, hT[:, k, :], xT[bl, k, :, :], writes=hb[k])
            for k in range(8):
                for c in range(4):
                    hb[k][c].w = (c_x, c_x.cnt)

        def main_chunks(xn, xnb, at, atb):
            chs = []
            for c in range(4):
                chs.append(dict(
                    N=512,
                    h=(lambda k, c=c: hT[:, k, c * 512:(c + 1) * 512]),
                    hb=(lambda k, c=c: hb[k][c]),
                    xn=(lambda k, c=c: xn[:, k, c * 512:(c + 1) * 512]),
                    xnb=(lambda k, c=c: xnb[k][c]),
                    at=(lambda kk, c=c: at[:, kk, c * 512:(c + 1) * 512]),
                    atb=(lambda kk, c=c: atb[kk][c]),
                ))
            return chs

        def norm_chunk(ch, gi, scr, out_fn=None, outb_fn=None):
            N = ch["N"]
            sq, sqb, sd, sdb, rstd, rstdb = scr
            pR, pRb = pb[7], pbb[7]
            for k in range(8):
                kb.op(act, lambda: nc.scalar.activation(out=sq[k % 2][:, :N], in_=ch["h"](k), func=AF.Square),
                      reads=[ch["hb"](k)], writes=[sqb[k % 2]])
                kb.op(pe, lambda: nc.tensor.matmul(pR[:, :N], lhsT=ones_bf, rhs=sq[k % 2][:, :N],
                                                   start=(k == 0), stop=(k == 7)),
                      reads=[sqb[k % 2], cstb], writes=[pRb])
            kb.op(act, lambda: nc.scalar.activation(out=sd[:, :N], in_=pR[:, :N], func=AF.Sqrt,
                                                    scale=1.0 / D, bias=epsb_t[:, 0:1]),
                  reads=[pRb, epsb], writes=[sdb])
            kb.op(dve, lambda: nc.vector.reciprocal(out=rstd[:, :N], in_=sd[:, :N]), reads=[sdb], writes=[rstdb])
            for k in range(8):
                o = ch["xn"](k) if out_fn is None else out_fn(k)
                ob = ch["xnb"](k) if outb_fn is None else outb_fn(k)
                kb.op(dve, lambda: nc.vector.scalar_tensor_tensor(
                    out=o, in0=ch["h"](k), scalar=gains[:, gi, k:k + 1], in1=rstd[:, :N],
                    op0=ALU.mult, op1=ALU.mult),
                    reads=[ch["hb"](k), rstdb, gainsb], writes=[ob])

        def ffn_items(wi, wo):
            items = []
            for (a, b) in FF_SPLITS:
                for m in range(a, b):
                    items.append((wi[m, :, :], 2048))
                for mo in range(8):
                    items.append((wo[mo, :, a * 128:b * 128], (b - a) * 128))
            return items

        def ffn(chunks, gi, scr, sg, sgb):
            for ch in chunks:
                norm_chunk(ch, gi, scr)
            gu = 0
            oi = 0
            for (a, b) in FF_SPLITS:
                nk = b - a
                for m in range(a, b):
                    ws, wb = stream.get()
                    for ch in chunks:
                        N = ch["N"]
                        pG, pGb = pb[gu % 2], pbb[gu % 2]
                        pU, pUb = pb[2 + gu % 2], pbb[2 + gu % 2]
                        for k in range(8):
                            kb.op(pe, lambda: nc.tensor.matmul(pG[:, :N], lhsT=ws[:, k * 256:k * 256 + 128],
                                                               rhs=ch["xn"](k), start=(k == 0), stop=(k == 7)),
                                  reads=[wb, ch["xnb"](k)], writes=[pGb])
                        for k in range(8):
                            kb.op(pe, lambda: nc.tensor.matmul(pU[:, :N], lhsT=ws[:, k * 256 + 128:k * 256 + 256],
                                                               rhs=ch["xn"](k), start=(k == 0), stop=(k == 7)),
                                  reads=[wb, ch["xnb"](k)], writes=[pUb])
                        s_ = sg[gu % 2]
                        kb.op(act, lambda: nc.scalar.activation(out=s_[:, :N], in_=pG[:, :N], func=AF.Silu),
                              reads=[pGb], writes=[sgb[gu % 2]])
                        kb.op(dve, lambda: nc.vector.tensor_tensor(out=ch["at"](m - a), in0=s_[:, :N], in1=pU[:, :N],
                                                                   op=ALU.mult),
                              reads=[sgb[gu % 2], pUb], writes=[ch["atb"](m - a)])
                        gu += 1
                for mo in range(8):
                    ws, wb = stream.get()
                    for ch in chunks:
                        N = ch["N"]
                        pO, pOb = pb[4 + oi % 2], pbb[4 + oi % 2]
                        for kk in range(nk):
                            kb.op(pe, lambda: nc.tensor.matmul(pO[:, :N], lhsT=ws[:, kk * 128:(kk + 1) * 128],
                                                               rhs=ch["at"](kk), start=(kk == 0), stop=(kk == nk - 1)),
                                  reads=[wb, ch["atb"](kk)], writes=[pOb])
                        hk = ch["h"](mo)
                        kb.op(dve, lambda: nc.vector.scalar_tensor_tensor(out=hk, in0=pO[:, :N], scalar=0.5, in1=hk,
                                                                          op0=ALU.mult, op1=ALU.add),
                              reads=[pOb, ch["hb"](mo)], writes=[ch["hb"](mo)])
                        oi += 1

        def ffn_phase(bl, gi, with_meta):
            with ExitStack() as ph:
                def psb(name, shape, dt):
                    return ph.enter_context(nc.sbuf_tensor(uname(name), list(shape), dt))
                xn = psb("xn", [128, 8, T], BF16)
                xnb = [[Buf() for c in range(4)] for k in range(8)]
                at = psb("at", [128, 8, T], BF16)
                atb = [[Buf() for c in range(4)] for k in range(8)]
                sq = [psb(f"sq{i}", [128, 512], BF16) for i in range(2)]
                sqb = [Buf(), Buf()]
                sd = psb("sd", [128, 512], F32)
                rstd = psb("rstd", [128, 512], F32)
                sg = [psb(f"sg{i}", [128, 512], BF16) for i in range(2)]
                sgb = [Buf(), Buf()]
                scr = (sq, sqb, sd, Buf(), rstd, Buf())
                chunks = main_chunks(xn, xnb, at, atb)
                if with_meta:
                    xnm = psb("xnm", [128, 8, NM], BF16)
                    xnmb = [Buf() for k in range(8)]
                    atm = psb("atm", [128, 8, NM], BF16)
                    atmb = [Buf() for k in range(8)]
                    chunks.append(dict(N=NM, h=lambda k: hm[:, k, :], hb=lambda k: hmb[k],
                                       xn=lambda k: xnm[:, k, :], xnb=lambda k: xnmb[k],
                                       at=lambda kk: atm[:, kk, :], atb=lambda kk: atmb[kk]))
                ffn(chunks, gi, scr, sg, sgb)
                kb.barrier()

        def final_phase(bl):
            with ExitStack() as ph:
                def psb(name, shape, dt):
                    return ph.enter_context(nc.sbuf_tensor(uname(name), list(shape), dt))
                sq = [psb(f"fsq{i}", [128, 512], BF16) for i in range(2)]
                sd = psb("fsd", [128, 512], F32)
                rstd = psb("frstd", [128, 512], F32)
                scr = (sq, [Buf(), Buf()], sd, Buf(), rstd, Buf())
                chunks = main_chunks(None, None, None, None)
                for c, ch in enumerate(chunks):
                    norm_chunk(ch, 3, scr, out_fn=ch["h"], outb_fn=ch["hb"])
                for k in range(8):
                    kb.dma(sp, c_out, outT[bl, k, :, :], hT[:, k, :], reads=hb[k])
                kb.barrier()

        def dump(name, ap, shape, reads):
            t = nc.dram_tensor("dbg_" + name, list(shape), ap.dtype, kind="ExternalOutput").ap()
            dbg_outs[name] = t
            kb.dma(sp, c_out, t, ap, reads=reads)

        def psbf(ph):
            def f(name, shape, dt):
                return ph.enter_context(nc.sbuf_tensor(uname(name), list(shape), dt))
            return f

        def V(fn, reads, writes):
            kb.op(dve, fn, reads=reads, writes=writes)

        def A(fn, reads, writes):
            kb.op(act, fn, reads=reads, writes=writes)

        def setup_na():
            with ExitStack() as ph:
                psb = psbf(ph)
                c_s = kb.ctr("s_setna")
                raw = psb("eraw", [128, 4, 15, 64], F32)
                rawb = Buf()
                exn = psb("eexp", [128, 4, 15, 64], F32)
                exb = Buf()
                cm = psb("cmask", [128, 64], F32)
                cmb = Buf()
                kb.dma(sp, c_s, cm[:, :], colmask_d[:, :], writes=[cmb])
                for hp in range(4):
                    for h2 in range(2):
                        h = 2 * hp + h2
                        src = bass.AP(rpbpad.tensor, h * 15 * 128, [[1, 64], [128, 15], [1, 64]])
                        kb.dma(sp, c_s, raw[h2 * 64:(h2 + 1) * 64, hp, :, :], src, writes=[rawb])
                rawb.w = (c_s, c_s.cnt)
                cmb.w = (c_s, c_s.cnt)
                A(lambda: nc.scalar.activation(out=sap(exn, 0, 128, 0, [[1, 3840]]), in_=sap(raw, 0, 128, 0, [[1, 3840]]),
                                               func=AF.Exp), [rawb], [exb])
                V(lambda: nc.vector.tensor_tensor(out=sap(Etab, 0, 128, 0, [[64, 60], [1, 64]]),
                                                  in0=sap(exn, 0, 128, 63, [[64, 60], [-1, 64]]),
                                                  in1=sap(cm, 0, 128, 0, [[0, 60], [1, 64]]), op=ALU.mult),
                  [exb, cmb], [Etabb])
                kb.barrier()

        MAGIC = 12582912.0

        def cexp(xr, xi, ore, oim, t, F_, b):
            R = [b]
            A(lambda: nc.scalar.activation(out=xr, in_=xr, func=AF.Exp), R, R)
            V(lambda: nc.vector.tensor_scalar(out=xi, in0=xi, scalar1=1.0 / TWO_PI, scalar2=None, op0=ALU.mult), R, R)
            for which, o in ((0, oim), (1, ore)):
                if which == 1:
                    V(lambda: nc.vector.tensor_scalar(out=xi, in0=xi, scalar1=0.25, scalar2=None, op0=ALU.add), R, R)
                V(lambda: nc.vector.tensor_scalar(out=t, in0=xi, scalar1=MAGIC, scalar2=None, op0=ALU.add), R, R)
                V(lambda: nc.vector.tensor_scalar(out=t, in0=t, scalar1=MAGIC, scalar2=None, op0=ALU.subtract), R, R)
                V(lambda: nc.vector.tensor_tensor(out=t, in0=xi, in1=t, op=ALU.subtract), R, R)
                A(lambda: nc.scalar.activation(out=t, in_=t, func=AF.Sin, scale=TWO_PI), R, R)
                V(lambda: nc.vector.tensor_tensor(out=o, in0=xr, in1=t, op=ALU.mult), R, R)

        def cmul(ore, oim, ar_, ai_, br_, bi_, t1, t2, R, W, neg_im=False):
            V(lambda: nc.vector.tensor_tensor(out=t1, in0=ar_, in1=br_, op=ALU.mult), R, R)
            V(lambda: nc.vector.tensor_tensor(out=t2, in0=ai_, in1=bi_, op=ALU.mult), R, R)
            V(lambda: nc.vector.tensor_tensor(out=ore, in0=t1, in1=t2, op=ALU.subtract), R, W)
            V(lambda: nc.vector.tensor_tensor(out=t1, in0=ar_, in1=bi_, op=ALU.mult), R, R)
            V(lambda: nc.vector.tensor_tensor(out=t2, in0=ai_, in1=br_, op=ALU.mult), R, R)
            if neg_im:
                V(lambda: nc.vector.scalar_tensor_tensor(out=oim, in0=t1, scalar=-1.0, in1=t2, op0=ALU.mult,
                                                         op1=ALU.subtract), R, W)
            else:
                V(lambda: nc.vector.tensor_tensor(out=oim, in0=t1, in1=t2, op=ALU.add), R, W)

        def lam_and_f(are, aim, ldt, ardt, aidt, lre, lim, fre, fim, t1, t2, t3, F_, b):
            R = [b]
            A(lambda: nc.scalar.activation(out=ldt, in_=ldt, func=AF.Exp), R, R)
            V(lambda: nc.vector.tensor_scalar(out=are, in0=are, scalar1=-1e-4, scalar2=None, op0=ALU.min), R, R)
            V(lambda: nc.vector.tensor_tensor(out=ardt, in0=are, in1=ldt, op=ALU.mult), R, R)
            V(lambda: nc.vector.tensor_tensor(out=aidt, in0=aim, in1=ldt, op=ALU.mult), R, R)
            V(lambda: nc.vector.tensor_copy(out=t1, in_=ardt), R, R)
            V(lambda: nc.vector.tensor_copy(out=t2, in_=aidt), R, R)
            cexp(t1, t2, lre, lim, t3, F_, b)
            V(lambda: nc.vector.tensor_tensor(out=t1, in0=are, in1=are, op=ALU.mult), R, R)
            V(lambda: nc.vector.tensor_tensor(out=t2, in0=aim, in1=aim, op=ALU.mult), R, R)
            V(lambda: nc.vector.tensor_tensor(out=t3, in0=t1, in1=t2, op=ALU.add), R, R)
            V(lambda: nc.vector.reciprocal(out=t3, in_=t3), R, R)
            V(lambda: nc.vector.tensor_scalar(out=t1, in0=lre, scalar1=-1.0, scalar2=None, op0=ALU.add), R, R)
            V(lambda: nc.vector.tensor_tensor(out=fre, in0=t1, in1=are, op=ALU.mult), R, R)
            V(lambda: nc.vector.tensor_tensor(out=t2, in0=lim, in1=aim, op=ALU.mult), R, R)
            V(lambda: nc.vector.tensor_tensor(out=fre, in0=fre, in1=t2, op=ALU.add), R, R)
            V(lambda: nc.vector.tensor_tensor(out=fre, in0=fre, in1=t3, op=ALU.mult), R, R)
            V(lambda: nc.vector.tensor_tensor(out=fim, in0=lim, in1=are, op=ALU.mult), R, R)
            V(lambda: nc.vector.tensor_tensor(out=t2, in0=t1, in1=aim, op=ALU.mult), R, R)
            V(lambda: nc.vector.tensor_tensor(out=fim, in0=fim, in1=t2, op=ALU.subtract), R, R)
            V(lambda: nc.vector.tensor_tensor(out=fim, in0=fim, in1=t3, op=ALU.mult), R, R)

        def setup_s5():
            c_s = kb.ctr("s_sets5")
            with ExitStack() as ph:
                psb = psbf(ph)
                b = Buf()
                F_ = NG
                names = ["are", "aim", "ldt", "ardt", "aidt", "lre", "lim", "fre", "fim", "t1", "t2", "t3"]
                tl = {n: psb("C" + n, [128, F_], F32) for n in names}
                fl = {n: tl[n][:, :] for n in names}
                Cc = psb("CC", [128, 2, 512], F32)
                Bc = psb("CB", [128, 2, 512], F32)
                eps = psb("Ceps", [128, 5, 8], F32)
                Dt = psb("CD", [128, NG], F32)
                msk = psb("Cmsk", [128, 3, 128], F32)
                for wi_, n in enumerate(["are", "aim", "ldt"]):
                    kb.dma(sp, c_s, tl[n][:, :], s5c_d[:, wi_, :], writes=[b])
                kb.dma(sp, c_s, Cc[:, :, :], s5cc_d[:, :, :], writes=[b])
                kb.dma(sp, c_s, Bc[:, :, :], s5cb_d[:, :, :], writes=[b])
                kb.dma(sp, c_s, eps[:, :, :], s5ce_d[:, :, :], writes=[b])
                kb.dma(sp, c_s, Dt[:, :], s5d_d[:, :], writes=[b])
                kb.dma(sp, c_s, msk[:, :, :], s5m_d[:, :, :], writes=[b])
                b.w = (c_s, c_s.cnt)
                R = [b]
                lam_and_f(fl["are"], fl["aim"], fl["ldt"], fl["ardt"], fl["aidt"], fl["lre"], fl["lim"],
                          fl["fre"], fl["fim"], fl["t1"], fl["t2"], fl["t3"], F_, b)
                V(lambda: nc.vector.tensor_scalar(out=fl["t1"], in0=fl["ardt"], scalar1=16.0, scalar2=None, op0=ALU.mult), R, R)
                V(lambda: nc.vector.tensor_scalar(out=fl["t2"], in0=fl["aidt"], scalar1=16.0, scalar2=None, op0=ALU.mult), R, R)
                cexp(fl["t1"], fl["t2"], fl["lre"], fl["lim"], fl["t3"], F_, b)
                for c_ in range(2):
                    for par_ in range(2):
                        oA = sap(scA, 0, 128, c_ * NG * 2 + par_, [[2, NG]])
                        oB = sap(scB, 0, 128, c_ * NG * 2 + par_, [[2, NG]])
                        V(lambda: nc.vector.tensor_copy(out=oA, in_=fl["lre"]), R, [scb])
                        if c_ == 0:
                            V(lambda: nc.vector.tensor_scalar(out=oB, in0=fl["lim"], scalar1=-1.0, scalar2=None, op0=ALU.mult),
                              R, [scb])
                        else:
                            V(lambda: nc.vector.tensor_copy(out=oB, in_=fl["lim"]), R, [scb])
                pw = {}
                G8 = NG * 8
                xr = psb("Cxr", [128, G8], F32)
                xi = psb("Cxi", [128, G8], F32)
                tt = psb("Ctt", [128, G8], F32)
                for ei, nm in enumerate(["C", "G", "A", "B", "B2"]):
                    pr = psb("Cp%sr" % nm, [128, G8], F32)
                    pi_ = psb("Cp%si" % nm, [128, G8], F32)
                    e_b = sap(eps, 0, 128, ei * 8, [[0, NG], [1, 8]])
                    V(lambda: nc.vector.tensor_tensor(out=sap(xr, 0, 128, 0, [[8, NG], [1, 8]]),
                                                      in0=sap(tl["ardt"], 0, 128, 0, [[1, NG], [0, 8]]), in1=e_b,
                                                      op=ALU.mult), R, R)
                    V(lambda: nc.vector.tensor_tensor(out=sap(xi, 0, 128, 0, [[8, NG], [1, 8]]),
                                                      in0=sap(tl["aidt"], 0, 128, 0, [[1, NG], [0, 8]]), in1=e_b,
                                                      op=ALU.mult), R, R)
                    cexp(xr[:, :], xi[:, :], pr[:, :], pi_[:, :], tt[:, :], G8, b)
                    pw[nm] = (pr, pi_)
                fr_b = sap(tl["fre"], 0, 128, 0, [[1, NG], [0, 8]])
                fi_b = sap(tl["fim"], 0, 128, 0, [[1, NG], [0, 8]])

                def g8(t_):
                    return sap(t_, 0, 128, 0, [[8, NG], [1, 8]])
                far = psb("Cfar", [128, G8], F32)
                fai = psb("Cfai", [128, G8], F32)
                cmul(g8(far), g8(fai), fr_b, fi_b, g8(pw["A"][0]), g8(pw["A"][1]), g8(xr), g8(xi), R, R)
                BIG = NG * 8 * 16
                t1b = psb("Ct1b", [128, BIG], F32)
                t2b = psb("Ct2b", [128, BIG], F32)
                Are = psb("CAre", [128, NG, 128], BF16)
                Aim = psb("CAim", [128, NG, 128], BF16)
                Gre = psb("CGre", [128, NG, 128], BF16)
                Gim = psb("CGim", [128, NG, 128], BF16)

                def bigv(t_):
                    return sap(t_, 0, 128, 0, [[128, NG], [16, 8], [1, 16]])

                def pwv(t_):
                    return sap(t_, 0, 128, 0, [[8, NG], [1, 8], [0, 16]])

                def cpv(t_, c):
                    return sap(t_, 0, 128, c * 512, [[16, NG], [0, 8], [1, 16]])
                cmul(sap(WC, 0, 128, 0, [[256, NG], [16, 8], [1, 16]]), sap(WC, 0, 128, 128, [[256, NG], [16, 8], [1, 16]]),
                     cpv(Cc, 0), cpv(Cc, 1), pwv(pw["C"][0]), pwv(pw["C"][1]), bigv(t1b), bigv(t2b), R, [b, WCb], neg_im=True)
                cmul(bigv(Gre), bigv(Gim), cpv(Cc, 0), cpv(Cc, 1), pwv(pw["G"][0]), pwv(pw["G"][1]), bigv(t1b), bigv(t2b),
                     R, R, neg_im=True)
                cmul(bigv(Are), bigv(Aim), pwv(far), pwv(fai), cpv(Bc, 0), cpv(Bc, 1), bigv(t1b), bigv(t2b), R, R)
                fbr = psb("Cfbr", [128, G8], F32)
                fbi = psb("Cfbi", [128, G8], F32)
                cmul(g8(fbr), g8(fbi), fr_b, fi_b, g8(pw["B"][0]), g8(pw["B"][1]), g8(xr), g8(xi), R, R)
                PBc = [psb("CPBre", [128, NG, 128], BF16), psb("CPBim", [128, NG, 128], BF16)]
                cmul(bigv(PBc[0]), bigv(PBc[1]), pwv(fbr), pwv(fbi), cpv(Bc, 0), cpv(Bc, 1), bigv(t1b), bigv(t2b), R, R)
                for g2 in range(NG // 2):
                    pw_, pwb_ = pb[2 + g2 % 2], pbb[2 + g2 % 2]
                    for gg in range(2):
                        for c in range(2):
                            blk = gg * 2 + c
                            kb.op(pe, lambda: nc.tensor.matmul(pw_[:, blk * 128:(blk + 1) * 128], lhsT=PBc[c][:, 2 * g2 + gg, :],
                                                               rhs=ident, start=True, stop=True),
                                  reads=[b, cstb], writes=[pwb_])
                    V(lambda: nc.vector.tensor_copy(out=sap(WB, 0, 128, 2 * g2 * 256, [[1, 512]]), in_=pw_[:, :]),
                      [pwb_], [WBb])
                cmul(g8(fbr), g8(fbi), fr_b, fi_b, g8(pw["B2"][0]), g8(pw["B2"][1]), g8(xr), g8(xi), R, R)
                cmul(bigv(PBc[0]), bigv(PBc[1]), pwv(fbr), pwv(fbi), cpv(Bc, 0), cpv(Bc, 1), bigv(t1b), bigv(t2b), R, R)
                wb2t = psb("Cwb2t", [128, NG * 512], BF16)
                wb2tb = Buf()
                V(lambda: nc.vector.memset(wb2t[:, :], 0.0), [], [wb2tb])
                for g2 in range(NG // 2):
                    pw_, pwb_ = pb[2 + g2 % 2], pbb[2 + g2 % 2]
                    for gg in range(2):
                        for c in range(2):
                            blk = gg * 2 + c
                            kb.op(pe, lambda: nc.tensor.matmul(pw_[:, blk * 128:(blk + 1) * 128], lhsT=PBc[c][:, 2 * g2 + gg, :],
                                                               rhs=ident, start=True, stop=True),
                                  reads=[b, cstb], writes=[pwb_])
                    V(lambda: nc.vector.tensor_copy(out=sap(wb2t, 0, 128, 2 * g2 * 512, [[256, 4], [1, 64]]),
                                                    in_=sap(pw_, 0, 128, 0, [[128, 4], [1, 64]])), [pwb_], [wb2tb])
                    V(lambda: nc.vector.tensor_copy(out=sap(wb2t, 0, 128, 2 * g2 * 512 + 128 + 64, [[256, 4], [1, 64]]),
                                                    in_=sap(pw_, 0, 128, 64, [[128, 4], [1, 64]])), [pwb_], [wb2tb])
                kb.dma(sp, c_scr, wb2_scr[:, :], wb2t[:, :], reads=[wb2tb], writes=[wb2_b])
                kb.wait_all(sp, [c_scr])
                tmpf = psb("Ctmpf", [128, 512], F32)
                for g4 in range(NG // 4):
                    pf, pfb = pb[0], pbb[0]
                    pq, pqb = pb[1], pbb[1]
                    for gg in range(4):
                        g = g4 * 4 + gg
                        for (P_, Pb_, lo) in ((pf, pfb, 0), (pq, pqb, 64)):
                            kb.op(pe, lambda: nc.tensor.matmul(P_[:, gg * 128:(gg + 1) * 128], lhsT=Are[lo:lo + 64, g, :],
                                                               rhs=Gre[lo:lo + 64, g, :], start=True, stop=False),
                                  reads=[b], writes=[Pb_])
                            kb.op(pe, lambda: nc.tensor.matmul(P_[:, gg * 128:(gg + 1) * 128], lhsT=Aim[lo:lo + 64, g, :],
                                                               rhs=Gim[lo:lo + 64, g, :], start=False, stop=True),
                                  reads=[b], writes=[Pb_])
                    mL = sap(msk, 0, 128, 0, [[0, 4], [1, 128]])
                    mU = sap(msk, 0, 128, 128, [[0, 4], [1, 128]])
                    t4 = sap(tmpf, 0, 128, 0, [[128, 4], [1, 128]])
                    V(lambda: nc.vector.tensor_tensor(out=t4, in0=sap(pf, 0, 128, 0, [[128, 4], [1, 128]]), in1=mL,
                                                      op=ALU.mult), [pfb, b], [b])
                    V(lambda: nc.vector.tensor_tensor(out=sap(t1b, 0, 128, 0, [[128, 4], [1, 128]]),
                                                      in0=sap(pq, 0, 128, 0, [[128, 4], [1, 128]]), in1=mU, op=ALU.mult),
                      [pqb, b], [b])
                    V(lambda: nc.vector.tensor_tensor(out=t4, in0=t4, in1=sap(t1b, 0, 128, 0, [[128, 4], [1, 128]]),
                                                      op=ALU.add), R, R)
                    for gg in range(4):
                        g = g4 * 4 + gg
                        V(lambda: nc.vector.scalar_tensor_tensor(out=WT[:, g, :], in0=msk[:, 2, :], scalar=Dt[:, g:g + 1],
                                                                 in1=tmpf[:, gg * 128:(gg + 1) * 128], op0=ALU.mult,
                                                                 op1=ALU.add), R, [b, WTb])
                kb.barrier()
        def mixer_items():
            items = []
            for mt in range(12):
                items.append((wx[mt, :, :], 1024))
            for half in range(2):
                items.append((wv2[half, :, :], 2048))
            return items

        def krs_for(qg):
            out = []
            for kr in range(32):
                rows = [r for r in range(8 * qg, 8 * qg + 8)
                        if min(max(r - 4, 0), 24) <= kr <= min(max(r - 4, 0), 24) + 7]
                if rows:
                    out.append((kr, rows[0], rows[-1]))
            return out

        def inproj_phase(bl, qT, qTb, kT, kTb, vd, vdb):
            with ExitStack() as ph:
                psb = psbf(ph)
                xn2 = psb("xn2", [128, 8, T], BF16)
                xn2b = [[Buf() for c in range(4)] for k in range(8)]
                sq = [psb(f"sq{i}", [128, 512], BF16) for i in range(2)]
                scr = (sq, [Buf(), Buf()], pb[6], pbb[6], pb[6], pbb[6])
                chunks = main_chunks(xn2, xn2b, None, None)
                with_meta = (bl == 0)
                if with_meta:
                    xnm = psb("xn2m", [128, 8, NM], BF16)
                    xnmb = [Buf() for k in range(8)]
                    chunks.append(dict(N=NM, h=lambda k: hm[:, k, :], hb=lambda k: hmb[k],
                                       xn=lambda k: xnm[:, k, :], xnb=lambda k: xnmb[k]))
                for ch in chunks:
                    norm_chunk(ch, 1, scr)
                ust = [psb(f"ust{i}", [128, 8, 64], BF16) for i in range(2)]
                ustb = [Buf(), Buf()]
                ui = 0
                it = 0
                for mt in range(12):
                    ws, wb = stream.get()
                    for ci, ch in enumerate(chunks):
                        N = ch["N"]
                        is_meta = (ci == 4)
                        if is_meta and 4 <= mt < 8:
                            continue
                        pS, pSb = pb[it % 4], pbb[it % 4]
                        it += 1
                        for k in range(8):
                            kb.op(pe, lambda: nc.tensor.matmul(pS[:, :N], lhsT=ws[:, k * 128:(k + 1) * 128], rhs=ch["xn"](k),
                                                               start=(k == 0), stop=(k == 7)),
                                  reads=[wb, ch["xnb"](k)], writes=[pSb])
                        if mt < 4:
                            if is_meta:
                                o = sap(umT, 0, 128, mt * 16, [[2, 8], [1, 2]])
                                i_ = sap(pS, 0, 128, 0, [[1, 8], [8, 2]])
                                V(lambda: nc.vector.tensor_copy(out=o, in_=i_), [pSb], [umTb])
                            else:
                                us_, usb_ = ust[ui % 2], ustb[ui % 2]
                                ui += 1
                                i_ = sap(pS, 0, 128, 0, [[1, 8], [8, 64]])
                                V(lambda: nc.vector.tensor_copy(out=us_[:, :, :], in_=i_), [pSb], [usb_])
                                kb.dma(sp, c_scr, u_scr[mt * 128:(mt + 1) * 128, :, 2 + 64 * ci:2 + 64 * ci + 64], us_[:, :, :],
                                       reads=[usb_], writes=[u_scr_b])
                        elif mt < 8:
                            A(lambda: nc.scalar.copy(out=qT[:, mt - 4, ci * 512:(ci + 1) * 512], in_=pS[:, :N]),
                              [pSb], [qTb[mt - 4][ci]])
                        else:
                            if is_meta:
                                A(lambda: nc.scalar.copy(out=kmT[:, mt - 8, :], in_=pS[:, :N]), [pSb], [kmTb])
                            else:
                                A(lambda: nc.scalar.copy(out=kT[:, mt - 8, ci * 512:(ci + 1) * 512], in_=pS[:, :N]),
                                  [pSb], [kTb[mt - 8]])
                    if mt < 4:
                        kb.dma(sp, c_scr, u_scr[mt * 128:(mt + 1) * 128, :, 0:2], umT[:, mt, :, :], reads=[umTb],
                               writes=[u_scr_b])
                for j in range(8):
                    srcA = bass.AP(u_scr.tensor, j * NBLK, [[8 * NBLK, 16], [16 * 8 * NBLK, NG], [1, NBLK]])
                    dstA = bass.AP(u_scr2.tensor, j * 16 * NG * NBLK, [[NG * NBLK, 16], [NBLK, NG], [1, NBLK]])
                    kb.dma(sp, c_scr, dstA, srcA, reads=[u_scr_b], writes=[u2b[j]])
                for j in range(8):
                    u2b[j].w = (c_scr, c_scr.cnt)
                for half in range(2):
                    ws, wb = stream.get()
                    for w in range(-1, 32):
                        pS, pSb = pb[it % 4], pbb[it % 4]
                        it += 1
                        if w == -1:
                            t0, M, p0 = 0, 64, 64
                        elif w == 31:
                            t0, M, p0 = 64 * 31, 64, 0
                        else:
                            t0, M, p0 = 64 * w, 128, 0
                        rb_ = [xn2b[0][t0 // 512], xn2b[0][(t0 + M - 1) // 512]]
                        for k in range(8):
                            kb.op(pe, lambda: nc.tensor.matmul(pS[p0:p0 + M, 0:256], lhsT=xn2[:, k, t0:t0 + M],
                                                               rhs=ws[:, k * 256:(k + 1) * 256], start=(k == 0), stop=(k == 7)),
                                  reads=[wb, xn2b[k][t0 // 512], xn2b[k][(t0 + M - 1) // 512]], writes=[pSb])
                        if w >= 0:
                            V(lambda: nc.vector.tensor_copy(out=vd[0:64, w, 2 * half:2 * half + 2, :],
                                                            in_=sap(pS, 0, 64, 0, [[128, 2], [1, 64]])), [pSb], [vdb])
                        if w < 31:
                            A(lambda: nc.scalar.copy(out=vd[64:128, w + 1, 2 * half:2 * half + 2, :],
                                                     in_=sap(pS, 64, 64, 64, [[128, 2], [1, 64]])), [pSb], [vdb])
                    if with_meta:
                        pS, pSb = pb[it % 4], pbb[it % 4]
                        it += 1
                        for p0 in (0, 64):
                            for k in range(8):
                                kb.op(pe, lambda: nc.tensor.matmul(pS[p0:p0 + NM, 0:256], lhsT=xnm[:, k, :],
                                                                   rhs=ws[:, k * 256:(k + 1) * 256], start=(k == 0),
                                                                   stop=(k == 7)),
                                      reads=[wb, xnmb[k]], writes=[pSb])
                        V(lambda: nc.vector.tensor_copy(out=vm[0:NM, 2 * half:2 * half + 2, :],
                                                        in_=sap(pS, 0, NM, 0, [[128, 2], [1, 64]])), [pSb], [vmb])
                        V(lambda: nc.vector.tensor_copy(out=vm[64:64 + NM, 2 * half:2 * half + 2, :],
                                                        in_=sap(pS, 64, NM, 64, [[128, 2], [1, 64]])), [pSb], [vmb])
                kb.barrier()

        def na_phase(bl, qT, qTb, kT, kTb, vd, vdb, oT, oTb):
            LA = 3
            NSB = 4
            with ExitStack() as ph:
                psb = psbf(ph)
                exs = [psb(f"nex{i}", [128, 512], BF16) for i in range(NSB)]
                exb = [Buf() for i in range(NSB)]
                Ps = [psb(f"nP{i}", [128, 512], BF16) for i in range(NSB)]
                Pb = [Buf() for i in range(NSB)]
                pms = [psb(f"npm{i}", [128, 512], BF16) for i in range(2)]
                pmb = [Buf(), Buf()]
                rec = psb("nrec", [128, 512], F32)
                recb = Buf()
                for i in range(2):
                    V(lambda: nc.vector.memset(pms[i][:, :], 0.0), [], [pmb[i]])
                tiles_ = []
                gi_ = 0
                for hp in range(4):
                    for qg in range(4):
                        lst = krs_for(qg)
                        tiles_.append(dict(kind="meta", hp=hp, qg=qg, gi=gi_, first=True, last=False))
                        for idx, (kr, ra, rb2) in enumerate(lst):
                            tiles_.append(dict(kind="kr", hp=hp, qg=qg, gi=gi_, kr=kr, ra=ra, rb=rb2, first=False,
                                               last=(idx == len(lst) - 1)))
                        gi_ += 1

                def emit_S(t, tl_):
                    hp, qg = tl_["hp"], tl_["qg"]
                    pS, pSb = pb[t % NSB], pbb[t % NSB]
                    if tl_["kind"] == "meta":
                        tok0 = 512 * qg
                        for lo in (0, 64):
                            kb.op(pe, lambda: nc.tensor.matmul(pS[lo:lo + NM, 0:512], lhsT=kmT[lo:lo + 64, hp, :],
                                                               rhs=qT[lo:lo + 64, hp, tok0:tok0 + 512], start=True, stop=True),
                                  reads=[kmTb, qTb[hp][qg]], writes=[pSb])
                    else:
                        kr, ra, rb2 = tl_["kr"], tl_["ra"], tl_["rb"]
                        N = 64 * (rb2 - ra + 1)
                        for lo in (0, 64):
                            kb.op(pe, lambda: nc.tensor.matmul(pS[lo:lo + 64, 0:N], lhsT=kT[lo:lo + 64, hp, 64 * kr:64 * kr + 64],
                                                               rhs=qT[lo:lo + 64, hp, 64 * ra:64 * (rb2 + 1)],
                                                               start=True, stop=True),
                                  reads=[kTb[hp], qTb[hp][qg]], writes=[pSb])

                def emit_E(t, tl_):
                    hp, qg = tl_["hp"], tl_["qg"]
                    pS, pSb = pb[t % NSB], pbb[t % NSB]
                    if tl_["kind"] == "meta":
                        pm, pmb_ = pms[tl_["gi"] % 2], pmb[tl_["gi"] % 2]
                        for lo in (0, 64):
                            A(lambda: nc.scalar.activation(out=pm[lo:lo + NM, :], in_=pS[lo:lo + NM, 0:512], func=AF.Exp,
                                                           scale=0.125), [pSb], [pmb_])
                    else:
                        kr, ra, rb2 = tl_["kr"], tl_["ra"], tl_["rb"]
                        N = 64 * (rb2 - ra + 1)
                        ex_, exb_ = exs[t % NSB], exb[t % NSB]
                        P_, Pb_ = Ps[t % NSB], Pb[t % NSB]
                        A(lambda: nc.scalar.activation(out=ex_[:, 0:N], in_=pS[:, 0:N], func=AF.Exp, scale=0.125),
                          [pSb], [exb_])
                        rho = ra - kr + 7
                        V(lambda: nc.vector.tensor_tensor(out=P_[:, 0:N], in0=ex_[:, 0:N],
                                                          in1=sap(Etab, 0, 128, (hp * 15 + rho) * 64, [[1, N]]),
                                                          op=ALU.mult), [exb_, Etabb], [Pb_])

                def emit_PV(t, tl_):
                    hp, qg, g_ = tl_["hp"], tl_["qg"], tl_["gi"]
                    pO, pOb = pb[4 + g_ % 2], pbb[4 + g_ % 2]
                    pD, pDb = pb[6 + g_ % 2], pbb[6 + g_ % 2]
                    if tl_["kind"] == "meta":
                        pm, pmb_ = pms[g_ % 2], pmb[g_ % 2]
                        for lo in (0, 64):
                            kb.op(pe, lambda: nc.tensor.matmul(pO[lo:lo + 64, :], lhsT=vm[lo:lo + NM, hp, :],
                                                               rhs=pm[lo:lo + NM, :], start=True, stop=False),
                                  reads=[vmb, pmb_], writes=[pOb])
                        kb.op(pe, lambda: nc.tensor.matmul(pD[:, :], lhsT=blockones, rhs=pm[:, :], start=True, stop=False),
                              reads=[cstb, pmb_], writes=[pDb])
                    else:
                        kr, ra, rb2 = tl_["kr"], tl_["ra"], tl_["rb"]
                        N = 64 * (rb2 - ra + 1)
                        c0 = 64 * (ra - 8 * qg)
                        last = tl_["last"]
                        P_, Pb_ = Ps[t % NSB], Pb[t % NSB]
                        for lo in (0, 64):
                            kb.op(pe, lambda: nc.tensor.matmul(pO[lo:lo + 64, c0:c0 + N], lhsT=vd[lo:lo + 64, kr, hp, :],
                                                               rhs=P_[lo:lo + 64, 0:N], start=False, stop=last),
                                  reads=[vdb, Pb_], writes=[pOb])
                        kb.op(pe, lambda: nc.tensor.matmul(pD[:, c0:c0 + N], lhsT=blockones, rhs=P_[:, 0:N],
                                                           start=False, stop=last),
                              reads=[cstb, Pb_], writes=[pDb])
                        if last:
                            tok0 = 512 * qg
                            V(lambda: nc.vector.reciprocal(out=rec[:, :], in_=pD[:, :]), [pDb], [recb])
                            V(lambda: nc.vector.tensor_tensor(out=oT[:, hp, tok0:tok0 + 512], in0=pO[:, :], in1=rec[:, :],
                                                              op=ALU.mult), [pOb, recb], [oTb[hp][qg]])

                nt = len(tiles_)
                for t in range(min(LA, nt)):
                    emit_S(t, tiles_[t])
                for t in range(nt):
                    emit_E(t, tiles_[t])
                    if t + LA < nt:
                        emit_S(t + LA, tiles_[t + LA])
                    emit_PV(t, tiles_[t])
                kb.barrier()

        def s5_phase(bl, zT, zTb):
            with ExitStack() as ph:
                psb = psbf(ph)
                def ubg(g):
                    return sap(zT, 0, 128, g * NBLK, [[1, NBLK]])

                def ubrows(j):
                    return sap(zT, 16 * j, 16, 0, [[NBLK, NG], [1, NBLK]])
                ubb = [Buf() for g in range(NG)]
                Vs = psb("Vs", [128, 2, NG, NBLK], BF16)
                Vsb = Buf()
                Ssb = Buf()
                X = [psb(f"X{i}", [128, 4, NG], F32) for i in range(2)]
                Xb = [Buf(), Buf()]
                t1 = psb("st1", [128, 4, NG], F32)
                t2 = psb("st2", [128, 4, NG], F32)
                tb = Buf()
                g1s = [psb(f"sg1{i}", [128, NBLK], F32) for i in range(2)]
                g2s = [psb(f"sg2{i}", [128, NBLK], F32) for i in range(2)]
                gbs = [Buf(), Buf()]
                kb.dma(sp, c_scr, sap(zT, 0, 128, 0, [[NBLK, NG], [1, NBLK]]), u_scr2[:, :, :], reads=u2b, writes=ubb)
                for g in range(NG):
                    ubb[g].w = (c_scr, c_scr.cnt)
                wb2 = psb("wb2", [128, 8 * 512], BF16)
                wb2b = Buf()

                def ubs(g, a, b_):
                    return sap(zT, 0, 128, g * NBLK + a, [[1, b_ - a]])
                for g in range(NG):
                    if g % 8 == 0:
                        kb.dma(sp, c_scr, wb2[:, :], wb2_scr[:, g * 512:(g + 8) * 512], reads=[wb2_b], writes=[wb2b])
                    pR_, pRb_ = pb[(2 * g) % 4], pbb[(2 * g) % 4]
                    pI_, pIb_ = pb[(2 * g + 1) % 4], pbb[(2 * g + 1) % 4]
                    for c, (P_, Pb_) in enumerate(((pR_, pRb_), (pI_, pIb_))):
                        w2o = (g % 8) * 512 + c * 256
                        kb.op(pe, lambda: nc.tensor.matmul(P_[:, 0:NBLK], lhsT=WB[:, g, c, :], rhs=ubg(g), start=True, stop=False),
                              reads=[WBb, ubb[g]], writes=[Pb_])
                        kb.op(pe, lambda: nc.tensor.matmul(P_[:, 1:NBLK], lhsT=wb2[:, w2o:w2o + 128], rhs=ubs(g, 0, NBLK - 1),
                                                           start=False, stop=False),
                              reads=[wb2b, ubb[g]], writes=[Pb_])
                        kb.op(pe, lambda: nc.tensor.matmul(P_[:, 1:NBLK - 1], lhsT=wb2[:, w2o + 128:w2o + 256], rhs=ubs(g, 2, NBLK),
                                                           start=False, stop=True),
                              reads=[wb2b, ubb[g]], writes=[Pb_])
                        V(lambda: nc.vector.tensor_copy(out=Vs[0:64, c, g, :], in_=P_[0:64, 0:NBLK]), [Pb_], [Vsb])
                        V(lambda: nc.vector.tensor_copy(out=Vs[64:128, c, g, :], in_=sap(P_, 64, 64, NBLK - 1, [[-1, NBLK]])),
                          [Pb_], [Vsb])
                V(lambda: nc.vector.memset(X[1][:, :, :], 0.0), [], [Xb[1]])
                F4 = 4 * NG
                for i in range(NBLK // 2):
                    Xp, Xpb = X[(i + 1) % 2], Xb[(i + 1) % 2]
                    Xn, Xnb = X[i % 2], Xb[i % 2]
                    sw = sap(Xp, 0, 128, 2 * NG, [[-2 * NG, 2], [1, 2 * NG]])
                    V(lambda: nc.vector.tensor_tensor(out=sap(t1, 0, 128, 0, [[1, F4]]), in0=sap(scA, 0, 128, 0, [[1, F4]]),
                                                      in1=sap(Xp, 0, 128, 0, [[1, F4]]), op=ALU.mult), [scb, Xpb], [tb])
                    V(lambda: nc.vector.tensor_tensor(out=sap(t2, 0, 128, 0, [[2 * NG, 2], [1, 2 * NG]]),
                                                      in0=sap(scB, 0, 128, 0, [[2 * NG, 2], [1, 2 * NG]]), in1=sw, op=ALU.mult),
                      [scb, Xpb], [tb])
                    V(lambda: nc.vector.tensor_tensor(out=sap(t1, 0, 128, 0, [[1, F4]]), in0=sap(t1, 0, 128, 0, [[1, F4]]),
                                                      in1=sap(t2, 0, 128, 0, [[1, F4]]), op=ALU.add), [tb], [tb])
                    vi = sap(Vs, 0, 128, 2 * i, [[NBLK, 2 * NG], [1, 2]])
                    x2 = sap(Xn, 0, 128, 0, [[2, 2 * NG], [1, 2]])
                    t1v = sap(t1, 0, 128, 0, [[2, 2 * NG], [1, 2]])
                    V(lambda: nc.vector.tensor_tensor(out=x2, in0=t1v, in1=vi, op=ALU.add), [tb, Vsb], [Xnb])
                    A(lambda: nc.scalar.copy(out=vi, in_=x2), [Xnb], [Ssb])
                al = [psb(f"al{i}", [128, 2, NBLK], BF16) for i in range(2)]
                alb = [Buf(), Buf()]
                for i in range(2):
                    V(lambda: nc.vector.memset(al[i][:, :, :], 0.0), [], [alb[i]])
                for g in range(NG):
                    pY, pYb = pb[4 + g % 2], pbb[4 + g % 2]
                    a_, ab_ = al[g % 2], alb[g % 2]
                    V(lambda: nc.vector.tensor_copy(out=a_[0:64, :, 1:NBLK],
                                                    in_=sap(Vs, 0, 64, g * NBLK, [[NG * NBLK, 2], [1, NBLK - 1]])),
                      [Ssb], [ab_])
                    V(lambda: nc.vector.tensor_copy(out=a_[64:128, :, 0:NBLK - 1],
                                                    in_=sap(Vs, 64, 64, g * NBLK + NBLK - 2, [[NG * NBLK, 2], [-1, NBLK - 1]])),
                      [Ssb], [ab_])
                    kb.op(pe, lambda: nc.tensor.matmul(pY[:, 0:NBLK], lhsT=WT[:, g, :], rhs=ubg(g), start=True, stop=False),
                          reads=[WTb, ubb[g]], writes=[pYb])
                    for c in range(2):
                        kb.op(pe, lambda: nc.tensor.matmul(pY[:, 0:NBLK], lhsT=WC[:, g, c, :], rhs=a_[:, c, :],
                                                           start=False, stop=(c == 1)),
                              reads=[WCb, ab_], writes=[pYb])
                    g1, g2, gb = g1s[g % 2], g2s[g % 2], gbs[g % 2]
                    A(lambda: nc.scalar.activation(out=g1[:, :], in_=pY[:, 0:NBLK], func=AF.Square), [pYb], [gb])
                    V(lambda: nc.vector.tensor_scalar(out=g1[:, :], in0=g1[:, :], scalar1=0.07135481627, scalar2=1.5957691216,
                                                      op0=ALU.mult, op1=ALU.add), [gb], [gb])
                    V(lambda: nc.vector.tensor_tensor(out=g2[:, :], in0=g1[:, :], in1=pY[:, 0:NBLK], op=ALU.mult), [gb, pYb], [gb])
                    A(lambda: nc.scalar.activation(out=g2[:, :], in_=g2[:, :], func=AF.Sigmoid), [gb], [gb])
                    V(lambda: nc.vector.tensor_tensor(out=ubg(g), in0=g2[:, :], in1=pY[:, 0:NBLK], op=ALU.mult),
                      [gb, pYb], [ubb[g]])
                z2b = Buf()
                kb.dma(sp, c_scr, z_scr2[:, :, :], sap(zT, 0, 128, 0, [[NBLK, NG], [1, NBLK]]), reads=ubb + zsb, writes=[z2b])
                for j in range(8):
                    srcZ = bass.AP(z_scr2.tensor, j * 16 * NG * NBLK, [[NG * NBLK, 16], [NBLK, NG], [1, NBLK]])
                    dstZ = bass.AP(z_scr.tensor, j * NBLK, [[8 * NBLK, 16], [16 * 8 * NBLK, NG], [1, NBLK]])
                    kb.dma(sp, c_scr, dstZ, srcZ, reads=[z2b], writes=[zsb[j]])
                for j in range(8):
                    zsb[j].w = (c_scr, c_scr.cnt)
                for mt in range(4):
                    kb.dma(sp, c_scr, zT[:, mt, :, :], z_scr[mt * 128:(mt + 1) * 128, :, :], reads=zsb, writes=zTb + ubb)
                for b_ in zTb:
                    b_.w = (c_scr, c_scr.cnt)
                kb.barrier()

        def merge_items():
            items = []
            for half in range(2):
                for mo in range(8):
                    items.append((wx[16 + mo, :, :], 1024))
                    items.append((wbs_d[mo, :, :], 512))
                    items.append((wx[24 + mo, :, :], 1024))
                    items.append((wbn_d[mo, :, :], 512))
                for mo in range(8):
                    items.append((wo_d[mo, :, :], 1024))
            return items

        c_wgl = kb.ctr("s_wgl")

        def merge_phase(bl, zT, zTb, oT, oTb):
            with ExitStack() as ph:
                psb = psbf(ph)
                wgl = psb("wgl", [128, 4, 512], BF16)
                wglb = Buf()
                for mt in range(4):
                    kb.dma(pool, c_wgl, wgl[:, mt, :], wglu_d[mt, :, :], writes=[wglb])
                wglb.w = (c_wgl, c_wgl.cnt)
                xn2 = psb("mxn2", [128, 8, 1024], BF16)
                mg = psb("mg", [128, 8, 1024], BF16)
                sq = [psb(f"msq{i}", [128, 512], BF16) for i in range(2)]
                sd = psb("msd", [128, 512], F32)
                rstd = psb("mrstd", [128, 512], F32)
                scr = (sq, [Buf(), Buf()], sd, Buf(), rstd, Buf())
                s1 = [psb(f"ms1{i}", [128, 512], BF16) for i in range(2)]
                s2 = [psb(f"ms2{i}", [128, 512], BF16) for i in range(2)]
                m1 = [psb(f"mm1{i}", [128, 512], BF16) for i in range(2)]
                sb1, sb2, mb1 = [Buf(), Buf()], [Buf(), Buf()], [Buf(), Buf()]

                def zperm(t_, mt, c):
                    return sap(t_, 0, 128, mt * 8 * NBLK + 2 + 64 * c, [[NBLK, 8], [1, 64]])
                for c in range(4):
                    for mt in range(4):
                        pS, pSb = pb[(c % 2) * 4 + mt], pbb[(c % 2) * 4 + mt]
                        for k in range(4):
                            kb.op(pe, lambda: nc.tensor.matmul(pS[:, :], lhsT=wgl[:, mt, k * 128:(k + 1) * 128], rhs=zperm(zT, k, c),
                                                               start=(k == 0), stop=(k == 3)),
                                  reads=[wglb, zTb[c]], writes=[pSb])
                    for mt in range(4):
                        pS, pSb = pb[(c % 2) * 4 + mt], pbb[(c % 2) * 4 + mt]
                        A(lambda: nc.scalar.activation(out=s1[mt % 2][:, :], in_=pS[:, :], func=AF.Sigmoid), [pSb], [sb1[mt % 2]])
                        V(lambda: nc.vector.tensor_tensor(out=zperm(zT, mt, c), in0=zperm(zT, mt, c),
                                                          in1=sap(s1[mt % 2], 0, 128, 0, [[64, 8], [1, 64]]), op=ALU.mult),
                          [sb1[mt % 2], zTb[c]], [zTb[c]])
                for half in range(2):
                    xnb = [[Buf() for c in range(2)] for k in range(8)]
                    mgb = [[Buf() for c in range(2)] for k in range(8)]
                    chs = []
                    for c2 in range(2):
                        c = 2 * half + c2
                        chs.append(dict(N=512, h=(lambda k, c=c: hT[:, k, c * 512:(c + 1) * 512]), hb=(lambda k, c=c: hb[k][c]),
                                        xn=(lambda k, c2=c2: xn2[:, k, c2 * 512:(c2 + 1) * 512]),
                                        xnb=(lambda k, c2=c2: xnb[k][c2])))
                    for ch in chs:
                        norm_chunk(ch, 1, scr)
                    for mo in range(8):
                        wgs, wgsb = stream.get()
                        for c2, ch in enumerate(chs):
                            p1, p1b = pb[c2], pbb[c2]
                            for k in range(8):
                                kb.op(pe, lambda: nc.tensor.matmul(p1[:, :], lhsT=wgs[:, k * 128:(k + 1) * 128], rhs=ch["xn"](k),
                                                                   start=(k == 0), stop=(k == 7)),
                                      reads=[wgsb, ch["xnb"](k)], writes=[p1b])
                            A(lambda: nc.scalar.activation(out=s1[c2][:, :], in_=p1[:, :], func=AF.Sigmoid), [p1b], [sb1[c2]])
                        wbs_, wbsb = stream.get()
                        for c2, ch in enumerate(chs):
                            c = 2 * half + c2
                            p3, p3b = pb[2 + c2], pbb[2 + c2]
                            for k in range(4):
                                kb.op(pe, lambda: nc.tensor.matmul(p3[:, :], lhsT=wbs_[:, k * 128:(k + 1) * 128], rhs=zperm(zT, k, c),
                                                                   start=(k == 0), stop=(k == 3)),
                                      reads=[wbsb, zTb[c]], writes=[p3b])
                            V(lambda: nc.vector.tensor_tensor(out=sap(m1[c2], 0, 128, 0, [[8, 64], [1, 8]]),
                                                              in0=sap(s1[c2], 0, 128, 0, [[8, 64], [1, 8]]),
                                                              in1=sap(p3, 0, 128, 0, [[1, 64], [64, 8]]), op=ALU.mult),
                              [sb1[c2], p3b], [mb1[c2]])
                        wgn, wgnb = stream.get()
                        for c2, ch in enumerate(chs):
                            p2, p2b = pb[4 + c2], pbb[4 + c2]
                            for k in range(8):
                                kb.op(pe, lambda: nc.tensor.matmul(p2[:, :], lhsT=wgn[:, k * 128:(k + 1) * 128], rhs=ch["xn"](k),
                                                                   start=(k == 0), stop=(k == 7)),
                                      reads=[wgnb, ch["xnb"](k)], writes=[p2b])
                            A(lambda: nc.scalar.activation(out=s2[c2][:, :], in_=p2[:, :], func=AF.Sigmoid), [p2b], [sb2[c2]])
                        wbn_, wbnb = stream.get()
                        for c2, ch in enumerate(chs):
                            c = 2 * half + c2
                            p4, p4b = pb[6 + c2], pbb[6 + c2]
                            for k in range(4):
                                kb.op(pe, lambda: nc.tensor.matmul(p4[:, :], lhsT=wbn_[:, k * 128:(k + 1) * 128],
                                                                   rhs=oT[:, k, c * 512:(c + 1) * 512], start=(k == 0), stop=(k == 3)),
                                      reads=[wbnb, oTb[k][c]], writes=[p4b])
                            V(lambda: nc.vector.tensor_tensor(out=s2[c2][:, :], in0=s2[c2][:, :], in1=p4[:, :], op=ALU.mult),
                              [sb2[c2], p4b], [sb2[c2]])
                            V(lambda: nc.vector.tensor_tensor(out=mg[:, mo, c2 * 512:(c2 + 1) * 512], in0=m1[c2][:, :],
                                                              in1=s2[c2][:, :], op=ALU.add), [mb1[c2], sb2[c2]], [mgb[mo][c2]])
                    for mo in range(8):
                        ws, wb = stream.get()
                        for c2 in range(2):
                            c = 2 * half + c2
                            pO, pOb = pb[(2 * mo + c2) % 8], pbb[(2 * mo + c2) % 8]
                            for k in range(8):
                                kb.op(pe, lambda: nc.tensor.matmul(pO[:, :], lhsT=ws[:, k * 128:(k + 1) * 128],
                                                                   rhs=mg[:, k, c2 * 512:(c2 + 1) * 512], start=(k == 0), stop=(k == 7)),
                                      reads=[wb, mgb[k][c2]], writes=[pOb])
                            hk = hT[:, mo, c * 512:(c + 1) * 512]
                            V(lambda: nc.vector.tensor_tensor(out=hk, in0=pO[:, :], in1=hk, op=ALU.add),
                              [pOb, hb[mo][c]], [hb[mo][c]])
                kb.barrier()

        def mixer(bl):
            with ExitStack() as mx:
                psb = psbf(mx)
                qT = psb("qoT", [128, 4, T], BF16)
                qTb = [[Buf() for c in range(4)] for i in range(4)]
                qTall = [b_ for r_ in qTb for b_ in r_]
                with ExitStack() as m2:
                    psb2 = psbf(m2)
                    kT = psb2("kT", [128, 4, T], BF16)
                    vd = psb2("vd", [128, 32, 4, 64], BF16)
                    kTb = [Buf() for i in range(4)]
                    vdb = Buf()
                    inproj_phase(bl, qT, qTb, kT, kTb, vd, vdb)
                    if dbg and bl == 0:
                        dump("qT", qT[:, :, :], [128, 4, T], qTall)
                        dump("kT", kT[:, :, :], [128, 4, T], kTb)
                        dump("vd", vd[:, :, :, :], [128, 32, 4, 64], [vdb])
                        kb.barrier()
                        kb.wait_all(sp, [c_out])
                        kb.wait_all(pe, [c_out])
                    if stage >= 7:
                        stream.extend(merge_items())
                    if stage >= 5:
                        na_phase(bl, qT, qTb, kT, kTb, vd, vdb, qT, qTb)
                        if dbg and bl == 0:
                            dump("oT", qT[:, :, :], [128, 4, T], qTall)
                            for E_ in (pe, act, dve, pool):
                                kb.wait_all(E_, [c_out])
                zT = psb("zT", [128, 4, 8, NBLK], BF16)
                zTb = [Buf() for c in range(4)]
                if stage >= 6:
                    s5_phase(bl, zT, zTb)
                    if dbg and bl == 0:
                        dump("zT", zT[:, :, :, :], [128, 4, 8, NBLK], zTb)
                        for E_ in (pe, act, dve, pool):
                            kb.wait_all(E_, [c_out])
                if stage >= 7:
                    merge_phase(bl, zT, zTb, qT, qTb)

        allhb = [hb[k][c] for k in range(8) for c in range(4)]
        if stage >= 2:
            setup_na()
        if stage >= 3:
            setup_s5()
        hT = sb("hT", [128, 8, T], F32)
        hm = sb("hm", [128, 8, NM], F32)
        stream = Stream(kb, nc, es, NSLOT, 2048)
        c_hm = kb.ctr("s_hm")
        kb.dma(sp, c_hm, hm[:, :, :], metaT[:, :, :], writes=hmb)
        for bl in range(nb_local):
            load_x(bl)
            stream.extend(ffn_items(w1i, w1o))
            if stage >= 4:
                stream.extend(mixer_items())
            ffn_phase(bl, 0, with_meta=(bl == 0))
            if dbg and bl == 0:
                dump("h1", hT[:, :, :], [128, 8, T], allhb)
                dump("hm1", hm[:, :, :], [128, 8, NM], hmb)
            if stage >= 4:
                mixer(bl)
                if dbg and bl == 0:
                    dump("h2", hT[:, :, :], [128, 8, T], allhb)
            stream.extend(ffn_items(w2i, w2o))
            ffn_phase(bl, 2, with_meta=False)
            final_phase(bl)
        kb.wait_all(sp, [c_out])
    return nc, dbg_outs


def host_inputs(inputs, core, nb_local=NB_LOCAL):
    f = np.float32
    x = inputs["x"]
    b0 = core * nb_local
    xT = np.ascontiguousarray(
        x[b0:b0 + nb_local].transpose(0, 2, 1).reshape(nb_local, 8, 128, T)).astype(f)
    metaT = np.ascontiguousarray(inputs["meta_tokens"].T.reshape(8, 128, NM).transpose(1, 0, 2)).astype(f)
    g = np.stack([inputs["norm_ffn1"][0], inputs["norm_mix"][0], inputs["norm_ffn2"][0], inputs["norm_final"]])
    gains = np.ascontiguousarray(g.reshape(4, 8, 128).transpose(2, 0, 1)).astype(f)

    def ffn_in(w):
        w = w.reshape(8, 128, 2, NFF, 128)
        return np.ascontiguousarray(w.transpose(3, 1, 0, 2, 4).reshape(NFF, 128, 8 * 256)).astype(f)

    def ffn_out(w):
        w = w.reshape(NFF, 128, 8, 128)
        return np.ascontiguousarray(w.transpose(2, 1, 0, 3).reshape(8, 128, NFF * 128)).astype(f)

    consts = np.zeros((128, 3, 128), f)
    consts[:, 0, :] = np.eye(128)
    consts[:, 1, :] = 1.0
    consts[:64, 2, :64] = 1.0
    consts[64:, 2, 64:] = 1.0
    w_in = inputs["w_in"][0]

    def tiles(w, nk, nm):
        return np.ascontiguousarray(w.reshape(nk, 128, nm, 128).transpose(2, 1, 0, 3).reshape(nm, 128, nk * 128)).astype(f)
    wx_ = tiles(w_in, 8, 32)
    wv = w_in[:, 1536:2048].reshape(8, 128, 2, 256)
    wv2 = np.ascontiguousarray(wv.transpose(2, 1, 0, 3).reshape(2, 128, 2048)).astype(f)
    rpb = inputs["na_rpb"][0]
    rpbpad = np.zeros((8, 15, 128), f)
    rpbpad[:, :, 48:48 + 31] = rpb[:, ::-1, :]
    qc = np.arange(64)
    cs = np.clip(qc - 8, 0, 48)
    kc = np.arange(64)
    cm = ((kc[:, None] >= cs[None, :]) & (kc[:, None] < cs[None, :] + 16)).astype(f)
    colmask = np.concatenate([cm, cm], axis=0)
    are = np.stack([inputs["ssm_a_re_fwd"][0], inputs["ssm_a_re_bwd"][0]])
    aim = np.stack([inputs["ssm_a_im_fwd"][0], inputs["ssm_a_im_bwd"][0]])
    ldt = np.stack([inputs["ssm_log_dt_fwd"][0], inputs["ssm_log_dt_bwd"][0]])
    ldt_gn = np.broadcast_to(ldt[:, :, None], (2, 32, 64))
    Bre, Bim = inputs["ssm_b_re"][0], inputs["ssm_b_im"][0]
    Cre, Cim = inputs["ssm_c_re"][0], inputs["ssm_c_im"][0]

    def layC2(a):
        return a.transpose(0, 2, 1).reshape(128, 32)
    s5c = np.ascontiguousarray(np.stack([layC2(are), layC2(aim), layC2(ldt_gn)], axis=1)).astype(f)

    def layCC(Cm):
        t = Cm.transpose(2, 0, 1).reshape(64, 512)
        return np.concatenate([t, t], axis=0)

    def layCB(Bm):
        t = Bm.transpose(1, 0, 2).reshape(64, 512)
        return np.concatenate([t, t], axis=0)
    s5cc = np.ascontiguousarray(np.stack([layCC(Cre), layCC(Cim)], axis=1)).astype(f)
    s5cb = np.ascontiguousarray(np.stack([layCB(Bre), layCB(Bim)], axis=1)).astype(f)
    j8 = np.arange(8)
    s5ce = np.zeros((128, 5, 8), f)
    s5ce[:64, 3] = 7 - j8
    s5ce[64:, 3] = j8
    s5ce[:64, 4] = 15 - j8
    s5ce[64:, 4] = j8 + 8
    s5ce[:64, 0] = j8 + 1
    s5ce[:64, 1] = j8
    s5ce[:64, 2] = -j8
    s5ce[64:, 0] = 8 - j8
    s5ce[64:, 1] = -j8
    s5ce[64:, 2] = j8
    Dm = inputs["ssm_d"][0]
    s5d = np.ascontiguousarray(np.broadcast_to(Dm.T[None], (8, 16, 32)).reshape(128, 32)).astype(f)
    ii = np.repeat(np.arange(8), 16)
    s5m = np.stack([(ii[:, None] <= ii[None, :]), (ii[:, None] >= ii[None, :]), np.eye(128, dtype=bool)], axis=1).astype(f)
    extra = {
        "wx": wx_, "wv2": wv2, "wglu": tiles(inputs["w_glu"][0], 4, 4), "wbs": tiles(inputs["w_branch_ssm"][0], 4, 8),
        "wbn": tiles(inputs["w_branch_na"][0], 4, 8), "wo": tiles(inputs["w_out"][0], 8, 8),
        "rpbpad": rpbpad, "colmask": colmask, "s5c": s5c, "s5cc": s5cc,
        "s5cb": s5cb, "s5ce": s5ce, "s5d": s5d, "s5m": np.ascontiguousarray(s5m),
    }
    return {
        **extra,
        "xT": xT, "metaT": metaT, "gains": gains,
        "w1i": ffn_in(inputs["w_ffn1_in"][0]), "w1o": ffn_out(inputs["w_ffn1_out"][0]),
        "w2i": ffn_in(inputs["w_ffn2_in"][0]), "w2o": ffn_out(inputs["w_ffn2_out"][0]),
        "consts": consts,
    }


def kernel(**inputs):
    inputs = {k: np.asarray(v) for k, v in inputs.items()}
    nc, _ = build_program()
    in_maps = [host_inputs(inputs, c) for c in range(NCORES)]
    res = run_bass_kernel_spmd(nc, in_maps, core_ids=list(range(NCORES)))
    out = np.empty((16, T, D), np.float32)
    for c in range(NCORES):
        o = np.asarray(res.results[c]["outT"])
        out[c * NB_LOCAL:(c + 1) * NB_LOCAL] = o.reshape(NB_LOCAL, D, T).transpose(0, 2, 1)
    return out
```

```python
import numpy as np
from contextlib import ExitStack
import concourse.bass as bass
import concourse.mybir as mybir
from concourse.bass_utils import run_bass_kernel_spmd

F32 = mybir.dt.float32
BF16 = mybir.dt.bfloat16
ALU = mybir.AluOpType
AF = mybir.ActivationFunctionType

D = 1024
T = 2048
NM = 16
DFF = 2816
NFF = 22
NG = 32
NBLK = 258
NB_LOCAL = 2
NCORES = 8
FF_SPLITS = [(0, 8), (8, 15), (15, 22)]
NSLOT = 2
TWO_PI = 6.283185307179586


class Ctr:
    def __init__(self, sem):
        self.sem = sem
        self.cnt = 0


class Buf:
    __slots__ = ("w", "r", "name")

    def __init__(self, name=""):
        self.w = None
        self.r = {}
        self.name = name


class Eng:
    def __init__(self, h, ctr, name):
        self.h = h
        self.ctr = ctr
        self.seen = {}
        self.name = name


class KB:
    def __init__(self, nc, es):
        self.nc = nc
        self.es = es

        def mk(h, n):
            return Eng(h, Ctr(es.enter_context(nc.semaphore(n))), n)

        self.pe = mk(nc.tensor, "s_pe")
        self.act = mk(nc.scalar, "s_act")
        self.dve = mk(nc.vector, "s_dve")
        self.pool = mk(nc.gpsimd, "s_pool")
        self.sp = mk(nc.sync, "s_sp")
        self.engs = [self.pe, self.act, self.dve, self.pool, self.sp]

    def ctr(self, name):
        return Ctr(self.es.enter_context(self.nc.semaphore(name)))

    def _need(self, E, dep, raw):
        ctr, val = dep
        if ctr is E.ctr and not raw:
            return
        if E.seen.get(id(ctr), 0) >= val:
            return
        E.h.wait_ge(ctr.sem, val)
        E.seen[id(ctr)] = val

    def _deps(self, E, reads, writes):
        for b in reads:
            if b.w is not None:
                self._need(E, b.w, True)
        for b in writes:
            if b.w is not None:
                self._need(E, b.w, False)
            for dep in b.r.values():
                self._need(E, dep, False)

    def op(self, E, fn, reads=(), writes=()):
        self._deps(E, reads, writes)
        ins = fn()
        E.ctr.cnt += 1
        ins.then_inc(E.ctr.sem, 1)
        for b in writes:
            b.w = (E.ctr, E.ctr.cnt)
            b.r = {}
        for b in reads:
            b.r[id(E.ctr)] = (E.ctr, E.ctr.cnt)

    def dma(self, Q, ctr, out, in_, reads=(), writes=()):
        self._deps(Q, reads, writes)
        Q.h.dma_start(out=out, in_=in_).then_inc(ctr.sem, 16)
        ctr.cnt += 16
        for b in writes:
            b.w = (ctr, ctr.cnt)
            b.r = {}
        for b in reads:
            b.r[id(ctr)] = (ctr, ctr.cnt)

    def barrier(self):
        for E in self.engs:
            for O in self.engs:
                if O is not E and O.ctr.cnt > 0:
                    self._need(E, (O.ctr, O.ctr.cnt), True)

    def wait_all(self, E, ctrs):
        for c in ctrs:
            if c.cnt > 0:
                self._need(E, (c, c.cnt), True)


class Stream:
    def __init__(self, kb, nc, es, nslot, cols):
        self.kb = kb
        self.items = []
        self.next_load = 0
        self.next_get = 0
        self.nslot = nslot
        self.slots = [es.enter_context(nc.sbuf_tensor(f"ring{i}", [128, cols], BF16)) for i in range(nslot)]
        self.bufs = [Buf(f"ring{i}") for i in range(nslot)]
        self.ctrs = [kb.ctr(f"s_ring{i}") for i in range(nslot)]

    def extend(self, items):
        self.items += items

    def _load(self, n):
        s = n % self.nslot
        src, ncols = self.items[n]
        self.kb.dma(self.kb.pool, self.ctrs[s], self.slots[s][:, 0:ncols], src, writes=[self.bufs[s]])

    def get(self):
        i = self.next_get
        self.next_get += 1
        while self.next_load < min(len(self.items), i + self.nslot):
            self._load(self.next_load)
            self.next_load += 1
        s = i % self.nslot
        return self.slots[s], self.bufs[s]


def sap(t, part0, nparts, off, dims):
    row = 1
    for d in t.shape[1:]:
        row *= d
    return bass.AP(t, part0 * row + off, [[row, nparts]] + [list(d) for d in dims])


def build_program(nb_local=NB_LOCAL, stage=99, dbg=False):
    nc = bass.Bass("TRN2", target_bir_lowering=False)

    def din(name, shape, dt=F32):
        return nc.dram_tensor(name, list(shape), dt, kind="ExternalInput").ap()

    xT = din("xT", [nb_local, 8, 128, T])
    metaT = din("metaT", [128, 8, NM])
    gains_d = din("gains", [128, 4, 8])
    w1i = din("w1i", [NFF, 128, 8 * 256])
    w1o = din("w1o", [8, 128, NFF * 128])
    w2i = din("w2i", [NFF, 128, 8 * 256])
    w2o = din("w2o", [8, 128, NFF * 128])
    consts_d = din("consts", [128, 3, 128])
    wx = din("wx", [32, 128, 1024])
    wv2 = din("wv2", [2, 128, 2048])
    wglu_d = din("wglu", [4, 128, 512])
    wbs_d = din("wbs", [8, 128, 512])
    wbn_d = din("wbn", [8, 128, 512])
    wo_d = din("wo", [8, 128, 1024])
    rpbpad = din("rpbpad", [8, 15, 128])
    colmask_d = din("colmask", [128, 64])
    s5c_d = din("s5c", [128, 3, 32])
    s5cc_d = din("s5cc", [128, 2, 512])
    s5cb_d = din("s5cb", [128, 2, 512])
    s5ce_d = din("s5ce", [128, 5, 8])
    s5d_d = din("s5d", [128, 32])
    s5m_d = din("s5m", [128, 3, 128])
    u_scr = nc.dram_tensor("u_scr", [512, 8, NBLK], BF16, kind="Internal").ap()
    z_scr = nc.dram_tensor("z_scr", [512, 8, NBLK], BF16, kind="Internal").ap()
    wb2_scr = nc.dram_tensor("wb2_scr", [128, NG * 512], BF16, kind="Internal").ap()
    u_scr2 = nc.dram_tensor("u_scr2", [128, NG, NBLK], BF16, kind="Internal").ap()
    z_scr2 = nc.dram_tensor("z_scr2", [128, NG, NBLK], BF16, kind="Internal").ap()
    outT = nc.dram_tensor("outT", [nb_local, 8, 128, T], F32, kind="ExternalOutput").ap()
    dbg_outs = {}

    with ExitStack() as es:
        kb = KB(nc, es)
        pe, act, dve, pool, sp = kb.pe, kb.act, kb.dve, kb.pool, kb.sp

        def sb(name, shape, dt):
            return es.enter_context(nc.sbuf_tensor(name, list(shape), dt))

        uid = [0]

        def uname(n):
            uid[0] += 1
            return f"{n}_{uid[0]}"


        Etab = sb("Etab", [128, 4, 15, 64], BF16)
        Etabb = Buf()
        kmT = sb("kmT", [128, 4, NM], BF16)
        kmTb = Buf()
        vm = sb("vm", [128, 4, 64], BF16)
        vmb = Buf()
        umT = sb("umT", [128, 4, 8, 2], BF16)
        umTb = Buf()
        WB = sb("WB", [128, NG, 2, 128], BF16)
        WBb = Buf()
        WC = sb("WC", [128, NG, 2, 128], BF16)
        WCb = Buf()
        WT = sb("WT", [128, NG, 128], BF16)
        WTb = Buf()
        scA = sb("scA", [128, 4, NG], F32)
        scB = sb("scB", [128, 4, NG], F32)
        wb2_b = Buf()
        scb = Buf()
        u_scr_b = Buf()
        z_scr_b = Buf()
        u2b = [Buf() for j in range(8)]
        zsb = [Buf() for j in range(8)]
        c_scr = kb.ctr("s_scr")

        hb = [[Buf(f"h{k}_{c}") for c in range(4)] for k in range(8)]
        hmb = [Buf(f"hm{k}") for k in range(8)]
        gains = sb("gains_sb", [128, 4, 8], F32)
        gainsb = Buf("gains")
        cst = sb("cst", [128, 3, 128], BF16)
        cstb = Buf("cst")
        epsb_t = sb("eps", [128, 1], F32)
        epsb = Buf("eps")
        pb = [es.enter_context(nc.psum_tensor(f"pb{i}", [128, 512], F32)) for i in range(8)]
        pbb = [Buf(f"pb{i}") for i in range(8)]
        ident = cst[:, 0, :]
        ones_bf = cst[:, 1, :]
        blockones = cst[:, 2, :]

        c_misc = kb.ctr("s_misc")
        c_x = kb.ctr("s_x")
        c_out = kb.ctr("s_out")

        kb.dma(sp, c_misc, gains[:, :, :], gains_d[:, :, :], writes=[gainsb])
        c_cst = kb.ctr("s_cst")
        kb.dma(pool, c_cst, cst[:, :, :], consts_d[:, :, :], writes=[cstb])
        kb.op(dve, lambda: nc.vector.memset(epsb_t[:, :], 1e-6), writes=[epsb])

        def load_x(bl):
            kb.wait_all(sp, [c_out])
            for c in range(4):
                for k in range(8):
                    kb.dma(sp, c_x, hT[:, k, c * 512:(c + 1) * 512], xT[bl, k, :, c * 512:(c + 1) * 512], writes=[hb[k][c]])
                for k in range(8):
                    hb[k][c].w = (c_x, c_x.cnt)
                kb.wait_all(sp, [c_x])

        def main_chunks(xn, xnb, at, atb):
            chs = []
            for c in range(4):
                chs.append(dict(
                    N=512,
                    h=(lambda k, c=c: hT[:, k, c * 512:(c + 1) * 512]),
                    hb=(lambda k, c=c: hb[k][c]),
                    xn=(lambda k, c=c: xn[:, k, c * 512:(c + 1) * 512]),
                    xnb=(lambda k, c=c: xnb[k][c]),
                    at=(lambda kk, c=c: at[:, kk, c * 512:(c + 1) * 512]),
                    atb=(lambda kk, c=c: atb[kk][c]),
                ))
            return chs

        def norm_chunk(ch, gi, scr, out_fn=None, outb_fn=None):
            N = ch["N"]
            sq, sqb, sd, sdb, rstd, rstdb = scr
            pR, pRb = pb[7], pbb[7]
            for k in range(8):
                kb.op(act, lambda: nc.scalar.activation(out=sq[k % 2][:, :N], in_=ch["h"](k), func=AF.Square),
                      reads=[ch["hb"](k)], writes=[sqb[k % 2]])
                kb.op(pe, lambda: nc.tensor.matmul(pR[:, :N], lhsT=ones_bf, rhs=sq[k % 2][:, :N],
                                                   start=(k == 0), stop=(k == 7)),
                      reads=[sqb[k % 2], cstb], writes=[pRb])
            kb.op(act, lambda: nc.scalar.activation(out=sd[:, :N], in_=pR[:, :N], func=AF.Sqrt,
                                                    scale=1.0 / D, bias=epsb_t[:, 0:1]),
                  reads=[pRb, epsb], writes=[sdb])
            kb.op(dve, lambda: nc.vector.reciprocal(out=rstd[:, :N], in_=sd[:, :N]), reads=[sdb], writes=[rstdb])
            for k in range(8):
                o = ch["xn"](k) if out_fn is None else out_fn(k)
                ob = ch["xnb"](k) if outb_fn is None else outb_fn(k)
                kb.op(dve, lambda: nc.vector.scalar_tensor_tensor(
                    out=o, in0=ch["h"](k), scalar=gains[:, gi, k:k + 1], in1=rstd[:, :N],
                    op0=ALU.mult, op1=ALU.mult),
                    reads=[ch["hb"](k), rstdb, gainsb], writes=[ob])

        def ffn_items(wi, wo):
            items = []
            for (a, b) in FF_SPLITS:
                for m in range(a, b):
                    items.append((wi[m, :, :], 2048))
                for mo in range(8):
                    items.append((wo[mo, :, a * 128:b * 128], (b - a) * 128))
            return items

        def ffn(chunks, gi, scr, sg, sgb):
            for ch in chunks:
                norm_chunk(ch, gi, scr)
            gu = 0
            oi = 0
            for (a, b) in FF_SPLITS:
                nk = b - a
                for m in range(a, b):
                    ws, wb = stream.get()
                    for ch in chunks:
                        N = ch["N"]
                        pG, pGb = pb[gu % 2], pbb[gu % 2]
                        pU, pUb = pb[2 + gu % 2], pbb[2 + gu % 2]
                        for k in range(8):
                            kb.op(pe, lambda: nc.tensor.matmul(pG[:, :N], lhsT=ws[:, k * 256:k * 256 + 128],
                                                               rhs=ch["xn"](k), start=(k == 0), stop=(k == 7)),
                                  reads=[wb, ch["xnb"](k)], writes=[pGb])
                        for k in range(8):
                            kb.op(pe, lambda: nc.tensor.matmul(pU[:, :N], lhsT=ws[:, k * 256 + 128:k * 256 + 256],
                                                               rhs=ch["xn"](k), start=(k == 0), stop=(k == 7)),
                                  reads=[wb, ch["xnb"](k)], writes=[pUb])
                        s_ = sg[gu % 2]
                        kb.op(act, lambda: nc.scalar.activation(out=s_[:, :N], in_=pG[:, :N], func=AF.Silu),
                              reads=[pGb], writes=[sgb[gu % 2]])
                        kb.op(dve, lambda: nc.vector.tensor_tensor(out=ch["at"](m - a), in0=s_[:, :N], in1=pU[:, :N],
                                                                   op=ALU.mult),
                              reads=[sgb[gu % 2], pUb], writes=[ch["atb"](m - a)])
                        gu += 1
                for mo in range(8):
                    ws, wb = stream.get()
                    for ch in chunks:
                        N = ch["N"]
                        pO, pOb = pb[4 + oi % 2], pbb[4 + oi % 2]
                        for kk in range(nk):
                            kb.op(pe, lambda: nc.tensor.matmul(pO[:, :N], lhsT=ws[:, kk * 128:(kk + 1) * 128],
                                                               rhs=ch["at"](kk), start=(kk == 0), stop=(kk == nk - 1)),
                                  reads=[wb, ch["atb"](kk)], writes=[pOb])
                        hk = ch["h"](mo)
                        kb.op(dve, lambda: nc.vector.scalar_tensor_tensor(out=hk, in0=pO[:, :N], scalar=0.5, in1=hk,
                                                                          op0=ALU.mult, op1=ALU.add),
                              reads=[pOb, ch["hb"](mo)], writes=[ch["hb"](mo)])
                        oi += 1

        def ffn_phase(bl, gi, with_meta):
            with ExitStack() as ph:
                def psb(name, shape, dt):
                    return ph.enter_context(nc.sbuf_tensor(uname(name), list(shape), dt))
                xn = psb("xn", [128, 8, T], BF16)
                xnb = [[Buf() for c in range(4)] for k in range(8)]
                at = psb("at", [128, 8, T], BF16)
                atb = [[Buf() for c in range(4)] for k in range(8)]
                sq = [psb(f"sq{i}", [128, 512], BF16) for i in range(2)]
                sqb = [Buf(), Buf()]
                sd = psb("sd", [128, 512], F32)
                rstd = psb("rstd", [128, 512], F32)
                sg = [psb(f"sg{i}", [128, 512], BF16) for i in range(2)]
                sgb = [Buf(), Buf()]
                scr = (sq, sqb, sd, Buf(), rstd, Buf())
                chunks = main_chunks(xn, xnb, at, atb)
                if with_meta:
                    xnm = psb("xnm", [128, 8, NM], BF16)
                    xnmb = [Buf() for k in range(8)]
                    atm = psb("atm", [128, 8, NM], BF16)
                    atmb = [Buf() for k in range(8)]
                    chunks.append(dict(N=NM, h=lambda k: hm[:, k, :], hb=lambda k: hmb[k],
                                       xn=lambda k: xnm[:, k, :], xnb=lambda k: xnmb[k],
                                       at=lambda kk: atm[:, kk, :], atb=lambda kk: atmb[kk]))
                ffn(chunks, gi, scr, sg, sgb)
                kb.barrier()

        def final_phase(bl):
            with ExitStack() as ph:
                def psb(name, shape, dt):
                    return ph.enter_context(nc.sbuf_tensor(uname(name), list(shape), dt))
                sq = [psb(f"fsq{i}", [128, 512], BF16) for i in range(2)]
                sd = psb("fsd", [128, 512], F32)
                rstd = psb("frstd", [128, 512], F32)
                scr = (sq, [Buf(), Buf()], sd, Buf(), rstd, Buf())
                chunks = main_chunks(None, None, None, None)
                for c, ch in enumerate(chunks):
                    norm_chunk(ch, 3, scr, out_fn=ch["h"], outb_fn=ch["hb"])
                    for k in range(8):
                        kb.dma(sp, c_out, outT[bl, k, :, c * 512:(c + 1) * 512], hT[:, k, c * 512:(c + 1) * 512],
                               reads=[hb[k][c]])
                kb.barrier()

        def dump(name, ap, shape, reads):
            t = nc.dram_tensor("dbg_" + name, list(shape), ap.dtype, kind="ExternalOutput").ap()
            dbg_outs[name] = t
            kb.dma(sp, c_out, t, ap, reads=reads)

        def psbf(ph):
            def f(name, shape, dt):
                return ph.enter_context(nc.sbuf_tensor(uname(name), list(shape), dt))
            return f

        def V(fn, reads, writes):
            kb.op(dve, fn, reads=reads, writes=writes)

        def A(fn, reads, writes):
            kb.op(act, fn, reads=reads, writes=writes)

        def setup_na():
            with ExitStack() as ph:
                psb = psbf(ph)
                c_s = kb.ctr("s_setna")
                raw = psb("eraw", [128, 4, 15, 64], F32)
                rawb = Buf()
                exn = psb("eexp", [128, 4, 15, 64], F32)
                exb = Buf()
                cm = psb("cmask", [128, 64], F32)
                cmb = Buf()
                kb.dma(sp, c_s, cm[:, :], colmask_d[:, :], writes=[cmb])
                for hp in range(4):
                    for h2 in range(2):
                        h = 2 * hp + h2
                        src = bass.AP(rpbpad.tensor, h * 15 * 128, [[1, 64], [128, 15], [1, 64]])
                        kb.dma(sp, c_s, raw[h2 * 64:(h2 + 1) * 64, hp, :, :], src, writes=[rawb])
                rawb.w = (c_s, c_s.cnt)
                cmb.w = (c_s, c_s.cnt)
                A(lambda: nc.scalar.activation(out=sap(exn, 0, 128, 0, [[1, 3840]]), in_=sap(raw, 0, 128, 0, [[1, 3840]]),
                                               func=AF.Exp), [rawb], [exb])
                V(lambda: nc.vector.tensor_tensor(out=sap(Etab, 0, 128, 0, [[64, 60], [1, 64]]),
                                                  in0=sap(exn, 0, 128, 63, [[64, 60], [-1, 64]]),
                                                  in1=sap(cm, 0, 128, 0, [[0, 60], [1, 64]]), op=ALU.mult),
                  [exb, cmb], [Etabb])
                kb.barrier()

        MAGIC = 12582912.0

        def cexp(xr, xi, ore, oim, t, F_, b):
            R = [b]
            A(lambda: nc.scalar.activation(out=xr, in_=xr, func=AF.Exp), R, R)
            V(lambda: nc.vector.tensor_scalar(out=xi, in0=xi, scalar1=1.0 / TWO_PI, scalar2=None, op0=ALU.mult), R, R)
            for which, o in ((0, oim), (1, ore)):
                if which == 1:
                    V(lambda: nc.vector.tensor_scalar(out=xi, in0=xi, scalar1=0.25, scalar2=None, op0=ALU.add), R, R)
                V(lambda: nc.vector.tensor_scalar(out=t, in0=xi, scalar1=MAGIC, scalar2=None, op0=ALU.add), R, R)
                V(lambda: nc.vector.tensor_scalar(out=t, in0=t, scalar1=MAGIC, scalar2=None, op0=ALU.subtract), R, R)
                V(lambda: nc.vector.tensor_tensor(out=t, in0=xi, in1=t, op=ALU.subtract), R, R)
                A(lambda: nc.scalar.activation(out=t, in_=t, func=AF.Sin, scale=TWO_PI), R, R)
                V(lambda: nc.vector.tensor_tensor(out=o, in0=xr, in1=t, op=ALU.mult), R, R)

        def cmul(ore, oim, ar_, ai_, br_, bi_, t1, t2, R, W, neg_im=False):
            V(lambda: nc.vector.tensor_tensor(out=t1, in0=ar_, in1=br_, op=ALU.mult), R, R)
            V(lambda: nc.vector.tensor_tensor(out=t2, in0=ai_, in1=bi_, op=ALU.mult), R, R)
            V(lambda: nc.vector.tensor_tensor(out=ore, in0=t1, in1=t2, op=ALU.subtract), R, W)
            V(lambda: nc.vector.tensor_tensor(out=t1, in0=ar_, in1=bi_, op=ALU.mult), R, R)
            V(lambda: nc.vector.tensor_tensor(out=t2, in0=ai_, in1=br_, op=ALU.mult), R, R)
            if neg_im:
                V(lambda: nc.vector.scalar_tensor_tensor(out=oim, in0=t1, scalar=-1.0, in1=t2, op0=ALU.mult,
                                                         op1=ALU.subtract), R, W)
            else:
                V(lambda: nc.vector.tensor_tensor(out=oim, in0=t1, in1=t2, op=ALU.add), R, W)

        def lam_and_f(are, aim, ldt, ardt, aidt, lre, lim, fre, fim, t1, t2, t3, F_, b):
            R = [b]
            A(lambda: nc.scalar.activation(out=ldt, in_=ldt, func=AF.Exp), R, R)
            V(lambda: nc.vector.tensor_scalar(out=are, in0=are, scalar1=-1e-4, scalar2=None, op0=ALU.min), R, R)
            V(lambda: nc.vector.tensor_tensor(out=ardt, in0=are, in1=ldt, op=ALU.mult), R, R)
            V(lambda: nc.vector.tensor_tensor(out=aidt, in0=aim, in1=ldt, op=ALU.mult), R, R)
            V(lambda: nc.vector.tensor_copy(out=t1, in_=ardt), R, R)
            V(lambda: nc.vector.tensor_copy(out=t2, in_=aidt), R, R)
            cexp(t1, t2, lre, lim, t3, F_, b)
            V(lambda: nc.vector.tensor_tensor(out=t1, in0=are, in1=are, op=ALU.mult), R, R)
            V(lambda: nc.vector.tensor_tensor(out=t2, in0=aim, in1=aim, op=ALU.mult), R, R)
            V(lambda: nc.vector.tensor_tensor(out=t3, in0=t1, in1=t2, op=ALU.add), R, R)
            V(lambda: nc.vector.reciprocal(out=t3, in_=t3), R, R)
            V(lambda: nc.vector.tensor_scalar(out=t1, in0=lre, scalar1=-1.0, scalar2=None, op0=ALU.add), R, R)
            V(lambda: nc.vector.tensor_tensor(out=fre, in0=t1, in1=are, op=ALU.mult), R, R)
            V(lambda: nc.vector.tensor_tensor(out=t2, in0=lim, in1=aim, op=ALU.mult), R, R)
            V(lambda: nc.vector.tensor_tensor(out=fre, in0=fre, in1=t2, op=ALU.add), R, R)
            V(lambda: nc.vector.tensor_tensor(out=fre, in0=fre, in1=t3, op=ALU.mult), R, R)
            V(lambda: nc.vector.tensor_tensor(out=fim, in0=lim, in1=are, op=ALU.mult), R, R)
            V(lambda: nc.vector.tensor_tensor(out=t2, in0=t1, in1=aim, op=ALU.mult), R, R)
            V(lambda: nc.vector.tensor_tensor(out=fim, in0=fim, in1=t2, op=ALU.subtract), R, R)
            V(lambda: nc.vector.tensor_tensor(out=fim, in0=fim, in1=t3, op=ALU.mult), R, R)

        def setup_s5():
            c_s = kb.ctr("s_sets5")
            with ExitStack() as ph:
                psb = psbf(ph)
                b = Buf()
                F_ = NG
                names = ["are", "aim", "ldt", "ardt", "aidt", "lre", "lim", "fre", "fim", "t1", "t2", "t3"]
                tl = {n: psb("C" + n, [128, F_], F32) for n in names}
                fl = {n: tl[n][:, :] for n in names}
                Cc = psb("CC", [128, 2, 512], F32)
                Bc = psb("CB", [128, 2, 512], F32)
                eps = psb("Ceps", [128, 5, 8], F32)
                Dt = psb("CD", [128, NG], F32)
                msk = psb("Cmsk", [128, 3, 128], F32)
                for wi_, n in enumerate(["are", "aim", "ldt"]):
                    kb.dma(sp, c_s, tl[n][:, :], s5c_d[:, wi_, :], writes=[b])
                kb.dma(sp, c_s, Cc[:, :, :], s5cc_d[:, :, :], writes=[b])
                kb.dma(sp, c_s, Bc[:, :, :], s5cb_d[:, :, :], writes=[b])
                kb.dma(sp, c_s, eps[:, :, :], s5ce_d[:, :, :], writes=[b])
                kb.dma(sp, c_s, Dt[:, :], s5d_d[:, :], writes=[b])
                kb.dma(sp, c_s, msk[:, :, :], s5m_d[:, :, :], writes=[b])
                b.w = (c_s, c_s.cnt)
                R = [b]
                lam_and_f(fl["are"], fl["aim"], fl["ldt"], fl["ardt"], fl["aidt"], fl["lre"], fl["lim"],
                          fl["fre"], fl["fim"], fl["t1"], fl["t2"], fl["t3"], F_, b)
                V(lambda: nc.vector.tensor_scalar(out=fl["t1"], in0=fl["ardt"], scalar1=16.0, scalar2=None, op0=ALU.mult), R, R)
                V(lambda: nc.vector.tensor_scalar(out=fl["t2"], in0=fl["aidt"], scalar1=16.0, scalar2=None, op0=ALU.mult), R, R)
                cexp(fl["t1"], fl["t2"], fl["lre"], fl["lim"], fl["t3"], F_, b)
                for c_ in range(2):
                    for par_ in range(2):
                        oA = sap(scA, 0, 128, c_ * NG * 2 + par_, [[2, NG]])
                        oB = sap(scB, 0, 128, c_ * NG * 2 + par_, [[2, NG]])
                        V(lambda: nc.vector.tensor_copy(out=oA, in_=fl["lre"]), R, [scb])
                        if c_ == 0:
                            V(lambda: nc.vector.tensor_scalar(out=oB, in0=fl["lim"], scalar1=-1.0, scalar2=None, op0=ALU.mult),
                              R, [scb])
                        else:
                            V(lambda: nc.vector.tensor_copy(out=oB, in_=fl["lim"]), R, [scb])
                pw = {}
                G8 = NG * 8
                xr = psb("Cxr", [128, G8], F32)
                xi = psb("Cxi", [128, G8], F32)
                tt = psb("Ctt", [128, G8], F32)
                for ei, nm in enumerate(["C", "G", "A", "B", "B2"]):
                    pr = psb("Cp%sr" % nm, [128, G8], F32)
                    pi_ = psb("Cp%si" % nm, [128, G8], F32)
                    e_b = sap(eps, 0, 128, ei * 8, [[0, NG], [1, 8]])
                    V(lambda: nc.vector.tensor_tensor(out=sap(xr, 0, 128, 0, [[8, NG], [1, 8]]),
                                                      in0=sap(tl["ardt"], 0, 128, 0, [[1, NG], [0, 8]]), in1=e_b,
                                                      op=ALU.mult), R, R)
                    V(lambda: nc.vector.tensor_tensor(out=sap(xi, 0, 128, 0, [[8, NG], [1, 8]]),
                                                      in0=sap(tl["aidt"], 0, 128, 0, [[1, NG], [0, 8]]), in1=e_b,
                                                      op=ALU.mult), R, R)
                    cexp(xr[:, :], xi[:, :], pr[:, :], pi_[:, :], tt[:, :], G8, b)
                    pw[nm] = (pr, pi_)
                fr_b = sap(tl["fre"], 0, 128, 0, [[1, NG], [0, 8]])
                fi_b = sap(tl["fim"], 0, 128, 0, [[1, NG], [0, 8]])

                def g8(t_):
                    return sap(t_, 0, 128, 0, [[8, NG], [1, 8]])
                far = psb("Cfar", [128, G8], F32)
                fai = psb("Cfai", [128, G8], F32)
                cmul(g8(far), g8(fai), fr_b, fi_b, g8(pw["A"][0]), g8(pw["A"][1]), g8(xr), g8(xi), R, R)
                BIG = NG * 8 * 16
                t1b = psb("Ct1b", [128, BIG], F32)
                t2b = psb("Ct2b", [128, BIG], F32)
                Are = psb("CAre", [128, NG, 128], BF16)
                Aim = psb("CAim", [128, NG, 128], BF16)
                Gre = psb("CGre", [128, NG, 128], BF16)
                Gim = psb("CGim", [128, NG, 128], BF16)

                def bigv(t_):
                    return sap(t_, 0, 128, 0, [[128, NG], [16, 8], [1, 16]])

                def pwv(t_):
                    return sap(t_, 0, 128, 0, [[8, NG], [1, 8], [0, 16]])

                def cpv(t_, c):
                    return sap(t_, 0, 128, c * 512, [[16, NG], [0, 8], [1, 16]])
                cmul(sap(WC, 0, 128, 0, [[256, NG], [16, 8], [1, 16]]), sap(WC, 0, 128, 128, [[256, NG], [16, 8], [1, 16]]),
                     cpv(Cc, 0), cpv(Cc, 1), pwv(pw["C"][0]), pwv(pw["C"][1]), bigv(t1b), bigv(t2b), R, [b, WCb], neg_im=True)
                cmul(bigv(Gre), bigv(Gim), cpv(Cc, 0), cpv(Cc, 1), pwv(pw["G"][0]), pwv(pw["G"][1]), bigv(t1b), bigv(t2b),
                     R, R, neg_im=True)
                cmul(bigv(Are), bigv(Aim), pwv(far), pwv(fai), cpv(Bc, 0), cpv(Bc, 1), bigv(t1b), bigv(t2b), R, R)
                fbr = psb("Cfbr", [128, G8], F32)
                fbi = psb("Cfbi", [128, G8], F32)
                cmul(g8(fbr), g8(fbi), fr_b, fi_b, g8(pw["B"][0]), g8(pw["B"][1]), g8(xr), g8(xi), R, R)
                PBc = [psb("CPBre", [128, NG, 128], BF16), psb("CPBim", [128, NG, 128], BF16)]
                cmul(bigv(PBc[0]), bigv(PBc[1]), pwv(fbr), pwv(fbi), cpv(Bc, 0), cpv(Bc, 1), bigv(t1b), bigv(t2b), R, R)
                for g2 in range(NG // 2):
                    pw_, pwb_ = pb[2 + g2 % 2], pbb[2 + g2 % 2]
                    for gg in range(2):
                        for c in range(2):
                            blk = gg * 2 + c
                            kb.op(pe, lambda: nc.tensor.matmul(pw_[:, blk * 128:(blk + 1) * 128], lhsT=PBc[c][:, 2 * g2 + gg, :],
                                                               rhs=ident, start=True, stop=True),
                                  reads=[b, cstb], writes=[pwb_])
                    V(lambda: nc.vector.tensor_copy(out=sap(WB, 0, 128, 2 * g2 * 256, [[1, 512]]), in_=pw_[:, :]),
                      [pwb_], [WBb])
                cmul(g8(fbr), g8(fbi), fr_b, fi_b, g8(pw["B2"][0]), g8(pw["B2"][1]), g8(xr), g8(xi), R, R)
                cmul(bigv(PBc[0]), bigv(PBc[1]), pwv(fbr), pwv(fbi), cpv(Bc, 0), cpv(Bc, 1), bigv(t1b), bigv(t2b), R, R)
                wb2t = psb("Cwb2t", [128, NG * 512], BF16)
                wb2tb = Buf()
                V(lambda: nc.vector.memset(wb2t[:, :], 0.0), [], [wb2tb])
                for g2 in range(NG // 2):
                    pw_, pwb_ = pb[2 + g2 % 2], pbb[2 + g2 % 2]
                    for gg in range(2):
                        for c in range(2):
                            blk = gg * 2 + c
                            kb.op(pe, lambda: nc.tensor.matmul(pw_[:, blk * 128:(blk + 1) * 128], lhsT=PBc[c][:, 2 * g2 + gg, :],
                                                               rhs=ident, start=True, stop=True),
                                  reads=[b, cstb], writes=[pwb_])
                    V(lambda: nc.vector.tensor_copy(out=sap(wb2t, 0, 128, 2 * g2 * 512, [[256, 4], [1, 64]]),
                                                    in_=sap(pw_, 0, 128, 0, [[128, 4], [1, 64]])), [pwb_], [wb2tb])
                    V(lambda: nc.vector.tensor_copy(out=sap(wb2t, 0, 128, 2 * g2 * 512 + 128 + 64, [[256, 4], [1, 64]]),
                                                    in_=sap(pw_, 0, 128, 64, [[128, 4], [1, 64]])), [pwb_], [wb2tb])
                kb.dma(sp, c_scr, wb2_scr[:, :], wb2t[:, :], reads=[wb2tb], writes=[wb2_b])
                kb.wait_all(sp, [c_scr])
                tmpf = psb("Ctmpf", [128, 512], F32)
                for g4 in range(NG // 4):
                    pf, pfb = pb[0], pbb[0]
                    pq, pqb = pb[1], pbb[1]
                    for gg in range(4):
                        g = g4 * 4 + gg
                        for (P_, Pb_, lo) in ((pf, pfb, 0), (pq, pqb, 64)):
                            kb.op(pe, lambda: nc.tensor.matmul(P_[:, gg * 128:(gg + 1) * 128], lhsT=Are[lo:lo + 64, g, :],
                                                               rhs=Gre[lo:lo + 64, g, :], start=True, stop=False),
                                  reads=[b], writes=[Pb_])
                            kb.op(pe, lambda: nc.tensor.matmul(P_[:, gg * 128:(gg + 1) * 128], lhsT=Aim[lo:lo + 64, g, :],
                                                               rhs=Gim[lo:lo + 64, g, :], start=False, stop=True),
                                  reads=[b], writes=[Pb_])
                    mL = sap(msk, 0, 128, 0, [[0, 4], [1, 128]])
                    mU = sap(msk, 0, 128, 128, [[0, 4], [1, 128]])
                    t4 = sap(tmpf, 0, 128, 0, [[128, 4], [1, 128]])
                    V(lambda: nc.vector.tensor_tensor(out=t4, in0=sap(pf, 0, 128, 0, [[128, 4], [1, 128]]), in1=mL,
                                                      op=ALU.mult), [pfb, b], [b])
                    V(lambda: nc.vector.tensor_tensor(out=sap(t1b, 0, 128, 0, [[128, 4], [1, 128]]),
                                                      in0=sap(pq, 0, 128, 0, [[128, 4], [1, 128]]), in1=mU, op=ALU.mult),
                      [pqb, b], [b])
                    V(lambda: nc.vector.tensor_tensor(out=t4, in0=t4, in1=sap(t1b, 0, 128, 0, [[128, 4], [1, 128]]),
                                                      op=ALU.add), R, R)
                    for gg in range(4):
                        g = g4 * 4 + gg
                        V(lambda: nc.vector.scalar_tensor_tensor(out=WT[:, g, :], in0=msk[:, 2, :], scalar=Dt[:, g:g + 1],
                                                                 in1=tmpf[:, gg * 128:(gg + 1) * 128], op0=ALU.mult,
                                                                 op1=ALU.add), R, [b, WTb])
                kb.barrier()
        def mixer_items():
            items = []
            for mt in range(12):
                items.append((wx[mt, :, :], 1024))
            for half in range(2):
                items.append((wv2[half, :, :], 2048))
            return items

        def krs_for(qg):
            out = []
            for kr in range(32):
                rows = [r for r in range(8 * qg, 8 * qg + 8)
                        if min(max(r - 4, 0), 24) <= kr <= min(max(r - 4, 0), 24) + 7]
                if rows:
                    out.append((kr, rows[0], rows[-1]))
            return out

        def inproj_phase(bl, qT, qTb, kT, kTb, vd, vdb):
            with ExitStack() as ph:
                psb = psbf(ph)
                xn2 = psb("xn2", [128, 8, T], BF16)
                xn2b = [[Buf() for c in range(4)] for k in range(8)]
                sq = [psb(f"sq{i}", [128, 512], BF16) for i in range(2)]
                scr = (sq, [Buf(), Buf()], pb[6], pbb[6], pb[6], pbb[6])
                chunks = main_chunks(xn2, xn2b, None, None)
                with_meta = (bl == 0)
                if with_meta:
                    xnm = psb("xn2m", [128, 8, NM], BF16)
                    xnmb = [Buf() for k in range(8)]
                    chunks.append(dict(N=NM, h=lambda k: hm[:, k, :], hb=lambda k: hmb[k],
                                       xn=lambda k: xnm[:, k, :], xnb=lambda k: xnmb[k]))
                for ch in chunks:
                    norm_chunk(ch, 1, scr)
                ust = [psb(f"ust{i}", [128, 8, 64], BF16) for i in range(2)]
                ustb = [Buf(), Buf()]
                ui = 0
                it = 0
                for mt in range(12):
                    ws, wb = stream.get()
                    for ci, ch in enumerate(chunks):
                        N = ch["N"]
                        is_meta = (ci == 4)
                        if is_meta and 4 <= mt < 8:
                            continue
                        pS, pSb = pb[it % 4], pbb[it % 4]
                        it += 1
                        for k in range(8):
                            kb.op(pe, lambda: nc.tensor.matmul(pS[:, :N], lhsT=ws[:, k * 128:(k + 1) * 128], rhs=ch["xn"](k),
                                                               start=(k == 0), stop=(k == 7)),
                                  reads=[wb, ch["xnb"](k)], writes=[pSb])
                        if mt < 4:
                            if is_meta:
                                o = sap(umT, 0, 128, mt * 16, [[2, 8], [1, 2]])
                                i_ = sap(pS, 0, 128, 0, [[1, 8], [8, 2]])
                                V(lambda: nc.vector.tensor_copy(out=o, in_=i_), [pSb], [umTb])
                            else:
                                us_, usb_ = ust[ui % 2], ustb[ui % 2]
                                ui += 1
                                i_ = sap(pS, 0, 128, 0, [[1, 8], [8, 64]])
                                V(lambda: nc.vector.tensor_copy(out=us_[:, :, :], in_=i_), [pSb], [usb_])
                                kb.dma(sp, c_scr, u_scr[mt * 128:(mt + 1) * 128, :, 2 + 64 * ci:2 + 64 * ci + 64], us_[:, :, :],
                                       reads=[usb_], writes=[u_scr_b])
                        elif mt < 8:
                            A(lambda: nc.scalar.copy(out=qT[:, mt - 4, ci * 512:(ci + 1) * 512], in_=pS[:, :N]),
                              [pSb], [qTb[mt - 4][ci]])
                        else:
                            if is_meta:
                                A(lambda: nc.scalar.copy(out=kmT[:, mt - 8, :], in_=pS[:, :N]), [pSb], [kmTb])
                            else:
                                A(lambda: nc.scalar.copy(out=kT[:, mt - 8, ci * 512:(ci + 1) * 512], in_=pS[:, :N]),
                                  [pSb], [kTb[mt - 8]])
                    if mt < 4:
                        kb.dma(sp, c_scr, u_scr[mt * 128:(mt + 1) * 128, :, 0:2], umT[:, mt, :, :], reads=[umTb],
                               writes=[u_scr_b])
                for j in range(8):
                    srcA = bass.AP(u_scr.tensor, j * NBLK, [[8 * NBLK, 16], [16 * 8 * NBLK, NG], [1, NBLK]])
                    dstA = bass.AP(u_scr2.tensor, j * 16 * NG * NBLK, [[NG * NBLK, 16], [NBLK, NG], [1, NBLK]])
                    kb.dma(sp, c_scr, dstA, srcA, reads=[u_scr_b], writes=[u2b[j]])
                for j in range(8):
                    u2b[j].w = (c_scr, c_scr.cnt)
                for half in range(2):
                    ws, wb = stream.get()
                    for w in range(-1, 32):
                        pS, pSb = pb[it % 4], pbb[it % 4]
                        it += 1
                        if w == -1:
                            t0, M, p0 = 0, 64, 64
                        elif w == 31:
                            t0, M, p0 = 64 * 31, 64, 0
                        else:
                            t0, M, p0 = 64 * w, 128, 0
                        rb_ = [xn2b[0][t0 // 512], xn2b[0][(t0 + M - 1) // 512]]
                        for k in range(8):
                            kb.op(pe, lambda: nc.tensor.matmul(pS[p0:p0 + M, 0:256], lhsT=xn2[:, k, t0:t0 + M],
                                                               rhs=ws[:, k * 256:(k + 1) * 256], start=(k == 0), stop=(k == 7)),
                                  reads=[wb, xn2b[k][t0 // 512], xn2b[k][(t0 + M - 1) // 512]], writes=[pSb])
                        if w >= 0:
                            V(lambda: nc.vector.tensor_copy(out=vd[0:64, w, 2 * half:2 * half + 2, :],
                                                            in_=sap(pS, 0, 64, 0, [[128, 2], [1, 64]])), [pSb], [vdb])
                        if w < 31:
                            A(lambda: nc.scalar.copy(out=vd[64:128, w + 1, 2 * half:2 * half + 2, :],
                                                     in_=sap(pS, 64, 64, 64, [[128, 2], [1, 64]])), [pSb], [vdb])
                    if with_meta:
                        pS, pSb = pb[it % 4], pbb[it % 4]
                        it += 1
                        for p0 in (0, 64):
                            for k in range(8):
                                kb.op(pe, lambda: nc.tensor.matmul(pS[p0:p0 + NM, 0:256], lhsT=xnm[:, k, :],
                                                                   rhs=ws[:, k * 256:(k + 1) * 256], start=(k == 0),
                                                                   stop=(k == 7)),
                                      reads=[wb, xnmb[k]], writes=[pSb])
                        V(lambda: nc.vector.tensor_copy(out=vm[0:NM, 2 * half:2 * half + 2, :],
                                                        in_=sap(pS, 0, NM, 0, [[128, 2], [1, 64]])), [pSb], [vmb])
                        V(lambda: nc.vector.tensor_copy(out=vm[64:64 + NM, 2 * half:2 * half + 2, :],
                                                        in_=sap(pS, 64, NM, 64, [[128, 2], [1, 64]])), [pSb], [vmb])
                kb.barrier()

        def na_phase(bl, qT, qTb, kT, kTb, vd, vdb, oT, oTb):
            LA = 3
            NSB = 4
            with ExitStack() as ph:
                psb = psbf(ph)
                exs = [psb(f"nex{i}", [128, 512], BF16) for i in range(NSB)]
                exb = [Buf() for i in range(NSB)]
                Ps = [psb(f"nP{i}", [128, 512], BF16) for i in range(NSB)]
                Pb = [Buf() for i in range(NSB)]
                pms = [psb(f"npm{i}", [128, 512], BF16) for i in range(2)]
                pmb = [Buf(), Buf()]
                rec = psb("nrec", [128, 512], F32)
                recb = Buf()
                for i in range(2):
                    V(lambda: nc.vector.memset(pms[i][:, :], 0.0), [], [pmb[i]])
                tiles_ = []
                gi_ = 0
                for hp in range(4):
                    for qg in range(4):
                        lst = krs_for(qg)
                        tiles_.append(dict(kind="meta", hp=hp, qg=qg, gi=gi_, first=True, last=False))
                        for idx, (kr, ra, rb2) in enumerate(lst):
                            tiles_.append(dict(kind="kr", hp=hp, qg=qg, gi=gi_, kr=kr, ra=ra, rb=rb2, first=False,
                                               last=(idx == len(lst) - 1)))
                        gi_ += 1

                def emit_S(t, tl_):
                    hp, qg = tl_["hp"], tl_["qg"]
                    pS, pSb = pb[t % NSB], pbb[t % NSB]
                    if tl_["kind"] == "meta":
                        tok0 = 512 * qg
                        for lo in (0, 64):
                            kb.op(pe, lambda: nc.tensor.matmul(pS[lo:lo + NM, 0:512], lhsT=kmT[lo:lo + 64, hp, :],
                                                               rhs=qT[lo:lo + 64, hp, tok0:tok0 + 512], start=True, stop=True),
                                  reads=[kmTb, qTb[hp][qg]], writes=[pSb])
                    else:
                        kr, ra, rb2 = tl_["kr"], tl_["ra"], tl_["rb"]
                        N = 64 * (rb2 - ra + 1)
                        for lo in (0, 64):
                            kb.op(pe, lambda: nc.tensor.matmul(pS[lo:lo + 64, 0:N], lhsT=kT[lo:lo + 64, hp, 64 * kr:64 * kr + 64],
                                                               rhs=qT[lo:lo + 64, hp, 64 * ra:64 * (rb2 + 1)],
                                                               start=True, stop=True),
                                  reads=[kTb[hp], qTb[hp][qg]], writes=[pSb])

                def emit_E(t, tl_):
                    hp, qg = tl_["hp"], tl_["qg"]
                    pS, pSb = pb[t % NSB], pbb[t % NSB]
                    if tl_["kind"] == "meta":
                        pm, pmb_ = pms[tl_["gi"] % 2], pmb[tl_["gi"] % 2]
                        for lo in (0, 64):
                            A(lambda: nc.scalar.activation(out=pm[lo:lo + NM, :], in_=pS[lo:lo + NM, 0:512], func=AF.Exp,
                                                           scale=0.125), [pSb], [pmb_])
                    else:
                        kr, ra, rb2 = tl_["kr"], tl_["ra"], tl_["rb"]
                        N = 64 * (rb2 - ra + 1)
                        ex_, exb_ = exs[t % NSB], exb[t % NSB]
                        P_, Pb_ = Ps[t % NSB], Pb[t % NSB]
                        A(lambda: nc.scalar.activation(out=ex_[:, 0:N], in_=pS[:, 0:N], func=AF.Exp, scale=0.125),
                          [pSb], [exb_])
                        rho = ra - kr + 7
                        V(lambda: nc.vector.tensor_tensor(out=P_[:, 0:N], in0=ex_[:, 0:N],
                                                          in1=sap(Etab, 0, 128, (hp * 15 + rho) * 64, [[1, N]]),
                                                          op=ALU.mult), [exb_, Etabb], [Pb_])

                def emit_PV(t, tl_):
                    hp, qg, g_ = tl_["hp"], tl_["qg"], tl_["gi"]
                    pO, pOb = pb[4 + g_ % 2], pbb[4 + g_ % 2]
                    pD, pDb = pb[6 + g_ % 2], pbb[6 + g_ % 2]
                    if tl_["kind"] == "meta":
                        pm, pmb_ = pms[g_ % 2], pmb[g_ % 2]
                        for lo in (0, 64):
                            kb.op(pe, lambda: nc.tensor.matmul(pO[lo:lo + 64, :], lhsT=vm[lo:lo + NM, hp, :],
                                                               rhs=pm[lo:lo + NM, :], start=True, stop=False),
                                  reads=[vmb, pmb_], writes=[pOb])
                        kb.op(pe, lambda: nc.tensor.matmul(pD[:, :], lhsT=blockones, rhs=pm[:, :], start=True, stop=False),
                              reads=[cstb, pmb_], writes=[pDb])
                    else:
                        kr, ra, rb2 = tl_["kr"], tl_["ra"], tl_["rb"]
                        N = 64 * (rb2 - ra + 1)
                        c0 = 64 * (ra - 8 * qg)
                        last = tl_["last"]
                        P_, Pb_ = Ps[t % NSB], Pb[t % NSB]
                        for lo in (0, 64):
                            kb.op(pe, lambda: nc.tensor.matmul(pO[lo:lo + 64, c0:c0 + N], lhsT=vd[lo:lo + 64, kr, hp, :],
                                                               rhs=P_[lo:lo + 64, 0:N], start=False, stop=last),
                                  reads=[vdb, Pb_], writes=[pOb])
                        kb.op(pe, lambda: nc.tensor.matmul(pD[:, c0:c0 + N], lhsT=blockones, rhs=P_[:, 0:N],
                                                           start=False, stop=last),
                              reads=[cstb, Pb_], writes=[pDb])
                        if last:
                            tok0 = 512 * qg
                            V(lambda: nc.vector.reciprocal(out=rec[:, :], in_=pD[:, :]), [pDb], [recb])
                            V(lambda: nc.vector.tensor_tensor(out=oT[:, hp, tok0:tok0 + 512], in0=pO[:, :], in1=rec[:, :],
                                                              op=ALU.mult), [pOb, recb], [oTb[hp][qg]])

                nt = len(tiles_)
                for t in range(min(LA, nt)):
                    emit_S(t, tiles_[t])
                for t in range(nt):
                    emit_E(t, tiles_[t])
                    if t + LA < nt:
                        emit_S(t + LA, tiles_[t + LA])
                    emit_PV(t, tiles_[t])
                kb.barrier()

        def s5_phase(bl, zT, zTb):
            with ExitStack() as ph:
                psb = psbf(ph)
                def ubg(g):
                    return sap(zT, 0, 128, g * NBLK, [[1, NBLK]])

                def ubrows(j):
                    return sap(zT, 16 * j, 16, 0, [[NBLK, NG], [1, NBLK]])
                ubb = [Buf() for g in range(NG)]
                Vs = psb("Vs", [128, 2, NG, NBLK], BF16)
                Vsb = Buf()
                Ssb = Buf()
                X = [psb(f"X{i}", [128, 4, NG], F32) for i in range(2)]
                Xb = [Buf(), Buf()]
                t1 = psb("st1", [128, 4, NG], F32)
                t2 = psb("st2", [128, 4, NG], F32)
                tb = Buf()
                kb.dma(sp, c_scr, sap(zT, 0, 128, 0, [[NBLK, NG], [1, NBLK]]), u_scr2[:, :, :], reads=u2b, writes=ubb)
                for g in range(NG):
                    ubb[g].w = (c_scr, c_scr.cnt)
                wb2 = psb("wb2", [128, 8 * 512], BF16)
                wb2b = Buf()

                def ubs(g, a, b_):
                    return sap(zT, 0, 128, g * NBLK + a, [[1, b_ - a]])
                for g in range(NG):
                    if g % 8 == 0:
                        kb.dma(sp, c_scr, wb2[:, :], wb2_scr[:, g * 512:(g + 8) * 512], reads=[wb2_b], writes=[wb2b])
                    pR_, pRb_ = pb[(2 * g) % 4], pbb[(2 * g) % 4]
                    pI_, pIb_ = pb[(2 * g + 1) % 4], pbb[(2 * g + 1) % 4]
                    for c, (P_, Pb_) in enumerate(((pR_, pRb_), (pI_, pIb_))):
                        w2o = (g % 8) * 512 + c * 256
                        kb.op(pe, lambda: nc.tensor.matmul(P_[:, 0:NBLK], lhsT=WB[:, g, c, :], rhs=ubg(g), start=True, stop=False),
                              reads=[WBb, ubb[g]], writes=[Pb_])
                        kb.op(pe, lambda: nc.tensor.matmul(P_[:, 1:NBLK], lhsT=wb2[:, w2o:w2o + 128], rhs=ubs(g, 0, NBLK - 1),
                                                           start=False, stop=False),
                              reads=[wb2b, ubb[g]], writes=[Pb_])
                        kb.op(pe, lambda: nc.tensor.matmul(P_[:, 1:NBLK - 1], lhsT=wb2[:, w2o + 128:w2o + 256], rhs=ubs(g, 2, NBLK),
                                                           start=False, stop=True),
                              reads=[wb2b, ubb[g]], writes=[Pb_])
                        V(lambda: nc.vector.tensor_copy(out=Vs[0:64, c, g, :], in_=P_[0:64, 0:NBLK]), [Pb_], [Vsb])
                        V(lambda: nc.vector.tensor_copy(out=Vs[64:128, c, g, :], in_=sap(P_, 64, 64, NBLK - 1, [[-1, NBLK]])),
                          [Pb_], [Vsb])
                V(lambda: nc.vector.memset(X[1][:, :, :], 0.0), [], [Xb[1]])
                F4 = 4 * NG
                for i in range(NBLK // 2):
                    Xp, Xpb = X[(i + 1) % 2], Xb[(i + 1) % 2]
                    Xn, Xnb = X[i % 2], Xb[i % 2]
                    sw = sap(Xp, 0, 128, 2 * NG, [[-2 * NG, 2], [1, 2 * NG]])
                    V(lambda: nc.vector.tensor_tensor(out=sap(t1, 0, 128, 0, [[1, F4]]), in0=sap(scA, 0, 128, 0, [[1, F4]]),
                                                      in1=sap(Xp, 0, 128, 0, [[1, F4]]), op=ALU.mult), [scb, Xpb], [tb])
                    V(lambda: nc.vector.tensor_tensor(out=sap(t2, 0, 128, 0, [[2 * NG, 2], [1, 2 * NG]]),
                                                      in0=sap(scB, 0, 128, 0, [[2 * NG, 2], [1, 2 * NG]]), in1=sw, op=ALU.mult),
                      [scb, Xpb], [tb])
                    V(lambda: nc.vector.tensor_tensor(out=sap(t1, 0, 128, 0, [[1, F4]]), in0=sap(t1, 0, 128, 0, [[1, F4]]),
                                                      in1=sap(t2, 0, 128, 0, [[1, F4]]), op=ALU.add), [tb], [tb])
                    vi = sap(Vs, 0, 128, 2 * i, [[NBLK, 2 * NG], [1, 2]])
                    x2 = sap(Xn, 0, 128, 0, [[2, 2 * NG], [1, 2]])
                    t1v = sap(t1, 0, 128, 0, [[2, 2 * NG], [1, 2]])
                    V(lambda: nc.vector.tensor_tensor(out=x2, in0=t1v, in1=vi, op=ALU.add), [tb, Vsb], [Xnb])
                    A(lambda: nc.scalar.copy(out=vi, in_=x2), [Xnb], [Ssb])
                al = [psb(f"al{i}", [128, 2, NBLK], BF16) for i in range(2)]
                alb = [Buf(), Buf()]
                for i in range(2):
                    V(lambda: nc.vector.memset(al[i][:, :, :], 0.0), [], [alb[i]])
                g1s = [psb(f"yg1{i}", [128, NBLK], F32) for i in range(3)]
                g2s = [psb(f"yg2{i}", [128, NBLK], F32) for i in range(3)]
                g1b = [Buf() for i in range(3)]
                g2b = [Buf() for i in range(3)]

                def y_A(g):
                    pY, pYb = pb[4 + g % 3], pbb[4 + g % 3]
                    a_, ab_ = al[g % 2], alb[g % 2]
                    V(lambda: nc.vector.tensor_copy(out=a_[0:64, :, 1:NBLK],
                                                    in_=sap(Vs, 0, 64, g * NBLK, [[NG * NBLK, 2], [1, NBLK - 1]])),
                      [Ssb], [ab_])
                    V(lambda: nc.vector.tensor_copy(out=a_[64:128, :, 0:NBLK - 1],
                                                    in_=sap(Vs, 64, 64, g * NBLK + NBLK - 2, [[NG * NBLK, 2], [-1, NBLK - 1]])),
                      [Ssb], [ab_])
                    kb.op(pe, lambda: nc.tensor.matmul(pY[:, 0:NBLK], lhsT=WT[:, g, :], rhs=ubg(g), start=True, stop=False),
                          reads=[WTb, ubb[g]], writes=[pYb])
                    for c in range(2):
                        kb.op(pe, lambda: nc.tensor.matmul(pY[:, 0:NBLK], lhsT=WC[:, g, c, :], rhs=a_[:, c, :],
                                                           start=False, stop=(c == 1)),
                              reads=[WCb, ab_], writes=[pYb])

                def y_B1(g):
                    pY, pYb = pb[4 + g % 3], pbb[4 + g % 3]
                    g1, g2 = g1s[g % 3], g2s[g % 3]
                    A(lambda: nc.scalar.activation(out=g1[:, :], in_=pY[:, 0:NBLK], func=AF.Square), [pYb], [g1b[g % 3]])
                    V(lambda: nc.vector.tensor_scalar(out=g1[:, :], in0=g1[:, :], scalar1=0.07135481627, scalar2=1.5957691216,
                                                      op0=ALU.mult, op1=ALU.add), [g1b[g % 3]], [g1b[g % 3]])
                    V(lambda: nc.vector.tensor_tensor(out=g2[:, :], in0=g1[:, :], in1=pY[:, 0:NBLK], op=ALU.mult),
                      [g1b[g % 3], pYb], [g2b[g % 3]])

                def y_B2(g):
                    pY, pYb = pb[4 + g % 3], pbb[4 + g % 3]
                    g2 = g2s[g % 3]
                    A(lambda: nc.scalar.activation(out=g2[:, :], in_=g2[:, :], func=AF.Sigmoid), [g2b[g % 3]], [g2b[g % 3]])
                    V(lambda: nc.vector.tensor_tensor(out=ubg(g), in0=g2[:, :], in1=pY[:, 0:NBLK], op=ALU.mult),
                      [g2b[g % 3], pYb], [ubb[g]])
                for t_ in range(NG + 2):
                    if t_ < NG:
                        y_A(t_)
                    if 1 <= t_ <= NG:
                        y_B1(t_ - 1)
                    if t_ >= 2:
                        y_B2(t_ - 2)
                z2b = Buf()
                kb.dma(sp, c_scr, z_scr2[:, :, :], sap(zT, 0, 128, 0, [[NBLK, NG], [1, NBLK]]), reads=ubb + zsb, writes=[z2b])
                for j in range(8):
                    srcZ = bass.AP(z_scr2.tensor, j * 16 * NG * NBLK, [[NG * NBLK, 16], [NBLK, NG], [1, NBLK]])
                    dstZ = bass.AP(z_scr.tensor, j * NBLK, [[8 * NBLK, 16], [16 * 8 * NBLK, NG], [1, NBLK]])
                    kb.dma(sp, c_scr, dstZ, srcZ, reads=[z2b], writes=[zsb[j]])
                for j in range(8):
                    zsb[j].w = (c_scr, c_scr.cnt)
                for mt in range(4):
                    kb.dma(sp, c_scr, zT[:, mt, :, :], z_scr[mt * 128:(mt + 1) * 128, :, :], reads=zsb, writes=zTb + ubb)
                for b_ in zTb:
                    b_.w = (c_scr, c_scr.cnt)
                kb.barrier()

        def merge_items():
            items = []
            for half in range(2):
                for mo in range(8):
                    items.append((wx[16 + mo, :, :], 1024))
                    items.append((wbs_d[mo, :, :], 512))
                    items.append((wx[24 + mo, :, :], 1024))
                    items.append((wbn_d[mo, :, :], 512))
                for mo in range(8):
                    items.append((wo_d[mo, :, :], 1024))
            return items

        c_wgl = kb.ctr("s_wgl")

        def merge_phase(bl, zT, zTb, oT, oTb):
            with ExitStack() as ph:
                psb = psbf(ph)
                wgl = psb("wgl", [128, 4, 512], BF16)
                wglb = Buf()
                for mt in range(4):
                    kb.dma(pool, c_wgl, wgl[:, mt, :], wglu_d[mt, :, :], writes=[wglb])
                wglb.w = (c_wgl, c_wgl.cnt)
                xn2 = psb("mxn2", [128, 8, 1024], BF16)
                mg = psb("mg", [128, 8, 1024], BF16)
                sq = [psb(f"msq{i}", [128, 512], BF16) for i in range(2)]
                sd = psb("msd", [128, 512], F32)
                rstd = psb("mrstd", [128, 512], F32)
                scr = (sq, [Buf(), Buf()], sd, Buf(), rstd, Buf())
                s1 = [psb(f"ms1{i}", [128, 512], BF16) for i in range(2)]
                s2 = [psb(f"ms2{i}", [128, 512], BF16) for i in range(2)]
                m1 = [psb(f"mm1{i}", [128, 512], BF16) for i in range(2)]
                sb1, sb2, mb1 = [Buf(), Buf()], [Buf(), Buf()], [Buf(), Buf()]

                def zperm(t_, mt, c):
                    return sap(t_, 0, 128, mt * 8 * NBLK + 2 + 64 * c, [[NBLK, 8], [1, 64]])
                for c in range(4):
                    for mt in range(4):
                        pS, pSb = pb[(c % 2) * 4 + mt], pbb[(c % 2) * 4 + mt]
                        for k in range(4):
                            kb.op(pe, lambda: nc.tensor.matmul(pS[:, :], lhsT=wgl[:, mt, k * 128:(k + 1) * 128], rhs=zperm(zT, k, c),
                                                               start=(k == 0), stop=(k == 3)),
                                  reads=[wglb, zTb[c]], writes=[pSb])
                    for mt in range(4):
                        pS, pSb = pb[(c % 2) * 4 + mt], pbb[(c % 2) * 4 + mt]
                        A(lambda: nc.scalar.activation(out=s1[mt % 2][:, :], in_=pS[:, :], func=AF.Sigmoid), [pSb], [sb1[mt % 2]])
                        V(lambda: nc.vector.tensor_tensor(out=zperm(zT, mt, c), in0=zperm(zT, mt, c),
                                                          in1=sap(s1[mt % 2], 0, 128, 0, [[64, 8], [1, 64]]), op=ALU.mult),
                          [sb1[mt % 2], zTb[c]], [zTb[c]])
                for half in range(2):
                    xnb = [[Buf() for c in range(2)] for k in range(8)]
                    mgb = [[Buf() for c in range(2)] for k in range(8)]
                    chs = []
                    for c2 in range(2):
                        c = 2 * half + c2
                        chs.append(dict(N=512, h=(lambda k, c=c: hT[:, k, c * 512:(c + 1) * 512]), hb=(lambda k, c=c: hb[k][c]),
                                        xn=(lambda k, c2=c2: xn2[:, k, c2 * 512:(c2 + 1) * 512]),
                                        xnb=(lambda k, c2=c2: xnb[k][c2])))
                    for ch in chs:
                        norm_chunk(ch, 1, scr)
                    for mo in range(8):
                        wgs, wgsb = stream.get()
                        for c2, ch in enumerate(chs):
                            p1, p1b = pb[c2], pbb[c2]
                            for k in range(8):
                                kb.op(pe, lambda: nc.tensor.matmul(p1[:, :], lhsT=wgs[:, k * 128:(k + 1) * 128], rhs=ch["xn"](k),
                                                                   start=(k == 0), stop=(k == 7)),
                                      reads=[wgsb, ch["xnb"](k)], writes=[p1b])
                            A(lambda: nc.scalar.activation(out=s1[c2][:, :], in_=p1[:, :], func=AF.Sigmoid), [p1b], [sb1[c2]])
                        wbs_, wbsb = stream.get()
                        for c2, ch in enumerate(chs):
                            c = 2 * half + c2
                            p3, p3b = pb[2 + c2], pbb[2 + c2]
                            for k in range(4):
                                kb.op(pe, lambda: nc.tensor.matmul(p3[:, :], lhsT=wbs_[:, k * 128:(k + 1) * 128], rhs=zperm(zT, k, c),
                                                                   start=(k == 0), stop=(k == 3)),
                                      reads=[wbsb, zTb[c]], writes=[p3b])
                            V(lambda: nc.vector.tensor_tensor(out=sap(m1[c2], 0, 128, 0, [[8, 64], [1, 8]]),
                                                              in0=sap(s1[c2], 0, 128, 0, [[8, 64], [1, 8]]),
                                                              in1=sap(p3, 0, 128, 0, [[1, 64], [64, 8]]), op=ALU.mult),
                              [sb1[c2], p3b], [mb1[c2]])
                        wgn, wgnb = stream.get()
                        for c2, ch in enumerate(chs):
                            p2, p2b = pb[4 + c2], pbb[4 + c2]
                            for k in range(8):
                                kb.op(pe, lambda: nc.tensor.matmul(p2[:, :], lhsT=wgn[:, k * 128:(k + 1) * 128], rhs=ch["xn"](k),
                                                                   start=(k == 0), stop=(k == 7)),
                                      reads=[wgnb, ch["xnb"](k)], writes=[p2b])
                            A(lambda: nc.scalar.activation(out=s2[c2][:, :], in_=p2[:, :], func=AF.Sigmoid), [p2b], [sb2[c2]])
                        wbn_, wbnb = stream.get()
                        for c2, ch in enumerate(chs):
                            c = 2 * half + c2
                            p4, p4b = pb[6 + c2], pbb[6 + c2]
                            for k in range(4):
                                kb.op(pe, lambda: nc.tensor.matmul(p4[:, :], lhsT=wbn_[:, k * 128:(k + 1) * 128],
                                                                   rhs=oT[:, k, c * 512:(c + 1) * 512], start=(k == 0), stop=(k == 3)),
                                      reads=[wbnb, oTb[k][c]], writes=[p4b])
                            V(lambda: nc.vector.tensor_tensor(out=s2[c2][:, :], in0=s2[c2][:, :], in1=p4[:, :], op=ALU.mult),
                              [sb2[c2], p4b], [sb2[c2]])
                            V(lambda: nc.vector.tensor_tensor(out=mg[:, mo, c2 * 512:(c2 + 1) * 512], in0=m1[c2][:, :],
                                                              in1=s2[c2][:, :], op=ALU.add), [mb1[c2], sb2[c2]], [mgb[mo][c2]])
                    for mo in range(8):
                        ws, wb = stream.get()
                        for c2 in range(2):
                            c = 2 * half + c2
                            pO, pOb = pb[(2 * mo + c2) % 8], pbb[(2 * mo + c2) % 8]
                            for k in range(8):
                                kb.op(pe, lambda: nc.tensor.matmul(pO[:, :], lhsT=ws[:, k * 128:(k + 1) * 128],
                                                                   rhs=mg[:, k, c2 * 512:(c2 + 1) * 512], start=(k == 0), stop=(k == 7)),
                                      reads=[wb, mgb[k][c2]], writes=[pOb])
                            hk = hT[:, mo, c * 512:(c + 1) * 512]
                            V(lambda: nc.vector.tensor_tensor(out=hk, in0=pO[:, :], in1=hk, op=ALU.add),
                              [pOb, hb[mo][c]], [hb[mo][c]])
                kb.barrier()

        def mixer(bl):
            with ExitStack() as mx:
                psb = psbf(mx)
                qT = psb("qoT", [128, 4, T], BF16)
                qTb = [[Buf() for c in range(4)] for i in range(4)]
                qTall = [b_ for r_ in qTb for b_ in r_]
                with ExitStack() as m2:
                    psb2 = psbf(m2)
                    kT = psb2("kT", [128, 4, T], BF16)
                    vd = psb2("vd", [128, 32, 4, 64], BF16)
                    kTb = [Buf() for i in range(4)]
                    vdb = Buf()
                    inproj_phase(bl, qT, qTb, kT, kTb, vd, vdb)
                    if dbg and bl == 0:
                        dump("qT", qT[:, :, :], [128, 4, T], qTall)
                        dump("kT", kT[:, :, :], [128, 4, T], kTb)
                        dump("vd", vd[:, :, :, :], [128, 32, 4, 64], [vdb])
                        kb.barrier()
                        kb.wait_all(sp, [c_out])
                        kb.wait_all(pe, [c_out])
                    if stage >= 7:
                        stream.extend(merge_items())
                    if stage >= 5:
                        na_phase(bl, qT, qTb, kT, kTb, vd, vdb, qT, qTb)
                        if dbg and bl == 0:
                            dump("oT", qT[:, :, :], [128, 4, T], qTall)
                            for E_ in (pe, act, dve, pool):
                                kb.wait_all(E_, [c_out])
                zT = psb("zT", [128, 4, 8, NBLK], BF16)
                zTb = [Buf() for c in range(4)]
                if stage >= 6:
                    s5_phase(bl, zT, zTb)
                    if dbg and bl == 0:
                        dump("zT", zT[:, :, :, :], [128, 4, 8, NBLK], zTb)
                        for E_ in (pe, act, dve, pool):
                            kb.wait_all(E_, [c_out])
                if stage >= 7:
                    merge_phase(bl, zT, zTb, qT, qTb)

        allhb = [hb[k][c] for k in range(8) for c in range(4)]
        if stage >= 2:
            setup_na()
        if stage >= 3:
            setup_s5()
        hT = sb("hT", [128, 8, T], F32)
        hm = sb("hm", [128, 8, NM], F32)
        stream = Stream(kb, nc, es, NSLOT, 2048)
        c_hm = kb.ctr("s_hm")
        kb.dma(sp, c_hm, hm[:, :, :], metaT[:, :, :], writes=hmb)
        for bl in range(nb_local):
            load_x(bl)
            stream.extend(ffn_items(w1i, w1o))
            if stage >= 4:
                stream.extend(mixer_items())
            ffn_phase(bl, 0, with_meta=(bl == 0))
            if dbg and bl == 0:
                dump("h1", hT[:, :, :], [128, 8, T], allhb)
                dump("hm1", hm[:, :, :], [128, 8, NM], hmb)
            if stage >= 4:
                mixer(bl)
                if dbg and bl == 0:
                    dump("h2", hT[:, :, :], [128, 8, T], allhb)
            stream.extend(ffn_items(w2i, w2o))
            ffn_phase(bl, 2, with_meta=False)
            final_phase(bl)
        kb.wait_all(sp, [c_out])
    return nc, dbg_outs


def host_inputs(inputs, core, nb_local=NB_LOCAL):
    f = np.float32
    x = inputs["x"]
    b0 = core * nb_local
    xT = np.ascontiguousarray(
        x[b0:b0 + nb_local].transpose(0, 2, 1).reshape(nb_local, 8, 128, T)).astype(f)
    metaT = np.ascontiguousarray(inputs["meta_tokens"].T.reshape(8, 128, NM).transpose(1, 0, 2)).astype(f)
    g = np.stack([inputs["norm_ffn1"][0], inputs["norm_mix"][0], inputs["norm_ffn2"][0], inputs["norm_final"]])
    gains = np.ascontiguousarray(g.reshape(4, 8, 128).transpose(2, 0, 1)).astype(f)

    def ffn_in(w):
        w = w.reshape(8, 128, 2, NFF, 128)
        return np.ascontiguousarray(w.transpose(3, 1, 0, 2, 4).reshape(NFF, 128, 8 * 256)).astype(f)

    def ffn_out(w):
        w = w.reshape(NFF, 128, 8, 128)
        return np.ascontiguousarray(w.transpose(2, 1, 0, 3).reshape(8, 128, NFF * 128)).astype(f)

    consts = np.zeros((128, 3, 128), f)
    consts[:, 0, :] = np.eye(128)
    consts[:, 1, :] = 1.0
    consts[:64, 2, :64] = 1.0
    consts[64:, 2, 64:] = 1.0
    w_in = inputs["w_in"][0]

    def tiles(w, nk, nm):
        return np.ascontiguousarray(w.reshape(nk, 128, nm, 128).transpose(2, 1, 0, 3).reshape(nm, 128, nk * 128)).astype(f)
    wx_ = tiles(w_in, 8, 32)
    wv = w_in[:, 1536:2048].reshape(8, 128, 2, 256)
    wv2 = np.ascontiguousarray(wv.transpose(2, 1, 0, 3).reshape(2, 128, 2048)).astype(f)
    rpb = inputs["na_rpb"][0]
    rpbpad = np.zeros((8, 15, 128), f)
    rpbpad[:, :, 48:48 + 31] = rpb[:, ::-1, :]
    qc = np.arange(64)
    cs = np.clip(qc - 8, 0, 48)
    kc = np.arange(64)
    cm = ((kc[:, None] >= cs[None, :]) & (kc[:, None] < cs[None, :] + 16)).astype(f)
    colmask = np.concatenate([cm, cm], axis=0)
    are = np.stack([inputs["ssm_a_re_fwd"][0], inputs["ssm_a_re_bwd"][0]])
    aim = np.stack([inputs["ssm_a_im_fwd"][0], inputs["ssm_a_im_bwd"][0]])
    ldt = np.stack([inputs["ssm_log_dt_fwd"][0], inputs["ssm_log_dt_bwd"][0]])
    ldt_gn = np.broadcast_to(ldt[:, :, None], (2, 32, 64))
    Bre, Bim = inputs["ssm_b_re"][0], inputs["ssm_b_im"][0]
    Cre, Cim = inputs["ssm_c_re"][0], inputs["ssm_c_im"][0]

    def layC2(a):
        return a.transpose(0, 2, 1).reshape(128, 32)
    s5c = np.ascontiguousarray(np.stack([layC2(are), layC2(aim), layC2(ldt_gn)], axis=1)).astype(f)

    def layCC(Cm):
        t = Cm.transpose(2, 0, 1).reshape(64, 512)
        return np.concatenate([t, t], axis=0)

    def layCB(Bm):
        t = Bm.transpose(1, 0, 2).reshape(64, 512)
        return np.concatenate([t, t], axis=0)
    s5cc = np.ascontiguousarray(np.stack([layCC(Cre), layCC(Cim)], axis=1)).astype(f)
    s5cb = np.ascontiguousarray(np.stack([layCB(Bre), layCB(Bim)], axis=1)).astype(f)
    j8 = np.arange(8)
    s5ce = np.zeros((128, 5, 8), f)
    s5ce[:64, 3] = 7 - j8
    s5ce[64:, 3] = j8
    s5ce[:64, 4] = 15 - j8
    s5ce[64:, 4] = j8 + 8
    s5ce[:64, 0] = j8 + 1
    s5ce[:64, 1] = j8
    s5ce[:64, 2] = -j8
    s5ce[64:, 0] = 8 - j8
    s5ce[64:, 1] = -j8
    s5ce[64:, 2] = j8
    Dm = inputs["ssm_d"][0]
    s5d = np.ascontiguousarray(np.broadcast_to(Dm.T[None], (8, 16, 32)).reshape(128, 32)).astype(f)
    ii = np.repeat(np.arange(8), 16)
    s5m = np.stack([(ii[:, None] <= ii[None, :]), (ii[:, None] >= ii[None, :]), np.eye(128, dtype=bool)], axis=1).astype(f)
    extra = {
        "wx": wx_, "wv2": wv2, "wglu": tiles(inputs["w_glu"][0], 4, 4), "wbs": tiles(inputs["w_branch_ssm"][0], 4, 8),
        "wbn": tiles(inputs["w_branch_na"][0], 4, 8), "wo": tiles(inputs["w_out"][0], 8, 8),
        "rpbpad": rpbpad, "colmask": colmask, "s5c": s5c, "s5cc": s5cc,
        "s5cb": s5cb, "s5ce": s5ce, "s5d": s5d, "s5m": np.ascontiguousarray(s5m),
    }
    return {
        **extra,
        "xT": xT, "metaT": metaT, "gains": gains,
        "w1i": ffn_in(inputs["w_ffn1_in"][0]), "w1o": ffn_out(inputs["w_ffn1_out"][0]),
        "w2i": ffn_in(inputs["w_ffn2_in"][0]), "w2o": ffn_out(inputs["w_ffn2_out"][0]),
        "consts": consts,
    }


def kernel(**inputs):
    inputs = {k: np.asarray(v) for k, v in inputs.items()}
    nc, _ = build_program()
    in_maps = [host_inputs(inputs, c) for c in range(NCORES)]
    res = run_bass_kernel_spmd(nc, in_maps, core_ids=list(range(NCORES)))
    out = np.empty((16, T, D), np.float32)
    for c in range(NCORES):
        o = np.asarray(res.results[c]["outT"])
        out[c * NB_LOCAL:(c + 1) * NB_LOCAL] = o.reshape(NB_LOCAL, D, T).transpose(0, 2, 1)
    return out
```

```python
import numpy as np
from contextlib import ExitStack
import concourse.bass as bass
import concourse.mybir as mybir
from concourse.bass_utils import run_bass_kernel_spmd

F32 = mybir.dt.float32
BF16 = mybir.dt.bfloat16
ALU = mybir.AluOpType
AF = mybir.ActivationFunctionType

D = 1024
T = 2048
NM = 16
DFF = 2816
NFF = 22
NG = 32
NBLK = 258
NB_LOCAL = 2
NCORES = 8
FF_SPLITS = [(0, 8), (8, 15), (15, 22)]
NSLOT = 2
TWO_PI = 6.283185307179586


class Ctr:
    def __init__(self, sem):
        self.sem = sem
        self.cnt = 0


class Buf:
    __slots__ = ("w", "r", "name")

    def __init__(self, name=""):
        self.w = None
        self.r = {}
        self.name = name


class Eng:
    def __init__(self, h, ctr, name):
        self.h = h
        self.ctr = ctr
        self.seen = {}
        self.name = name


class KB:
    def __init__(self, nc, es):
        self.nc = nc
        self.es = es

        def mk(h, n):
            return Eng(h, Ctr(es.enter_context(nc.semaphore(n))), n)

        self.pe = mk(nc.tensor, "s_pe")
        self.act = mk(nc.scalar, "s_act")
        self.dve = mk(nc.vector, "s_dve")
        self.pool = mk(nc.gpsimd, "s_pool")
        self.sp = mk(nc.sync, "s_sp")
        self.engs = [self.pe, self.act, self.dve, self.pool, self.sp]

    def ctr(self, name):
        return Ctr(self.es.enter_context(self.nc.semaphore(name)))

    def _need(self, E, dep, raw):
        ctr, val = dep
        if ctr is E.ctr and not raw:
            return
        if E.seen.get(id(ctr), 0) >= val:
            return
        E.h.wait_ge(ctr.sem, val)
        E.seen[id(ctr)] = val

    def _deps(self, E, reads, writes):
        for b in reads:
            if b.w is not None:
                self._need(E, b.w, True)
        for b in writes:
            if b.w is not None:
                self._need(E, b.w, False)
            for dep in b.r.values():
                self._need(E, dep, False)

    def op(self, E, fn, reads=(), writes=()):
        self._deps(E, reads, writes)
        ins = fn()
        E.ctr.cnt += 1
        ins.then_inc(E.ctr.sem, 1)
        for b in writes:
            b.w = (E.ctr, E.ctr.cnt)
            b.r = {}
        for b in reads:
            b.r[id(E.ctr)] = (E.ctr, E.ctr.cnt)

    def dma(self, Q, ctr, out, in_, reads=(), writes=()):
        self._deps(Q, reads, writes)
        Q.h.dma_start(out=out, in_=in_).then_inc(ctr.sem, 16)
        ctr.cnt += 16
        for b in writes:
            b.w = (ctr, ctr.cnt)
            b.r = {}
        for b in reads:
            b.r[id(ctr)] = (ctr, ctr.cnt)

    def barrier(self):
        for E in self.engs:
            for O in self.engs:
                if O is not E and O.ctr.cnt > 0:
                    self._need(E, (O.ctr, O.ctr.cnt), True)

    def wait_all(self, E, ctrs):
        for c in ctrs:
            if c.cnt > 0:
                self._need(E, (c, c.cnt), True)


class Stream:
    def __init__(self, kb, nc, es, nslot, cols):
        self.kb = kb
        self.items = []
        self.next_load = 0
        self.next_get = 0
        self.nslot = nslot
        self.slots = [es.enter_context(nc.sbuf_tensor(f"ring{i}", [128, cols], BF16)) for i in range(nslot)]
        self.bufs = [Buf(f"ring{i}") for i in range(nslot)]
        self.ctrs = [kb.ctr(f"s_ring{i}") for i in range(nslot)]

    def extend(self, items):
        self.items += items

    def _load(self, n):
        s = n % self.nslot
        src, ncols = self.items[n]
        self.kb.dma(self.kb.pool, self.ctrs[s], self.slots[s][:, 0:ncols], src, writes=[self.bufs[s]])

    def get(self):
        i = self.next_get
        self.next_get += 1
        while self.next_load < min(len(self.items), i + self.nslot):
            self._load(self.next_load)
            self.next_load += 1
        s = i % self.nslot
        return self.slots[s], self.bufs[s]


def sap(t, part0, nparts, off, dims):
    row = 1
    for d in t.shape[1:]:
        row *= d
    return bass.AP(t, part0 * row + off, [[row, nparts]] + [list(d) for d in dims])


def build_program(nb_local=NB_LOCAL, stage=99, dbg=False):
    nc = bass.Bass("TRN2", target_bir_lowering=False)

    def din(name, shape, dt=F32):
        return nc.dram_tensor(name, list(shape), dt, kind="ExternalInput").ap()

    xT = din("xT", [nb_local, 8, 128, T])
    metaT = din("metaT", [128, 8, NM])
    gains_d = din("gains", [128, 4, 8])
    w1i = din("w1i", [NFF, 128, 8 * 256])
    w1o = din("w1o", [8, 128, NFF * 128])
    w2i = din("w2i", [NFF, 128, 8 * 256])
    w2o = din("w2o", [8, 128, NFF * 128])
    consts_d = din("consts", [128, 3, 128])
    wx = din("wx", [32, 128, 1024])
    wv2 = din("wv2", [2, 128, 2048])
    wglu_d = din("wglu", [4, 128, 512])
    wbs_d = din("wbs", [8, 128, 512])
    wbn_d = din("wbn", [8, 128, 512])
    wo_d = din("wo", [4, 128, 2048])
    wgg_d = din("wgg", [8, 128, 2048])
    wbb_d = din("wbb", [8, 128, 1024])
    rpbpad = din("rpbpad", [8, 15, 128])
    colmask_d = din("colmask", [128, 64])
    s5c_d = din("s5c", [128, 3, 32])
    s5cc_d = din("s5cc", [128, 2, 512])
    s5cb_d = din("s5cb", [128, 2, 512])
    s5ce_d = din("s5ce", [128, 5, 8])
    s5d_d = din("s5d", [128, 32])
    s5m_d = din("s5m", [128, 3, 128])
    u_scr = nc.dram_tensor("u_scr", [512, 8, NBLK], BF16, kind="Internal").ap()
    z_scr = nc.dram_tensor("z_scr", [512, 8, NBLK], BF16, kind="Internal").ap()
    wb2_scr = nc.dram_tensor("wb2_scr", [128, NG * 512], BF16, kind="Internal").ap()
    u_scr2 = nc.dram_tensor("u_scr2", [128, NG, NBLK], BF16, kind="Internal").ap()
    z_scr2 = nc.dram_tensor("z_scr2", [128, NG, NBLK], BF16, kind="Internal").ap()
    outT = nc.dram_tensor("outT", [nb_local, 8, 128, T], F32, kind="ExternalOutput").ap()
    dbg_outs = {}

    with ExitStack() as es:
        kb = KB(nc, es)
        pe, act, dve, pool, sp = kb.pe, kb.act, kb.dve, kb.pool, kb.sp

        def sb(name, shape, dt):
            return es.enter_context(nc.sbuf_tensor(name, list(shape), dt))

        uid = [0]

        def uname(n):
            uid[0] += 1
            return f"{n}_{uid[0]}"


        Etab = sb("Etab", [128, 4, 15, 64], BF16)
        Etabb = Buf()
        kmT = sb("kmT", [128, 4, NM], BF16)
        kmTb = Buf()
        vm = sb("vm", [128, 4, 64], BF16)
        vmb = Buf()
        umT = sb("umT", [128, 4, 8, 2], BF16)
        umTb = Buf()
        WB = sb("WB", [128, NG, 2, 128], BF16)
        WBb = Buf()
        WC = sb("WC", [128, NG, 2, 128], BF16)
        WCb = Buf()
        WT = sb("WT", [128, NG, 128], BF16)
        WTb = Buf()
        scA = sb("scA", [128, 4, NG], F32)
        scB = sb("scB", [128, 4, NG], F32)
        wb2_b = Buf()
        scb = Buf()
        u_scr_b = Buf()
        z_scr_b = Buf()
        u2b = [Buf() for j in range(8)]
        zsb = [Buf() for j in range(8)]
        c_scr = kb.ctr("s_scr")

        hb = [[Buf(f"h{k}_{c}") for c in range(4)] for k in range(8)]
        hmb = [Buf(f"hm{k}") for k in range(8)]
        gains = sb("gains_sb", [128, 4, 8], F32)
        gainsb = Buf("gains")
        cst = sb("cst", [128, 3, 128], BF16)
        cstb = Buf("cst")
        epsb_t = sb("eps", [128, 1], F32)
        epsb = Buf("eps")
        pb = [es.enter_context(nc.psum_tensor(f"pb{i}", [128, 512], F32)) for i in range(8)]
        pbb = [Buf(f"pb{i}") for i in range(8)]
        ident = cst[:, 0, :]
        ones_bf = cst[:, 1, :]
        blockones = cst[:, 2, :]

        c_misc = kb.ctr("s_misc")
        c_x = kb.ctr("s_x")
        c_out = kb.ctr("s_out")

        kb.dma(sp, c_misc, gains[:, :, :], gains_d[:, :, :], writes=[gainsb])
        c_cst = kb.ctr("s_cst")
        kb.dma(pool, c_cst, cst[:, :, :], consts_d[:, :, :], writes=[cstb])
        kb.op(dve, lambda: nc.vector.memset(epsb_t[:, :], 1e-6), writes=[epsb])

        def load_x(bl):
            kb.wait_all(sp, [c_out])
            for c in range(4):
                for k in range(8):
                    kb.dma(sp, c_x, hT[:, k, c * 512:(c + 1) * 512], xT[bl, k, :, c * 512:(c + 1) * 512], writes=[hb[k][c]])
                for k in range(8):
                    hb[k][c].w = (c_x, c_x.cnt)
                kb.wait_all(sp, [c_x])

        def main_chunks(xn, xnb, at, atb):
            chs = []
            for c in range(4):
                chs.append(dict(
                    N=512,
                    h=(lambda k, c=c: hT[:, k, c * 512:(c + 1) * 512]),
                    hb=(lambda k, c=c: hb[k][c]),
                    xn=(lambda k, c=c: xn[:, k, c * 512:(c + 1) * 512]),
                    xnb=(lambda k, c=c: xnb[k][c]),
                    at=(lambda kk, c=c: at[:, kk, c * 512:(c + 1) * 512]),
                    atb=(lambda kk, c=c: atb[kk][c]),
                ))
            return chs

        def norm_chunk(ch, gi, scr, out_fn=None, outb_fn=None):
            N = ch["N"]
            sq, sqb, sd, sdb, rstd, rstdb = scr
            pR, pRb = pb[7], pbb[7]
            for k in range(8):
                kb.op(act, lambda: nc.scalar.activation(out=sq[k % 2][:, :N], in_=ch["h"](k), func=AF.Square),
                      reads=[ch["hb"](k)], writes=[sqb[k % 2]])
                kb.op(pe, lambda: nc.tensor.matmul(pR[:, :N], lhsT=ones_bf, rhs=sq[k % 2][:, :N],
                                                   start=(k == 0), stop=(k == 7)),
                      reads=[sqb[k % 2], cstb], writes=[pRb])
            kb.op(act, lambda: nc.scalar.activation(out=sd[:, :N], in_=pR[:, :N], func=AF.Sqrt,
                                                    scale=1.0 / D, bias=epsb_t[:, 0:1]),
                  reads=[pRb, epsb], writes=[sdb])
            kb.op(dve, lambda: nc.vector.reciprocal(out=rstd[:, :N], in_=sd[:, :N]), reads=[sdb], writes=[rstdb])
            for k in range(8):
                o = ch["xn"](k) if out_fn is None else out_fn(k)
                ob = ch["xnb"](k) if outb_fn is None else outb_fn(k)
                kb.op(dve, lambda: nc.vector.scalar_tensor_tensor(
                    out=o, in0=ch["h"](k), scalar=gains[:, gi, k:k + 1], in1=rstd[:, :N],
                    op0=ALU.mult, op1=ALU.mult),
                    reads=[ch["hb"](k), rstdb, gainsb], writes=[ob])

        def ffn_items(wi, wo):
            items = []
            for (a, b) in FF_SPLITS:
                for m in range(a, b):
                    items.append((wi[m, :, :], 2048))
                for mo in range(8):
                    items.append((wo[mo, :, a * 128:b * 128], (b - a) * 128))
            return items

        def ffn(chunks, gi, scr, sg, sgb):
            for ch in chunks:
                norm_chunk(ch, gi, scr)
            gu = 0
            oi = 0
            for (a, b) in FF_SPLITS:
                nk = b - a
                for m in range(a, b):
                    ws, wb = stream.get()
                    for ch in chunks:
                        N = ch["N"]
                        pG, pGb = pb[gu % 2], pbb[gu % 2]
                        pU, pUb = pb[2 + gu % 2], pbb[2 + gu % 2]
                        for k in range(8):
                            kb.op(pe, lambda: nc.tensor.matmul(pG[:, :N], lhsT=ws[:, k * 256:k * 256 + 128],
                                                               rhs=ch["xn"](k), start=(k == 0), stop=(k == 7)),
                                  reads=[wb, ch["xnb"](k)], writes=[pGb])
                        for k in range(8):
                            kb.op(pe, lambda: nc.tensor.matmul(pU[:, :N], lhsT=ws[:, k * 256 + 128:k * 256 + 256],
                                                               rhs=ch["xn"](k), start=(k == 0), stop=(k == 7)),
                                  reads=[wb, ch["xnb"](k)], writes=[pUb])
                        s_ = sg[gu % 2]
                        kb.op(act, lambda: nc.scalar.activation(out=s_[:, :N], in_=pG[:, :N], func=AF.Silu),
                              reads=[pGb], writes=[sgb[gu % 2]])
                        kb.op(dve, lambda: nc.vector.tensor_tensor(out=ch["at"](m - a), in0=s_[:, :N], in1=pU[:, :N],
                                                                   op=ALU.mult),
                              reads=[sgb[gu % 2], pUb], writes=[ch["atb"](m - a)])
                        gu += 1
                for mo in range(8):
                    ws, wb = stream.get()
                    for ch in chunks:
                        N = ch["N"]
                        pO, pOb = pb[4 + oi % 2], pbb[4 + oi % 2]
                        for kk in range(nk):
                            kb.op(pe, lambda: nc.tensor.matmul(pO[:, :N], lhsT=ws[:, kk * 128:(kk + 1) * 128],
                                                               rhs=ch["at"](kk), start=(kk == 0), stop=(kk == nk - 1)),
                                  reads=[wb, ch["atb"](kk)], writes=[pOb])
                        hk = ch["h"](mo)
                        kb.op(dve, lambda: nc.vector.scalar_tensor_tensor(out=hk, in0=pO[:, :N], scalar=0.5, in1=hk,
                                                                          op0=ALU.mult, op1=ALU.add),
                              reads=[pOb, ch["hb"](mo)], writes=[ch["hb"](mo)])
                        oi += 1

        def ffn_phase(bl, gi, with_meta):
            with ExitStack() as ph:
                def psb(name, shape, dt):
                    return ph.enter_context(nc.sbuf_tensor(uname(name), list(shape), dt))
                xn = psb("xn", [128, 8, T], BF16)
                xnb = [[Buf() for c in range(4)] for k in range(8)]
                at = psb("at", [128, 8, T], BF16)
                atb = [[Buf() for c in range(4)] for k in range(8)]
                sq = [psb(f"sq{i}", [128, 512], BF16) for i in range(2)]
                sqb = [Buf(), Buf()]
                sd = psb("sd", [128, 512], F32)
                rstd = psb("rstd", [128, 512], F32)
                sg = [psb(f"sg{i}", [128, 512], BF16) for i in range(2)]
                sgb = [Buf(), Buf()]
                scr = (sq, sqb, sd, Buf(), rstd, Buf())
                chunks = main_chunks(xn, xnb, at, atb)
                if with_meta:
                    xnm = psb("xnm", [128, 8, NM], BF16)
                    xnmb = [Buf() for k in range(8)]
                    atm = psb("atm", [128, 8, NM], BF16)
                    atmb = [Buf() for k in range(8)]
                    chunks.append(dict(N=NM, h=lambda k: hm[:, k, :], hb=lambda k: hmb[k],
                                       xn=lambda k: xnm[:, k, :], xnb=lambda k: xnmb[k],
                                       at=lambda kk: atm[:, kk, :], atb=lambda kk: atmb[kk]))
                ffn(chunks, gi, scr, sg, sgb)
                kb.barrier()

        def final_phase(bl):
            with ExitStack() as ph:
                def psb(name, shape, dt):
                    return ph.enter_context(nc.sbuf_tensor(uname(name), list(shape), dt))
                sq = [psb(f"fsq{i}", [128, 512], BF16) for i in range(2)]
                sd = psb("fsd", [128, 512], F32)
                rstd = psb("frstd", [128, 512], F32)
                scr = (sq, [Buf(), Buf()], sd, Buf(), rstd, Buf())
                chunks = main_chunks(None, None, None, None)
                for c, ch in enumerate(chunks):
                    norm_chunk(ch, 3, scr, out_fn=ch["h"], outb_fn=ch["hb"])
                    for k in range(8):
                        kb.dma(sp, c_out, outT[bl, k, :, c * 512:(c + 1) * 512], hT[:, k, c * 512:(c + 1) * 512],
                               reads=[hb[k][c]])
                kb.barrier()

        def dump(name, ap, shape, reads):
            t = nc.dram_tensor("dbg_" + name, list(shape), ap.dtype, kind="ExternalOutput").ap()
            dbg_outs[name] = t
            kb.dma(sp, c_out, t, ap, reads=reads)

        def psbf(ph):
            def f(name, shape, dt):
                return ph.enter_context(nc.sbuf_tensor(uname(name), list(shape), dt))
            return f

        def V(fn, reads, writes):
            kb.op(dve, fn, reads=reads, writes=writes)

        def A(fn, reads, writes):
            kb.op(act, fn, reads=reads, writes=writes)

        def setup_na():
            with ExitStack() as ph:
                psb = psbf(ph)
                c_s = kb.ctr("s_setna")
                raw = psb("eraw", [128, 4, 15, 64], F32)
                rawb = Buf()
                exn = psb("eexp", [128, 4, 15, 64], F32)
                exb = Buf()
                cm = psb("cmask", [128, 64], F32)
                cmb = Buf()
                kb.dma(sp, c_s, cm[:, :], colmask_d[:, :], writes=[cmb])
                for hp in range(4):
                    for h2 in range(2):
                        h = 2 * hp + h2
                        src = bass.AP(rpbpad.tensor, h * 15 * 128, [[1, 64], [128, 15], [1, 64]])
                        kb.dma(sp, c_s, raw[h2 * 64:(h2 + 1) * 64, hp, :, :], src, writes=[rawb])
                rawb.w = (c_s, c_s.cnt)
                cmb.w = (c_s, c_s.cnt)
                A(lambda: nc.scalar.activation(out=sap(exn, 0, 128, 0, [[1, 3840]]), in_=sap(raw, 0, 128, 0, [[1, 3840]]),
                                               func=AF.Exp), [rawb], [exb])
                V(lambda: nc.vector.tensor_tensor(out=sap(Etab, 0, 128, 0, [[64, 60], [1, 64]]),
                                                  in0=sap(exn, 0, 128, 63, [[64, 60], [-1, 64]]),
                                                  in1=sap(cm, 0, 128, 0, [[0, 60], [1, 64]]), op=ALU.mult),
                  [exb, cmb], [Etabb])
                kb.barrier()

        MAGIC = 12582912.0

        def cexp(xr, xi, ore, oim, t, F_, b):
            R = [b]
            A(lambda: nc.scalar.activation(out=xr, in_=xr, func=AF.Exp), R, R)
            V(lambda: nc.vector.tensor_scalar(out=xi, in0=xi, scalar1=1.0 / TWO_PI, scalar2=None, op0=ALU.mult), R, R)
            for which, o in ((0, oim), (1, ore)):
                if which == 1:
                    V(lambda: nc.vector.tensor_scalar(out=xi, in0=xi, scalar1=0.25, scalar2=None, op0=ALU.add), R, R)
                V(lambda: nc.vector.tensor_scalar(out=t, in0=xi, scalar1=MAGIC, scalar2=None, op0=ALU.add), R, R)
                V(lambda: nc.vector.tensor_scalar(out=t, in0=t, scalar1=MAGIC, scalar2=None, op0=ALU.subtract), R, R)
                V(lambda: nc.vector.tensor_tensor(out=t, in0=xi, in1=t, op=ALU.subtract), R, R)
                A(lambda: nc.scalar.activation(out=t, in_=t, func=AF.Sin, scale=TWO_PI), R, R)
                V(lambda: nc.vector.tensor_tensor(out=o, in0=xr, in1=t, op=ALU.mult), R, R)

        def cmul(ore, oim, ar_, ai_, br_, bi_, t1, t2, R, W, neg_im=False):
            V(lambda: nc.vector.tensor_tensor(out=t1, in0=ar_, in1=br_, op=ALU.mult), R, R)
            V(lambda: nc.vector.tensor_tensor(out=t2, in0=ai_, in1=bi_, op=ALU.mult), R, R)
            V(lambda: nc.vector.tensor_tensor(out=ore, in0=t1, in1=t2, op=ALU.subtract), R, W)
            V(lambda: nc.vector.tensor_tensor(out=t1, in0=ar_, in1=bi_, op=ALU.mult), R, R)
            V(lambda: nc.vector.tensor_tensor(out=t2, in0=ai_, in1=br_, op=ALU.mult), R, R)
            if neg_im:
                V(lambda: nc.vector.scalar_tensor_tensor(out=oim, in0=t1, scalar=-1.0, in1=t2, op0=ALU.mult,
                                                         op1=ALU.subtract), R, W)
            else:
                V(lambda: nc.vector.tensor_tensor(out=oim, in0=t1, in1=t2, op=ALU.add), R, W)

        def lam_and_f(are, aim, ldt, ardt, aidt, lre, lim, fre, fim, t1, t2, t3, F_, b):
            R = [b]
            A(lambda: nc.scalar.activation(out=ldt, in_=ldt, func=AF.Exp), R, R)
            V(lambda: nc.vector.tensor_scalar(out=are, in0=are, scalar1=-1e-4, scalar2=None, op0=ALU.min), R, R)
            V(lambda: nc.vector.tensor_tensor(out=ardt, in0=are, in1=ldt, op=ALU.mult), R, R)
            V(lambda: nc.vector.tensor_tensor(out=aidt, in0=aim, in1=ldt, op=ALU.mult), R, R)
            V(lambda: nc.vector.tensor_copy(out=t1, in_=ardt), R, R)
            V(lambda: nc.vector.tensor_copy(out=t2, in_=aidt), R, R)
            cexp(t1, t2, lre, lim, t3, F_, b)
            V(lambda: nc.vector.tensor_tensor(out=t1, in0=are, in1=are, op=ALU.mult), R, R)
            V(lambda: nc.vector.tensor_tensor(out=t2, in0=aim, in1=aim, op=ALU.mult), R, R)
            V(lambda: nc.vector.tensor_tensor(out=t3, in0=t1, in1=t2, op=ALU.add), R, R)
            V(lambda: nc.vector.reciprocal(out=t3, in_=t3), R, R)
            V(lambda: nc.vector.tensor_scalar(out=t1, in0=lre, scalar1=-1.0, scalar2=None, op0=ALU.add), R, R)
            V(lambda: nc.vector.tensor_tensor(out=fre, in0=t1, in1=are, op=ALU.mult), R, R)
            V(lambda: nc.vector.tensor_tensor(out=t2, in0=lim, in1=aim, op=ALU.mult), R, R)
            V(lambda: nc.vector.tensor_tensor(out=fre, in0=fre, in1=t2, op=ALU.add), R, R)
            V(lambda: nc.vector.tensor_tensor(out=fre, in0=fre, in1=t3, op=ALU.mult), R, R)
            V(lambda: nc.vector.tensor_tensor(out=fim, in0=lim, in1=are, op=ALU.mult), R, R)
            V(lambda: nc.vector.tensor_tensor(out=t2, in0=t1, in1=aim, op=ALU.mult), R, R)
            V(lambda: nc.vector.tensor_tensor(out=fim, in0=fim, in1=t2, op=ALU.subtract), R, R)
            V(lambda: nc.vector.tensor_tensor(out=fim, in0=fim, in1=t3, op=ALU.mult), R, R)

        def setup_s5():
            c_s = kb.ctr("s_sets5")
            with ExitStack() as ph:
                psb = psbf(ph)
                b = Buf()
                F_ = NG
                names = ["are", "aim", "ldt", "ardt", "aidt", "lre", "lim", "fre", "fim", "t1", "t2", "t3"]
                tl = {n: psb("C" + n, [128, F_], F32) for n in names}
                fl = {n: tl[n][:, :] for n in names}
                Cc = psb("CC", [128, 2, 512], F32)
                Bc = psb("CB", [128, 2, 512], F32)
                eps = psb("Ceps", [128, 5, 8], F32)
                Dt = psb("CD", [128, NG], F32)
                msk = psb("Cmsk", [128, 3, 128], F32)
                for wi_, n in enumerate(["are", "aim", "ldt"]):
                    kb.dma(sp, c_s, tl[n][:, :], s5c_d[:, wi_, :], writes=[b])
                kb.dma(sp, c_s, Cc[:, :, :], s5cc_d[:, :, :], writes=[b])
                kb.dma(sp, c_s, Bc[:, :, :], s5cb_d[:, :, :], writes=[b])
                kb.dma(sp, c_s, eps[:, :, :], s5ce_d[:, :, :], writes=[b])
                kb.dma(sp, c_s, Dt[:, :], s5d_d[:, :], writes=[b])
                kb.dma(sp, c_s, msk[:, :, :], s5m_d[:, :, :], writes=[b])
                b.w = (c_s, c_s.cnt)
                R = [b]
                lam_and_f(fl["are"], fl["aim"], fl["ldt"], fl["ardt"], fl["aidt"], fl["lre"], fl["lim"],
                          fl["fre"], fl["fim"], fl["t1"], fl["t2"], fl["t3"], F_, b)
                V(lambda: nc.vector.tensor_scalar(out=fl["t1"], in0=fl["ardt"], scalar1=16.0, scalar2=None, op0=ALU.mult), R, R)
                V(lambda: nc.vector.tensor_scalar(out=fl["t2"], in0=fl["aidt"], scalar1=16.0, scalar2=None, op0=ALU.mult), R, R)
                cexp(fl["t1"], fl["t2"], fl["lre"], fl["lim"], fl["t3"], F_, b)
                for c_ in range(2):
                    for par_ in range(2):
                        oA = sap(scA, 0, 128, c_ * NG * 2 + par_, [[2, NG]])
                        oB = sap(scB, 0, 128, c_ * NG * 2 + par_, [[2, NG]])
                        V(lambda: nc.vector.tensor_copy(out=oA, in_=fl["lre"]), R, [scb])
                        if c_ == 0:
                            V(lambda: nc.vector.tensor_scalar(out=oB, in0=fl["lim"], scalar1=-1.0, scalar2=None, op0=ALU.mult),
                              R, [scb])
                        else:
                            V(lambda: nc.vector.tensor_copy(out=oB, in_=fl["lim"]), R, [scb])
                pw = {}
                G8 = NG * 8
                xr = psb("Cxr", [128, G8], F32)
                xi = psb("Cxi", [128, G8], F32)
                tt = psb("Ctt", [128, G8], F32)
                for ei, nm in enumerate(["C", "G", "A", "B", "B2"]):
                    pr = psb("Cp%sr" % nm, [128, G8], F32)
                    pi_ = psb("Cp%si" % nm, [128, G8], F32)
                    e_b = sap(eps, 0, 128, ei * 8, [[0, NG], [1, 8]])
                    V(lambda: nc.vector.tensor_tensor(out=sap(xr, 0, 128, 0, [[8, NG], [1, 8]]),
                                                      in0=sap(tl["ardt"], 0, 128, 0, [[1, NG], [0, 8]]), in1=e_b,
                                                      op=ALU.mult), R, R)
                    V(lambda: nc.vector.tensor_tensor(out=sap(xi, 0, 128, 0, [[8, NG], [1, 8]]),
                                                      in0=sap(tl["aidt"], 0, 128, 0, [[1, NG], [0, 8]]), in1=e_b,
                                                      op=ALU.mult), R, R)
                    cexp(xr[:, :], xi[:, :], pr[:, :], pi_[:, :], tt[:, :], G8, b)
                    pw[nm] = (pr, pi_)
                fr_b = sap(tl["fre"], 0, 128, 0, [[1, NG], [0, 8]])
                fi_b = sap(tl["fim"], 0, 128, 0, [[1, NG], [0, 8]])

                def g8(t_):
                    return sap(t_, 0, 128, 0, [[8, NG], [1, 8]])
                far = psb("Cfar", [128, G8], F32)
                fai = psb("Cfai", [128, G8], F32)
                cmul(g8(far), g8(fai), fr_b, fi_b, g8(pw["A"][0]), g8(pw["A"][1]), g8(xr), g8(xi), R, R)
                BIG = NG * 8 * 16
                t1b = psb("Ct1b", [128, BIG], F32)
                t2b = psb("Ct2b", [128, BIG], F32)
                Are = psb("CAre", [128, NG, 128], BF16)
                Aim = psb("CAim", [128, NG, 128], BF16)
                Gre = psb("CGre", [128, NG, 128], BF16)
                Gim = psb("CGim", [128, NG, 128], BF16)

                def bigv(t_):
                    return sap(t_, 0, 128, 0, [[128, NG], [16, 8], [1, 16]])

                def pwv(t_):
                    return sap(t_, 0, 128, 0, [[8, NG], [1, 8], [0, 16]])

                def cpv(t_, c):
                    return sap(t_, 0, 128, c * 512, [[16, NG], [0, 8], [1, 16]])
                cmul(sap(WC, 0, 128, 0, [[256, NG], [16, 8], [1, 16]]), sap(WC, 0, 128, 128, [[256, NG], [16, 8], [1, 16]]),
                     cpv(Cc, 0), cpv(Cc, 1), pwv(pw["C"][0]), pwv(pw["C"][1]), bigv(t1b), bigv(t2b), R, [b, WCb], neg_im=True)
                cmul(bigv(Gre), bigv(Gim), cpv(Cc, 0), cpv(Cc, 1), pwv(pw["G"][0]), pwv(pw["G"][1]), bigv(t1b), bigv(t2b),
                     R, R, neg_im=True)
                cmul(bigv(Are), bigv(Aim), pwv(far), pwv(fai), cpv(Bc, 0), cpv(Bc, 1), bigv(t1b), bigv(t2b), R, R)
                fbr = psb("Cfbr", [128, G8], F32)
                fbi = psb("Cfbi", [128, G8], F32)
                cmul(g8(fbr), g8(fbi), fr_b, fi_b, g8(pw["B"][0]), g8(pw["B"][1]), g8(xr), g8(xi), R, R)
                PBc = [psb("CPBre", [128, NG, 128], BF16), psb("CPBim", [128, NG, 128], BF16)]
                cmul(bigv(PBc[0]), bigv(PBc[1]), pwv(fbr), pwv(fbi), cpv(Bc, 0), cpv(Bc, 1), bigv(t1b), bigv(t2b), R, R)
                for g2 in range(NG // 2):
                    pw_, pwb_ = pb[2 + g2 % 2], pbb[2 + g2 % 2]
                    for gg in range(2):
                        for c in range(2):
                            blk = gg * 2 + c
                            kb.op(pe, lambda: nc.tensor.matmul(pw_[:, blk * 128:(blk + 1) * 128], lhsT=PBc[c][:, 2 * g2 + gg, :],
                                                               rhs=ident, start=True, stop=True),
                                  reads=[b, cstb], writes=[pwb_])
                    V(lambda: nc.vector.tensor_copy(out=sap(WB, 0, 128, 2 * g2 * 256, [[1, 512]]), in_=pw_[:, :]),
                      [pwb_], [WBb])
                cmul(g8(fbr), g8(fbi), fr_b, fi_b, g8(pw["B2"][0]), g8(pw["B2"][1]), g8(xr), g8(xi), R, R)
                cmul(bigv(PBc[0]), bigv(PBc[1]), pwv(fbr), pwv(fbi), cpv(Bc, 0), cpv(Bc, 1), bigv(t1b), bigv(t2b), R, R)
                wb2t = psb("Cwb2t", [128, NG * 512], BF16)
                wb2tb = Buf()
                V(lambda: nc.vector.memset(wb2t[:, :], 0.0), [], [wb2tb])
                for g2 in range(NG // 2):
                    pw_, pwb_ = pb[2 + g2 % 2], pbb[2 + g2 % 2]
                    for gg in range(2):
                        for c in range(2):
                            blk = gg * 2 + c
                            kb.op(pe, lambda: nc.tensor.matmul(pw_[:, blk * 128:(blk + 1) * 128], lhsT=PBc[c][:, 2 * g2 + gg, :],
                                                               rhs=ident, start=True, stop=True),
                                  reads=[b, cstb], writes=[pwb_])
                    V(lambda: nc.vector.tensor_copy(out=sap(wb2t, 0, 128, 2 * g2 * 512, [[256, 4], [1, 64]]),
                                                    in_=sap(pw_, 0, 128, 0, [[128, 4], [1, 64]])), [pwb_], [wb2tb])
                    V(lambda: nc.vector.tensor_copy(out=sap(wb2t, 0, 128, 2 * g2 * 512 + 128 + 64, [[256, 4], [1, 64]]),
                                                    in_=sap(pw_, 0, 128, 64, [[128, 4], [1, 64]])), [pwb_], [wb2tb])
                kb.dma(sp, c_scr, wb2_scr[:, :], wb2t[:, :], reads=[wb2tb], writes=[wb2_b])
                kb.wait_all(sp, [c_scr])
                tmpf = psb("Ctmpf", [128, 512], F32)
                for g4 in range(NG // 4):
                    pf, pfb = pb[0], pbb[0]
                    pq, pqb = pb[1], pbb[1]
                    for gg in range(4):
                        g = g4 * 4 + gg
                        for (P_, Pb_, lo) in ((pf, pfb, 0), (pq, pqb, 64)):
                            kb.op(pe, lambda: nc.tensor.matmul(P_[:, gg * 128:(gg + 1) * 128], lhsT=Are[lo:lo + 64, g, :],
                                                               rhs=Gre[lo:lo + 64, g, :], start=True, stop=False),
                                  reads=[b], writes=[Pb_])
                            kb.op(pe, lambda: nc.tensor.matmul(P_[:, gg * 128:(gg + 1) * 128], lhsT=Aim[lo:lo + 64, g, :],
                                                               rhs=Gim[lo:lo + 64, g, :], start=False, stop=True),
                                  reads=[b], writes=[Pb_])
                    mL = sap(msk, 0, 128, 0, [[0, 4], [1, 128]])
                    mU = sap(msk, 0, 128, 128, [[0, 4], [1, 128]])
                    t4 = sap(tmpf, 0, 128, 0, [[128, 4], [1, 128]])
                    V(lambda: nc.vector.tensor_tensor(out=t4, in0=sap(pf, 0, 128, 0, [[128, 4], [1, 128]]), in1=mL,
                                                      op=ALU.mult), [pfb, b], [b])
                    V(lambda: nc.vector.tensor_tensor(out=sap(t1b, 0, 128, 0, [[128, 4], [1, 128]]),
                                                      in0=sap(pq, 0, 128, 0, [[128, 4], [1, 128]]), in1=mU, op=ALU.mult),
                      [pqb, b], [b])
                    V(lambda: nc.vector.tensor_tensor(out=t4, in0=t4, in1=sap(t1b, 0, 128, 0, [[128, 4], [1, 128]]),
                                                      op=ALU.add), R, R)
                    for gg in range(4):
                        g = g4 * 4 + gg
                        V(lambda: nc.vector.scalar_tensor_tensor(out=WT[:, g, :], in0=msk[:, 2, :], scalar=Dt[:, g:g + 1],
                                                                 in1=tmpf[:, gg * 128:(gg + 1) * 128], op0=ALU.mult,
                                                                 op1=ALU.add), R, [b, WTb])
                kb.barrier()
        def mixer_items():
            items = []
            for mt in range(12):
                items.append((wx[mt, :, :], 1024))
            for half in range(2):
                items.append((wv2[half, :, :], 2048))
            return items

        def krs_for(qg):
            out = []
            for kr in range(32):
                rows = [r for r in range(8 * qg, 8 * qg + 8)
                        if min(max(r - 4, 0), 24) <= kr <= min(max(r - 4, 0), 24) + 7]
                if rows:
                    out.append((kr, rows[0], rows[-1]))
            return out

        def inproj_phase(bl, qT, qTb, kT, kTb, vd, vdb):
            with ExitStack() as ph:
                psb = psbf(ph)
                xn2 = psb("xn2", [128, 8, T], BF16)
                xn2b = [[Buf() for c in range(4)] for k in range(8)]
                sq = [psb(f"sq{i}", [128, 512], BF16) for i in range(2)]
                scr = (sq, [Buf(), Buf()], pb[6], pbb[6], pb[6], pbb[6])
                chunks = main_chunks(xn2, xn2b, None, None)
                with_meta = (bl == 0)
                if with_meta:
                    xnm = psb("xn2m", [128, 8, NM], BF16)
                    xnmb = [Buf() for k in range(8)]
                    chunks.append(dict(N=NM, h=lambda k: hm[:, k, :], hb=lambda k: hmb[k],
                                       xn=lambda k: xnm[:, k, :], xnb=lambda k: xnmb[k]))
                for ch in chunks:
                    norm_chunk(ch, 1, scr)
                ust = [psb(f"ust{i}", [128, 8, 64], BF16) for i in range(2)]
                ustb = [Buf(), Buf()]
                ui = 0
                it = 0
                for mt in range(12):
                    ws, wb = stream.get()
                    for ci, ch in enumerate(chunks):
                        N = ch["N"]
                        is_meta = (ci == 4)
                        if is_meta and 4 <= mt < 8:
                            continue
                        pS, pSb = pb[it % 4], pbb[it % 4]
                        it += 1
                        for k in range(8):
                            kb.op(pe, lambda: nc.tensor.matmul(pS[:, :N], lhsT=ws[:, k * 128:(k + 1) * 128], rhs=ch["xn"](k),
                                                               start=(k == 0), stop=(k == 7)),
                                  reads=[wb, ch["xnb"](k)], writes=[pSb])
                        if mt < 4:
                            if is_meta:
                                o = sap(umT, 0, 128, mt * 16, [[2, 8], [1, 2]])
                                i_ = sap(pS, 0, 128, 0, [[1, 8], [8, 2]])
                                V(lambda: nc.vector.tensor_copy(out=o, in_=i_), [pSb], [umTb])
                            else:
                                us_, usb_ = ust[ui % 2], ustb[ui % 2]
                                ui += 1
                                i_ = sap(pS, 0, 128, 0, [[1, 8], [8, 64]])
                                V(lambda: nc.vector.tensor_copy(out=us_[:, :, :], in_=i_), [pSb], [usb_])
                                kb.dma(sp, c_scr, u_scr[mt * 128:(mt + 1) * 128, :, 2 + 64 * ci:2 + 64 * ci + 64], us_[:, :, :],
                                       reads=[usb_], writes=[u_scr_b])
                        elif mt < 8:
                            A(lambda: nc.scalar.copy(out=qT[:, mt - 4, ci * 512:(ci + 1) * 512], in_=pS[:, :N]),
                              [pSb], [qTb[mt - 4][ci]])
                        else:
                            if is_meta:
                                A(lambda: nc.scalar.copy(out=kmT[:, mt - 8, :], in_=pS[:, :N]), [pSb], [kmTb])
                            else:
                                A(lambda: nc.scalar.copy(out=kT[:, mt - 8, ci * 512:(ci + 1) * 512], in_=pS[:, :N]),
                                  [pSb], [kTb[mt - 8]])
                    if mt < 4:
                        kb.dma(sp, c_scr, u_scr[mt * 128:(mt + 1) * 128, :, 0:2], umT[:, mt, :, :], reads=[umTb],
                               writes=[u_scr_b])
                for j in range(8):
                    srcA = bass.AP(u_scr.tensor, j * NBLK, [[8 * NBLK, 16], [16 * 8 * NBLK, NG], [1, NBLK]])
                    dstA = bass.AP(u_scr2.tensor, j * 16 * NG * NBLK, [[NG * NBLK, 16], [NBLK, NG], [1, NBLK]])
                    kb.dma(sp, c_scr, dstA, srcA, reads=[u_scr_b], writes=[u2b[j]])
                for j in range(8):
                    u2b[j].w = (c_scr, c_scr.cnt)
                for half in range(2):
                    ws, wb = stream.get()
                    for w in range(-1, 32):
                        pS, pSb = pb[it % 4], pbb[it % 4]
                        it += 1
                        if w == -1:
                            t0, M, p0 = 0, 64, 64
                        elif w == 31:
                            t0, M, p0 = 64 * 31, 64, 0
                        else:
                            t0, M, p0 = 64 * w, 128, 0
                        rb_ = [xn2b[0][t0 // 512], xn2b[0][(t0 + M - 1) // 512]]
                        for k in range(8):
                            kb.op(pe, lambda: nc.tensor.matmul(pS[p0:p0 + M, 0:256], lhsT=xn2[:, k, t0:t0 + M],
                                                               rhs=ws[:, k * 256:(k + 1) * 256], start=(k == 0), stop=(k == 7)),
                                  reads=[wb, xn2b[k][t0 // 512], xn2b[k][(t0 + M - 1) // 512]], writes=[pSb])
                        if w >= 0:
                            V(lambda: nc.vector.tensor_copy(out=vd[0:64, w, 2 * half:2 * half + 2, :],
                                                            in_=sap(pS, 0, 64, 0, [[128, 2], [1, 64]])), [pSb], [vdb])
                        if w < 31:
                            A(lambda: nc.scalar.copy(out=vd[64:128, w + 1, 2 * half:2 * half + 2, :],
                                                     in_=sap(pS, 64, 64, 64, [[128, 2], [1, 64]])), [pSb], [vdb])
                    if with_meta:
                        pS, pSb = pb[it % 4], pbb[it % 4]
                        it += 1
                        for p0 in (0, 64):
                            for k in range(8):
                                kb.op(pe, lambda: nc.tensor.matmul(pS[p0:p0 + NM, 0:256], lhsT=xnm[:, k, :],
                                                                   rhs=ws[:, k * 256:(k + 1) * 256], start=(k == 0),
                                                                   stop=(k == 7)),
                                      reads=[wb, xnmb[k]], writes=[pSb])
                        V(lambda: nc.vector.tensor_copy(out=vm[0:NM, 2 * half:2 * half + 2, :],
                                                        in_=sap(pS, 0, NM, 0, [[128, 2], [1, 64]])), [pSb], [vmb])
                        V(lambda: nc.vector.tensor_copy(out=vm[64:64 + NM, 2 * half:2 * half + 2, :],
                                                        in_=sap(pS, 64, NM, 64, [[128, 2], [1, 64]])), [pSb], [vmb])
                kb.barrier()

        def na_phase(bl, qT, qTb, kT, kTb, vd, vdb, oT, oTb):
            LA = 3
            NSB = 4
            with ExitStack() as ph:
                psb = psbf(ph)
                exs = [psb(f"nex{i}", [128, 512], BF16) for i in range(NSB)]
                exb = [Buf() for i in range(NSB)]
                Ps = [psb(f"nP{i}", [128, 512], BF16) for i in range(NSB)]
                Pb = [Buf() for i in range(NSB)]
                pms = [psb(f"npm{i}", [128, 512], BF16) for i in range(2)]
                pmb = [Buf(), Buf()]
                rec = psb("nrec", [128, 512], F32)
                recb = Buf()
                for i in range(2):
                    V(lambda: nc.vector.memset(pms[i][:, :], 0.0), [], [pmb[i]])
                tiles_ = []
                gi_ = 0
                for hp in range(4):
                    for qg in range(4):
                        lst = krs_for(qg)
                        tiles_.append(dict(kind="meta", hp=hp, qg=qg, gi=gi_, first=True, last=False))
                        for idx, (kr, ra, rb2) in enumerate(lst):
                            tiles_.append(dict(kind="kr", hp=hp, qg=qg, gi=gi_, kr=kr, ra=ra, rb=rb2, first=False,
                                               last=(idx == len(lst) - 1)))
                        gi_ += 1

                def emit_S(t, tl_):
                    hp, qg = tl_["hp"], tl_["qg"]
                    pS, pSb = pb[t % NSB], pbb[t % NSB]
                    if tl_["kind"] == "meta":
                        tok0 = 512 * qg
                        for lo in (0, 64):
                            kb.op(pe, lambda: nc.tensor.matmul(pS[lo:lo + NM, 0:512], lhsT=kmT[lo:lo + 64, hp, :],
                                                               rhs=qT[lo:lo + 64, hp, tok0:tok0 + 512], start=True, stop=True),
                                  reads=[kmTb, qTb[hp][qg]], writes=[pSb])
                    else:
                        kr, ra, rb2 = tl_["kr"], tl_["ra"], tl_["rb"]
                        N = 64 * (rb2 - ra + 1)
                        for lo in (0, 64):
                            kb.op(pe, lambda: nc.tensor.matmul(pS[lo:lo + 64, 0:N], lhsT=kT[lo:lo + 64, hp, 64 * kr:64 * kr + 64],
                                                               rhs=qT[lo:lo + 64, hp, 64 * ra:64 * (rb2 + 1)],
                                                               start=True, stop=True),
                                  reads=[kTb[hp], qTb[hp][qg]], writes=[pSb])

                def emit_E(t, tl_):
                    hp, qg = tl_["hp"], tl_["qg"]
                    pS, pSb = pb[t % NSB], pbb[t % NSB]
                    if tl_["kind"] == "meta":
                        pm, pmb_ = pms[tl_["gi"] % 2], pmb[tl_["gi"] % 2]
                        for lo in (0, 64):
                            A(lambda: nc.scalar.activation(out=pm[lo:lo + NM, :], in_=pS[lo:lo + NM, 0:512], func=AF.Exp,
                                                           scale=0.125), [pSb], [pmb_])
                    else:
                        kr, ra, rb2 = tl_["kr"], tl_["ra"], tl_["rb"]
                        N = 64 * (rb2 - ra + 1)
                        ex_, exb_ = exs[t % NSB], exb[t % NSB]
                        P_, Pb_ = Ps[t % NSB], Pb[t % NSB]
                        A(lambda: nc.scalar.activation(out=ex_[:, 0:N], in_=pS[:, 0:N], func=AF.Exp, scale=0.125),
                          [pSb], [exb_])
                        rho = ra - kr + 7
                        V(lambda: nc.vector.tensor_tensor(out=P_[:, 0:N], in0=ex_[:, 0:N],
                                                          in1=sap(Etab, 0, 128, (hp * 15 + rho) * 64, [[1, N]]),
                                                          op=ALU.mult), [exb_, Etabb], [Pb_])

                def emit_PV(t, tl_):
                    hp, qg, g_ = tl_["hp"], tl_["qg"], tl_["gi"]
                    pO, pOb = pb[4 + g_ % 2], pbb[4 + g_ % 2]
                    pD, pDb = pb[6 + g_ % 2], pbb[6 + g_ % 2]
                    if tl_["kind"] == "meta":
                        pm, pmb_ = pms[g_ % 2], pmb[g_ % 2]
                        for lo in (0, 64):
                            kb.op(pe, lambda: nc.tensor.matmul(pO[lo:lo + 64, :], lhsT=vm[lo:lo + NM, hp, :],
                                                               rhs=pm[lo:lo + NM, :], start=True, stop=False),
                                  reads=[vmb, pmb_], writes=[pOb])
                        kb.op(pe, lambda: nc.tensor.matmul(pD[:, :], lhsT=blockones, rhs=pm[:, :], start=True, stop=False),
                              reads=[cstb, pmb_], writes=[pDb])
                    else:
                        kr, ra, rb2 = tl_["kr"], tl_["ra"], tl_["rb"]
                        N = 64 * (rb2 - ra + 1)
                        c0 = 64 * (ra - 8 * qg)
                        last = tl_["last"]
                        P_, Pb_ = Ps[t % NSB], Pb[t % NSB]
                        for lo in (0, 64):
                            kb.op(pe, lambda: nc.tensor.matmul(pO[lo:lo + 64, c0:c0 + N], lhsT=vd[lo:lo + 64, kr, hp, :],
                                                               rhs=P_[lo:lo + 64, 0:N], start=False, stop=last),
                                  reads=[vdb, Pb_], writes=[pOb])
                        kb.op(pe, lambda: nc.tensor.matmul(pD[:, c0:c0 + N], lhsT=blockones, rhs=P_[:, 0:N],
                                                           start=False, stop=last),
                              reads=[cstb, Pb_], writes=[pDb])
                        if last:
                            tok0 = 512 * qg
                            V(lambda: nc.vector.reciprocal(out=rec[:, :], in_=pD[:, :]), [pDb], [recb])
                            V(lambda: nc.vector.tensor_tensor(out=oT[:, hp, tok0:tok0 + 512], in0=pO[:, :], in1=rec[:, :],
                                                              op=ALU.mult), [pOb, recb], [oTb[hp][qg]])

                nt = len(tiles_)
                for t in range(min(LA, nt)):
                    emit_S(t, tiles_[t])
                for t in range(nt):
                    emit_E(t, tiles_[t])
                    if t + LA < nt:
                        emit_S(t + LA, tiles_[t + LA])
                    emit_PV(t, tiles_[t])
                kb.barrier()

        def s5_phase(bl, zT, zTb):
            with ExitStack() as ph:
                psb = psbf(ph)
                def ubg(g):
                    return sap(zT, 0, 128, g * NBLK, [[1, NBLK]])

                def ubrows(j):
                    return sap(zT, 16 * j, 16, 0, [[NBLK, NG], [1, NBLK]])
                ubb = [Buf() for g in range(NG)]
                Vs = psb("Vs", [128, 2, NG, NBLK], BF16)
                Vsb = Buf()
                Ssb = Buf()
                X = [psb(f"X{i}", [128, 4, NG], F32) for i in range(2)]
                Xb = [Buf(), Buf()]
                t1 = psb("st1", [128, 4, NG], F32)
                t2 = psb("st2", [128, 4, NG], F32)
                tb = Buf()
                kb.dma(sp, c_scr, sap(zT, 0, 128, 0, [[NBLK, NG], [1, NBLK]]), u_scr2[:, :, :], reads=u2b, writes=ubb)
                for g in range(NG):
                    ubb[g].w = (c_scr, c_scr.cnt)
                wb2 = psb("wb2", [128, 8 * 512], BF16)
                wb2b = Buf()

                def ubs(g, a, b_):
                    return sap(zT, 0, 128, g * NBLK + a, [[1, b_ - a]])
                for g in range(NG):
                    if g % 8 == 0:
                        kb.dma(sp, c_scr, wb2[:, :], wb2_scr[:, g * 512:(g + 8) * 512], reads=[wb2_b], writes=[wb2b])
                    pR_, pRb_ = pb[(2 * g) % 4], pbb[(2 * g) % 4]
                    pI_, pIb_ = pb[(2 * g + 1) % 4], pbb[(2 * g + 1) % 4]
                    for c, (P_, Pb_) in enumerate(((pR_, pRb_), (pI_, pIb_))):
                        w2o = (g % 8) * 512 + c * 256
                        kb.op(pe, lambda: nc.tensor.matmul(P_[:, 0:NBLK], lhsT=WB[:, g, c, :], rhs=ubg(g), start=True, stop=False),
                              reads=[WBb, ubb[g]], writes=[Pb_])
                        kb.op(pe, lambda: nc.tensor.matmul(P_[:, 1:NBLK], lhsT=wb2[:, w2o:w2o + 128], rhs=ubs(g, 0, NBLK - 1),
                                                           start=False, stop=False),
                              reads=[wb2b, ubb[g]], writes=[Pb_])
                        kb.op(pe, lambda: nc.tensor.matmul(P_[:, 1:NBLK - 1], lhsT=wb2[:, w2o + 128:w2o + 256], rhs=ubs(g, 2, NBLK),
                                                           start=False, stop=True),
                              reads=[wb2b, ubb[g]], writes=[Pb_])
                        V(lambda: nc.vector.tensor_copy(out=Vs[0:64, c, g, :], in_=P_[0:64, 0:NBLK]), [Pb_], [Vsb])
                        V(lambda: nc.vector.tensor_copy(out=Vs[64:128, c, g, :], in_=sap(P_, 64, 64, NBLK - 1, [[-1, NBLK]])),
                          [Pb_], [Vsb])
                V(lambda: nc.vector.memset(X[1][:, :, :], 0.0), [], [Xb[1]])
                F4 = 4 * NG
                for i in range(NBLK // 2):
                    Xp, Xpb = X[(i + 1) % 2], Xb[(i + 1) % 2]
                    Xn, Xnb = X[i % 2], Xb[i % 2]
                    sw = sap(Xp, 0, 128, 2 * NG, [[-2 * NG, 2], [1, 2 * NG]])
                    V(lambda: nc.vector.tensor_tensor(out=sap(t1, 0, 128, 0, [[1, F4]]), in0=sap(scA, 0, 128, 0, [[1, F4]]),
                                                      in1=sap(Xp, 0, 128, 0, [[1, F4]]), op=ALU.mult), [scb, Xpb], [tb])
                    V(lambda: nc.vector.tensor_tensor(out=sap(t2, 0, 128, 0, [[2 * NG, 2], [1, 2 * NG]]),
                                                      in0=sap(scB, 0, 128, 0, [[2 * NG, 2], [1, 2 * NG]]), in1=sw, op=ALU.mult),
                      [scb, Xpb], [tb])
                    V(lambda: nc.vector.tensor_tensor(out=sap(t1, 0, 128, 0, [[1, F4]]), in0=sap(t1, 0, 128, 0, [[1, F4]]),
                                                      in1=sap(t2, 0, 128, 0, [[1, F4]]), op=ALU.add), [tb], [tb])
                    vi = sap(Vs, 0, 128, 2 * i, [[NBLK, 2 * NG], [1, 2]])
                    x2 = sap(Xn, 0, 128, 0, [[2, 2 * NG], [1, 2]])
                    t1v = sap(t1, 0, 128, 0, [[2, 2 * NG], [1, 2]])
                    V(lambda: nc.vector.tensor_tensor(out=x2, in0=t1v, in1=vi, op=ALU.add), [tb, Vsb], [Xnb])
                    A(lambda: nc.scalar.copy(out=vi, in_=x2), [Xnb], [Ssb])
                al = [psb(f"al{i}", [128, 2, NBLK], BF16) for i in range(2)]
                alb = [Buf(), Buf()]
                for i in range(2):
                    V(lambda: nc.vector.memset(al[i][:, :, :], 0.0), [], [alb[i]])
                g1s = [psb(f"yg1{i}", [128, NBLK], F32) for i in range(3)]
                g2s = [psb(f"yg2{i}", [128, NBLK], F32) for i in range(3)]
                g1b = [Buf() for i in range(3)]
                g2b = [Buf() for i in range(3)]

                def y_A(g):
                    pY, pYb = pb[4 + g % 3], pbb[4 + g % 3]
                    a_, ab_ = al[g % 2], alb[g % 2]
                    V(lambda: nc.vector.tensor_copy(out=a_[0:64, :, 1:NBLK],
                                                    in_=sap(Vs, 0, 64, g * NBLK, [[NG * NBLK, 2], [1, NBLK - 1]])),
                      [Ssb], [ab_])
                    V(lambda: nc.vector.tensor_copy(out=a_[64:128, :, 0:NBLK - 1],
                                                    in_=sap(Vs, 64, 64, g * NBLK + NBLK - 2, [[NG * NBLK, 2], [-1, NBLK - 1]])),
                      [Ssb], [ab_])
                    kb.op(pe, lambda: nc.tensor.matmul(pY[:, 0:NBLK], lhsT=WT[:, g, :], rhs=ubg(g), start=True, stop=False),
                          reads=[WTb, ubb[g]], writes=[pYb])
                    for c in range(2):
                        kb.op(pe, lambda: nc.tensor.matmul(pY[:, 0:NBLK], lhsT=WC[:, g, c, :], rhs=a_[:, c, :],
                                                           start=False, stop=(c == 1)),
                              reads=[WCb, ab_], writes=[pYb])

                def y_B1(g):
                    pY, pYb = pb[4 + g % 3], pbb[4 + g % 3]
                    g1, g2 = g1s[g % 3], g2s[g % 3]
                    A(lambda: nc.scalar.activation(out=g1[:, :], in_=pY[:, 0:NBLK], func=AF.Square), [pYb], [g1b[g % 3]])
                    V(lambda: nc.vector.tensor_scalar(out=g1[:, :], in0=g1[:, :], scalar1=0.07135481627, scalar2=1.5957691216,
                                                      op0=ALU.mult, op1=ALU.add), [g1b[g % 3]], [g1b[g % 3]])
                    V(lambda: nc.vector.tensor_tensor(out=g2[:, :], in0=g1[:, :], in1=pY[:, 0:NBLK], op=ALU.mult),
                      [g1b[g % 3], pYb], [g2b[g % 3]])

                def y_B2(g):
                    pY, pYb = pb[4 + g % 3], pbb[4 + g % 3]
                    g2 = g2s[g % 3]
                    A(lambda: nc.scalar.activation(out=g2[:, :], in_=g2[:, :], func=AF.Sigmoid), [g2b[g % 3]], [g2b[g % 3]])
                    V(lambda: nc.vector.tensor_tensor(out=ubg(g), in0=g2[:, :], in1=pY[:, 0:NBLK], op=ALU.mult),
                      [g2b[g % 3], pYb], [ubb[g]])
                for t_ in range(NG + 2):
                    if t_ < NG:
                        y_A(t_)
                    if 1 <= t_ <= NG:
                        y_B1(t_ - 1)
                    if t_ >= 2:
                        y_B2(t_ - 2)
                z2b = Buf()
                kb.dma(sp, c_scr, z_scr2[:, :, :], sap(zT, 0, 128, 0, [[NBLK, NG], [1, NBLK]]), reads=ubb + zsb, writes=[z2b])
                for j in range(8):
                    srcZ = bass.AP(z_scr2.tensor, j * 16 * NG * NBLK, [[NG * NBLK, 16], [NBLK, NG], [1, NBLK]])
                    dstZ = bass.AP(z_scr.tensor, j * NBLK, [[8 * NBLK, 16], [16 * 8 * NBLK, NG], [1, NBLK]])
                    kb.dma(sp, c_scr, dstZ, srcZ, reads=[z2b], writes=[zsb[j]])
                for j in range(8):
                    zsb[j].w = (c_scr, c_scr.cnt)
                for mt in range(4):
                    kb.dma(sp, c_scr, zT[:, mt, :, :], z_scr[mt * 128:(mt + 1) * 128, :, :], reads=zsb, writes=zTb + ubb)
                for b_ in zTb:
                    b_.w = (c_scr, c_scr.cnt)
                kb.barrier()

        def merge_items():
            items = []
            for half in range(2):
                for mo in range(8):
                    items.append((wgg_d[mo, :, :], 2048))
                    items.append((wbb_d[mo, :, :], 1024))
                for mo2 in range(4):
                    items.append((wo_d[mo2, :, :], 2048))
            return items

        c_wgl = kb.ctr("s_wgl")

        def merge_phase(bl, zT, zTb, oT, oTb):
            with ExitStack() as ph:
                psb = psbf(ph)
                wgl = psb("wgl", [128, 4, 512], BF16)
                wglb = Buf()
                for mt in range(4):
                    kb.dma(pool, c_wgl, wgl[:, mt, :], wglu_d[mt, :, :], writes=[wglb])
                wglb.w = (c_wgl, c_wgl.cnt)
                xn2 = psb("mxn2", [128, 8, 1024], BF16)
                mg = psb("mg", [128, 8, 1024], BF16)
                sq = [psb(f"msq{i}", [128, 512], BF16) for i in range(2)]
                sd = psb("msd", [128, 512], F32)
                rstd = psb("mrstd", [128, 512], F32)
                scr = (sq, [Buf(), Buf()], sd, Buf(), rstd, Buf())
                s1 = [psb(f"ms1{i}", [128, 512], BF16) for i in range(2)]
                s2 = [psb(f"ms2{i}", [128, 512], BF16) for i in range(2)]
                m1 = [psb(f"mm1{i}", [128, 512], BF16) for i in range(2)]
                sb1, sb2, mb1 = [Buf(), Buf()], [Buf(), Buf()], [Buf(), Buf()]

                def zperm(t_, mt, c):
                    return sap(t_, 0, 128, mt * 8 * NBLK + 2 + 64 * c, [[NBLK, 8], [1, 64]])
                for c in range(4):
                    for mt in range(4):
                        pS, pSb = pb[(c % 2) * 4 + mt], pbb[(c % 2) * 4 + mt]
                        for k in range(4):
                            kb.op(pe, lambda: nc.tensor.matmul(pS[:, :], lhsT=wgl[:, mt, k * 128:(k + 1) * 128], rhs=zperm(zT, k, c),
                                                               start=(k == 0), stop=(k == 3)),
                                  reads=[wglb, zTb[c]], writes=[pSb])
                    for mt in range(4):
                        pS, pSb = pb[(c % 2) * 4 + mt], pbb[(c % 2) * 4 + mt]
                        A(lambda: nc.scalar.activation(out=s1[mt % 2][:, :], in_=pS[:, :], func=AF.Sigmoid), [pSb], [sb1[mt % 2]])
                        V(lambda: nc.vector.tensor_tensor(out=zperm(zT, mt, c), in0=zperm(zT, mt, c),
                                                          in1=sap(s1[mt % 2], 0, 128, 0, [[64, 8], [1, 64]]), op=ALU.mult),
                          [sb1[mt % 2], zTb[c]], [zTb[c]])
                for half in range(2):
                    xnb = [[Buf() for c in range(2)] for k in range(8)]
                    mgb = [[Buf() for c in range(2)] for k in range(8)]
                    chs = []
                    for c2 in range(2):
                        c = 2 * half + c2
                        chs.append(dict(N=512, h=(lambda k, c=c: hT[:, k, c * 512:(c + 1) * 512]), hb=(lambda k, c=c: hb[k][c]),
                                        xn=(lambda k, c2=c2: xn2[:, k, c2 * 512:(c2 + 1) * 512]),
                                        xnb=(lambda k, c2=c2: xnb[k][c2])))
                    for ch in chs:
                        norm_chunk(ch, 1, scr)
                    for mo in range(8):
                        wgg, wggb = stream.get()
                        for c2, ch in enumerate(chs):
                            p1, p1b = pb[c2], pbb[c2]
                            for k in range(8):
                                kb.op(pe, lambda: nc.tensor.matmul(p1[:, :], lhsT=wgg[:, k * 128:(k + 1) * 128], rhs=ch["xn"](k),
                                                                   start=(k == 0), stop=(k == 7)),
                                      reads=[wggb, ch["xnb"](k)], writes=[p1b])
                            A(lambda: nc.scalar.activation(out=s1[c2][:, :], in_=p1[:, :], func=AF.Sigmoid), [p1b], [sb1[c2]])
                        for c2, ch in enumerate(chs):
                            p2, p2b = pb[4 + c2], pbb[4 + c2]
                            for k in range(8):
                                kb.op(pe, lambda: nc.tensor.matmul(p2[:, :], lhsT=wgg[:, 1024 + k * 128:1024 + (k + 1) * 128],
                                                                   rhs=ch["xn"](k), start=(k == 0), stop=(k == 7)),
                                      reads=[wggb, ch["xnb"](k)], writes=[p2b])
                            A(lambda: nc.scalar.activation(out=s2[c2][:, :], in_=p2[:, :], func=AF.Sigmoid), [p2b], [sb2[c2]])
                        wbb, wbbb = stream.get()
                        for c2, ch in enumerate(chs):
                            c = 2 * half + c2
                            p3, p3b = pb[2 + c2], pbb[2 + c2]
                            for k in range(4):
                                kb.op(pe, lambda: nc.tensor.matmul(p3[:, :], lhsT=wbb[:, k * 128:(k + 1) * 128], rhs=zperm(zT, k, c),
                                                                   start=(k == 0), stop=(k == 3)),
                                      reads=[wbbb, zTb[c]], writes=[p3b])
                            V(lambda: nc.vector.tensor_tensor(out=sap(m1[c2], 0, 128, 0, [[8, 64], [1, 8]]),
                                                              in0=sap(s1[c2], 0, 128, 0, [[8, 64], [1, 8]]),
                                                              in1=sap(p3, 0, 128, 0, [[1, 64], [64, 8]]), op=ALU.mult),
                              [sb1[c2], p3b], [mb1[c2]])
                        for c2, ch in enumerate(chs):
                            c = 2 * half + c2
                            p4, p4b = pb[6 + c2], pbb[6 + c2]
                            for k in range(4):
                                kb.op(pe, lambda: nc.tensor.matmul(p4[:, :], lhsT=wbb[:, 512 + k * 128:512 + (k + 1) * 128],
                                                                   rhs=oT[:, k, c * 512:(c + 1) * 512], start=(k == 0), stop=(k == 3)),
                                      reads=[wbbb, oTb[k][c]], writes=[p4b])
                            V(lambda: nc.vector.tensor_tensor(out=s2[c2][:, :], in0=s2[c2][:, :], in1=p4[:, :], op=ALU.mult),
                              [sb2[c2], p4b], [sb2[c2]])
                            V(lambda: nc.vector.tensor_tensor(out=mg[:, mo, c2 * 512:(c2 + 1) * 512], in0=m1[c2][:, :],
                                                              in1=s2[c2][:, :], op=ALU.add), [mb1[c2], sb2[c2]], [mgb[mo][c2]])
                    for mo in range(8):
                        if mo % 2 == 0:
                            ws, wb = stream.get()
                        wo0 = (mo % 2) * 1024
                        for c2 in range(2):
                            c = 2 * half + c2
                            pO, pOb = pb[(2 * mo + c2) % 8], pbb[(2 * mo + c2) % 8]
                            for k in range(8):
                                kb.op(pe, lambda: nc.tensor.matmul(pO[:, :], lhsT=ws[:, wo0 + k * 128:wo0 + (k + 1) * 128],
                                                                   rhs=mg[:, k, c2 * 512:(c2 + 1) * 512], start=(k == 0), stop=(k == 7)),
                                      reads=[wb, mgb[k][c2]], writes=[pOb])
                            hk = hT[:, mo, c * 512:(c + 1) * 512]
                            V(lambda: nc.vector.tensor_tensor(out=hk, in0=pO[:, :], in1=hk, op=ALU.add),
                              [pOb, hb[mo][c]], [hb[mo][c]])
                kb.barrier()

        def mixer(bl):
            with ExitStack() as mx:
                psb = psbf(mx)
                qT = psb("qoT", [128, 4, T], BF16)
                qTb = [[Buf() for c in range(4)] for i in range(4)]
                qTall = [b_ for r_ in qTb for b_ in r_]
                with ExitStack() as m2:
                    psb2 = psbf(m2)
                    kT = psb2("kT", [128, 4, T], BF16)
                    vd = psb2("vd", [128, 32, 4, 64], BF16)
                    kTb = [Buf() for i in range(4)]
                    vdb = Buf()
                    inproj_phase(bl, qT, qTb, kT, kTb, vd, vdb)
                    if dbg and bl == 0:
                        dump("qT", qT[:, :, :], [128, 4, T], qTall)
                        dump("kT", kT[:, :, :], [128, 4, T], kTb)
                        dump("vd", vd[:, :, :, :], [128, 32, 4, 64], [vdb])
                        kb.barrier()
                        kb.wait_all(sp, [c_out])
                        kb.wait_all(pe, [c_out])
                    if stage >= 7:
                        stream.extend(merge_items())
                    if stage >= 5:
                        na_phase(bl, qT, qTb, kT, kTb, vd, vdb, qT, qTb)
                        if dbg and bl == 0:
                            dump("oT", qT[:, :, :], [128, 4, T], qTall)
                            for E_ in (pe, act, dve, pool):
                                kb.wait_all(E_, [c_out])
                zT = psb("zT", [128, 4, 8, NBLK], BF16)
                zTb = [Buf() for c in range(4)]
                if stage >= 6:
                    s5_phase(bl, zT, zTb)
                    if dbg and bl == 0:
                        dump("zT", zT[:, :, :, :], [128, 4, 8, NBLK], zTb)
                        for E_ in (pe, act, dve, pool):
                            kb.wait_all(E_, [c_out])
                if stage >= 7:
                    merge_phase(bl, zT, zTb, qT, qTb)

        allhb = [hb[k][c] for k in range(8) for c in range(4)]
        if stage >= 2:
            setup_na()
        if stage >= 3:
            setup_s5()
        hT = sb("hT", [128, 8, T], F32)
        hm = sb("hm", [128, 8, NM], F32)
        stream = Stream(kb, nc, es, NSLOT, 2048)
        c_hm = kb.ctr("s_hm")
        kb.dma(sp, c_hm, hm[:, :, :], metaT[:, :, :], writes=hmb)
        for bl in range(nb_local):
            load_x(bl)
            stream.extend(ffn_items(w1i, w1o))
            if stage >= 4:
                stream.extend(mixer_items())
            ffn_phase(bl, 0, with_meta=(bl == 0))
            if dbg and bl == 0:
                dump("h1", hT[:, :, :], [128, 8, T], allhb)
                dump("hm1", hm[:, :, :], [128, 8, NM], hmb)
            if stage >= 4:
                mixer(bl)
                if dbg and bl == 0:
                    dump("h2", hT[:, :, :], [128, 8, T], allhb)
            stream.extend(ffn_items(w2i, w2o))
            ffn_phase(bl, 2, with_meta=False)
            final_phase(bl)
        kb.wait_all(sp, [c_out])
    return nc, dbg_outs


def host_inputs(inputs, core, nb_local=NB_LOCAL):
    f = np.float32
    x = inputs["x"]
    b0 = core * nb_local
    xT = np.ascontiguousarray(
        x[b0:b0 + nb_local].transpose(0, 2, 1).reshape(nb_local, 8, 128, T)).astype(f)
    metaT = np.ascontiguousarray(inputs["meta_tokens"].T.reshape(8, 128, NM).transpose(1, 0, 2)).astype(f)
    g = np.stack([inputs["norm_ffn1"][0], inputs["norm_mix"][0], inputs["norm_ffn2"][0], inputs["norm_final"]])
    gains = np.ascontiguousarray(g.reshape(4, 8, 128).transpose(2, 0, 1)).astype(f)

    def ffn_in(w):
        w = w.reshape(8, 128, 2, NFF, 128)
        return np.ascontiguousarray(w.transpose(3, 1, 0, 2, 4).reshape(NFF, 128, 8 * 256)).astype(f)

    def ffn_out(w):
        w = w.reshape(NFF, 128, 8, 128)
        return np.ascontiguousarray(w.transpose(2, 1, 0, 3).reshape(8, 128, NFF * 128)).astype(f)

    consts = np.zeros((128, 3, 128), f)
    consts[:, 0, :] = np.eye(128)
    consts[:, 1, :] = 1.0
    consts[:64, 2, :64] = 1.0
    consts[64:, 2, 64:] = 1.0
    w_in = inputs["w_in"][0]

    def tiles(w, nk, nm):
        return np.ascontiguousarray(w.reshape(nk, 128, nm, 128).transpose(2, 1, 0, 3).reshape(nm, 128, nk * 128)).astype(f)
    wx_ = tiles(w_in, 8, 32)
    wv = w_in[:, 1536:2048].reshape(8, 128, 2, 256)
    wv2 = np.ascontiguousarray(wv.transpose(2, 1, 0, 3).reshape(2, 128, 2048)).astype(f)
    rpb = inputs["na_rpb"][0]
    rpbpad = np.zeros((8, 15, 128), f)
    rpbpad[:, :, 48:48 + 31] = rpb[:, ::-1, :]
    qc = np.arange(64)
    cs = np.clip(qc - 8, 0, 48)
    kc = np.arange(64)
    cm = ((kc[:, None] >= cs[None, :]) & (kc[:, None] < cs[None, :] + 16)).astype(f)
    colmask = np.concatenate([cm, cm], axis=0)
    are = np.stack([inputs["ssm_a_re_fwd"][0], inputs["ssm_a_re_bwd"][0]])
    aim = np.stack([inputs["ssm_a_im_fwd"][0], inputs["ssm_a_im_bwd"][0]])
    ldt = np.stack([inputs["ssm_log_dt_fwd"][0], inputs["ssm_log_dt_bwd"][0]])
    ldt_gn = np.broadcast_to(ldt[:, :, None], (2, 32, 64))
    Bre, Bim = inputs["ssm_b_re"][0], inputs["ssm_b_im"][0]
    Cre, Cim = inputs["ssm_c_re"][0], inputs["ssm_c_im"][0]

    def layC2(a):
        return a.transpose(0, 2, 1).reshape(128, 32)
    s5c = np.ascontiguousarray(np.stack([layC2(are), layC2(aim), layC2(ldt_gn)], axis=1)).astype(f)

    def layCC(Cm):
        t = Cm.transpose(2, 0, 1).reshape(64, 512)
        return np.concatenate([t, t], axis=0)

    def layCB(Bm):
        t = Bm.transpose(1, 0, 2).reshape(64, 512)
        return np.concatenate([t, t], axis=0)
    s5cc = np.ascontiguousarray(np.stack([layCC(Cre), layCC(Cim)], axis=1)).astype(f)
    s5cb = np.ascontiguousarray(np.stack([layCB(Bre), layCB(Bim)], axis=1)).astype(f)
    j8 = np.arange(8)
    s5ce = np.zeros((128, 5, 8), f)
    s5ce[:64, 3] = 7 - j8
    s5ce[64:, 3] = j8
    s5ce[:64, 4] = 15 - j8
    s5ce[64:, 4] = j8 + 8
    s5ce[:64, 0] = j8 + 1
    s5ce[:64, 1] = j8
    s5ce[:64, 2] = -j8
    s5ce[64:, 0] = 8 - j8
    s5ce[64:, 1] = -j8
    s5ce[64:, 2] = j8
    Dm = inputs["ssm_d"][0]
    s5d = np.ascontiguousarray(np.broadcast_to(Dm.T[None], (8, 16, 32)).reshape(128, 32)).astype(f)
    ii = np.repeat(np.arange(8), 16)
    s5m = np.stack([(ii[:, None] <= ii[None, :]), (ii[:, None] >= ii[None, :]), np.eye(128, dtype=bool)], axis=1).astype(f)
    extra = {
        "wx": wx_, "wv2": wv2, "wglu": tiles(inputs["w_glu"][0], 4, 4), "wbs": tiles(inputs["w_branch_ssm"][0], 4, 8),
        "wbn": tiles(inputs["w_branch_na"][0], 4, 8),
        "wo": np.ascontiguousarray(tiles(inputs["w_out"][0], 8, 8).reshape(4, 2, 128, 1024).transpose(0, 2, 1, 3).reshape(4, 128, 2048)),
        "wgg": np.ascontiguousarray(np.concatenate([wx_[16:24], wx_[24:32]], axis=2)),
        "wbb": np.ascontiguousarray(np.concatenate([tiles(inputs["w_branch_ssm"][0], 4, 8), tiles(inputs["w_branch_na"][0], 4, 8)], axis=2)),
        "rpbpad": rpbpad, "colmask": colmask, "s5c": s5c, "s5cc": s5cc,
        "s5cb": s5cb, "s5ce": s5ce, "s5d": s5d, "s5m": np.ascontiguousarray(s5m),
    }
    return {
        **extra,
        "xT": xT, "metaT": metaT, "gains": gains,
        "w1i": ffn_in(inputs["w_ffn1_in"][0]), "w1o": ffn_out(inputs["w_ffn1_out"][0]),
        "w2i": ffn_in(inputs["w_ffn2_in"][0]), "w2o": ffn_out(inputs["w_ffn2_out"][0]),
        "consts": consts,
    }


def kernel(**inputs):
    inputs = {k: np.asarray(v) for k, v in inputs.items()}
    nc, _ = build_program()
    in_maps = [host_inputs(inputs, c) for c in range(NCORES)]
    res = run_bass_kernel_spmd(nc, in_maps, core_ids=list(range(NCORES)))
    out = np.empty((16, T, D), np.float32)
    for c in range(NCORES):
        o = np.asarray(res.results[c]["outT"])
        out[c * NB_LOCAL:(c + 1) * NB_LOCAL] = o.reshape(NB_LOCAL, D, T).transpose(0, 2, 1)
    return out
```

```python
import numpy as np
from contextlib import ExitStack
import concourse.bass as bass
import concourse.mybir as mybir
from concourse.bass_utils import run_bass_kernel_spmd

F32 = mybir.dt.float32
BF16 = mybir.dt.bfloat16
ALU = mybir.AluOpType
AF = mybir.ActivationFunctionType

D = 1024
T = 2048
NM = 16
DFF = 2816
NFF = 22
NG = 32
NBLK = 258
NB_LOCAL = 2
NCORES = 8
FF_SPLITS = [(0, 8), (8, 15), (15, 22)]
NSLOT = 2
TWO_PI = 6.283185307179586


class Ctr:
    def __init__(self, sem):
        self.sem = sem
        self.cnt = 0


class Buf:
    __slots__ = ("w", "r", "name")

    def __init__(self, name=""):
        self.w = None
        self.r = {}
        self.name = name


class Eng:
    def __init__(self, h, ctr, name):
        self.h = h
        self.ctr = ctr
        self.seen = {}
        self.name = name


class KB:
    def __init__(self, nc, es):
        self.nc = nc
        self.es = es

        def mk(h, n):
            return Eng(h, Ctr(es.enter_context(nc.semaphore(n))), n)

        self.pe = mk(nc.tensor, "s_pe")
        self.act = mk(nc.scalar, "s_act")
        self.dve = mk(nc.vector, "s_dve")
        self.pool = mk(nc.gpsimd, "s_pool")
        self.sp = mk(nc.sync, "s_sp")
        self.engs = [self.pe, self.act, self.dve, self.pool, self.sp]

    def ctr(self, name):
        return Ctr(self.es.enter_context(self.nc.semaphore(name)))

    def _need(self, E, dep, raw):
        ctr, val = dep
        if ctr is E.ctr and not raw:
            return
        if E.seen.get(id(ctr), 0) >= val:
            return
        E.h.wait_ge(ctr.sem, val)
        E.seen[id(ctr)] = val

    def _deps(self, E, reads, writes):
        for b in reads:
            if b.w is not None:
                self._need(E, b.w, True)
        for b in writes:
            if b.w is not None:
                self._need(E, b.w, False)
            for dep in b.r.values():
                self._need(E, dep, False)

    def op(self, E, fn, reads=(), writes=()):
        self._deps(E, reads, writes)
        ins = fn()
        E.ctr.cnt += 1
        ins.then_inc(E.ctr.sem, 1)
        for b in writes:
            b.w = (E.ctr, E.ctr.cnt)
            b.r = {}
        for b in reads:
            b.r[id(E.ctr)] = (E.ctr, E.ctr.cnt)

    def dma(self, Q, ctr, out, in_, reads=(), writes=()):
        self._deps(Q, reads, writes)
        Q.h.dma_start(out=out, in_=in_).then_inc(ctr.sem, 16)
        ctr.cnt += 16
        for b in writes:
            b.w = (ctr, ctr.cnt)
            b.r = {}
        for b in reads:
            b.r[id(ctr)] = (ctr, ctr.cnt)

    def barrier(self):
        for E in self.engs:
            for O in self.engs:
                if O is not E and O.ctr.cnt > 0:
                    self._need(E, (O.ctr, O.ctr.cnt), True)

    def wait_all(self, E, ctrs):
        for c in ctrs:
            if c.cnt > 0:
                self._need(E, (c, c.cnt), True)


class Stream:
    def __init__(self, kb, nc, es, nslot, cols):
        self.kb = kb
        self.items = []
        self.next_load = 0
        self.next_get = 0
        self.nslot = nslot
        self.slots = [es.enter_context(nc.sbuf_tensor(f"ring{i}", [128, cols], BF16)) for i in range(nslot)]
        self.bufs = [Buf(f"ring{i}") for i in range(nslot)]
        self.ctrs = [kb.ctr(f"s_ring{i}") for i in range(nslot)]

    def extend(self, items):
        self.items += items

    def _load(self, n):
        s = n % self.nslot
        it = self.items[n]
        src, ncols = it[0], it[1]
        if len(it) > 2:
            dst = sap(self.slots[s], 0, 128, 0, [[ncols // 2, 2], [1, ncols // 2]])
        else:
            dst = self.slots[s][:, 0:ncols]
        self.kb.dma(self.kb.pool, self.ctrs[s], dst, src, writes=[self.bufs[s]])

    def get(self):
        i = self.next_get
        self.next_get += 1
        while self.next_load < min(len(self.items), i + self.nslot):
            self._load(self.next_load)
            self.next_load += 1
        s = i % self.nslot
        return self.slots[s], self.bufs[s]


def sap(t, part0, nparts, off, dims):
    row = 1
    for d in t.shape[1:]:
        row *= d
    return bass.AP(t, part0 * row + off, [[row, nparts]] + [list(d) for d in dims])


def build_program(nb_local=NB_LOCAL, stage=99, dbg=False):
    nc = bass.Bass("TRN2", target_bir_lowering=False)

    def din(name, shape, dt=F32):
        return nc.dram_tensor(name, list(shape), dt, kind="ExternalInput").ap()

    xT = din("xT", [nb_local, 8, 128, T])
    metaT = din("metaT", [128, 8, NM])
    gains_d = din("gains", [128, 4, 8])
    w1i = din("w1i", [NFF, 128, 8 * 256])
    w1o = din("w1o", [8, 128, NFF * 128])
    w2i = din("w2i", [NFF, 128, 8 * 256])
    w2o = din("w2o", [8, 128, NFF * 128])
    consts_d = din("consts", [128, 3, 128])
    wx = din("wx", [32, 128, 1024])
    wv2 = din("wv2", [2, 128, 2048])
    wglu_d = din("wglu", [4, 128, 512])
    wbs_d = din("wbs", [8, 128, 512])
    wbn_d = din("wbn", [8, 128, 512])
    wo_d = din("wo", [4, 128, 2048])
    wgg_d = din("wgg", [8, 128, 2048])
    wbb_d = din("wbb", [8, 128, 1024])
    rpbpad = din("rpbpad", [8, 15, 128])
    colmask_d = din("colmask", [128, 64])
    s5c_d = din("s5c", [128, 3, 32])
    s5cc_d = din("s5cc", [128, 2, 512])
    s5cb_d = din("s5cb", [128, 2, 512])
    s5ce_d = din("s5ce", [128, 5, 8])
    s5d_d = din("s5d", [128, 32])
    s5m_d = din("s5m", [128, 3, 128])
    u_scr = nc.dram_tensor("u_scr", [512, 8, NBLK], BF16, kind="Internal").ap()
    z_scr = nc.dram_tensor("z_scr", [512, 8, NBLK], BF16, kind="Internal").ap()
    wb2_scr = nc.dram_tensor("wb2_scr", [128, NG * 512], BF16, kind="Internal").ap()
    u_scr2 = nc.dram_tensor("u_scr2", [128, NG, NBLK], BF16, kind="Internal").ap()
    z_scr2 = nc.dram_tensor("z_scr2", [128, NG, NBLK], BF16, kind="Internal").ap()
    outT = nc.dram_tensor("outT", [nb_local, 8, 128, T], F32, kind="ExternalOutput").ap()
    dbg_outs = {}

    with ExitStack() as es:
        kb = KB(nc, es)
        pe, act, dve, pool, sp = kb.pe, kb.act, kb.dve, kb.pool, kb.sp

        def sb(name, shape, dt):
            return es.enter_context(nc.sbuf_tensor(name, list(shape), dt))

        uid = [0]

        def uname(n):
            uid[0] += 1
            return f"{n}_{uid[0]}"


        Etab = sb("Etab", [128, 4, 15, 64], BF16)
        Etabb = Buf()
        kmT = sb("kmT", [128, 4, NM], BF16)
        kmTb = Buf()
        vm = sb("vm", [128, 4, 64], BF16)
        vmb = Buf()
        umT = sb("umT", [128, 4, 8, 2], BF16)
        umTb = Buf()
        WB = sb("WB", [128, NG, 2, 128], BF16)
        WBb = Buf()
        WC = sb("WC", [128, NG, 2, 128], BF16)
        WCb = Buf()
        WT = sb("WT", [128, NG, 128], BF16)
        WTb = Buf()
        scA = sb("scA", [128, 4, NG], F32)
        scB = sb("scB", [128, 4, NG], F32)
        wb2_b = Buf()
        scb = Buf()
        u_scr_b = Buf()
        z_scr_b = Buf()
        u2b = [Buf() for j in range(8)]
        zsb = [Buf() for j in range(8)]
        c_scr = kb.ctr("s_scr")

        hb = [[Buf(f"h{k}_{c}") for c in range(4)] for k in range(8)]
        hmb = [Buf(f"hm{k}") for k in range(8)]
        gains = sb("gains_sb", [128, 4, 8], F32)
        gainsb = Buf("gains")
        cst = sb("cst", [128, 3, 128], BF16)
        cstb = Buf("cst")
        epsb_t = sb("eps", [128, 1], F32)
        epsb = Buf("eps")
        pb = [es.enter_context(nc.psum_tensor(f"pb{i}", [128, 512], F32)) for i in range(8)]
        pbb = [Buf(f"pb{i}") for i in range(8)]
        ident = cst[:, 0, :]
        ones_bf = cst[:, 1, :]
        blockones = cst[:, 2, :]

        c_misc = kb.ctr("s_misc")
        c_x = kb.ctr("s_x")
        c_out = kb.ctr("s_out")

        kb.dma(sp, c_misc, gains[:, :, :], gains_d[:, :, :], writes=[gainsb])
        c_cst = kb.ctr("s_cst")
        kb.dma(pool, c_cst, cst[:, :, :], consts_d[:, :, :], writes=[cstb])
        kb.op(dve, lambda: nc.vector.memset(epsb_t[:, :], 1e-6), writes=[epsb])

        def load_x(bl):
            kb.wait_all(sp, [c_out])
            for c in range(4):
                for k in range(8):
                    kb.dma(sp, c_x, hT[:, k, c * 512:(c + 1) * 512], xT[bl, k, :, c * 512:(c + 1) * 512], writes=[hb[k][c]])
                for k in range(8):
                    hb[k][c].w = (c_x, c_x.cnt)
                kb.wait_all(sp, [c_x])

        def main_chunks(xn, xnb, at, atb):
            chs = []
            for c in range(4):
                chs.append(dict(
                    N=512,
                    h=(lambda k, c=c: hT[:, k, c * 512:(c + 1) * 512]),
                    hb=(lambda k, c=c: hb[k][c]),
                    xn=(lambda k, c=c: xn[:, k, c * 512:(c + 1) * 512]),
                    xnb=(lambda k, c=c: xnb[k][c]),
                    at=(lambda kk, c=c: at[:, kk, c * 512:(c + 1) * 512]),
                    atb=(lambda kk, c=c: atb[kk][c]),
                ))
            return chs

        def norm_chunk(ch, gi, scr, out_fn=None, outb_fn=None):
            N = ch["N"]
            sq, sqb, sd, sdb, rstd, rstdb = scr
            pR, pRb = pb[7], pbb[7]
            for k in range(8):
                kb.op(act, lambda: nc.scalar.activation(out=sq[k % 2][:, :N], in_=ch["h"](k), func=AF.Square),
                      reads=[ch["hb"](k)], writes=[sqb[k % 2]])
                kb.op(pe, lambda: nc.tensor.matmul(pR[:, :N], lhsT=ones_bf, rhs=sq[k % 2][:, :N],
                                                   start=(k == 0), stop=(k == 7)),
                      reads=[sqb[k % 2], cstb], writes=[pRb])
            kb.op(act, lambda: nc.scalar.activation(out=sd[:, :N], in_=pR[:, :N], func=AF.Sqrt,
                                                    scale=1.0 / D, bias=epsb_t[:, 0:1]),
                  reads=[pRb, epsb], writes=[sdb])
            kb.op(dve, lambda: nc.vector.reciprocal(out=rstd[:, :N], in_=sd[:, :N]), reads=[sdb], writes=[rstdb])
            for k in range(8):
                o = ch["xn"](k) if out_fn is None else out_fn(k)
                ob = ch["xnb"](k) if outb_fn is None else outb_fn(k)
                kb.op(dve, lambda: nc.vector.scalar_tensor_tensor(
                    out=o, in0=ch["h"](k), scalar=gains[:, gi, k:k + 1], in1=rstd[:, :N],
                    op0=ALU.mult, op1=ALU.mult),
                    reads=[ch["hb"](k), rstdb, gainsb], writes=[ob])

        def ffn_items(wi, wo):
            items = []
            for (a, b) in FF_SPLITS:
                for m in range(a, b):
                    items.append((wi[m, :, :], 2048))
                for mo2 in range(4):
                    src = bass.AP(wo.tensor, 2 * mo2 * 128 * NFF * 128 + a * 128,
                                  [[NFF * 128, 128], [128 * NFF * 128, 2], [1, (b - a) * 128]])
                    items.append((src, 2 * (b - a) * 128, True))
            return items

        def ffn(chunks, gi, scr, sg, sgb):
            for ch in chunks:
                norm_chunk(ch, gi, scr)
            gu = 0
            oi = 0
            for (a, b) in FF_SPLITS:
                nk = b - a
                for m in range(a, b):
                    ws, wb = stream.get()
                    for ch in chunks:
                        N = ch["N"]
                        pG, pGb = pb[gu % 2], pbb[gu % 2]
                        pU, pUb = pb[2 + gu % 2], pbb[2 + gu % 2]
                        for k in range(8):
                            kb.op(pe, lambda: nc.tensor.matmul(pG[:, :N], lhsT=ws[:, k * 256:k * 256 + 128],
                                                               rhs=ch["xn"](k), start=(k == 0), stop=(k == 7)),
                                  reads=[wb, ch["xnb"](k)], writes=[pGb])
                        for k in range(8):
                            kb.op(pe, lambda: nc.tensor.matmul(pU[:, :N], lhsT=ws[:, k * 256 + 128:k * 256 + 256],
                                                               rhs=ch["xn"](k), start=(k == 0), stop=(k == 7)),
                                  reads=[wb, ch["xnb"](k)], writes=[pUb])
                        s_ = sg[gu % 2]
                        kb.op(act, lambda: nc.scalar.activation(out=s_[:, :N], in_=pG[:, :N], func=AF.Silu),
                              reads=[pGb], writes=[sgb[gu % 2]])
                        kb.op(dve, lambda: nc.vector.tensor_tensor(out=ch["at"](m - a), in0=s_[:, :N], in1=pU[:, :N],
                                                                   op=ALU.mult),
                              reads=[sgb[gu % 2], pUb], writes=[ch["atb"](m - a)])
                        gu += 1
                for mo in range(8):
                    if mo % 2 == 0:
                        ws, wb = stream.get()
                    wo0 = (mo % 2) * nk * 128
                    for ch in chunks:
                        N = ch["N"]
                        pO, pOb = pb[4 + oi % 2], pbb[4 + oi % 2]
                        for kk in range(nk):
                            kb.op(pe, lambda: nc.tensor.matmul(pO[:, :N], lhsT=ws[:, wo0 + kk * 128:wo0 + (kk + 1) * 128],
                                                               rhs=ch["at"](kk), start=(kk == 0), stop=(kk == nk - 1)),
                                  reads=[wb, ch["atb"](kk)], writes=[pOb])
                        hk = ch["h"](mo)
                        kb.op(dve, lambda: nc.vector.scalar_tensor_tensor(out=hk, in0=pO[:, :N], scalar=0.5, in1=hk,
                                                                          op0=ALU.mult, op1=ALU.add),
                              reads=[pOb, ch["hb"](mo)], writes=[ch["hb"](mo)])
                        oi += 1

        def ffn_phase(bl, gi, with_meta):
            with ExitStack() as ph:
                def psb(name, shape, dt):
                    return ph.enter_context(nc.sbuf_tensor(uname(name), list(shape), dt))
                xn = psb("xn", [128, 8, T], BF16)
                xnb = [[Buf() for c in range(4)] for k in range(8)]
                at = psb("at", [128, 8, T], BF16)
                atb = [[Buf() for c in range(4)] for k in range(8)]
                sq = [psb(f"sq{i}", [128, 512], BF16) for i in range(2)]
                sqb = [Buf(), Buf()]
                sd = psb("sd", [128, 512], F32)
                rstd = psb("rstd", [128, 512], F32)
                sg = [psb(f"sg{i}", [128, 512], BF16) for i in range(2)]
                sgb = [Buf(), Buf()]
                scr = (sq, sqb, sd, Buf(), rstd, Buf())
                chunks = main_chunks(xn, xnb, at, atb)
                if with_meta:
                    xnm = psb("xnm", [128, 8, NM], BF16)
                    xnmb = [Buf() for k in range(8)]
                    atm = psb("atm", [128, 8, NM], BF16)
                    atmb = [Buf() for k in range(8)]
                    chunks.append(dict(N=NM, h=lambda k: hm[:, k, :], hb=lambda k: hmb[k],
                                       xn=lambda k: xnm[:, k, :], xnb=lambda k: xnmb[k],
                                       at=lambda kk: atm[:, kk, :], atb=lambda kk: atmb[kk]))
                ffn(chunks, gi, scr, sg, sgb)
                kb.barrier()

        def final_phase(bl):
            with ExitStack() as ph:
                def psb(name, shape, dt):
                    return ph.enter_context(nc.sbuf_tensor(uname(name), list(shape), dt))
                sq = [psb(f"fsq{i}", [128, 512], BF16) for i in range(2)]
                sd = psb("fsd", [128, 512], F32)
                rstd = psb("frstd", [128, 512], F32)
                scr = (sq, [Buf(), Buf()], sd, Buf(), rstd, Buf())
                chunks = main_chunks(None, None, None, None)
                for c, ch in enumerate(chunks):
                    norm_chunk(ch, 3, scr, out_fn=ch["h"], outb_fn=ch["hb"])
                    for k in range(8):
                        kb.dma(sp, c_out, outT[bl, k, :, c * 512:(c + 1) * 512], hT[:, k, c * 512:(c + 1) * 512],
                               reads=[hb[k][c]])
                kb.barrier()

        def dump(name, ap, shape, reads):
            t = nc.dram_tensor("dbg_" + name, list(shape), ap.dtype, kind="ExternalOutput").ap()
            dbg_outs[name] = t
            kb.dma(sp, c_out, t, ap, reads=reads)

        def psbf(ph):
            def f(name, shape, dt):
                return ph.enter_context(nc.sbuf_tensor(uname(name), list(shape), dt))
            return f

        def V(fn, reads, writes):
            kb.op(dve, fn, reads=reads, writes=writes)

        def A(fn, reads, writes):
            kb.op(act, fn, reads=reads, writes=writes)

        def setup_na():
            with ExitStack() as ph:
                psb = psbf(ph)
                c_s = kb.ctr("s_setna")
                raw = psb("eraw", [128, 4, 15, 64], F32)
                rawb = Buf()
                exn = psb("eexp", [128, 4, 15, 64], F32)
                exb = Buf()
                cm = psb("cmask", [128, 64], F32)
                cmb = Buf()
                kb.dma(sp, c_s, cm[:, :], colmask_d[:, :], writes=[cmb])
                for hp in range(4):
                    for h2 in range(2):
                        h = 2 * hp + h2
                        src = bass.AP(rpbpad.tensor, h * 15 * 128, [[1, 64], [128, 15], [1, 64]])
                        kb.dma(sp, c_s, raw[h2 * 64:(h2 + 1) * 64, hp, :, :], src, writes=[rawb])
                rawb.w = (c_s, c_s.cnt)
                cmb.w = (c_s, c_s.cnt)
                A(lambda: nc.scalar.activation(out=sap(exn, 0, 128, 0, [[1, 3840]]), in_=sap(raw, 0, 128, 0, [[1, 3840]]),
                                               func=AF.Exp), [rawb], [exb])
                V(lambda: nc.vector.tensor_tensor(out=sap(Etab, 0, 128, 0, [[64, 60], [1, 64]]),
                                                  in0=sap(exn, 0, 128, 63, [[64, 60], [-1, 64]]),
                                                  in1=sap(cm, 0, 128, 0, [[0, 60], [1, 64]]), op=ALU.mult),
                  [exb, cmb], [Etabb])
                kb.barrier()

        MAGIC = 12582912.0

        def cexp(xr, xi, ore, oim, t, F_, b):
            R = [b]
            A(lambda: nc.scalar.activation(out=xr, in_=xr, func=AF.Exp), R, R)
            V(lambda: nc.vector.tensor_scalar(out=xi, in0=xi, scalar1=1.0 / TWO_PI, scalar2=None, op0=ALU.mult), R, R)
            for which, o in ((0, oim), (1, ore)):
                if which == 1:
                    V(lambda: nc.vector.tensor_scalar(out=xi, in0=xi, scalar1=0.25, scalar2=None, op0=ALU.add), R, R)
                V(lambda: nc.vector.tensor_scalar(out=t, in0=xi, scalar1=MAGIC, scalar2=None, op0=ALU.add), R, R)
                V(lambda: nc.vector.tensor_scalar(out=t, in0=t, scalar1=MAGIC, scalar2=None, op0=ALU.subtract), R, R)
                V(lambda: nc.vector.tensor_tensor(out=t, in0=xi, in1=t, op=ALU.subtract), R, R)
                A(lambda: nc.scalar.activation(out=t, in_=t, func=AF.Sin, scale=TWO_PI), R, R)
                V(lambda: nc.vector.tensor_tensor(out=o, in0=xr, in1=t, op=ALU.mult), R, R)

        def cmul(ore, oim, ar_, ai_, br_, bi_, t1, t2, R, W, neg_im=False):
            V(lambda: nc.vector.tensor_tensor(out=t1, in0=ar_, in1=br_, op=ALU.mult), R, R)
            V(lambda: nc.vector.tensor_tensor(out=t2, in0=ai_, in1=bi_, op=ALU.mult), R, R)
            V(lambda: nc.vector.tensor_tensor(out=ore, in0=t1, in1=t2, op=ALU.subtract), R, W)
            V(lambda: nc.vector.tensor_tensor(out=t1, in0=ar_, in1=bi_, op=ALU.mult), R, R)
            V(lambda: nc.vector.tensor_tensor(out=t2, in0=ai_, in1=br_, op=ALU.mult), R, R)
            if neg_im:
                V(lambda: nc.vector.scalar_tensor_tensor(out=oim, in0=t1, scalar=-1.0, in1=t2, op0=ALU.mult,
                                                         op1=ALU.subtract), R, W)
            else:
                V(lambda: nc.vector.tensor_tensor(out=oim, in0=t1, in1=t2, op=ALU.add), R, W)

        def lam_and_f(are, aim, ldt, ardt, aidt, lre, lim, fre, fim, t1, t2, t3, F_, b):
            R = [b]
            A(lambda: nc.scalar.activation(out=ldt, in_=ldt, func=AF.Exp), R, R)
            V(lambda: nc.vector.tensor_scalar(out=are, in0=are, scalar1=-1e-4, scalar2=None, op0=ALU.min), R, R)
            V(lambda: nc.vector.tensor_tensor(out=ardt, in0=are, in1=ldt, op=ALU.mult), R, R)
            V(lambda: nc.vector.tensor_tensor(out=aidt, in0=aim, in1=ldt, op=ALU.mult), R, R)
            V(lambda: nc.vector.tensor_copy(out=t1, in_=ardt), R, R)
            V(lambda: nc.vector.tensor_copy(out=t2, in_=aidt), R, R)
            cexp(t1, t2, lre, lim, t3, F_, b)
            V(lambda: nc.vector.tensor_tensor(out=t1, in0=are, in1=are, op=ALU.mult), R, R)
            V(lambda: nc.vector.tensor_tensor(out=t2, in0=aim, in1=aim, op=ALU.mult), R, R)
            V(lambda: nc.vector.tensor_tensor(out=t3, in0=t1, in1=t2, op=ALU.add), R, R)
            V(lambda: nc.vector.reciprocal(out=t3, in_=t3), R, R)
            V(lambda: nc.vector.tensor_scalar(out=t1, in0=lre, scalar1=-1.0, scalar2=None, op0=ALU.add), R, R)
            V(lambda: nc.vector.tensor_tensor(out=fre, in0=t1, in1=are, op=ALU.mult), R, R)
            V(lambda: nc.vector.tensor_tensor(out=t2, in0=lim, in1=aim, op=ALU.mult), R, R)
            V(lambda: nc.vector.tensor_tensor(out=fre, in0=fre, in1=t2, op=ALU.add), R, R)
            V(lambda: nc.vector.tensor_tensor(out=fre, in0=fre, in1=t3, op=ALU.mult), R, R)
            V(lambda: nc.vector.tensor_tensor(out=fim, in0=lim, in1=are, op=ALU.mult), R, R)
            V(lambda: nc.vector.tensor_tensor(out=t2, in0=t1, in1=aim, op=ALU.mult), R, R)
            V(lambda: nc.vector.tensor_tensor(out=fim, in0=fim, in1=t2, op=ALU.subtract), R, R)
            V(lambda: nc.vector.tensor_tensor(out=fim, in0=fim, in1=t3, op=ALU.mult), R, R)

        def setup_s5():
            c_s = kb.ctr("s_sets5")
            with ExitStack() as ph:
                psb = psbf(ph)
                b = Buf()
                F_ = NG
                names = ["are", "aim", "ldt", "ardt", "aidt", "lre", "lim", "fre", "fim", "t1", "t2", "t3"]
                tl = {n: psb("C" + n, [128, F_], F32) for n in names}
                fl = {n: tl[n][:, :] for n in names}
                Cc = psb("CC", [128, 2, 512], F32)
                Bc = psb("CB", [128, 2, 512], F32)
                eps = psb("Ceps", [128, 5, 8], F32)
                Dt = psb("CD", [128, NG], F32)
                msk = psb("Cmsk", [128, 3, 128], F32)
                for wi_, n in enumerate(["are", "aim", "ldt"]):
                    kb.dma(sp, c_s, tl[n][:, :], s5c_d[:, wi_, :], writes=[b])
                kb.dma(sp, c_s, Cc[:, :, :], s5cc_d[:, :, :], writes=[b])
                kb.dma(sp, c_s, Bc[:, :, :], s5cb_d[:, :, :], writes=[b])
                kb.dma(sp, c_s, eps[:, :, :], s5ce_d[:, :, :], writes=[b])
                kb.dma(sp, c_s, Dt[:, :], s5d_d[:, :], writes=[b])
                kb.dma(sp, c_s, msk[:, :, :], s5m_d[:, :, :], writes=[b])
                b.w = (c_s, c_s.cnt)
                R = [b]
                lam_and_f(fl["are"], fl["aim"], fl["ldt"], fl["ardt"], fl["aidt"], fl["lre"], fl["lim"],
                          fl["fre"], fl["fim"], fl["t1"], fl["t2"], fl["t3"], F_, b)
                V(lambda: nc.vector.tensor_scalar(out=fl["t1"], in0=fl["ardt"], scalar1=16.0, scalar2=None, op0=ALU.mult), R, R)
                V(lambda: nc.vector.tensor_scalar(out=fl["t2"], in0=fl["aidt"], scalar1=16.0, scalar2=None, op0=ALU.mult), R, R)
                cexp(fl["t1"], fl["t2"], fl["lre"], fl["lim"], fl["t3"], F_, b)
                for c_ in range(2):
                    for par_ in range(2):
                        oA = sap(scA, 0, 128, c_ * NG * 2 + par_, [[2, NG]])
                        oB = sap(scB, 0, 128, c_ * NG * 2 + par_, [[2, NG]])
                        V(lambda: nc.vector.tensor_copy(out=oA, in_=fl["lre"]), R, [scb])
                        if c_ == 0:
                            V(lambda: nc.vector.tensor_scalar(out=oB, in0=fl["lim"], scalar1=-1.0, scalar2=None, op0=ALU.mult),
                              R, [scb])
                        else:
                            V(lambda: nc.vector.tensor_copy(out=oB, in_=fl["lim"]), R, [scb])
                pw = {}
                G8 = NG * 8
                xr = psb("Cxr", [128, G8], F32)
                xi = psb("Cxi", [128, G8], F32)
                tt = psb("Ctt", [128, G8], F32)
                for ei, nm in enumerate(["C", "G", "A", "B", "B2"]):
                    pr = psb("Cp%sr" % nm, [128, G8], F32)
                    pi_ = psb("Cp%si" % nm, [128, G8], F32)
                    e_b = sap(eps, 0, 128, ei * 8, [[0, NG], [1, 8]])
                    V(lambda: nc.vector.tensor_tensor(out=sap(xr, 0, 128, 0, [[8, NG], [1, 8]]),
                                                      in0=sap(tl["ardt"], 0, 128, 0, [[1, NG], [0, 8]]), in1=e_b,
                                                      op=ALU.mult), R, R)
                    V(lambda: nc.vector.tensor_tensor(out=sap(xi, 0, 128, 0, [[8, NG], [1, 8]]),
                                                      in0=sap(tl["aidt"], 0, 128, 0, [[1, NG], [0, 8]]), in1=e_b,
                                                      op=ALU.mult), R, R)
                    cexp(xr[:, :], xi[:, :], pr[:, :], pi_[:, :], tt[:, :], G8, b)
                    pw[nm] = (pr, pi_)
                fr_b = sap(tl["fre"], 0, 128, 0, [[1, NG], [0, 8]])
                fi_b = sap(tl["fim"], 0, 128, 0, [[1, NG], [0, 8]])

                def g8(t_):
                    return sap(t_, 0, 128, 0, [[8, NG], [1, 8]])
                far = psb("Cfar", [128, G8], F32)
                fai = psb("Cfai", [128, G8], F32)
                cmul(g8(far), g8(fai), fr_b, fi_b, g8(pw["A"][0]), g8(pw["A"][1]), g8(xr), g8(xi), R, R)
                BIG = NG * 8 * 16
                t1b = psb("Ct1b", [128, BIG], F32)
                t2b = psb("Ct2b", [128, BIG], F32)
                Are = psb("CAre", [128, NG, 128], BF16)
                Aim = psb("CAim", [128, NG, 128], BF16)
                Gre = psb("CGre", [128, NG, 128], BF16)
                Gim = psb("CGim", [128, NG, 128], BF16)

                def bigv(t_):
                    return sap(t_, 0, 128, 0, [[128, NG], [16, 8], [1, 16]])

                def pwv(t_):
                    return sap(t_, 0, 128, 0, [[8, NG], [1, 8], [0, 16]])

                def cpv(t_, c):
                    return sap(t_, 0, 128, c * 512, [[16, NG], [0, 8], [1, 16]])
                cmul(sap(WC, 0, 128, 0, [[256, NG], [16, 8], [1, 16]]), sap(WC, 0, 128, 128, [[256, NG], [16, 8], [1, 16]]),
                     cpv(Cc, 0), cpv(Cc, 1), pwv(pw["C"][0]), pwv(pw["C"][1]), bigv(t1b), bigv(t2b), R, [b, WCb], neg_im=True)
                cmul(bigv(Gre), bigv(Gim), cpv(Cc, 0), cpv(Cc, 1), pwv(pw["G"][0]), pwv(pw["G"][1]), bigv(t1b), bigv(t2b),
                     R, R, neg_im=True)
                cmul(bigv(Are), bigv(Aim), pwv(far), pwv(fai), cpv(Bc, 0), cpv(Bc, 1), bigv(t1b), bigv(t2b), R, R)
                fbr = psb("Cfbr", [128, G8], F32)
                fbi = psb("Cfbi", [128, G8], F32)
                cmul(g8(fbr), g8(fbi), fr_b, fi_b, g8(pw["B"][0]), g8(pw["B"][1]), g8(xr), g8(xi), R, R)
                PBc = [psb("CPBre", [128, NG, 128], BF16), psb("CPBim", [128, NG, 128], BF16)]
                cmul(bigv(PBc[0]), bigv(PBc[1]), pwv(fbr), pwv(fbi), cpv(Bc, 0), cpv(Bc, 1), bigv(t1b), bigv(t2b), R, R)
                for g2 in range(NG // 2):
                    pw_, pwb_ = pb[2 + g2 % 2], pbb[2 + g2 % 2]
                    for gg in range(2):
                        for c in range(2):
                            blk = gg * 2 + c
                            kb.op(pe, lambda: nc.tensor.matmul(pw_[:, blk * 128:(blk + 1) * 128], lhsT=PBc[c][:, 2 * g2 + gg, :],
                                                               rhs=ident, start=True, stop=True),
                                  reads=[b, cstb], writes=[pwb_])
                    V(lambda: nc.vector.tensor_copy(out=sap(WB, 0, 128, 2 * g2 * 256, [[1, 512]]), in_=pw_[:, :]),
                      [pwb_], [WBb])
                cmul(g8(fbr), g8(fbi), fr_b, fi_b, g8(pw["B2"][0]), g8(pw["B2"][1]), g8(xr), g8(xi), R, R)
                cmul(bigv(PBc[0]), bigv(PBc[1]), pwv(fbr), pwv(fbi), cpv(Bc, 0), cpv(Bc, 1), bigv(t1b), bigv(t2b), R, R)
                wb2t = psb("Cwb2t", [128, NG * 512], BF16)
                wb2tb = Buf()
                V(lambda: nc.vector.memset(wb2t[:, :], 0.0), [], [wb2tb])
                for g2 in range(NG // 2):
                    pw_, pwb_ = pb[2 + g2 % 2], pbb[2 + g2 % 2]
                    for gg in range(2):
                        for c in range(2):
                            blk = gg * 2 + c
                            kb.op(pe, lambda: nc.tensor.matmul(pw_[:, blk * 128:(blk + 1) * 128], lhsT=PBc[c][:, 2 * g2 + gg, :],
                                                               rhs=ident, start=True, stop=True),
                                  reads=[b, cstb], writes=[pwb_])
                    V(lambda: nc.vector.tensor_copy(out=sap(wb2t, 0, 128, 2 * g2 * 512, [[256, 4], [1, 64]]),
                                                    in_=sap(pw_, 0, 128, 0, [[128, 4], [1, 64]])), [pwb_], [wb2tb])
                    V(lambda: nc.vector.tensor_copy(out=sap(wb2t, 0, 128, 2 * g2 * 512 + 128 + 64, [[256, 4], [1, 64]]),
                                                    in_=sap(pw_, 0, 128, 64, [[128, 4], [1, 64]])), [pwb_], [wb2tb])
                kb.dma(sp, c_scr, wb2_scr[:, :], wb2t[:, :], reads=[wb2tb], writes=[wb2_b])
                kb.wait_all(sp, [c_scr])
                tmpf = psb("Ctmpf", [128, 512], F32)
                for g4 in range(NG // 4):
                    pf, pfb = pb[0], pbb[0]
                    pq, pqb = pb[1], pbb[1]
                    for gg in range(4):
                        g = g4 * 4 + gg
                        for (P_, Pb_, lo) in ((pf, pfb, 0), (pq, pqb, 64)):
                            kb.op(pe, lambda: nc.tensor.matmul(P_[:, gg * 128:(gg + 1) * 128], lhsT=Are[lo:lo + 64, g, :],
                                                               rhs=Gre[lo:lo + 64, g, :], start=True, stop=False),
                                  reads=[b], writes=[Pb_])
                            kb.op(pe, lambda: nc.tensor.matmul(P_[:, gg * 128:(gg + 1) * 128], lhsT=Aim[lo:lo + 64, g, :],
                                                               rhs=Gim[lo:lo + 64, g, :], start=False, stop=True),
                                  reads=[b], writes=[Pb_])
                    mL = sap(msk, 0, 128, 0, [[0, 4], [1, 128]])
                    mU = sap(msk, 0, 128, 128, [[0, 4], [1, 128]])
                    t4 = sap(tmpf, 0, 128, 0, [[128, 4], [1, 128]])
                    V(lambda: nc.vector.tensor_tensor(out=t4, in0=sap(pf, 0, 128, 0, [[128, 4], [1, 128]]), in1=mL,
                                                      op=ALU.mult), [pfb, b], [b])
                    V(lambda: nc.vector.tensor_tensor(out=sap(t1b, 0, 128, 0, [[128, 4], [1, 128]]),
                                                      in0=sap(pq, 0, 128, 0, [[128, 4], [1, 128]]), in1=mU, op=ALU.mult),
                      [pqb, b], [b])
                    V(lambda: nc.vector.tensor_tensor(out=t4, in0=t4, in1=sap(t1b, 0, 128, 0, [[128, 4], [1, 128]]),
                                                      op=ALU.add), R, R)
                    for gg in range(4):
                        g = g4 * 4 + gg
                        V(lambda: nc.vector.scalar_tensor_tensor(out=WT[:, g, :], in0=msk[:, 2, :], scalar=Dt[:, g:g + 1],
                                                                 in1=tmpf[:, gg * 128:(gg + 1) * 128], op0=ALU.mult,
                                                                 op1=ALU.add), R, [b, WTb])
                kb.barrier()
        def mixer_items():
            items = []
            for mt2 in range(6):
                src = bass.AP(wx.tensor, 2 * mt2 * 128 * 1024, [[1024, 128], [128 * 1024, 2], [1, 1024]])
                items.append((src, 2048, True))
            for half in range(2):
                items.append((wv2[half, :, :], 2048))
            return items

        def krs_for(qg):
            out = []
            for kr in range(32):
                rows = [r for r in range(8 * qg, 8 * qg + 8)
                        if min(max(r - 4, 0), 24) <= kr <= min(max(r - 4, 0), 24) + 7]
                if rows:
                    out.append((kr, rows[0], rows[-1]))
            return out

        def inproj_phase(bl, qT, qTb, kT, kTb, vd, vdb):
            with ExitStack() as ph:
                psb = psbf(ph)
                xn2 = psb("xn2", [128, 8, T], BF16)
                xn2b = [[Buf() for c in range(4)] for k in range(8)]
                sq = [psb(f"sq{i}", [128, 512], BF16) for i in range(2)]
                scr = (sq, [Buf(), Buf()], pb[6], pbb[6], pb[6], pbb[6])
                chunks = main_chunks(xn2, xn2b, None, None)
                with_meta = (bl == 0)
                if with_meta:
                    xnm = psb("xn2m", [128, 8, NM], BF16)
                    xnmb = [Buf() for k in range(8)]
                    chunks.append(dict(N=NM, h=lambda k: hm[:, k, :], hb=lambda k: hmb[k],
                                       xn=lambda k: xnm[:, k, :], xnb=lambda k: xnmb[k]))
                for ch in chunks:
                    norm_chunk(ch, 1, scr)
                ust = [psb(f"ust{i}", [128, 8, 64], BF16) for i in range(2)]
                ustb = [Buf(), Buf()]
                ui = 0
                it = 0
                for mt in range(12):
                    if mt % 2 == 0:
                        ws, wb = stream.get()
                    wx0 = (mt % 2) * 1024
                    for ci, ch in enumerate(chunks):
                        N = ch["N"]
                        is_meta = (ci == 4)
                        if is_meta and 4 <= mt < 8:
                            continue
                        pS, pSb = pb[it % 4], pbb[it % 4]
                        it += 1
                        for k in range(8):
                            kb.op(pe, lambda: nc.tensor.matmul(pS[:, :N], lhsT=ws[:, wx0 + k * 128:wx0 + (k + 1) * 128], rhs=ch["xn"](k),
                                                               start=(k == 0), stop=(k == 7)),
                                  reads=[wb, ch["xnb"](k)], writes=[pSb])
                        if mt < 4:
                            if is_meta:
                                o = sap(umT, 0, 128, mt * 16, [[2, 8], [1, 2]])
                                i_ = sap(pS, 0, 128, 0, [[1, 8], [8, 2]])
                                V(lambda: nc.vector.tensor_copy(out=o, in_=i_), [pSb], [umTb])
                            else:
                                us_, usb_ = ust[ui % 2], ustb[ui % 2]
                                ui += 1
                                i_ = sap(pS, 0, 128, 0, [[1, 8], [8, 64]])
                                V(lambda: nc.vector.tensor_copy(out=us_[:, :, :], in_=i_), [pSb], [usb_])
                                kb.dma(sp, c_scr, u_scr[mt * 128:(mt + 1) * 128, :, 2 + 64 * ci:2 + 64 * ci + 64], us_[:, :, :],
                                       reads=[usb_], writes=[u_scr_b])
                        elif mt < 8:
                            A(lambda: nc.scalar.copy(out=qT[:, mt - 4, ci * 512:(ci + 1) * 512], in_=pS[:, :N]),
                              [pSb], [qTb[mt - 4][ci]])
                        else:
                            if is_meta:
                                A(lambda: nc.scalar.copy(out=kmT[:, mt - 8, :], in_=pS[:, :N]), [pSb], [kmTb])
                            else:
                                A(lambda: nc.scalar.copy(out=kT[:, mt - 8, ci * 512:(ci + 1) * 512], in_=pS[:, :N]),
                                  [pSb], [kTb[mt - 8]])
                    if mt < 4:
                        kb.dma(sp, c_scr, u_scr[mt * 128:(mt + 1) * 128, :, 0:2], umT[:, mt, :, :], reads=[umTb],
                               writes=[u_scr_b])
                for j in range(8):
                    srcA = bass.AP(u_scr.tensor, j * NBLK, [[8 * NBLK, 16], [16 * 8 * NBLK, NG], [1, NBLK]])
                    dstA = bass.AP(u_scr2.tensor, j * 16 * NG * NBLK, [[NG * NBLK, 16], [NBLK, NG], [1, NBLK]])
                    kb.dma(sp, c_scr, dstA, srcA, reads=[u_scr_b], writes=[u2b[j]])
                for j in range(8):
                    u2b[j].w = (c_scr, c_scr.cnt)
                for half in range(2):
                    ws, wb = stream.get()
                    for w in range(-1, 32):
                        pS, pSb = pb[it % 4], pbb[it % 4]
                        it += 1
                        if w == -1:
                            t0, M, p0 = 0, 64, 64
                        elif w == 31:
                            t0, M, p0 = 64 * 31, 64, 0
                        else:
                            t0, M, p0 = 64 * w, 128, 0
                        rb_ = [xn2b[0][t0 // 512], xn2b[0][(t0 + M - 1) // 512]]
                        for k in range(8):
                            kb.op(pe, lambda: nc.tensor.matmul(pS[p0:p0 + M, 0:256], lhsT=xn2[:, k, t0:t0 + M],
                                                               rhs=ws[:, k * 256:(k + 1) * 256], start=(k == 0), stop=(k == 7)),
                                  reads=[wb, xn2b[k][t0 // 512], xn2b[k][(t0 + M - 1) // 512]], writes=[pSb])
                        if w >= 0:
                            V(lambda: nc.vector.tensor_copy(out=vd[0:64, w, 2 * half:2 * half + 2, :],
                                                            in_=sap(pS, 0, 64, 0, [[128, 2], [1, 64]])), [pSb], [vdb])
                        if w < 31:
                            A(lambda: nc.scalar.copy(out=vd[64:128, w + 1, 2 * half:2 * half + 2, :],
                                                     in_=sap(pS, 64, 64, 64, [[128, 2], [1, 64]])), [pSb], [vdb])
                    if with_meta:
                        pS, pSb = pb[it % 4], pbb[it % 4]
                        it += 1
                        for p0 in (0, 64):
                            for k in range(8):
                                kb.op(pe, lambda: nc.tensor.matmul(pS[p0:p0 + NM, 0:256], lhsT=xnm[:, k, :],
                                                                   rhs=ws[:, k * 256:(k + 1) * 256], start=(k == 0),
                                                                   stop=(k == 7)),
                                      reads=[wb, xnmb[k]], writes=[pSb])
                        V(lambda: nc.vector.tensor_copy(out=vm[0:NM, 2 * half:2 * half + 2, :],
                                                        in_=sap(pS, 0, NM, 0, [[128, 2], [1, 64]])), [pSb], [vmb])
                        V(lambda: nc.vector.tensor_copy(out=vm[64:64 + NM, 2 * half:2 * half + 2, :],
                                                        in_=sap(pS, 64, NM, 64, [[128, 2], [1, 64]])), [pSb], [vmb])
                kb.barrier()

        def na_phase(bl, qT, qTb, kT, kTb, vd, vdb, oT, oTb):
            LA = 3
            NSB = 4
            with ExitStack() as ph:
                psb = psbf(ph)
                exs = [psb(f"nex{i}", [128, 512], BF16) for i in range(NSB)]
                exb = [Buf() for i in range(NSB)]
                Ps = [psb(f"nP{i}", [128, 512], BF16) for i in range(NSB)]
                Pb = [Buf() for i in range(NSB)]
                pms = [psb(f"npm{i}", [128, 512], BF16) for i in range(2)]
                pmb = [Buf(), Buf()]
                rec = psb("nrec", [128, 512], F32)
                recb = Buf()
                for i in range(2):
                    V(lambda: nc.vector.memset(pms[i][:, :], 0.0), [], [pmb[i]])
                tiles_ = []
                gi_ = 0
                for hp in range(4):
                    for qg in range(4):
                        lst = krs_for(qg)
                        tiles_.append(dict(kind="meta", hp=hp, qg=qg, gi=gi_, first=True, last=False))
                        for idx, (kr, ra, rb2) in enumerate(lst):
                            tiles_.append(dict(kind="kr", hp=hp, qg=qg, gi=gi_, kr=kr, ra=ra, rb=rb2, first=False,
                                               last=(idx == len(lst) - 1)))
                        gi_ += 1

                def emit_S(t, tl_):
                    hp, qg = tl_["hp"], tl_["qg"]
                    pS, pSb = pb[t % NSB], pbb[t % NSB]
                    if tl_["kind"] == "meta":
                        tok0 = 512 * qg
                        for lo in (0, 64):
                            kb.op(pe, lambda: nc.tensor.matmul(pS[lo:lo + NM, 0:512], lhsT=kmT[lo:lo + 64, hp, :],
                                                               rhs=qT[lo:lo + 64, hp, tok0:tok0 + 512], start=True, stop=True),
                                  reads=[kmTb, qTb[hp][qg]], writes=[pSb])
                    else:
                        kr, ra, rb2 = tl_["kr"], tl_["ra"], tl_["rb"]
                        N = 64 * (rb2 - ra + 1)
                        for lo in (0, 64):
                            kb.op(pe, lambda: nc.tensor.matmul(pS[lo:lo + 64, 0:N], lhsT=kT[lo:lo + 64, hp, 64 * kr:64 * kr + 64],
                                                               rhs=qT[lo:lo + 64, hp, 64 * ra:64 * (rb2 + 1)],
                                                               start=True, stop=True),
                                  reads=[kTb[hp], qTb[hp][qg]], writes=[pSb])

                def emit_E(t, tl_):
                    hp, qg = tl_["hp"], tl_["qg"]
                    pS, pSb = pb[t % NSB], pbb[t % NSB]
                    if tl_["kind"] == "meta":
                        pm, pmb_ = pms[tl_["gi"] % 2], pmb[tl_["gi"] % 2]
                        for lo in (0, 64):
                            A(lambda: nc.scalar.activation(out=pm[lo:lo + NM, :], in_=pS[lo:lo + NM, 0:512], func=AF.Exp,
                                                           scale=0.125), [pSb], [pmb_])
                    else:
                        kr, ra, rb2 = tl_["kr"], tl_["ra"], tl_["rb"]
                        N = 64 * (rb2 - ra + 1)
                        ex_, exb_ = exs[t % NSB], exb[t % NSB]
                        P_, Pb_ = Ps[t % NSB], Pb[t % NSB]
                        A(lambda: nc.scalar.activation(out=ex_[:, 0:N], in_=pS[:, 0:N], func=AF.Exp, scale=0.125),
                          [pSb], [exb_])
                        rho = ra - kr + 7
                        V(lambda: nc.vector.tensor_tensor(out=P_[:, 0:N], in0=ex_[:, 0:N],
                                                          in1=sap(Etab, 0, 128, (hp * 15 + rho) * 64, [[1, N]]),
                                                          op=ALU.mult), [exb_, Etabb], [Pb_])

                def emit_PV(t, tl_):
                    hp, qg, g_ = tl_["hp"], tl_["qg"], tl_["gi"]
                    pO, pOb = pb[4 + g_ % 2], pbb[4 + g_ % 2]
                    pD, pDb = pb[6 + g_ % 2], pbb[6 + g_ % 2]
                    if tl_["kind"] == "meta":
                        pm, pmb_ = pms[g_ % 2], pmb[g_ % 2]
                        for lo in (0, 64):
                            kb.op(pe, lambda: nc.tensor.matmul(pO[lo:lo + 64, :], lhsT=vm[lo:lo + NM, hp, :],
                                                               rhs=pm[lo:lo + NM, :], start=True, stop=False),
                                  reads=[vmb, pmb_], writes=[pOb])
                        kb.op(pe, lambda: nc.tensor.matmul(pD[:, :], lhsT=blockones, rhs=pm[:, :], start=True, stop=False),
                              reads=[cstb, pmb_], writes=[pDb])
                    else:
                        kr, ra, rb2 = tl_["kr"], tl_["ra"], tl_["rb"]
                        N = 64 * (rb2 - ra + 1)
                        c0 = 64 * (ra - 8 * qg)
                        last = tl_["last"]
                        P_, Pb_ = Ps[t % NSB], Pb[t % NSB]
                        for lo in (0, 64):
                            kb.op(pe, lambda: nc.tensor.matmul(pO[lo:lo + 64, c0:c0 + N], lhsT=vd[lo:lo + 64, kr, hp, :],
                                                               rhs=P_[lo:lo + 64, 0:N], start=False, stop=last),
                                  reads=[vdb, Pb_], writes=[pOb])
                        kb.op(pe, lambda: nc.tensor.matmul(pD[:, c0:c0 + N], lhsT=blockones, rhs=P_[:, 0:N],
                                                           start=False, stop=last),
                              reads=[cstb, Pb_], writes=[pDb])
                        if last:
                            tok0 = 512 * qg
                            V(lambda: nc.vector.reciprocal(out=rec[:, :], in_=pD[:, :]), [pDb], [recb])
                            V(lambda: nc.vector.tensor_tensor(out=oT[:, hp, tok0:tok0 + 512], in0=pO[:, :], in1=rec[:, :],
                                                              op=ALU.mult), [pOb, recb], [oTb[hp][qg]])

                nt = len(tiles_)
                for t in range(min(LA, nt)):
                    emit_S(t, tiles_[t])
                for t in range(nt):
                    emit_E(t, tiles_[t])
                    if t + LA < nt:
                        emit_S(t + LA, tiles_[t + LA])
                    emit_PV(t, tiles_[t])
                kb.barrier()

        def s5_phase(bl, zT, zTb):
            with ExitStack() as ph:
                psb = psbf(ph)
                def ubg(g):
                    return sap(zT, 0, 128, g * NBLK, [[1, NBLK]])

                def ubrows(j):
                    return sap(zT, 16 * j, 16, 0, [[NBLK, NG], [1, NBLK]])
                ubb = [Buf() for g in range(NG)]
                Vs = psb("Vs", [128, 2, NG, NBLK], BF16)
                Vsb = Buf()
                Ssb = Buf()
                X = [psb(f"X{i}", [128, 4, NG], F32) for i in range(2)]
                Xb = [Buf(), Buf()]
                t1 = psb("st1", [128, 4, NG], F32)
                t2 = psb("st2", [128, 4, NG], F32)
                tb = Buf()
                kb.dma(sp, c_scr, sap(zT, 0, 128, 0, [[NBLK, NG], [1, NBLK]]), u_scr2[:, :, :], reads=u2b, writes=ubb)
                for g in range(NG):
                    ubb[g].w = (c_scr, c_scr.cnt)
                wb2 = psb("wb2", [128, 8 * 512], BF16)
                wb2b = Buf()

                def ubs(g, a, b_):
                    return sap(zT, 0, 128, g * NBLK + a, [[1, b_ - a]])
                for g in range(NG):
                    if g % 8 == 0:
                        kb.dma(sp, c_scr, wb2[:, :], wb2_scr[:, g * 512:(g + 8) * 512], reads=[wb2_b], writes=[wb2b])
                    pR_, pRb_ = pb[(2 * g) % 4], pbb[(2 * g) % 4]
                    pI_, pIb_ = pb[(2 * g + 1) % 4], pbb[(2 * g + 1) % 4]
                    for c, (P_, Pb_) in enumerate(((pR_, pRb_), (pI_, pIb_))):
                        w2o = (g % 8) * 512 + c * 256
                        kb.op(pe, lambda: nc.tensor.matmul(P_[:, 0:NBLK], lhsT=WB[:, g, c, :], rhs=ubg(g), start=True, stop=False),
                              reads=[WBb, ubb[g]], writes=[Pb_])
                        kb.op(pe, lambda: nc.tensor.matmul(P_[:, 1:NBLK], lhsT=wb2[:, w2o:w2o + 128], rhs=ubs(g, 0, NBLK - 1),
                                                           start=False, stop=False),
                              reads=[wb2b, ubb[g]], writes=[Pb_])
                        kb.op(pe, lambda: nc.tensor.matmul(P_[:, 1:NBLK - 1], lhsT=wb2[:, w2o + 128:w2o + 256], rhs=ubs(g, 2, NBLK),
                                                           start=False, stop=True),
                              reads=[wb2b, ubb[g]], writes=[Pb_])
                        V(lambda: nc.vector.tensor_copy(out=Vs[0:64, c, g, :], in_=P_[0:64, 0:NBLK]), [Pb_], [Vsb])
                        V(lambda: nc.vector.tensor_copy(out=Vs[64:128, c, g, :], in_=sap(P_, 64, 64, NBLK - 1, [[-1, NBLK]])),
                          [Pb_], [Vsb])
                V(lambda: nc.vector.memset(X[1][:, :, :], 0.0), [], [Xb[1]])
                F4 = 4 * NG
                for i in range(NBLK // 2):
                    Xp, Xpb = X[(i + 1) % 2], Xb[(i + 1) % 2]
                    Xn, Xnb = X[i % 2], Xb[i % 2]
                    sw = sap(Xp, 0, 128, 2 * NG, [[-2 * NG, 2], [1, 2 * NG]])
                    V(lambda: nc.vector.tensor_tensor(out=sap(t1, 0, 128, 0, [[1, F4]]), in0=sap(scA, 0, 128, 0, [[1, F4]]),
                                                      in1=sap(Xp, 0, 128, 0, [[1, F4]]), op=ALU.mult), [scb, Xpb], [tb])
                    V(lambda: nc.vector.tensor_tensor(out=sap(t2, 0, 128, 0, [[2 * NG, 2], [1, 2 * NG]]),
                                                      in0=sap(scB, 0, 128, 0, [[2 * NG, 2], [1, 2 * NG]]), in1=sw, op=ALU.mult),
                      [scb, Xpb], [tb])
                    V(lambda: nc.vector.tensor_tensor(out=sap(t1, 0, 128, 0, [[1, F4]]), in0=sap(t1, 0, 128, 0, [[1, F4]]),
                                                      in1=sap(t2, 0, 128, 0, [[1, F4]]), op=ALU.add), [tb], [tb])
                    vi = sap(Vs, 0, 128, 2 * i, [[NBLK, 2 * NG], [1, 2]])
                    x2 = sap(Xn, 0, 128, 0, [[2, 2 * NG], [1, 2]])
                    t1v = sap(t1, 0, 128, 0, [[2, 2 * NG], [1, 2]])
                    V(lambda: nc.vector.tensor_tensor(out=x2, in0=t1v, in1=vi, op=ALU.add), [tb, Vsb], [Xnb])
                    A(lambda: nc.scalar.copy(out=vi, in_=x2), [Xnb], [Ssb])
                al = [psb(f"al{i}", [128, 2, NBLK], BF16) for i in range(2)]
                alb = [Buf(), Buf()]
                for i in range(2):
                    V(lambda: nc.vector.memset(al[i][:, :, :], 0.0), [], [alb[i]])
                g1s = [psb(f"yg1{i}", [128, NBLK], F32) for i in range(3)]
                g2s = [psb(f"yg2{i}", [128, NBLK], F32) for i in range(3)]
                g1b = [Buf() for i in range(3)]
                g2b = [Buf() for i in range(3)]

                def y_A(g):
                    pY, pYb = pb[4 + g % 3], pbb[4 + g % 3]
                    a_, ab_ = al[g % 2], alb[g % 2]
                    V(lambda: nc.vector.tensor_copy(out=a_[0:64, :, 1:NBLK],
                                                    in_=sap(Vs, 0, 64, g * NBLK, [[NG * NBLK, 2], [1, NBLK - 1]])),
                      [Ssb], [ab_])
                    V(lambda: nc.vector.tensor_copy(out=a_[64:128, :, 0:NBLK - 1],
                                                    in_=sap(Vs, 64, 64, g * NBLK + NBLK - 2, [[NG * NBLK, 2], [-1, NBLK - 1]])),
                      [Ssb], [ab_])
                    kb.op(pe, lambda: nc.tensor.matmul(pY[:, 0:NBLK], lhsT=WT[:, g, :], rhs=ubg(g), start=True, stop=False),
                          reads=[WTb, ubb[g]], writes=[pYb])
                    for c in range(2):
                        kb.op(pe, lambda: nc.tensor.matmul(pY[:, 0:NBLK], lhsT=WC[:, g, c, :], rhs=a_[:, c, :],
                                                           start=False, stop=(c == 1)),
                              reads=[WCb, ab_], writes=[pYb])

                def y_B1(g):
                    pY, pYb = pb[4 + g % 3], pbb[4 + g % 3]
                    g1, g2 = g1s[g % 3], g2s[g % 3]
                    A(lambda: nc.scalar.activation(out=g1[:, :], in_=pY[:, 0:NBLK], func=AF.Square), [pYb], [g1b[g % 3]])
                    V(lambda: nc.vector.tensor_scalar(out=g1[:, :], in0=g1[:, :], scalar1=0.07135481627, scalar2=1.5957691216,
                                                      op0=ALU.mult, op1=ALU.add), [g1b[g % 3]], [g1b[g % 3]])
                    V(lambda: nc.vector.tensor_tensor(out=g2[:, :], in0=g1[:, :], in1=pY[:, 0:NBLK], op=ALU.mult),
                      [g1b[g % 3], pYb], [g2b[g % 3]])

                def y_B2(g):
                    pY, pYb = pb[4 + g % 3], pbb[4 + g % 3]
                    g2 = g2s[g % 3]
                    A(lambda: nc.scalar.activation(out=g2[:, :], in_=g2[:, :], func=AF.Sigmoid), [g2b[g % 3]], [g2b[g % 3]])
                    V(lambda: nc.vector.tensor_tensor(out=ubg(g), in0=g2[:, :], in1=pY[:, 0:NBLK], op=ALU.mult),
                      [g2b[g % 3], pYb], [ubb[g]])
                for t_ in range(NG + 2):
                    if t_ < NG:
                        y_A(t_)
                    if 1 <= t_ <= NG:
                        y_B1(t_ - 1)
                    if t_ >= 2:
                        y_B2(t_ - 2)
                z2b = Buf()
                kb.dma(sp, c_scr, z_scr2[:, :, :], sap(zT, 0, 128, 0, [[NBLK, NG], [1, NBLK]]), reads=ubb + zsb, writes=[z2b])
                for j in range(8):
                    srcZ = bass.AP(z_scr2.tensor, j * 16 * NG * NBLK, [[NG * NBLK, 16], [NBLK, NG], [1, NBLK]])
                    dstZ = bass.AP(z_scr.tensor, j * NBLK, [[8 * NBLK, 16], [16 * 8 * NBLK, NG], [1, NBLK]])
                    kb.dma(sp, c_scr, dstZ, srcZ, reads=[z2b], writes=[zsb[j]])
                for j in range(8):
                    zsb[j].w = (c_scr, c_scr.cnt)
                for mt in range(4):
                    kb.dma(sp, c_scr, zT[:, mt, :, :], z_scr[mt * 128:(mt + 1) * 128, :, :], reads=zsb, writes=zTb + ubb)
                for b_ in zTb:
                    b_.w = (c_scr, c_scr.cnt)
                kb.barrier()

        def merge_items():
            items = []
            for half in range(2):
                for mo in range(8):
                    items.append((wgg_d[mo, :, :], 2048))
                    items.append((wbb_d[mo, :, :], 1024))
                for mo2 in range(4):
                    items.append((wo_d[mo2, :, :], 2048))
            return items

        c_wgl = kb.ctr("s_wgl")

        def merge_phase(bl, zT, zTb, oT, oTb):
            with ExitStack() as ph:
                psb = psbf(ph)
                wgl = psb("wgl", [128, 4, 512], BF16)
                wglb = Buf()
                for mt in range(4):
                    kb.dma(pool, c_wgl, wgl[:, mt, :], wglu_d[mt, :, :], writes=[wglb])
                wglb.w = (c_wgl, c_wgl.cnt)
                xn2 = psb("mxn2", [128, 8, 1024], BF16)
                mg = psb("mg", [128, 8, 1024], BF16)
                sq = [psb(f"msq{i}", [128, 512], BF16) for i in range(2)]
                sd = psb("msd", [128, 512], F32)
                rstd = psb("mrstd", [128, 512], F32)
                scr = (sq, [Buf(), Buf()], sd, Buf(), rstd, Buf())
                s1 = [psb(f"ms1{i}", [128, 512], BF16) for i in range(2)]
                s2 = [psb(f"ms2{i}", [128, 512], BF16) for i in range(2)]
                m1 = [psb(f"mm1{i}", [128, 512], BF16) for i in range(2)]
                sb1, sb2, mb1 = [Buf(), Buf()], [Buf(), Buf()], [Buf(), Buf()]

                def zperm(t_, mt, c):
                    return sap(t_, 0, 128, mt * 8 * NBLK + 2 + 64 * c, [[NBLK, 8], [1, 64]])
                for c in range(4):
                    for mt in range(4):
                        pS, pSb = pb[(c % 2) * 4 + mt], pbb[(c % 2) * 4 + mt]
                        for k in range(4):
                            kb.op(pe, lambda: nc.tensor.matmul(pS[:, :], lhsT=wgl[:, mt, k * 128:(k + 1) * 128], rhs=zperm(zT, k, c),
                                                               start=(k == 0), stop=(k == 3)),
                                  reads=[wglb, zTb[c]], writes=[pSb])
                    for mt in range(4):
                        pS, pSb = pb[(c % 2) * 4 + mt], pbb[(c % 2) * 4 + mt]
                        A(lambda: nc.scalar.activation(out=s1[mt % 2][:, :], in_=pS[:, :], func=AF.Sigmoid), [pSb], [sb1[mt % 2]])
                        V(lambda: nc.vector.tensor_tensor(out=zperm(zT, mt, c), in0=zperm(zT, mt, c),
                                                          in1=sap(s1[mt % 2], 0, 128, 0, [[64, 8], [1, 64]]), op=ALU.mult),
                          [sb1[mt % 2], zTb[c]], [zTb[c]])
                for half in range(2):
                    xnb = [[Buf() for c in range(2)] for k in range(8)]
                    mgb = [[Buf() for c in range(2)] for k in range(8)]
                    chs = []
                    for c2 in range(2):
                        c = 2 * half + c2
                        chs.append(dict(N=512, h=(lambda k, c=c: hT[:, k, c * 512:(c + 1) * 512]), hb=(lambda k, c=c: hb[k][c]),
                                        xn=(lambda k, c2=c2: xn2[:, k, c2 * 512:(c2 + 1) * 512]),
                                        xnb=(lambda k, c2=c2: xnb[k][c2])))
                    for ch in chs:
                        norm_chunk(ch, 1, scr)
                    for mo in range(8):
                        wgg, wggb = stream.get()
                        for c2, ch in enumerate(chs):
                            p1, p1b = pb[c2], pbb[c2]
                            for k in range(8):
                                kb.op(pe, lambda: nc.tensor.matmul(p1[:, :], lhsT=wgg[:, k * 128:(k + 1) * 128], rhs=ch["xn"](k),
                                                                   start=(k == 0), stop=(k == 7)),
                                      reads=[wggb, ch["xnb"](k)], writes=[p1b])
                            A(lambda: nc.scalar.activation(out=s1[c2][:, :], in_=p1[:, :], func=AF.Sigmoid), [p1b], [sb1[c2]])
                        for c2, ch in enumerate(chs):
                            p2, p2b = pb[4 + c2], pbb[4 + c2]
                            for k in range(8):
                                kb.op(pe, lambda: nc.tensor.matmul(p2[:, :], lhsT=wgg[:, 1024 + k * 128:1024 + (k + 1) * 128],
                                                                   rhs=ch["xn"](k), start=(k == 0), stop=(k == 7)),
                                      reads=[wggb, ch["xnb"](k)], writes=[p2b])
                            A(lambda: nc.scalar.activation(out=s2[c2][:, :], in_=p2[:, :], func=AF.Sigmoid), [p2b], [sb2[c2]])
                        wbb, wbbb = stream.get()
                        for c2, ch in enumerate(chs):
                            c = 2 * half + c2
                            p3, p3b = pb[2 + c2], pbb[2 + c2]
                            for k in range(4):
                                kb.op(pe, lambda: nc.tensor.matmul(p3[:, :], lhsT=wbb[:, k * 128:(k + 1) * 128], rhs=zperm(zT, k, c),
                                                                   start=(k == 0), stop=(k == 3)),
                                      reads=[wbbb, zTb[c]], writes=[p3b])
                            V(lambda: nc.vector.tensor_tensor(out=sap(m1[c2], 0, 128, 0, [[8, 64], [1, 8]]),
                                                              in0=sap(s1[c2], 0, 128, 0, [[8, 64], [1, 8]]),
                                                              in1=sap(p3, 0, 128, 0, [[1, 64], [64, 8]]), op=ALU.mult),
                              [sb1[c2], p3b], [mb1[c2]])
                        for c2, ch in enumerate(chs):
                            c = 2 * half + c2
                            p4, p4b = pb[6 + c2], pbb[6 + c2]
                            for k in range(4):
                                kb.op(pe, lambda: nc.tensor.matmul(p4[:, :], lhsT=wbb[:, 512 + k * 128:512 + (k + 1) * 128],
                                                                   rhs=oT[:, k, c * 512:(c + 1) * 512], start=(k == 0), stop=(k == 3)),
                                      reads=[wbbb, oTb[k][c]], writes=[p4b])
                            V(lambda: nc.vector.tensor_tensor(out=s2[c2][:, :], in0=s2[c2][:, :], in1=p4[:, :], op=ALU.mult),
                              [sb2[c2], p4b], [sb2[c2]])
                            V(lambda: nc.vector.tensor_tensor(out=mg[:, mo, c2 * 512:(c2 + 1) * 512], in0=m1[c2][:, :],
                                                              in1=s2[c2][:, :], op=ALU.add), [mb1[c2], sb2[c2]], [mgb[mo][c2]])
                    for mo in range(8):
                        if mo % 2 == 0:
                            ws, wb = stream.get()
                        wo0 = (mo % 2) * 1024
                        for c2 in range(2):
                            c = 2 * half + c2
                            pO, pOb = pb[(2 * mo + c2) % 8], pbb[(2 * mo + c2) % 8]
                            for k in range(8):
                                kb.op(pe, lambda: nc.tensor.matmul(pO[:, :], lhsT=ws[:, wo0 + k * 128:wo0 + (k + 1) * 128],
                                                                   rhs=mg[:, k, c2 * 512:(c2 + 1) * 512], start=(k == 0), stop=(k == 7)),
                                      reads=[wb, mgb[k][c2]], writes=[pOb])
                            hk = hT[:, mo, c * 512:(c + 1) * 512]
                            V(lambda: nc.vector.tensor_tensor(out=hk, in0=pO[:, :], in1=hk, op=ALU.add),
                              [pOb, hb[mo][c]], [hb[mo][c]])
                kb.barrier()

        def mixer(bl):
            with ExitStack() as mx:
                psb = psbf(mx)
                qT = psb("qoT", [128, 4, T], BF16)
                qTb = [[Buf() for c in range(4)] for i in range(4)]
                qTall = [b_ for r_ in qTb for b_ in r_]
                with ExitStack() as m2:
                    psb2 = psbf(m2)
                    kT = psb2("kT", [128, 4, T], BF16)
                    vd = psb2("vd", [128, 32, 4, 64], BF16)
                    kTb = [Buf() for i in range(4)]
                    vdb = Buf()
                    inproj_phase(bl, qT, qTb, kT, kTb, vd, vdb)
                    if dbg and bl == 0:
                        dump("qT", qT[:, :, :], [128, 4, T], qTall)
                        dump("kT", kT[:, :, :], [128, 4, T], kTb)
                        dump("vd", vd[:, :, :, :], [128, 32, 4, 64], [vdb])
                        kb.barrier()
                        kb.wait_all(sp, [c_out])
                        kb.wait_all(pe, [c_out])
                    if stage >= 7:
                        stream.extend(merge_items())
                    if stage >= 5:
                        na_phase(bl, qT, qTb, kT, kTb, vd, vdb, qT, qTb)
                        if dbg and bl == 0:
                            dump("oT", qT[:, :, :], [128, 4, T], qTall)
                            for E_ in (pe, act, dve, pool):
                                kb.wait_all(E_, [c_out])
                zT = psb("zT", [128, 4, 8, NBLK], BF16)
                zTb = [Buf() for c in range(4)]
                if stage >= 6:
                    s5_phase(bl, zT, zTb)
                    if dbg and bl == 0:
                        dump("zT", zT[:, :, :, :], [128, 4, 8, NBLK], zTb)
                        for E_ in (pe, act, dve, pool):
                            kb.wait_all(E_, [c_out])
                if stage >= 7:
                    merge_phase(bl, zT, zTb, qT, qTb)

        allhb = [hb[k][c] for k in range(8) for c in range(4)]
        if stage >= 2:
            setup_na()
        if stage >= 3:
            setup_s5()
        hT = sb("hT", [128, 8, T], F32)
        hm = sb("hm", [128, 8, NM], F32)
        stream = Stream(kb, nc, es, NSLOT, 2048)
        c_hm = kb.ctr("s_hm")
        kb.dma(sp, c_hm, hm[:, :, :], metaT[:, :, :], writes=hmb)
        for bl in range(nb_local):
            load_x(bl)
            stream.extend(ffn_items(w1i, w1o))
            if stage >= 4:
                stream.extend(mixer_items())
            ffn_phase(bl, 0, with_meta=(bl == 0))
            if dbg and bl == 0:
                dump("h1", hT[:, :, :], [128, 8, T], allhb)
                dump("hm1", hm[:, :, :], [128, 8, NM], hmb)
            if stage >= 4:
                mixer(bl)
                if dbg and bl == 0:
                    dump("h2", hT[:, :, :], [128, 8, T], allhb)
            stream.extend(ffn_items(w2i, w2o))
            ffn_phase(bl, 2, with_meta=False)
            final_phase(bl)
        kb.wait_all(sp, [c_out])
    return nc, dbg_outs


def host_inputs(inputs, core, nb_local=NB_LOCAL):
    f = np.float32
    x = inputs["x"]
    b0 = core * nb_local
    xT = np.ascontiguousarray(
        x[b0:b0 + nb_local].transpose(0, 2, 1).reshape(nb_local, 8, 128, T)).astype(f)
    metaT = np.ascontiguousarray(inputs["meta_tokens"].T.reshape(8, 128, NM).transpose(1, 0, 2)).astype(f)
    g = np.stack([inputs["norm_ffn1"][0], inputs["norm_mix"][0], inputs["norm_ffn2"][0], inputs["norm_final"]])
    gains = np.ascontiguousarray(g.reshape(4, 8, 128).transpose(2, 0, 1)).astype(f)

    def ffn_in(w):
        w = w.reshape(8, 128, 2, NFF, 128)
        return np.ascontiguousarray(w.transpose(3, 1, 0, 2, 4).reshape(NFF, 128, 8 * 256)).astype(f)

    def ffn_out(w):
        w = w.reshape(NFF, 128, 8, 128)
        return np.ascontiguousarray(w.transpose(2, 1, 0, 3).reshape(8, 128, NFF * 128)).astype(f)

    consts = np.zeros((128, 3, 128), f)
    consts[:, 0, :] = np.eye(128)
    consts[:, 1, :] = 1.0
    consts[:64, 2, :64] = 1.0
    consts[64:, 2, 64:] = 1.0
    w_in = inputs["w_in"][0]

    def tiles(w, nk, nm):
        return np.ascontiguousarray(w.reshape(nk, 128, nm, 128).transpose(2, 1, 0, 3).reshape(nm, 128, nk * 128)).astype(f)
    wx_ = tiles(w_in, 8, 32)
    wv = w_in[:, 1536:2048].reshape(8, 128, 2, 256)
    wv2 = np.ascontiguousarray(wv.transpose(2, 1, 0, 3).reshape(2, 128, 2048)).astype(f)
    rpb = inputs["na_rpb"][0]
    rpbpad = np.zeros((8, 15, 128), f)
    rpbpad[:, :, 48:48 + 31] = rpb[:, ::-1, :]
    qc = np.arange(64)
    cs = np.clip(qc - 8, 0, 48)
    kc = np.arange(64)
    cm = ((kc[:, None] >= cs[None, :]) & (kc[:, None] < cs[None, :] + 16)).astype(f)
    colmask = np.concatenate([cm, cm], axis=0)
    are = np.stack([inputs["ssm_a_re_fwd"][0], inputs["ssm_a_re_bwd"][0]])
    aim = np.stack([inputs["ssm_a_im_fwd"][0], inputs["ssm_a_im_bwd"][0]])
    ldt = np.stack([inputs["ssm_log_dt_fwd"][0], inputs["ssm_log_dt_bwd"][0]])
    ldt_gn = np.broadcast_to(ldt[:, :, None], (2, 32, 64))
    Bre, Bim = inputs["ssm_b_re"][0], inputs["ssm_b_im"][0]
    Cre, Cim = inputs["ssm_c_re"][0], inputs["ssm_c_im"][0]

    def layC2(a):
        return a.transpose(0, 2, 1).reshape(128, 32)
    s5c = np.ascontiguousarray(np.stack([layC2(are), layC2(aim), layC2(ldt_gn)], axis=1)).astype(f)

    def layCC(Cm):
        t = Cm.transpose(2, 0, 1).reshape(64, 512)
        return np.concatenate([t, t], axis=0)

    def layCB(Bm):
        t = Bm.transpose(1, 0, 2).reshape(64, 512)
        return np.concatenate([t, t], axis=0)
    s5cc = np.ascontiguousarray(np.stack([layCC(Cre), layCC(Cim)], axis=1)).astype(f)
    s5cb = np.ascontiguousarray(np.stack([layCB(Bre), layCB(Bim)], axis=1)).astype(f)
    j8 = np.arange(8)
    s5ce = np.zeros((128, 5, 8), f)
    s5ce[:64, 3] = 7 - j8
    s5ce[64:, 3] = j8
    s5ce[:64, 4] = 15 - j8
    s5ce[64:, 4] = j8 + 8
    s5ce[:64, 0] = j8 + 1
    s5ce[:64, 1] = j8
    s5ce[:64, 2] = -j8
    s5ce[64:, 0] = 8 - j8
    s5ce[64:, 1] = -j8
    s5ce[64:, 2] = j8
    Dm = inputs["ssm_d"][0]
    s5d = np.ascontiguousarray(np.broadcast_to(Dm.T[None], (8, 16, 32)).reshape(128, 32)).astype(f)
    ii = np.repeat(np.arange(8), 16)
    s5m = np.stack([(ii[:, None] <= ii[None, :]), (ii[:, None] >= ii[None, :]), np.eye(128, dtype=bool)], axis=1).astype(f)
    extra = {
        "wx": wx_, "wv2": wv2, "wglu": tiles(inputs["w_glu"][0], 4, 4), "wbs": tiles(inputs["w_branch_ssm"][0], 4, 8),
        "wbn": tiles(inputs["w_branch_na"][0], 4, 8),
        "wo": np.ascontiguousarray(tiles(inputs["w_out"][0], 8, 8).reshape(4, 2, 128, 1024).transpose(0, 2, 1, 3).reshape(4, 128, 2048)),
        "wgg": np.ascontiguousarray(np.concatenate([wx_[16:24], wx_[24:32]], axis=2)),
        "wbb": np.ascontiguousarray(np.concatenate([tiles(inputs["w_branch_ssm"][0], 4, 8), tiles(inputs["w_branch_na"][0], 4, 8)], axis=2)),
        "rpbpad": rpbpad, "colmask": colmask, "s5c": s5c, "s5cc": s5cc,
        "s5cb": s5cb, "s5ce": s5ce, "s5d": s5d, "s5m": np.ascontiguousarray(s5m),
    }
    return {
        **extra,
        "xT": xT, "metaT": metaT, "gains": gains,
        "w1i": ffn_in(inputs["w_ffn1_in"][0]), "w1o": ffn_out(inputs["w_ffn1_out"][0]),
        "w2i": ffn_in(inputs["w_ffn2_in"][0]), "w2o": ffn_out(inputs["w_ffn2_out"][0]),
        "consts": consts,
    }


def kernel(**inputs):
    inputs = {k: np.asarray(v) for k, v in inputs.items()}
    nc, _ = build_program()
    in_maps = [host_inputs(inputs, c) for c in range(NCORES)]
    res = run_bass_kernel_spmd(nc, in_maps, core_ids=list(range(NCORES)))
    out = np.empty((16, T, D), np.float32)
    for c in range(NCORES):
        o = np.asarray(res.results[c]["outT"])
        out[c * NB_LOCAL:(c + 1) * NB_LOCAL] = o.reshape(NB_LOCAL, D, T).transpose(0, 2, 1)
    return out
```

```python
import numpy as np
from contextlib import ExitStack
import concourse.bass as bass
import concourse.mybir as mybir
from concourse.bass_utils import run_bass_kernel_spmd

F32 = mybir.dt.float32
BF16 = mybir.dt.bfloat16
ALU = mybir.AluOpType
AF = mybir.ActivationFunctionType

D = 1024
T = 2048
NM = 16
DFF = 2816
NFF = 22
NG = 32
NBLK = 258
NB_LOCAL = 2
NCORES = 8
FF_SPLITS = [(0, 8), (8, 15), (15, 22)]
NSLOT = 2
TWO_PI = 6.283185307179586


class Ctr:
    def __init__(self, sem):
        self.sem = sem
        self.cnt = 0


class Buf:
    __slots__ = ("w", "r", "name")

    def __init__(self, name=""):
        self.w = None
        self.r = {}
        self.name = name


class Eng:
    def __init__(self, h, ctr, name):
        self.h = h
        self.ctr = ctr
        self.seen = {}
        self.name = name


class KB:
    def __init__(self, nc, es):
        self.nc = nc
        self.es = es

        def mk(h, n):
            return Eng(h, Ctr(es.enter_context(nc.semaphore(n))), n)

        self.pe = mk(nc.tensor, "s_pe")
        self.act = mk(nc.scalar, "s_act")
        self.dve = mk(nc.vector, "s_dve")
        self.pool = mk(nc.gpsimd, "s_pool")
        self.sp = mk(nc.sync, "s_sp")
        self.engs = [self.pe, self.act, self.dve, self.pool, self.sp]

    def ctr(self, name):
        return Ctr(self.es.enter_context(self.nc.semaphore(name)))

    def _need(self, E, dep, raw):
        ctr, val = dep
        if ctr is E.ctr and not raw:
            return
        if E.seen.get(id(ctr), 0) >= val:
            return
        E.h.wait_ge(ctr.sem, val)
        E.seen[id(ctr)] = val

    def _deps(self, E, reads, writes):
        for b in reads:
            if b.w is not None:
                self._need(E, b.w, True)
        for b in writes:
            if b.w is not None:
                self._need(E, b.w, False)
            for dep in b.r.values():
                self._need(E, dep, False)

    def op(self, E, fn, reads=(), writes=()):
        self._deps(E, reads, writes)
        ins = fn()
        E.ctr.cnt += 1
        ins.then_inc(E.ctr.sem, 1)
        for b in writes:
            b.w = (E.ctr, E.ctr.cnt)
            b.r = {}
        for b in reads:
            b.r[id(E.ctr)] = (E.ctr, E.ctr.cnt)

    def dma(self, Q, ctr, out, in_, reads=(), writes=()):
        self._deps(Q, reads, writes)
        Q.h.dma_start(out=out, in_=in_).then_inc(ctr.sem, 16)
        ctr.cnt += 16
        for b in writes:
            b.w = (ctr, ctr.cnt)
            b.r = {}
        for b in reads:
            b.r[id(ctr)] = (ctr, ctr.cnt)

    def barrier(self):
        for E in self.engs:
            for O in self.engs:
                if O is not E and O.ctr.cnt > 0:
                    self._need(E, (O.ctr, O.ctr.cnt), True)

    def wait_all(self, E, ctrs):
        for c in ctrs:
            if c.cnt > 0:
                self._need(E, (c, c.cnt), True)


class Stream:
    def __init__(self, kb, nc, es, nslot, cols):
        self.kb = kb
        self.items = []
        self.next_load = 0
        self.next_get = 0
        self.nslot = nslot
        self.slots = [es.enter_context(nc.sbuf_tensor(f"ring{i}", [128, cols], BF16)) for i in range(nslot)]
        self.bufs = [Buf(f"ring{i}") for i in range(nslot)]
        self.ctrs = [kb.ctr(f"s_ring{i}") for i in range(nslot)]

    def extend(self, items):
        self.items += items

    def _load(self, n):
        s = n % self.nslot
        it = self.items[n]
        src, ncols = it[0], it[1]
        if len(it) > 2:
            dst = sap(self.slots[s], 0, 128, 0, [[ncols // 2, 2], [1, ncols // 2]])
        else:
            dst = self.slots[s][:, 0:ncols]
        self.kb.dma(self.kb.pool, self.ctrs[s], dst, src, writes=[self.bufs[s]])

    def get(self):
        i = self.next_get
        self.next_get += 1
        while self.next_load < min(len(self.items), i + self.nslot):
            self._load(self.next_load)
            self.next_load += 1
        s = i % self.nslot
        return self.slots[s], self.bufs[s]


def sap(t, part0, nparts, off, dims):
    row = 1
    for d in t.shape[1:]:
        row *= d
    return bass.AP(t, part0 * row + off, [[row, nparts]] + [list(d) for d in dims])


def build_program(nb_local=NB_LOCAL, stage=99, dbg=False):
    nc = bass.Bass("TRN2", target_bir_lowering=False)

    def din(name, shape, dt=F32):
        return nc.dram_tensor(name, list(shape), dt, kind="ExternalInput").ap()

    xT = din("xT", [nb_local, 8, 128, T])
    metaT = din("metaT", [128, 8, NM])
    gains_d = din("gains", [128, 4, 8])
    w1i = din("w1i", [NFF, 128, 8 * 256])
    w1o = din("w1o", [8, 128, NFF * 128])
    w2i = din("w2i", [NFF, 128, 8 * 256])
    w2o = din("w2o", [8, 128, NFF * 128])
    consts_d = din("consts", [128, 3, 128])
    wx = din("wx", [32, 128, 1024])
    wv2 = din("wv2", [2, 128, 2048])
    wglu_d = din("wglu", [4, 128, 512])
    wbs_d = din("wbs", [8, 128, 512])
    wbn_d = din("wbn", [8, 128, 512])
    wo_d = din("wo", [4, 128, 2048])
    wgg_d = din("wgg", [8, 128, 2048])
    wbb_d = din("wbb", [8, 128, 1024])
    rpbpad = din("rpbpad", [8, 15, 128])
    colmask_d = din("colmask", [128, 64])
    s5c_d = din("s5c", [128, 3, 32])
    s5cc_d = din("s5cc", [128, 2, 512])
    s5cb_d = din("s5cb", [128, 2, 512])
    s5ce_d = din("s5ce", [128, 5, 8])
    s5d_d = din("s5d", [128, 32])
    s5m_d = din("s5m", [128, 3, 128])
    u_scr = nc.dram_tensor("u_scr", [512, 8, NBLK], BF16, kind="Internal").ap()
    z_scr = nc.dram_tensor("z_scr", [512, 8, NBLK], BF16, kind="Internal").ap()
    wb2_scr = nc.dram_tensor("wb2_scr", [128, NG * 512], BF16, kind="Internal").ap()
    u_scr2 = nc.dram_tensor("u_scr2", [128, NG, NBLK], BF16, kind="Internal").ap()
    z_scr2 = nc.dram_tensor("z_scr2", [128, NG, NBLK], BF16, kind="Internal").ap()
    outT = nc.dram_tensor("outT", [nb_local, 8, 128, T], F32, kind="ExternalOutput").ap()
    dbg_outs = {}

    with ExitStack() as es:
        kb = KB(nc, es)
        pe, act, dve, pool, sp = kb.pe, kb.act, kb.dve, kb.pool, kb.sp

        def sb(name, shape, dt):
            return es.enter_context(nc.sbuf_tensor(name, list(shape), dt))

        uid = [0]

        def uname(n):
            uid[0] += 1
            return f"{n}_{uid[0]}"


        Etab = sb("Etab", [128, 4, 15, 64], BF16)
        Etabb = Buf()
        kmT = sb("kmT", [128, 4, NM], BF16)
        kmTb = Buf()
        vm = sb("vm", [128, 4, 64], BF16)
        vmb = Buf()
        umT = sb("umT", [128, 4, 8, 2], BF16)
        umTb = Buf()
        WB = sb("WB", [128, NG, 2, 128], BF16)
        WBb = Buf()
        WC = sb("WC", [128, NG, 2, 128], BF16)
        WCb = Buf()
        WT = sb("WT", [128, NG, 128], BF16)
        WTb = Buf()
        scA = sb("scA", [128, 4, NG], F32)
        scB = sb("scB", [128, 4, NG], F32)
        wb2_b = Buf()
        scb = Buf()
        u_scr_b = Buf()
        z_scr_b = Buf()
        u2b = [Buf() for j in range(8)]
        zsb = [Buf() for j in range(8)]
        c_scr = kb.ctr("s_scr")

        hb = [[Buf(f"h{k}_{c}") for c in range(4)] for k in range(8)]
        hmb = [Buf(f"hm{k}") for k in range(8)]
        gains = sb("gains_sb", [128, 4, 8], F32)
        gainsb = Buf("gains")
        cst = sb("cst", [128, 3, 128], BF16)
        cstb = Buf("cst")
        epsb_t = sb("eps", [128, 1], F32)
        epsb = Buf("eps")
        pb = [es.enter_context(nc.psum_tensor(f"pb{i}", [128, 512], F32)) for i in range(8)]
        pbb = [Buf(f"pb{i}") for i in range(8)]
        ident = cst[:, 0, :]
        ones_bf = cst[:, 1, :]
        blockones = cst[:, 2, :]

        c_misc = kb.ctr("s_misc")
        c_x = kb.ctr("s_x")
        c_out = kb.ctr("s_out")

        kb.dma(sp, c_misc, gains[:, :, :], gains_d[:, :, :], writes=[gainsb])
        c_cst = kb.ctr("s_cst")
        kb.dma(pool, c_cst, cst[:, :, :], consts_d[:, :, :], writes=[cstb])
        kb.op(dve, lambda: nc.vector.memset(epsb_t[:, :], 1e-6), writes=[epsb])

        def load_x(bl):
            kb.wait_all(sp, [c_out])
            for c in range(4):
                for k in range(8):
                    kb.dma(sp, c_x, hT[:, k, c * 512:(c + 1) * 512], xT[bl, k, :, c * 512:(c + 1) * 512], writes=[hb[k][c]])
                for k in range(8):
                    hb[k][c].w = (c_x, c_x.cnt)
                kb.wait_all(sp, [c_x])

        def main_chunks(xn, xnb, at, atb):
            chs = []
            for c in range(4):
                chs.append(dict(
                    N=512,
                    h=(lambda k, c=c: hT[:, k, c * 512:(c + 1) * 512]),
                    hb=(lambda k, c=c: hb[k][c]),
                    xn=(lambda k, c=c: xn[:, k, c * 512:(c + 1) * 512]),
                    xnb=(lambda k, c=c: xnb[k][c]),
                    at=(lambda kk, c=c: at[:, kk, c * 512:(c + 1) * 512]),
                    atb=(lambda kk, c=c: atb[kk][c]),
                ))
            return chs

        def norm_chunk(ch, gi, scr, out_fn=None, outb_fn=None):
            N = ch["N"]
            sq, sqb, sd, sdb, rstd, rstdb = scr
            pR, pRb = pb[7], pbb[7]
            for k in range(8):
                kb.op(act, lambda: nc.scalar.activation(out=sq[k % 2][:, :N], in_=ch["h"](k), func=AF.Square),
                      reads=[ch["hb"](k)], writes=[sqb[k % 2]])
                kb.op(pe, lambda: nc.tensor.matmul(pR[:, :N], lhsT=ones_bf, rhs=sq[k % 2][:, :N],
                                                   start=(k == 0), stop=(k == 7)),
                      reads=[sqb[k % 2], cstb], writes=[pRb])
            kb.op(act, lambda: nc.scalar.activation(out=sd[:, :N], in_=pR[:, :N], func=AF.Sqrt,
                                                    scale=1.0 / D, bias=epsb_t[:, 0:1]),
                  reads=[pRb, epsb], writes=[sdb])
            kb.op(dve, lambda: nc.vector.reciprocal(out=rstd[:, :N], in_=sd[:, :N]), reads=[sdb], writes=[rstdb])
            for k in range(8):
                o = ch["xn"](k) if out_fn is None else out_fn(k)
                ob = ch["xnb"](k) if outb_fn is None else outb_fn(k)
                kb.op(dve, lambda: nc.vector.scalar_tensor_tensor(
                    out=o, in0=ch["h"](k), scalar=gains[:, gi, k:k + 1], in1=rstd[:, :N],
                    op0=ALU.mult, op1=ALU.mult),
                    reads=[ch["hb"](k), rstdb, gainsb], writes=[ob])

        def ffn_items(wi, wo):
            items = []
            for (a, b) in FF_SPLITS:
                for m in range(a, b):
                    items.append((wi[m, :, :], 2048))
                for mo2 in range(4):
                    src = bass.AP(wo.tensor, 2 * mo2 * 128 * NFF * 128 + a * 128,
                                  [[NFF * 128, 128], [128 * NFF * 128, 2], [1, (b - a) * 128]])
                    items.append((src, 2 * (b - a) * 128, True))
            return items

        def ffn(chunks, gi, scr, sg, sgb):
            for ch in chunks:
                norm_chunk(ch, gi, scr)
            gu = 0
            oi = 0
            for (a, b) in FF_SPLITS:
                nk = b - a
                for m in range(a, b):
                    ws, wb = stream.get()
                    for ch in chunks:
                        N = ch["N"]
                        pG, pGb = pb[gu % 2], pbb[gu % 2]
                        pU, pUb = pb[2 + gu % 2], pbb[2 + gu % 2]
                        for k in range(8):
                            kb.op(pe, lambda: nc.tensor.matmul(pG[:, :N], lhsT=ws[:, k * 256:k * 256 + 128],
                                                               rhs=ch["xn"](k), start=(k == 0), stop=(k == 7)),
                                  reads=[wb, ch["xnb"](k)], writes=[pGb])
                        for k in range(8):
                            kb.op(pe, lambda: nc.tensor.matmul(pU[:, :N], lhsT=ws[:, k * 256 + 128:k * 256 + 256],
                                                               rhs=ch["xn"](k), start=(k == 0), stop=(k == 7)),
                                  reads=[wb, ch["xnb"](k)], writes=[pUb])
                        s_ = sg[gu % 2]
                        kb.op(act, lambda: nc.scalar.activation(out=s_[:, :N], in_=pG[:, :N], func=AF.Silu),
                              reads=[pGb], writes=[sgb[gu % 2]])
                        kb.op(dve, lambda: nc.vector.tensor_tensor(out=ch["at"](m - a), in0=s_[:, :N], in1=pU[:, :N],
                                                                   op=ALU.mult),
                              reads=[sgb[gu % 2], pUb], writes=[ch["atb"](m - a)])
                        gu += 1
                for mo in range(8):
                    if mo % 2 == 0:
                        ws, wb = stream.get()
                    wo0 = (mo % 2) * nk * 128
                    for ch in chunks:
                        N = ch["N"]
                        pO, pOb = pb[4 + oi % 2], pbb[4 + oi % 2]
                        for kk in range(nk):
                            kb.op(pe, lambda: nc.tensor.matmul(pO[:, :N], lhsT=ws[:, wo0 + kk * 128:wo0 + (kk + 1) * 128],
                                                               rhs=ch["at"](kk), start=(kk == 0), stop=(kk == nk - 1)),
                                  reads=[wb, ch["atb"](kk)], writes=[pOb])
                        hk = ch["h"](mo)
                        kb.op(dve, lambda: nc.vector.scalar_tensor_tensor(out=hk, in0=pO[:, :N], scalar=0.5, in1=hk,
                                                                          op0=ALU.mult, op1=ALU.add),
                              reads=[pOb, ch["hb"](mo)], writes=[ch["hb"](mo)])
                        oi += 1

        def ffn_phase(bl, gi, with_meta):
            with ExitStack() as ph:
                def psb(name, shape, dt):
                    return ph.enter_context(nc.sbuf_tensor(uname(name), list(shape), dt))
                xn = psb("xn", [128, 8, T], BF16)
                xnb = [[Buf() for c in range(4)] for k in range(8)]
                at = psb("at", [128, 8, T], BF16)
                atb = [[Buf() for c in range(4)] for k in range(8)]
                sq = [psb(f"sq{i}", [128, 512], BF16) for i in range(2)]
                sqb = [Buf(), Buf()]
                sd = psb("sd", [128, 512], F32)
                rstd = psb("rstd", [128, 512], F32)
                sg = [psb(f"sg{i}", [128, 512], BF16) for i in range(2)]
                sgb = [Buf(), Buf()]
                scr = (sq, sqb, sd, Buf(), rstd, Buf())
                chunks = main_chunks(xn, xnb, at, atb)
                if with_meta:
                    xnm = psb("xnm", [128, 8, NM], BF16)
                    xnmb = [Buf() for k in range(8)]
                    atm = psb("atm", [128, 8, NM], BF16)
                    atmb = [Buf() for k in range(8)]
                    chunks.append(dict(N=NM, h=lambda k: hm[:, k, :], hb=lambda k: hmb[k],
                                       xn=lambda k: xnm[:, k, :], xnb=lambda k: xnmb[k],
                                       at=lambda kk: atm[:, kk, :], atb=lambda kk: atmb[kk]))
                ffn(chunks, gi, scr, sg, sgb)
                kb.barrier()

        def final_phase(bl):
            with ExitStack() as ph:
                def psb(name, shape, dt):
                    return ph.enter_context(nc.sbuf_tensor(uname(name), list(shape), dt))
                sq = [psb(f"fsq{i}", [128, 512], BF16) for i in range(2)]
                sd = psb("fsd", [128, 512], F32)
                rstd = psb("frstd", [128, 512], F32)
                scr = (sq, [Buf(), Buf()], sd, Buf(), rstd, Buf())
                chunks = main_chunks(None, None, None, None)
                for c, ch in enumerate(chunks):
                    norm_chunk(ch, 3, scr, out_fn=ch["h"], outb_fn=ch["hb"])
                    for k in range(8):
                        kb.dma(sp, c_out, outT[bl, k, :, c * 512:(c + 1) * 512], hT[:, k, c * 512:(c + 1) * 512],
                               reads=[hb[k][c]])
                kb.barrier()

        def dump(name, ap, shape, reads):
            t = nc.dram_tensor("dbg_" + name, list(shape), ap.dtype, kind="ExternalOutput").ap()
            dbg_outs[name] = t
            kb.dma(sp, c_out, t, ap, reads=reads)

        def psbf(ph):
            def f(name, shape, dt):
                return ph.enter_context(nc.sbuf_tensor(uname(name), list(shape), dt))
            return f

        def V(fn, reads, writes):
            kb.op(dve, fn, reads=reads, writes=writes)

        def A(fn, reads, writes):
            kb.op(act, fn, reads=reads, writes=writes)

        def setup_na():
            with ExitStack() as ph:
                psb = psbf(ph)
                c_s = kb.ctr("s_setna")
                raw = psb("eraw", [128, 4, 15, 64], F32)
                rawb = Buf()
                exn = psb("eexp", [128, 4, 15, 64], F32)
                exb = Buf()
                cm = psb("cmask", [128, 64], F32)
                cmb = Buf()
                kb.dma(sp, c_s, cm[:, :], colmask_d[:, :], writes=[cmb])
                for hp in range(4):
                    for h2 in range(2):
                        h = 2 * hp + h2
                        src = bass.AP(rpbpad.tensor, h * 15 * 128, [[1, 64], [128, 15], [1, 64]])
                        kb.dma(sp, c_s, raw[h2 * 64:(h2 + 1) * 64, hp, :, :], src, writes=[rawb])
                rawb.w = (c_s, c_s.cnt)
                cmb.w = (c_s, c_s.cnt)
                A(lambda: nc.scalar.activation(out=sap(exn, 0, 128, 0, [[1, 3840]]), in_=sap(raw, 0, 128, 0, [[1, 3840]]),
                                               func=AF.Exp), [rawb], [exb])
                V(lambda: nc.vector.tensor_tensor(out=sap(Etab, 0, 128, 0, [[64, 60], [1, 64]]),
                                                  in0=sap(exn, 0, 128, 63, [[64, 60], [-1, 64]]),
                                                  in1=sap(cm, 0, 128, 0, [[0, 60], [1, 64]]), op=ALU.mult),
                  [exb, cmb], [Etabb])
                kb.barrier()

        MAGIC = 12582912.0

        def cexp(xr, xi, ore, oim, t, F_, b):
            R = [b]
            A(lambda: nc.scalar.activation(out=xr, in_=xr, func=AF.Exp), R, R)
            V(lambda: nc.vector.tensor_scalar(out=xi, in0=xi, scalar1=1.0 / TWO_PI, scalar2=None, op0=ALU.mult), R, R)
            for which, o in ((0, oim), (1, ore)):
                if which == 1:
                    V(lambda: nc.vector.tensor_scalar(out=xi, in0=xi, scalar1=0.25, scalar2=None, op0=ALU.add), R, R)
                V(lambda: nc.vector.tensor_scalar(out=t, in0=xi, scalar1=MAGIC, scalar2=None, op0=ALU.add), R, R)
                V(lambda: nc.vector.tensor_scalar(out=t, in0=t, scalar1=MAGIC, scalar2=None, op0=ALU.subtract), R, R)
                V(lambda: nc.vector.tensor_tensor(out=t, in0=xi, in1=t, op=ALU.subtract), R, R)
                A(lambda: nc.scalar.activation(out=t, in_=t, func=AF.Sin, scale=TWO_PI), R, R)
                V(lambda: nc.vector.tensor_tensor(out=o, in0=xr, in1=t, op=ALU.mult), R, R)

        def cmul(ore, oim, ar_, ai_, br_, bi_, t1, t2, R, W, neg_im=False):
            V(lambda: nc.vector.tensor_tensor(out=t1, in0=ar_, in1=br_, op=ALU.mult), R, R)
            V(lambda: nc.vector.tensor_tensor(out=t2, in0=ai_, in1=bi_, op=ALU.mult), R, R)
            V(lambda: nc.vector.tensor_tensor(out=ore, in0=t1, in1=t2, op=ALU.subtract), R, W)
            V(lambda: nc.vector.tensor_tensor(out=t1, in0=ar_, in1=bi_, op=ALU.mult), R, R)
            V(lambda: nc.vector.tensor_tensor(out=t2, in0=ai_, in1=br_, op=ALU.mult), R, R)
            if neg_im:
                V(lambda: nc.vector.scalar_tensor_tensor(out=oim, in0=t1, scalar=-1.0, in1=t2, op0=ALU.mult,
                                                         op1=ALU.subtract), R, W)
            else:
                V(lambda: nc.vector.tensor_tensor(out=oim, in0=t1, in1=t2, op=ALU.add), R, W)

        def lam_and_f(are, aim, ldt, ardt, aidt, lre, lim, fre, fim, t1, t2, t3, F_, b):
            R = [b]
            A(lambda: nc.scalar.activation(out=ldt, in_=ldt, func=AF.Exp), R, R)
            V(lambda: nc.vector.tensor_scalar(out=are, in0=are, scalar1=-1e-4, scalar2=None, op0=ALU.min), R, R)
            V(lambda: nc.vector.tensor_tensor(out=ardt, in0=are, in1=ldt, op=ALU.mult), R, R)
            V(lambda: nc.vector.tensor_tensor(out=aidt, in0=aim, in1=ldt, op=ALU.mult), R, R)
            V(lambda: nc.vector.tensor_copy(out=t1, in_=ardt), R, R)
            V(lambda: nc.vector.tensor_copy(out=t2, in_=aidt), R, R)
            cexp(t1, t2, lre, lim, t3, F_, b)
            V(lambda: nc.vector.tensor_tensor(out=t1, in0=are, in1=are, op=ALU.mult), R, R)
            V(lambda: nc.vector.tensor_tensor(out=t2, in0=aim, in1=aim, op=ALU.mult), R, R)
            V(lambda: nc.vector.tensor_tensor(out=t3, in0=t1, in1=t2, op=ALU.add), R, R)
            V(lambda: nc.vector.reciprocal(out=t3, in_=t3), R, R)
            V(lambda: nc.vector.tensor_scalar(out=t1, in0=lre, scalar1=-1.0, scalar2=None, op0=ALU.add), R, R)
            V(lambda: nc.vector.tensor_tensor(out=fre, in0=t1, in1=are, op=ALU.mult), R, R)
            V(lambda: nc.vector.tensor_tensor(out=t2, in0=lim, in1=aim, op=ALU.mult), R, R)
            V(lambda: nc.vector.tensor_tensor(out=fre, in0=fre, in1=t2, op=ALU.add), R, R)
            V(lambda: nc.vector.tensor_tensor(out=fre, in0=fre, in1=t3, op=ALU.mult), R, R)
            V(lambda: nc.vector.tensor_tensor(out=fim, in0=lim, in1=are, op=ALU.mult), R, R)
            V(lambda: nc.vector.tensor_tensor(out=t2, in0=t1, in1=aim, op=ALU.mult), R, R)
            V(lambda: nc.vector.tensor_tensor(out=fim, in0=fim, in1=t2, op=ALU.subtract), R, R)
            V(lambda: nc.vector.tensor_tensor(out=fim, in0=fim, in1=t3, op=ALU.mult), R, R)

        def setup_s5():
            c_s = kb.ctr("s_sets5")
            with ExitStack() as ph:
                psb = psbf(ph)
                b = Buf()
                F_ = NG
                names = ["are", "aim", "ldt", "ardt", "aidt", "lre", "lim", "fre", "fim", "t1", "t2", "t3"]
                tl = {n: psb("C" + n, [128, F_], F32) for n in names}
                fl = {n: tl[n][:, :] for n in names}
                Cc = psb("CC", [128, 2, 512], F32)
                Bc = psb("CB", [128, 2, 512], F32)
                eps = psb("Ceps", [128, 5, 8], F32)
                Dt = psb("CD", [128, NG], F32)
                msk = psb("Cmsk", [128, 3, 128], F32)
                for wi_, n in enumerate(["are", "aim", "ldt"]):
                    kb.dma(sp, c_s, tl[n][:, :], s5c_d[:, wi_, :], writes=[b])
                kb.dma(sp, c_s, Cc[:, :, :], s5cc_d[:, :, :], writes=[b])
                kb.dma(sp, c_s, Bc[:, :, :], s5cb_d[:, :, :], writes=[b])
                kb.dma(sp, c_s, eps[:, :, :], s5ce_d[:, :, :], writes=[b])
                kb.dma(sp, c_s, Dt[:, :], s5d_d[:, :], writes=[b])
                kb.dma(sp, c_s, msk[:, :, :], s5m_d[:, :, :], writes=[b])
                b.w = (c_s, c_s.cnt)
                R = [b]
                lam_and_f(fl["are"], fl["aim"], fl["ldt"], fl["ardt"], fl["aidt"], fl["lre"], fl["lim"],
                          fl["fre"], fl["fim"], fl["t1"], fl["t2"], fl["t3"], F_, b)
                V(lambda: nc.vector.tensor_scalar(out=fl["t1"], in0=fl["ardt"], scalar1=16.0, scalar2=None, op0=ALU.mult), R, R)
                V(lambda: nc.vector.tensor_scalar(out=fl["t2"], in0=fl["aidt"], scalar1=16.0, scalar2=None, op0=ALU.mult), R, R)
                cexp(fl["t1"], fl["t2"], fl["lre"], fl["lim"], fl["t3"], F_, b)
                for c_ in range(2):
                    for par_ in range(2):
                        oA = sap(scA, 0, 128, c_ * NG * 2 + par_, [[2, NG]])
                        oB = sap(scB, 0, 128, c_ * NG * 2 + par_, [[2, NG]])
                        V(lambda: nc.vector.tensor_copy(out=oA, in_=fl["lre"]), R, [scb])
                        if c_ == 0:
                            V(lambda: nc.vector.tensor_scalar(out=oB, in0=fl["lim"], scalar1=-1.0, scalar2=None, op0=ALU.mult),
                              R, [scb])
                        else:
                            V(lambda: nc.vector.tensor_copy(out=oB, in_=fl["lim"]), R, [scb])
                pw = {}
                G8 = NG * 8
                xr = psb("Cxr", [128, G8], F32)
                xi = psb("Cxi", [128, G8], F32)
                tt = psb("Ctt", [128, G8], F32)
                for ei, nm in enumerate(["C", "G", "A", "B", "B2"]):
                    pr = psb("Cp%sr" % nm, [128, G8], F32)
                    pi_ = psb("Cp%si" % nm, [128, G8], F32)
                    e_b = sap(eps, 0, 128, ei * 8, [[0, NG], [1, 8]])
                    V(lambda: nc.vector.tensor_tensor(out=sap(xr, 0, 128, 0, [[8, NG], [1, 8]]),
                                                      in0=sap(tl["ardt"], 0, 128, 0, [[1, NG], [0, 8]]), in1=e_b,
                                                      op=ALU.mult), R, R)
                    V(lambda: nc.vector.tensor_tensor(out=sap(xi, 0, 128, 0, [[8, NG], [1, 8]]),
                                                      in0=sap(tl["aidt"], 0, 128, 0, [[1, NG], [0, 8]]), in1=e_b,
                                                      op=ALU.mult), R, R)
                    cexp(xr[:, :], xi[:, :], pr[:, :], pi_[:, :], tt[:, :], G8, b)
                    pw[nm] = (pr, pi_)
                fr_b = sap(tl["fre"], 0, 128, 0, [[1, NG], [0, 8]])
                fi_b = sap(tl["fim"], 0, 128, 0, [[1, NG], [0, 8]])

                def g8(t_):
                    return sap(t_, 0, 128, 0, [[8, NG], [1, 8]])
                far = psb("Cfar", [128, G8], F32)
                fai = psb("Cfai", [128, G8], F32)
                cmul(g8(far), g8(fai), fr_b, fi_b, g8(pw["A"][0]), g8(pw["A"][1]), g8(xr), g8(xi), R, R)
                BIG = NG * 8 * 16
                t1b = psb("Ct1b", [128, BIG], F32)
                t2b = psb("Ct2b", [128, BIG], F32)
                Are = psb("CAre", [128, NG, 128], BF16)
                Aim = psb("CAim", [128, NG, 128], BF16)
                Gre = psb("CGre", [128, NG, 128], BF16)
                Gim = psb("CGim", [128, NG, 128], BF16)

                def bigv(t_):
                    return sap(t_, 0, 128, 0, [[128, NG], [16, 8], [1, 16]])

                def pwv(t_):
                    return sap(t_, 0, 128, 0, [[8, NG], [1, 8], [0, 16]])

                def cpv(t_, c):
                    return sap(t_, 0, 128, c * 512, [[16, NG], [0, 8], [1, 16]])
                cmul(sap(WC, 0, 128, 0, [[256, NG], [16, 8], [1, 16]]), sap(WC, 0, 128, 128, [[256, NG], [16, 8], [1, 16]]),
                     cpv(Cc, 0), cpv(Cc, 1), pwv(pw["C"][0]), pwv(pw["C"][1]), bigv(t1b), bigv(t2b), R, [b, WCb], neg_im=True)
                cmul(bigv(Gre), bigv(Gim), cpv(Cc, 0), cpv(Cc, 1), pwv(pw["G"][0]), pwv(pw["G"][1]), bigv(t1b), bigv(t2b),
                     R, R, neg_im=True)
                cmul(bigv(Are), bigv(Aim), pwv(far), pwv(fai), cpv(Bc, 0), cpv(Bc, 1), bigv(t1b), bigv(t2b), R, R)
                fbr = psb("Cfbr", [128, G8], F32)
                fbi = psb("Cfbi", [128, G8], F32)
                cmul(g8(fbr), g8(fbi), fr_b, fi_b, g8(pw["B"][0]), g8(pw["B"][1]), g8(xr), g8(xi), R, R)
                PBc = [psb("CPBre", [128, NG, 128], BF16), psb("CPBim", [128, NG, 128], BF16)]
                cmul(bigv(PBc[0]), bigv(PBc[1]), pwv(fbr), pwv(fbi), cpv(Bc, 0), cpv(Bc, 1), bigv(t1b), bigv(t2b), R, R)
                for g2 in range(NG // 2):
                    pw_, pwb_ = pb[2 + g2 % 2], pbb[2 + g2 % 2]
                    for gg in range(2):
                        for c in range(2):
                            blk = gg * 2 + c
                            kb.op(pe, lambda: nc.tensor.matmul(pw_[:, blk * 128:(blk + 1) * 128], lhsT=PBc[c][:, 2 * g2 + gg, :],
                                                               rhs=ident, start=True, stop=True),
                                  reads=[b, cstb], writes=[pwb_])
                    V(lambda: nc.vector.tensor_copy(out=sap(WB, 0, 128, 2 * g2 * 256, [[1, 512]]), in_=pw_[:, :]),
                      [pwb_], [WBb])
                cmul(g8(fbr), g8(fbi), fr_b, fi_b, g8(pw["B2"][0]), g8(pw["B2"][1]), g8(xr), g8(xi), R, R)
                cmul(bigv(PBc[0]), bigv(PBc[1]), pwv(fbr), pwv(fbi), cpv(Bc, 0), cpv(Bc, 1), bigv(t1b), bigv(t2b), R, R)
                wb2t = psb("Cwb2t", [128, NG * 512], BF16)
                wb2tb = Buf()
                V(lambda: nc.vector.memset(wb2t[:, :], 0.0), [], [wb2tb])
                for g2 in range(NG // 2):
                    pw_, pwb_ = pb[2 + g2 % 2], pbb[2 + g2 % 2]
                    for gg in range(2):
                        for c in range(2):
                            blk = gg * 2 + c
                            kb.op(pe, lambda: nc.tensor.matmul(pw_[:, blk * 128:(blk + 1) * 128], lhsT=PBc[c][:, 2 * g2 + gg, :],
                                                               rhs=ident, start=True, stop=True),
                                  reads=[b, cstb], writes=[pwb_])
                    V(lambda: nc.vector.tensor_copy(out=sap(wb2t, 0, 128, 2 * g2 * 512, [[256, 4], [1, 64]]),
                                                    in_=sap(pw_, 0, 128, 0, [[128, 4], [1, 64]])), [pwb_], [wb2tb])
                    V(lambda: nc.vector.tensor_copy(out=sap(wb2t, 0, 128, 2 * g2 * 512 + 128 + 64, [[256, 4], [1, 64]]),
                                                    in_=sap(pw_, 0, 128, 64, [[128, 4], [1, 64]])), [pwb_], [wb2tb])
                kb.dma(sp, c_scr, wb2_scr[:, :], wb2t[:, :], reads=[wb2tb], writes=[wb2_b])
                kb.wait_all(sp, [c_scr])
                tmpf = psb("Ctmpf", [128, 512], F32)
                for g4 in range(NG // 4):
                    pf, pfb = pb[0], pbb[0]
                    pq, pqb = pb[1], pbb[1]
                    for gg in range(4):
                        g = g4 * 4 + gg
                        for (P_, Pb_, lo) in ((pf, pfb, 0), (pq, pqb, 64)):
                            kb.op(pe, lambda: nc.tensor.matmul(P_[:, gg * 128:(gg + 1) * 128], lhsT=Are[lo:lo + 64, g, :],
                                                               rhs=Gre[lo:lo + 64, g, :], start=True, stop=False),
                                  reads=[b], writes=[Pb_])
                            kb.op(pe, lambda: nc.tensor.matmul(P_[:, gg * 128:(gg + 1) * 128], lhsT=Aim[lo:lo + 64, g, :],
                                                               rhs=Gim[lo:lo + 64, g, :], start=False, stop=True),
                                  reads=[b], writes=[Pb_])
                    mL = sap(msk, 0, 128, 0, [[0, 4], [1, 128]])
                    mU = sap(msk, 0, 128, 128, [[0, 4], [1, 128]])
                    t4 = sap(tmpf, 0, 128, 0, [[128, 4], [1, 128]])
                    V(lambda: nc.vector.tensor_tensor(out=t4, in0=sap(pf, 0, 128, 0, [[128, 4], [1, 128]]), in1=mL,
                                                      op=ALU.mult), [pfb, b], [b])
                    V(lambda: nc.vector.tensor_tensor(out=sap(t1b, 0, 128, 0, [[128, 4], [1, 128]]),
                                                      in0=sap(pq, 0, 128, 0, [[128, 4], [1, 128]]), in1=mU, op=ALU.mult),
                      [pqb, b], [b])
                    V(lambda: nc.vector.tensor_tensor(out=t4, in0=t4, in1=sap(t1b, 0, 128, 0, [[128, 4], [1, 128]]),
                                                      op=ALU.add), R, R)
                    for gg in range(4):
                        g = g4 * 4 + gg
                        V(lambda: nc.vector.scalar_tensor_tensor(out=WT[:, g, :], in0=msk[:, 2, :], scalar=Dt[:, g:g + 1],
                                                                 in1=tmpf[:, gg * 128:(gg + 1) * 128], op0=ALU.mult,
                                                                 op1=ALU.add), R, [b, WTb])
                kb.barrier()
        def mixer_items():
            items = []
            for mt2 in range(6):
                src = bass.AP(wx.tensor, 2 * mt2 * 128 * 1024, [[1024, 128], [128 * 1024, 2], [1, 1024]])
                items.append((src, 2048, True))
            for half in range(2):
                items.append((wv2[half, :, :], 2048))
            return items

        def krs_for(qg):
            out = []
            for kr in range(32):
                rows = [r for r in range(8 * qg, 8 * qg + 8)
                        if min(max(r - 4, 0), 24) <= kr <= min(max(r - 4, 0), 24) + 7]
                if rows:
                    out.append((kr, rows[0], rows[-1]))
            return out

        def inproj_phase(bl, qT, qTb, kT, kTb, vd, vdb):
            with ExitStack() as ph:
                psb = psbf(ph)
                xn2 = psb("xn2", [128, 8, T], BF16)
                xn2b = [[Buf() for c in range(4)] for k in range(8)]
                sq = [psb(f"sq{i}", [128, 512], BF16) for i in range(2)]
                scr = (sq, [Buf(), Buf()], pb[6], pbb[6], pb[6], pbb[6])
                chunks = main_chunks(xn2, xn2b, None, None)
                with_meta = (bl == 0)
                if with_meta:
                    xnm = psb("xn2m", [128, 8, NM], BF16)
                    xnmb = [Buf() for k in range(8)]
                    chunks.append(dict(N=NM, h=lambda k: hm[:, k, :], hb=lambda k: hmb[k],
                                       xn=lambda k: xnm[:, k, :], xnb=lambda k: xnmb[k]))
                for ch in chunks:
                    norm_chunk(ch, 1, scr)
                ust = [psb(f"ust{i}", [128, 8, 64], BF16) for i in range(2)]
                ustb = [Buf(), Buf()]
                ui = 0
                it = 0
                for mt in range(12):
                    if mt % 2 == 0:
                        ws, wb = stream.get()
                    wx0 = (mt % 2) * 1024
                    for ci, ch in enumerate(chunks):
                        N = ch["N"]
                        is_meta = (ci == 4)
                        if is_meta and 4 <= mt < 8:
                            continue
                        pS, pSb = pb[it % 4], pbb[it % 4]
                        it += 1
                        for k in range(8):
                            kb.op(pe, lambda: nc.tensor.matmul(pS[:, :N], lhsT=ws[:, wx0 + k * 128:wx0 + (k + 1) * 128], rhs=ch["xn"](k),
                                                               start=(k == 0), stop=(k == 7)),
                                  reads=[wb, ch["xnb"](k)], writes=[pSb])
                        if mt < 4:
                            if is_meta:
                                o = sap(umT, 0, 128, mt * 16, [[2, 8], [1, 2]])
                                i_ = sap(pS, 0, 128, 0, [[1, 8], [8, 2]])
                                V(lambda: nc.vector.tensor_copy(out=o, in_=i_), [pSb], [umTb])
                            else:
                                us_, usb_ = ust[ui % 2], ustb[ui % 2]
                                ui += 1
                                i_ = sap(pS, 0, 128, 0, [[1, 8], [8, 64]])
                                V(lambda: nc.vector.tensor_copy(out=us_[:, :, :], in_=i_), [pSb], [usb_])
                                kb.dma(sp, c_scr, u_scr[mt * 128:(mt + 1) * 128, :, 2 + 64 * ci:2 + 64 * ci + 64], us_[:, :, :],
                                       reads=[usb_], writes=[u_scr_b])
                        elif mt < 8:
                            A(lambda: nc.scalar.copy(out=qT[:, mt - 4, ci * 512:(ci + 1) * 512], in_=pS[:, :N]),
                              [pSb], [qTb[mt - 4][ci]])
                        else:
                            if is_meta:
                                A(lambda: nc.scalar.copy(out=kmT[:, mt - 8, :], in_=pS[:, :N]), [pSb], [kmTb])
                            else:
                                A(lambda: nc.scalar.copy(out=kT[:, mt - 8, ci * 512:(ci + 1) * 512], in_=pS[:, :N]),
                                  [pSb], [kTb[mt - 8]])
                    if mt < 4:
                        kb.dma(sp, c_scr, u_scr[mt * 128:(mt + 1) * 128, :, 0:2], umT[:, mt, :, :], reads=[umTb],
                               writes=[u_scr_b])
                for j in range(8):
                    srcA = bass.AP(u_scr.tensor, j * NBLK, [[8 * NBLK, 16], [16 * 8 * NBLK, NG], [1, NBLK]])
                    dstA = bass.AP(u_scr2.tensor, j * 16 * NG * NBLK, [[NG * NBLK, 16], [NBLK, NG], [1, NBLK]])
                    kb.dma(sp, c_scr, dstA, srcA, reads=[u_scr_b], writes=[u2b[j]])
                for j in range(8):
                    u2b[j].w = (c_scr, c_scr.cnt)
                for half in range(2):
                    ws, wb = stream.get()
                    for w in range(-1, 32):
                        pS, pSb = pb[it % 4], pbb[it % 4]
                        it += 1
                        if w == -1:
                            t0, M, p0 = 0, 64, 64
                        elif w == 31:
                            t0, M, p0 = 64 * 31, 64, 0
                        else:
                            t0, M, p0 = 64 * w, 128, 0
                        rb_ = [xn2b[0][t0 // 512], xn2b[0][(t0 + M - 1) // 512]]
                        for k in range(8):
                            kb.op(pe, lambda: nc.tensor.matmul(pS[p0:p0 + M, 0:256], lhsT=xn2[:, k, t0:t0 + M],
                                                               rhs=ws[:, k * 256:(k + 1) * 256], start=(k == 0), stop=(k == 7)),
                                  reads=[wb, xn2b[k][t0 // 512], xn2b[k][(t0 + M - 1) // 512]], writes=[pSb])
                        if w >= 0:
                            V(lambda: nc.vector.tensor_copy(out=vd[0:64, w, 2 * half:2 * half + 2, :],
                                                            in_=sap(pS, 0, 64, 0, [[128, 2], [1, 64]])), [pSb], [vdb])
                        if w < 31:
                            A(lambda: nc.scalar.copy(out=vd[64:128, w + 1, 2 * half:2 * half + 2, :],
                                                     in_=sap(pS, 64, 64, 64, [[128, 2], [1, 64]])), [pSb], [vdb])
                    if with_meta:
                        pS, pSb = pb[it % 4], pbb[it % 4]
                        it += 1
                        for p0 in (0, 64):
                            for k in range(8):
                                kb.op(pe, lambda: nc.tensor.matmul(pS[p0:p0 + NM, 0:256], lhsT=xnm[:, k, :],
                                                                   rhs=ws[:, k * 256:(k + 1) * 256], start=(k == 0),
                                                                   stop=(k == 7)),
                                      reads=[wb, xnmb[k]], writes=[pSb])
                        V(lambda: nc.vector.tensor_copy(out=vm[0:NM, 2 * half:2 * half + 2, :],
                                                        in_=sap(pS, 0, NM, 0, [[128, 2], [1, 64]])), [pSb], [vmb])
                        V(lambda: nc.vector.tensor_copy(out=vm[64:64 + NM, 2 * half:2 * half + 2, :],
                                                        in_=sap(pS, 64, NM, 64, [[128, 2], [1, 64]])), [pSb], [vmb])
                kb.barrier()

        def na_phase(bl, qT, qTb, kT, kTb, vd, vdb, oT, oTb):
            LA = 3
            NSB = 4
            with ExitStack() as ph:
                psb = psbf(ph)
                exs = [psb(f"nex{i}", [128, 512], BF16) for i in range(NSB)]
                exb = [Buf() for i in range(NSB)]
                Ps = [psb(f"nP{i}", [128, 512], BF16) for i in range(NSB)]
                Pb = [Buf() for i in range(NSB)]
                pms = [psb(f"npm{i}", [128, 512], BF16) for i in range(2)]
                pmb = [Buf(), Buf()]
                rec = psb("nrec", [128, 512], F32)
                recb = Buf()
                for i in range(2):
                    V(lambda: nc.vector.memset(pms[i][:, :], 0.0), [], [pmb[i]])
                tiles_ = []
                gi_ = 0
                for hp in range(4):
                    for qg in range(4):
                        lst = krs_for(qg)
                        tiles_.append(dict(kind="meta", hp=hp, qg=qg, gi=gi_, first=True, last=False))
                        for idx, (kr, ra, rb2) in enumerate(lst):
                            tiles_.append(dict(kind="kr", hp=hp, qg=qg, gi=gi_, kr=kr, ra=ra, rb=rb2, first=False,
                                               last=(idx == len(lst) - 1)))
                        gi_ += 1

                def emit_S(t, tl_):
                    hp, qg = tl_["hp"], tl_["qg"]
                    pS, pSb = pb[t % NSB], pbb[t % NSB]
                    if tl_["kind"] == "meta":
                        tok0 = 512 * qg
                        for lo in (0, 64):
                            kb.op(pe, lambda: nc.tensor.matmul(pS[lo:lo + NM, 0:512], lhsT=kmT[lo:lo + 64, hp, :],
                                                               rhs=qT[lo:lo + 64, hp, tok0:tok0 + 512], start=True, stop=True),
                                  reads=[kmTb, qTb[hp][qg]], writes=[pSb])
                    else:
                        kr, ra, rb2 = tl_["kr"], tl_["ra"], tl_["rb"]
                        N = 64 * (rb2 - ra + 1)
                        for lo in (0, 64):
                            kb.op(pe, lambda: nc.tensor.matmul(pS[lo:lo + 64, 0:N], lhsT=kT[lo:lo + 64, hp, 64 * kr:64 * kr + 64],
                                                               rhs=qT[lo:lo + 64, hp, 64 * ra:64 * (rb2 + 1)],
                                                               start=True, stop=True),
                                  reads=[kTb[hp], qTb[hp][qg]], writes=[pSb])

                def emit_E(t, tl_):
                    hp, qg = tl_["hp"], tl_["qg"]
                    pS, pSb = pb[t % NSB], pbb[t % NSB]
                    if tl_["kind"] == "meta":
                        pm, pmb_ = pms[tl_["gi"] % 2], pmb[tl_["gi"] % 2]
                        for lo in (0, 64):
                            A(lambda: nc.scalar.activation(out=pm[lo:lo + NM, :], in_=pS[lo:lo + NM, 0:512], func=AF.Exp,
                                                           scale=0.125), [pSb], [pmb_])
                    else:
                        kr, ra, rb2 = tl_["kr"], tl_["ra"], tl_["rb"]
                        N = 64 * (rb2 - ra + 1)
                        ex_, exb_ = exs[t % NSB], exb[t % NSB]
                        P_, Pb_ = Ps[t % NSB], Pb[t % NSB]
                        A(lambda: nc.scalar.activation(out=ex_[:, 0:N], in_=pS[:, 0:N], func=AF.Exp, scale=0.125),
                          [pSb], [exb_])
                        rho = ra - kr + 7
                        V(lambda: nc.vector.tensor_tensor(out=P_[:, 0:N], in0=ex_[:, 0:N],
                                                          in1=sap(Etab, 0, 128, (hp * 15 + rho) * 64, [[1, N]]),
                                                          op=ALU.mult), [exb_, Etabb], [Pb_])

                def emit_PV(t, tl_):
                    hp, qg, g_ = tl_["hp"], tl_["qg"], tl_["gi"]
                    pO, pOb = pb[4 + g_ % 2], pbb[4 + g_ % 2]
                    pD, pDb = pb[6 + g_ % 2], pbb[6 + g_ % 2]
                    if tl_["kind"] == "meta":
                        pm, pmb_ = pms[g_ % 2], pmb[g_ % 2]
                        for lo in (0, 64):
                            kb.op(pe, lambda: nc.tensor.matmul(pO[lo:lo + 64, :], lhsT=vm[lo:lo + NM, hp, :],
                                                               rhs=pm[lo:lo + NM, :], start=True, stop=False),
                                  reads=[vmb, pmb_], writes=[pOb])
                        kb.op(pe, lambda: nc.tensor.matmul(pD[:, :], lhsT=blockones, rhs=pm[:, :], start=True, stop=False),
                              reads=[cstb, pmb_], writes=[pDb])
                    else:
                        kr, ra, rb2 = tl_["kr"], tl_["ra"], tl_["rb"]
                        N = 64 * (rb2 - ra + 1)
                        c0 = 64 * (ra - 8 * qg)
                        last = tl_["last"]
                        P_, Pb_ = Ps[t % NSB], Pb[t % NSB]
                        for lo in (0, 64):
                            kb.op(pe, lambda: nc.tensor.matmul(pO[lo:lo + 64, c0:c0 + N], lhsT=vd[lo:lo + 64, kr, hp, :],
                                                               rhs=P_[lo:lo + 64, 0:N], start=False, stop=last),
                                  reads=[vdb, Pb_], writes=[pOb])
                        kb.op(pe, lambda: nc.tensor.matmul(pD[:, c0:c0 + N], lhsT=blockones, rhs=P_[:, 0:N],
                                                           start=False, stop=last),
                              reads=[cstb, Pb_], writes=[pDb])
                        if last:
                            tok0 = 512 * qg
                            V(lambda: nc.vector.reciprocal(out=rec[:, :], in_=pD[:, :]), [pDb], [recb])
                            V(lambda: nc.vector.tensor_tensor(out=oT[:, hp, tok0:tok0 + 512], in0=pO[:, :], in1=rec[:, :],
                                                              op=ALU.mult), [pOb, recb], [oTb[hp][qg]])

                nt = len(tiles_)
                for t in range(min(LA, nt)):
                    emit_S(t, tiles_[t])
                for t in range(nt):
                    emit_E(t, tiles_[t])
                    if t + LA < nt:
                        emit_S(t + LA, tiles_[t + LA])
                    emit_PV(t, tiles_[t])
                kb.barrier()

        def s5_phase(bl, zT, zTb):
            with ExitStack() as ph:
                psb = psbf(ph)
                def ubg(g):
                    return sap(zT, 0, 128, g * NBLK, [[1, NBLK]])

                def ubrows(j):
                    return sap(zT, 16 * j, 16, 0, [[NBLK, NG], [1, NBLK]])
                ubb = [Buf() for g in range(NG)]
                Vs = psb("Vs", [128, 2, NG, NBLK], BF16)
                Vsb = Buf()
                Ssb = Buf()
                X = [psb(f"X{i}", [128, 4, NG], F32) for i in range(2)]
                Xb = [Buf(), Buf()]
                t1 = psb("st1", [128, 4, NG], F32)
                t2 = psb("st2", [128, 4, NG], F32)
                tb = Buf()
                kb.dma(sp, c_scr, sap(zT, 0, 128, 0, [[NBLK, NG], [1, NBLK]]), u_scr2[:, :, :], reads=u2b, writes=ubb)
                for g in range(NG):
                    ubb[g].w = (c_scr, c_scr.cnt)
                wb2s = [psb(f"wb2_{i}", [128, 4 * 512], BF16) for i in range(2)]
                wb2bs = [Buf(), Buf()]

                def load_wb2(bk):
                    kb.dma(sp, c_scr if bk % 2 == 0 else c_misc, wb2s[bk % 2][:, :], wb2_scr[:, bk * 2048:(bk + 1) * 2048],
                           reads=[wb2_b], writes=[wb2bs[bk % 2]])
                load_wb2(0)

                def ubs(g, a, b_):
                    return sap(zT, 0, 128, g * NBLK + a, [[1, b_ - a]])
                for g in range(NG):
                    if g % 4 == 0 and g // 4 + 1 < NG // 4:
                        load_wb2(g // 4 + 1)
                    wb2, wb2b = wb2s[(g // 4) % 2], wb2bs[(g // 4) % 2]
                    pR_, pRb_ = pb[(2 * g) % 4], pbb[(2 * g) % 4]
                    pI_, pIb_ = pb[(2 * g + 1) % 4], pbb[(2 * g + 1) % 4]
                    for c, (P_, Pb_) in enumerate(((pR_, pRb_), (pI_, pIb_))):
                        w2o = (g % 4) * 512 + c * 256
                        kb.op(pe, lambda: nc.tensor.matmul(P_[:, 0:NBLK], lhsT=WB[:, g, c, :], rhs=ubg(g), start=True, stop=False),
                              reads=[WBb, ubb[g]], writes=[Pb_])
                        kb.op(pe, lambda: nc.tensor.matmul(P_[:, 1:NBLK], lhsT=wb2[:, w2o:w2o + 128], rhs=ubs(g, 0, NBLK - 1),
                                                           start=False, stop=False),
                              reads=[wb2b, ubb[g]], writes=[Pb_])
                        kb.op(pe, lambda: nc.tensor.matmul(P_[:, 1:NBLK - 1], lhsT=wb2[:, w2o + 128:w2o + 256], rhs=ubs(g, 2, NBLK),
                                                           start=False, stop=True),
                              reads=[wb2b, ubb[g]], writes=[Pb_])
                        V(lambda: nc.vector.tensor_copy(out=Vs[0:64, c, g, :], in_=P_[0:64, 0:NBLK]), [Pb_], [Vsb])
                        V(lambda: nc.vector.tensor_copy(out=Vs[64:128, c, g, :], in_=sap(P_, 64, 64, NBLK - 1, [[-1, NBLK]])),
                          [Pb_], [Vsb])
                V(lambda: nc.vector.memset(X[1][:, :, :], 0.0), [], [Xb[1]])
                F4 = 4 * NG
                for i in range(NBLK // 2):
                    Xp, Xpb = X[(i + 1) % 2], Xb[(i + 1) % 2]
                    Xn, Xnb = X[i % 2], Xb[i % 2]
                    sw = sap(Xp, 0, 128, 2 * NG, [[-2 * NG, 2], [1, 2 * NG]])
                    V(lambda: nc.vector.tensor_tensor(out=sap(t1, 0, 128, 0, [[1, F4]]), in0=sap(scA, 0, 128, 0, [[1, F4]]),
                                                      in1=sap(Xp, 0, 128, 0, [[1, F4]]), op=ALU.mult), [scb, Xpb], [tb])
                    V(lambda: nc.vector.tensor_tensor(out=sap(t2, 0, 128, 0, [[2 * NG, 2], [1, 2 * NG]]),
                                                      in0=sap(scB, 0, 128, 0, [[2 * NG, 2], [1, 2 * NG]]), in1=sw, op=ALU.mult),
                      [scb, Xpb], [tb])
                    V(lambda: nc.vector.tensor_tensor(out=sap(t1, 0, 128, 0, [[1, F4]]), in0=sap(t1, 0, 128, 0, [[1, F4]]),
                                                      in1=sap(t2, 0, 128, 0, [[1, F4]]), op=ALU.add), [tb], [tb])
                    vi = sap(Vs, 0, 128, 2 * i, [[NBLK, 2 * NG], [1, 2]])
                    x2 = sap(Xn, 0, 128, 0, [[2, 2 * NG], [1, 2]])
                    t1v = sap(t1, 0, 128, 0, [[2, 2 * NG], [1, 2]])
                    V(lambda: nc.vector.tensor_tensor(out=x2, in0=t1v, in1=vi, op=ALU.add), [tb, Vsb], [Xnb])
                    A(lambda: nc.scalar.copy(out=vi, in_=x2), [Xnb], [Ssb])
                al = [psb(f"al{i}", [128, 2, NBLK], BF16) for i in range(2)]
                alb = [Buf(), Buf()]
                for i in range(2):
                    V(lambda: nc.vector.memset(al[i][:, :, :], 0.0), [], [alb[i]])
                g1s = [psb(f"yg1{i}", [128, NBLK], F32) for i in range(3)]
                g2s = [psb(f"yg2{i}", [128, NBLK], F32) for i in range(3)]
                g1b = [Buf() for i in range(3)]
                g2b = [Buf() for i in range(3)]

                def y_A(g):
                    pY, pYb = pb[4 + g % 3], pbb[4 + g % 3]
                    a_, ab_ = al[g % 2], alb[g % 2]
                    V(lambda: nc.vector.tensor_copy(out=a_[0:64, :, 1:NBLK],
                                                    in_=sap(Vs, 0, 64, g * NBLK, [[NG * NBLK, 2], [1, NBLK - 1]])),
                      [Ssb], [ab_])
                    V(lambda: nc.vector.tensor_copy(out=a_[64:128, :, 0:NBLK - 1],
                                                    in_=sap(Vs, 64, 64, g * NBLK + NBLK - 2, [[NG * NBLK, 2], [-1, NBLK - 1]])),
                      [Ssb], [ab_])
                    kb.op(pe, lambda: nc.tensor.matmul(pY[:, 0:NBLK], lhsT=WT[:, g, :], rhs=ubg(g), start=True, stop=False),
                          reads=[WTb, ubb[g]], writes=[pYb])
                    for c in range(2):
                        kb.op(pe, lambda: nc.tensor.matmul(pY[:, 0:NBLK], lhsT=WC[:, g, c, :], rhs=a_[:, c, :],
                                                           start=False, stop=(c == 1)),
                              reads=[WCb, ab_], writes=[pYb])

                def y_B1(g):
                    pY, pYb = pb[4 + g % 3], pbb[4 + g % 3]
                    g1, g2 = g1s[g % 3], g2s[g % 3]
                    A(lambda: nc.scalar.activation(out=g1[:, :], in_=pY[:, 0:NBLK], func=AF.Square), [pYb], [g1b[g % 3]])
                    V(lambda: nc.vector.tensor_scalar(out=g1[:, :], in0=g1[:, :], scalar1=0.07135481627, scalar2=1.5957691216,
                                                      op0=ALU.mult, op1=ALU.add), [g1b[g % 3]], [g1b[g % 3]])
                    V(lambda: nc.vector.tensor_tensor(out=g2[:, :], in0=g1[:, :], in1=pY[:, 0:NBLK], op=ALU.mult),
                      [g1b[g % 3], pYb], [g2b[g % 3]])

                def y_B2(g):
                    pY, pYb = pb[4 + g % 3], pbb[4 + g % 3]
                    g2 = g2s[g % 3]
                    A(lambda: nc.scalar.activation(out=g2[:, :], in_=g2[:, :], func=AF.Sigmoid), [g2b[g % 3]], [g2b[g % 3]])
                    V(lambda: nc.vector.tensor_tensor(out=ubg(g), in0=g2[:, :], in1=pY[:, 0:NBLK], op=ALU.mult),
                      [g2b[g % 3], pYb], [ubb[g]])
                for t_ in range(NG + 2):
                    if t_ < NG:
                        y_A(t_)
                    if 1 <= t_ <= NG:
                        y_B1(t_ - 1)
                    if t_ >= 2:
                        y_B2(t_ - 2)
                z2b = Buf()
                kb.dma(sp, c_scr, z_scr2[:, :, :], sap(zT, 0, 128, 0, [[NBLK, NG], [1, NBLK]]), reads=ubb + zsb, writes=[z2b])
                for j in range(8):
                    srcZ = bass.AP(z_scr2.tensor, j * 16 * NG * NBLK, [[NG * NBLK, 16], [NBLK, NG], [1, NBLK]])
                    dstZ = bass.AP(z_scr.tensor, j * NBLK, [[8 * NBLK, 16], [16 * 8 * NBLK, NG], [1, NBLK]])
                    kb.dma(sp, c_scr, dstZ, srcZ, reads=[z2b], writes=[zsb[j]])
                for j in range(8):
                    zsb[j].w = (c_scr, c_scr.cnt)
                for mt in range(4):
                    kb.dma(sp, c_scr, zT[:, mt, :, :], z_scr[mt * 128:(mt + 1) * 128, :, :], reads=zsb, writes=zTb + ubb)
                for b_ in zTb:
                    b_.w = (c_scr, c_scr.cnt)
                kb.barrier()

        def merge_items():
            items = []
            for half in range(2):
                for mo in range(8):
                    items.append((wgg_d[mo, :, :], 2048))
                    items.append((wbb_d[mo, :, :], 1024))
                for mo2 in range(4):
                    items.append((wo_d[mo2, :, :], 2048))
            return items

        c_wgl = kb.ctr("s_wgl")

        def merge_phase(bl, zT, zTb, oT, oTb):
            with ExitStack() as ph:
                psb = psbf(ph)
                wgl = psb("wgl", [128, 4, 512], BF16)
                wglb = Buf()
                for mt in range(4):
                    kb.dma(pool, c_wgl, wgl[:, mt, :], wglu_d[mt, :, :], writes=[wglb])
                wglb.w = (c_wgl, c_wgl.cnt)
                xn2 = psb("mxn2", [128, 8, 1024], BF16)
                mg = psb("mg", [128, 8, 1024], BF16)
                sq = [psb(f"msq{i}", [128, 512], BF16) for i in range(2)]
                sd = psb("msd", [128, 512], F32)
                rstd = psb("mrstd", [128, 512], F32)
                scr = (sq, [Buf(), Buf()], sd, Buf(), rstd, Buf())
                s1 = [psb(f"ms1{i}", [128, 512], BF16) for i in range(2)]
                s2 = [psb(f"ms2{i}", [128, 512], BF16) for i in range(2)]
                m1 = [psb(f"mm1{i}", [128, 512], BF16) for i in range(2)]
                sb1, sb2, mb1 = [Buf(), Buf()], [Buf(), Buf()], [Buf(), Buf()]

                def zperm(t_, mt, c):
                    return sap(t_, 0, 128, mt * 8 * NBLK + 2 + 64 * c, [[NBLK, 8], [1, 64]])
                for c in range(4):
                    for mt in range(4):
                        pS, pSb = pb[(c % 2) * 4 + mt], pbb[(c % 2) * 4 + mt]
                        for k in range(4):
                            kb.op(pe, lambda: nc.tensor.matmul(pS[:, :], lhsT=wgl[:, mt, k * 128:(k + 1) * 128], rhs=zperm(zT, k, c),
                                                               start=(k == 0), stop=(k == 3)),
                                  reads=[wglb, zTb[c]], writes=[pSb])
                    for mt in range(4):
                        pS, pSb = pb[(c % 2) * 4 + mt], pbb[(c % 2) * 4 + mt]
                        A(lambda: nc.scalar.activation(out=s1[mt % 2][:, :], in_=pS[:, :], func=AF.Sigmoid), [pSb], [sb1[mt % 2]])
                        V(lambda: nc.vector.tensor_tensor(out=zperm(zT, mt, c), in0=zperm(zT, mt, c),
                                                          in1=sap(s1[mt % 2], 0, 128, 0, [[64, 8], [1, 64]]), op=ALU.mult),
                          [sb1[mt % 2], zTb[c]], [zTb[c]])
                for half in range(2):
                    xnb = [[Buf() for c in range(2)] for k in range(8)]
                    mgb = [[Buf() for c in range(2)] for k in range(8)]
                    chs = []
                    for c2 in range(2):
                        c = 2 * half + c2
                        chs.append(dict(N=512, h=(lambda k, c=c: hT[:, k, c * 512:(c + 1) * 512]), hb=(lambda k, c=c: hb[k][c]),
                                        xn=(lambda k, c2=c2: xn2[:, k, c2 * 512:(c2 + 1) * 512]),
                                        xnb=(lambda k, c2=c2: xnb[k][c2])))
                    for ch in chs:
                        norm_chunk(ch, 1, scr)
                    for mo in range(8):
                        wgg, wggb = stream.get()
                        for c2, ch in enumerate(chs):
                            p1, p1b = pb[c2], pbb[c2]
                            for k in range(8):
                                kb.op(pe, lambda: nc.tensor.matmul(p1[:, :], lhsT=wgg[:, k * 128:(k + 1) * 128], rhs=ch["xn"](k),
                                                                   start=(k == 0), stop=(k == 7)),
                                      reads=[wggb, ch["xnb"](k)], writes=[p1b])
                            A(lambda: nc.scalar.activation(out=s1[c2][:, :], in_=p1[:, :], func=AF.Sigmoid), [p1b], [sb1[c2]])
                        for c2, ch in enumerate(chs):
                            p2, p2b = pb[4 + c2], pbb[4 + c2]
                            for k in range(8):
                                kb.op(pe, lambda: nc.tensor.matmul(p2[:, :], lhsT=wgg[:, 1024 + k * 128:1024 + (k + 1) * 128],
                                                                   rhs=ch["xn"](k), start=(k == 0), stop=(k == 7)),
                                      reads=[wggb, ch["xnb"](k)], writes=[p2b])
                            A(lambda: nc.scalar.activation(out=s2[c2][:, :], in_=p2[:, :], func=AF.Sigmoid), [p2b], [sb2[c2]])
                        wbb, wbbb = stream.get()
                        for c2, ch in enumerate(chs):
                            c = 2 * half + c2
                            p3, p3b = pb[2 + c2], pbb[2 + c2]
                            for k in range(4):
                                kb.op(pe, lambda: nc.tensor.matmul(p3[:, :], lhsT=wbb[:, k * 128:(k + 1) * 128], rhs=zperm(zT, k, c),
                                                                   start=(k == 0), stop=(k == 3)),
                                      reads=[wbbb, zTb[c]], writes=[p3b])
                            V(lambda: nc.vector.tensor_tensor(out=sap(m1[c2], 0, 128, 0, [[8, 64], [1, 8]]),
                                                              in0=sap(s1[c2], 0, 128, 0, [[8, 64], [1, 8]]),
                                                              in1=sap(p3, 0, 128, 0, [[1, 64], [64, 8]]), op=ALU.mult),
                              [sb1[c2], p3b], [mb1[c2]])
                        for c2, ch in enumerate(chs):
                            c = 2 * half + c2
                            p4, p4b = pb[6 + c2], pbb[6 + c2]
                            for k in range(4):
                                kb.op(pe, lambda: nc.tensor.matmul(p4[:, :], lhsT=wbb[:, 512 + k * 128:512 + (k + 1) * 128],
                                                                   rhs=oT[:, k, c * 512:(c + 1) * 512], start=(k == 0), stop=(k == 3)),
                                      reads=[wbbb, oTb[k][c]], writes=[p4b])
                            V(lambda: nc.vector.tensor_tensor(out=s2[c2][:, :], in0=s2[c2][:, :], in1=p4[:, :], op=ALU.mult),
                              [sb2[c2], p4b], [sb2[c2]])
                            V(lambda: nc.vector.tensor_tensor(out=mg[:, mo, c2 * 512:(c2 + 1) * 512], in0=m1[c2][:, :],
                                                              in1=s2[c2][:, :], op=ALU.add), [mb1[c2], sb2[c2]], [mgb[mo][c2]])
                    for mo in range(8):
                        if mo % 2 == 0:
                            ws, wb = stream.get()
                        wo0 = (mo % 2) * 1024
                        for c2 in range(2):
                            c = 2 * half + c2
                            pO, pOb = pb[(2 * mo + c2) % 8], pbb[(2 * mo + c2) % 8]
                            for k in range(8):
                                kb.op(pe, lambda: nc.tensor.matmul(pO[:, :], lhsT=ws[:, wo0 + k * 128:wo0 + (k + 1) * 128],
                                                                   rhs=mg[:, k, c2 * 512:(c2 + 1) * 512], start=(k == 0), stop=(k == 7)),
                                      reads=[wb, mgb[k][c2]], writes=[pOb])
                            hk = hT[:, mo, c * 512:(c + 1) * 512]
                            V(lambda: nc.vector.tensor_tensor(out=hk, in0=pO[:, :], in1=hk, op=ALU.add),
                              [pOb, hb[mo][c]], [hb[mo][c]])
                kb.barrier()

        def mixer(bl):
            with ExitStack() as mx:
                psb = psbf(mx)
                qT = psb("qoT", [128, 4, T], BF16)
                qTb = [[Buf() for c in range(4)] for i in range(4)]
                qTall = [b_ for r_ in qTb for b_ in r_]
                with ExitStack() as m2:
                    psb2 = psbf(m2)
                    kT = psb2("kT", [128, 4, T], BF16)
                    vd = psb2("vd", [128, 32, 4, 64], BF16)
                    kTb = [Buf() for i in range(4)]
                    vdb = Buf()
                    inproj_phase(bl, qT, qTb, kT, kTb, vd, vdb)
                    if dbg and bl == 0:
                        dump("qT", qT[:, :, :], [128, 4, T], qTall)
                        dump("kT", kT[:, :, :], [128, 4, T], kTb)
                        dump("vd", vd[:, :, :, :], [128, 32, 4, 64], [vdb])
                        kb.barrier()
                        kb.wait_all(sp, [c_out])
                        kb.wait_all(pe, [c_out])
                    if stage >= 7:
                        stream.extend(merge_items())
                    if stage >= 5:
                        na_phase(bl, qT, qTb, kT, kTb, vd, vdb, qT, qTb)
                        if dbg and bl == 0:
                            dump("oT", qT[:, :, :], [128, 4, T], qTall)
                            for E_ in (pe, act, dve, pool):
                                kb.wait_all(E_, [c_out])
                zT = psb("zT", [128, 4, 8, NBLK], BF16)
                zTb = [Buf() for c in range(4)]
                if stage >= 6:
                    s5_phase(bl, zT, zTb)
                    if dbg and bl == 0:
                        dump("zT", zT[:, :, :, :], [128, 4, 8, NBLK], zTb)
                        for E_ in (pe, act, dve, pool):
                            kb.wait_all(E_, [c_out])
                if stage >= 7:
                    merge_phase(bl, zT, zTb, qT, qTb)

        allhb = [hb[k][c] for k in range(8) for c in range(4)]
        if stage >= 2:
            setup_na()
        if stage >= 3:
            setup_s5()
        hT = sb("hT", [128, 8, T], F32)
        hm = sb("hm", [128, 8, NM], F32)
        stream = Stream(kb, nc, es, NSLOT, 2048)
        c_hm = kb.ctr("s_hm")
        kb.dma(sp, c_hm, hm[:, :, :], metaT[:, :, :], writes=hmb)
        for bl in range(nb_local):
            load_x(bl)
            stream.extend(ffn_items(w1i, w1o))
            if stage >= 4:
                stream.extend(mixer_items())
            ffn_phase(bl, 0, with_meta=(bl == 0))
            if dbg and bl == 0:
                dump("h1", hT[:, :, :], [128, 8, T], allhb)
                dump("hm1", hm[:, :, :], [128, 8, NM], hmb)
            if stage >= 4:
                mixer(bl)
                if dbg and bl == 0:
                    dump("h2", hT[:, :, :], [128, 8, T], allhb)
            stream.extend(ffn_items(w2i, w2o))
            ffn_phase(bl, 2, with_meta=False)
            final_phase(bl)
        kb.wait_all(sp, [c_out])
    return nc, dbg_outs


def host_inputs(inputs, core, nb_local=NB_LOCAL):
    f = np.float32
    x = inputs["x"]
    b0 = core * nb_local
    xT = np.ascontiguousarray(
        x[b0:b0 + nb_local].transpose(0, 2, 1).reshape(nb_local, 8, 128, T)).astype(f)
    metaT = np.ascontiguousarray(inputs["meta_tokens"].T.reshape(8, 128, NM).transpose(1, 0, 2)).astype(f)
    g = np.stack([inputs["norm_ffn1"][0], inputs["norm_mix"][0], inputs["norm_ffn2"][0], inputs["norm_final"]])
    gains = np.ascontiguousarray(g.reshape(4, 8, 128).transpose(2, 0, 1)).astype(f)

    def ffn_in(w):
        w = w.reshape(8, 128, 2, NFF, 128)
        return np.ascontiguousarray(w.transpose(3, 1, 0, 2, 4).reshape(NFF, 128, 8 * 256)).astype(f)

    def ffn_out(w):
        w = w.reshape(NFF, 128, 8, 128)
        return np.ascontiguousarray(w.transpose(2, 1, 0, 3).reshape(8, 128, NFF * 128)).astype(f)

    consts = np.zeros((128, 3, 128), f)
    consts[:, 0, :] = np.eye(128)
    consts[:, 1, :] = 1.0
    consts[:64, 2, :64] = 1.0
    consts[64:, 2, 64:] = 1.0
    w_in = inputs["w_in"][0]

    def tiles(w, nk, nm):
        return np.ascontiguousarray(w.reshape(nk, 128, nm, 128).transpose(2, 1, 0, 3).reshape(nm, 128, nk * 128)).astype(f)
    wx_ = tiles(w_in, 8, 32)
    wv = w_in[:, 1536:2048].reshape(8, 128, 2, 256)
    wv2 = np.ascontiguousarray(wv.transpose(2, 1, 0, 3).reshape(2, 128, 2048)).astype(f)
    rpb = inputs["na_rpb"][0]
    rpbpad = np.zeros((8, 15, 128), f)
    rpbpad[:, :, 48:48 + 31] = rpb[:, ::-1, :]
    qc = np.arange(64)
    cs = np.clip(qc - 8, 0, 48)
    kc = np.arange(64)
    cm = ((kc[:, None] >= cs[None, :]) & (kc[:, None] < cs[None, :] + 16)).astype(f)
    colmask = np.concatenate([cm, cm], axis=0)
    are = np.stack([inputs["ssm_a_re_fwd"][0], inputs["ssm_a_re_bwd"][0]])
    aim = np.stack([inputs["ssm_a_im_fwd"][0], inputs["ssm_a_im_bwd"][0]])
    ldt = np.stack([inputs["ssm_log_dt_fwd"][0], inputs["ssm_log_dt_bwd"][0]])
    ldt_gn = np.broadcast_to(ldt[:, :, None], (2, 32, 64))
    Bre, Bim = inputs["ssm_b_re"][0], inputs["ssm_b_im"][0]
    Cre, Cim = inputs["ssm_c_re"][0], inputs["ssm_c_im"][0]

    def layC2(a):
        return a.transpose(0, 2, 1).reshape(128, 32)
    s5c = np.ascontiguousarray(np.stack([layC2(are), layC2(aim), layC2(ldt_gn)], axis=1)).astype(f)

    def layCC(Cm):
        t = Cm.transpose(2, 0, 1).reshape(64, 512)
        return np.concatenate([t, t], axis=0)

    def layCB(Bm):
        t = Bm.transpose(1, 0, 2).reshape(64, 512)
        return np.concatenate([t, t], axis=0)
    s5cc = np.ascontiguousarray(np.stack([layCC(Cre), layCC(Cim)], axis=1)).astype(f)
    s5cb = np.ascontiguousarray(np.stack([layCB(Bre), layCB(Bim)], axis=1)).astype(f)
    j8 = np.arange(8)
    s5ce = np.zeros((128, 5, 8), f)
    s5ce[:64, 3] = 7 - j8
    s5ce[64:, 3] = j8
    s5ce[:64, 4] = 15 - j8
    s5ce[64:, 4] = j8 + 8
    s5ce[:64, 0] = j8 + 1
    s5ce[:64, 1] = j8
    s5ce[:64, 2] = -j8
    s5ce[64:, 0] = 8 - j8
    s5ce[64:, 1] = -j8
    s5ce[64:, 2] = j8
    Dm = inputs["ssm_d"][0]
    s5d = np.ascontiguousarray(np.broadcast_to(Dm.T[None], (8, 16, 32)).reshape(128, 32)).astype(f)
    ii = np.repeat(np.arange(8), 16)
    s5m = np.stack([(ii[:, None] <= ii[None, :]), (ii[:, None] >= ii[None, :]), np.eye(128, dtype=bool)], axis=1).astype(f)
    extra = {
        "wx": wx_, "wv2": wv2, "wglu": tiles(inputs["w_glu"][0], 4, 4), "wbs": tiles(inputs["w_branch_ssm"][0], 4, 8),
        "wbn": tiles(inputs["w_branch_na"][0], 4, 8),
        "wo": np.ascontiguousarray(tiles(inputs["w_out"][0], 8, 8).reshape(4, 2, 128, 1024).transpose(0, 2, 1, 3).reshape(4, 128, 2048)),
        "wgg": np.ascontiguousarray(np.concatenate([wx_[16:24], wx_[24:32]], axis=2)),
        "wbb": np.ascontiguousarray(np.concatenate([tiles(inputs["w_branch_ssm"][0], 4, 8), tiles(inputs["w_branch_na"][0], 4, 8)], axis=2)),
        "rpbpad": rpbpad, "colmask": colmask, "s5c": s5c, "s5cc": s5cc,
        "s5cb": s5cb, "s5ce": s5ce, "s5d": s5d, "s5m": np.ascontiguousarray(s5m),
    }
    return {
        **extra,
        "xT": xT, "metaT": metaT, "gains": gains,
        "w1i": ffn_in(inputs["w_ffn1_in"][0]), "w1o": ffn_out(inputs["w_ffn1_out"][0]),
        "w2i": ffn_in(inputs["w_ffn2_in"][0]), "w2o": ffn_out(inputs["w_ffn2_out"][0]),
        "consts": consts,
    }


def kernel(**inputs):
    inputs = {k: np.asarray(v) for k, v in inputs.items()}
    nc, _ = build_program()
    in_maps = [host_inputs(inputs, c) for c in range(NCORES)]
    res = run_bass_kernel_spmd(nc, in_maps, core_ids=list(range(NCORES)))
    out = np.empty((16, T, D), np.float32)
    for c in range(NCORES):
        o = np.asarray(res.results[c]["outT"])
        out[c * NB_LOCAL:(c + 1) * NB_LOCAL] = o.reshape(NB_LOCAL, D, T).transpose(0, 2, 1)
    return out
```
